# Optimizing a Trainium2 kernel written in Bass

```python
import math
import jax
import jax.numpy as jnp
from jax import lax
import numpy as np

D_MODEL = 1024
BATCH = 8
SEQ = 8192
DEPTH = 2
DEC_BATCH = 16
DEC_SEQ = 64
PAST_LEN = 4096

CHUNK = 64
PLE_DIM = 256
N_EVEN = (DEPTH + 1) // 2
N_ODD = DEPTH // 2
EPS = 1e-6
F32 = jnp.float32
A_WIDTH = 512
A_GROUPS = 4
A_GROUP_DIM = A_WIDTH // A_GROUPS
MLP_CHUNK = 128
B_WIDTH = 512
B_GROUP_DIM = 16
B_GROUPS = B_WIDTH // B_GROUP_DIM
S5_STATE = 64
SCAN_BLOCK = 128
C_HEADS = 8
C_HEAD_DIM = 64
C_WIDTH = C_HEADS * C_HEAD_DIM
SB_BLOCK = 128
D_HEADS = 4
D_KEY_DIM = 128
D_VAL_DIM = 128
D_WIDTH = D_HEADS * D_VAL_DIM
CONV_W = 4
D_CONV_CH = 2 * D_HEADS * D_KEY_DIM + D_WIDTH
EVEN_IN = 3 * A_WIDTH + 2 * B_WIDTH
EVEN_MIX = A_WIDTH + B_WIDTH
ODD_IN = 4 * C_WIDTH + D_CONV_CH + D_WIDTH + 2 * D_HEADS
ODD_MIX = C_WIDTH + D_WIDTH

kernel_name = 'hybrid_streaming_encoder_step'


def rmsnorm(x, g):
    xf = x.astype(F32)
    var = jnp.mean(xf * xf, axis=-1, keepdims=True)
    return (xf * lax.rsqrt(var + EPS) * g.astype(F32)).astype(x.dtype)


def layernorm(x, g, b):
    xf = x.astype(F32)
    mu = jnp.mean(xf, axis=-1, keepdims=True)
    xc = xf - mu
    var = jnp.mean(xc * xc, axis=-1, keepdims=True)
    return (xc * lax.rsqrt(var + EPS) * g.astype(F32) + b.astype(F32)).astype(x.dtype)


def l2norm(x):
    xf = x.astype(F32)
    return xf * lax.rsqrt(jnp.sum(xf * xf, axis=-1, keepdims=True) + EPS)


def chunk_mlp_mix(u, v, w_s, b_s):
    bsz, L, _ = v.shape
    c = min(L, MLP_CHUNK)
    nc = L // c
    vg = v.reshape(bsz, nc, c, A_GROUPS, A_GROUP_DIM)
    mask = jnp.tril(jnp.ones((c, c), dtype=bool))
    w = jnp.where(mask, w_s[:, :c, :c], 0.0).astype(v.dtype)
    s = jnp.einsum('gts,bnsgd->bntgd', w, vg) + b_s[:, :c].T[None, None, :, :, None].astype(v.dtype)
    return u * s.reshape(bsz, L, A_WIDTH)


def s5_discretize(lam_re, lam_im, log_dt, b_re, b_im):
    dt = jnp.exp(log_dt.astype(F32))[:, None]
    lr, li = lam_re.astype(F32), lam_im.astype(F32)
    mag = jnp.exp(lr * dt)
    ar, ai = mag * jnp.cos(li * dt), mag * jnp.sin(li * dt)
    den = lr * lr + li * li
    cr = ((ar - 1.0) * lr + ai * li) / den
    ci = (ai * lr - (ar - 1.0) * li) / den
    br, bi = b_re.astype(F32), b_im.astype(F32)
    bbr = cr[..., None] * br - ci[..., None] * bi
    bbi = cr[..., None] * bi + ci[..., None] * br
    return ar, ai, bbr, bbi


def complex_affine_combine(e1, e2):
    ar1, ai1, br1, bi1 = e1
    ar2, ai2, br2, bi2 = e2
    return (ar2 * ar1 - ai2 * ai1, ar2 * ai1 + ai2 * ar1,
            ar2 * br1 - ai2 * bi1 + br2, ar2 * bi1 + ai2 * br1 + bi2)


def s5_mix(u, x0_re, x0_im, lam_re, lam_im, log_dt, b_re, b_im, c_re, c_im, d_skip):
    bsz, L, _ = u.shape
    ar, ai, bbr, bbi = s5_discretize(lam_re, lam_im, log_dt, b_re, b_im)
    blk = min(L, SCAN_BLOCK)
    nb = L // blk
    ug = u.astype(F32).reshape(bsz, nb, blk, B_GROUPS, B_GROUP_DIM).swapaxes(0, 1)
    a_re = jnp.broadcast_to(ar, (bsz, blk, B_GROUPS, S5_STATE))
    a_im = jnp.broadcast_to(ai, (bsz, blk, B_GROUPS, S5_STATE))
    cr, ci, dk = c_re.astype(F32), c_im.astype(F32), d_skip.astype(F32)

    def block(carry, ub):
        xr0, xi0 = carry
        bur = jnp.einsum('btgp,gnp->btgn', ub, bbr)
        bui = jnp.einsum('btgp,gnp->btgn', ub, bbi)
        pr, pi, lr_, li_ = lax.associative_scan(complex_affine_combine, (a_re, a_im, bur, bui), axis=1)
        xr = pr * xr0[:, None] - pi * xi0[:, None] + lr_
        xi = pr * xi0[:, None] + pi * xr0[:, None] + li_
        y = jnp.einsum('btgn,gpn->btgp', xr, cr) - jnp.einsum('btgn,gpn->btgp', xi, ci) + dk * ub
        return (xr[:, -1], xi[:, -1]), y

    (xr, xi), y = lax.scan(block, (x0_re.astype(F32), x0_im.astype(F32)), ug)
    y = y.swapaxes(0, 1).reshape(bsz, L, B_WIDTH)
    return y.astype(u.dtype), xr, xi


def stick_breaking_attention(q, k, v, q_offset):
    bsz, L, H, d = q.shape
    S = k.shape[1]
    blk = min(L, SB_BLOCK)
    nb = L // blk
    qb = q.reshape(bsz, nb, blk, H, d).transpose(1, 0, 3, 2, 4)
    kt = k.transpose(0, 2, 1, 3)
    vt = v.transpose(0, 2, 1, 3)
    kpos = jnp.arange(S)
    scale = 1.0 / math.sqrt(d)

    def one_block(args):
        j, qj = args
        qpos = q_offset + j * blk + jnp.arange(blk)
        z = jnp.einsum('bhtd,bhsd->bhts', qj, kt).astype(F32) * scale
        mask = kpos[None, :] < qpos[:, None]
        log_beta = jax.nn.log_sigmoid(z)
        log_keep = jnp.where(mask, log_beta - z, 0.0)
        after = lax.cumsum(log_keep, axis=3, reverse=True) - log_keep
        w = jnp.where(mask, jnp.exp(log_beta + after), 0.0)
        return jnp.einsum('bhts,bhsd->bhtd', w.astype(vt.dtype), vt)

    o = lax.map(one_block, (jnp.arange(nb), qb))
    return o.transpose(1, 0, 3, 2, 4).reshape(bsz, L, H * d)


def causal_conv(x, buf, w):
    L = x.shape[1]
    xp = jnp.concatenate([buf.astype(x.dtype), x], axis=1)
    y = xp[:, 0:L] * w[0]
    for i in range(1, CONV_W):
        y = y + xp[:, i:i + L] * w[i]
    return y, xp[:, -(CONV_W - 1):]


def gdn_chunk(S, q, k, v, g, beta):
    c = q.shape[-2]
    causal = jnp.tril(jnp.ones((c, c), dtype=bool))
    strict = jnp.tril(jnp.ones((c, c), dtype=bool), k=-1)
    G = jnp.cumsum(g, axis=-1)
    decay = jnp.exp(jnp.where(causal, G[..., :, None] - G[..., None, :], -jnp.inf))
    kk = jnp.einsum('bhtd,bhsd->bhts', k, k)
    tri = jnp.where(strict, beta[..., :, None] * decay * kk, 0.0) + jnp.eye(c, dtype=F32)
    rhs = beta[..., None] * (v - jnp.exp(G)[..., None] * jnp.einsum('bhtd,bhde->bhte', k, S))
    delta = lax.linalg.triangular_solve(tri, rhs, left_side=True, lower=True, unit_diagonal=True)
    qk = jnp.einsum('bhtd,bhsd->bhts', q, k) * decay
    o = jnp.exp(G)[..., None] * jnp.einsum('bhtd,bhde->bhte', q, S) + jnp.einsum('bhts,bhse->bhte', qk, delta)
    g_last = G[..., -1:]
    S_new = jnp.exp(g_last)[..., None] * S + jnp.einsum('bhsd,bhse->bhde', k * jnp.exp(g_last - G)[..., None], delta)
    return S_new, o


def gated_delta_mix(q, k, v, g, beta, S0):
    bsz, L = q.shape[:2]
    c = min(L, CHUNK)
    nc = L // c

    def split(x):
        x = x.reshape((bsz, nc, c) + x.shape[2:])
        return jnp.moveaxis(jnp.moveaxis(x, 1, 0), 2, 3)

    S, o = lax.scan(lambda s, xs: gdn_chunk(s, *xs), S0, (split(q), split(k), split(v), split(g), split(beta)))
    o = jnp.moveaxis(jnp.moveaxis(o, 3, 2), 0, 1).reshape(bsz, L, D_HEADS, D_VAL_DIM)
    return o, S


def run_trunk(x, p, past_b_re, past_b_im, past_k_c, past_v_c, past_s_d, past_conv_d,
              norm_g, final_norm_g, ple_proj, ple_gate_w, ple_norm_g,
              even_w_in, even_w_out, a_ln_g, a_ln_b, a_w_s, a_b_s,
              b_lam_re, b_lam_im, b_log_dt, b_B_re, b_B_im, b_C_re, b_C_im, b_D, b_glu_w,
              odd_w_in, odd_w_out, d_conv_w, d_A_log, d_dt_bias, d_norm_g):
    bsz, L, _ = x.shape
    h = x
    nb_re, nb_im, na_v, nk, nv, ns, nconv = [], [], [], [], [], [], []
    for i in range(DEPTH):
        j = i // 2
        hn = rmsnorm(h, norm_g[i])
        if i % 2 == 0:
            proj = hn @ even_w_in[j]
            ua, va, za, ub, zb = jnp.split(
                proj, [A_WIDTH, 2 * A_WIDTH, 3 * A_WIDTH, 3 * A_WIDTH + B_WIDTH], axis=-1)
            ua = jax.nn.gelu(ua)
            va = layernorm(jax.nn.gelu(va), a_ln_g[j], a_ln_b[j])
            a_out = chunk_mlp_mix(ua, va, a_w_s[j], a_b_s[j]) * jax.nn.silu(za)
            if past_b_re is None:
                x0r = jnp.zeros((bsz, B_GROUPS, S5_STATE), F32)
                x0i = jnp.zeros((bsz, B_GROUPS, S5_STATE), F32)
            else:
                x0r, x0i = past_b_re[j], past_b_im[j]
            yb, xr, xi = s5_mix(ub, x0r, x0i, b_lam_re[j], b_lam_im[j], b_log_dt[j],
                                b_B_re[j], b_B_im[j], b_C_re[j], b_C_im[j], b_D[j])
            yb = jax.nn.gelu(yb)
            yb = yb * jax.nn.sigmoid(yb @ b_glu_w[j])
            b_out = yb * jax.nn.silu(zb)
            mix = jnp.concatenate([a_out, b_out], axis=-1) @ even_w_out[j]
            na_v.append(va)
            nb_re.append(xr)
            nb_im.append(xi)
        else:
            o1 = 4 * C_WIDTH
            o2 = o1 + D_CONV_CH
            o3 = o2 + D_WIDTH
            o4 = o3 + D_HEADS
            proj = hn @ odd_w_in[j]
            qc, kc, vc, zc, qkv_d, zd, a_d, b_d = jnp.split(
                proj, [C_WIDTH, 2 * C_WIDTH, 3 * C_WIDTH, o1, o2, o3, o4], axis=-1)
            qc = qc.reshape(bsz, L, C_HEADS, C_HEAD_DIM)
            kc = kc.reshape(bsz, L, C_HEADS, C_HEAD_DIM)
            vc = vc.reshape(bsz, L, C_HEADS, C_HEAD_DIM)
            if past_k_c is None:
                k_all, v_all, offset = kc, vc, 0
            else:
                pk, pv = past_k_c[j], past_v_c[j]
                k_all = jnp.concatenate([pk.astype(kc.dtype), kc], axis=1)
                v_all = jnp.concatenate([pv.astype(vc.dtype), vc], axis=1)
                offset = pk.shape[1]
            c_out = stick_breaking_attention(qc, k_all, v_all, offset) * jax.nn.silu(zc)
            if past_conv_d is None:
                buf = jnp.zeros((bsz, CONV_W - 1, D_CONV_CH), x.dtype)
                S0 = jnp.zeros((bsz, D_HEADS, D_KEY_DIM, D_VAL_DIM), F32)
            else:
                buf = past_conv_d[j]
                S0 = past_s_d[j].astype(F32)
            qkv_d, conv_new = causal_conv(qkv_d, buf, d_conv_w[j])
            qkv_d = jax.nn.silu(qkv_d)
            qd, kd, vd = jnp.split(qkv_d, [D_HEADS * D_KEY_DIM, 2 * D_HEADS * D_KEY_DIM], axis=-1)
            qd = l2norm(qd.reshape(bsz, L, D_HEADS, D_KEY_DIM)) * (D_KEY_DIM ** -0.5)
            kd = l2norm(kd.reshape(bsz, L, D_HEADS, D_KEY_DIM))
            vd = vd.reshape(bsz, L, D_HEADS, D_VAL_DIM).astype(F32)
            g = -jnp.exp(d_A_log[j].astype(F32)) * jax.nn.softplus(a_d.astype(F32) + d_dt_bias[j].astype(F32))
            beta = jax.nn.sigmoid(b_d.astype(F32))
            o_d, S = gated_delta_mix(qd, kd, vd, g, beta, S0)
            o_d = rmsnorm(o_d, d_norm_g[j]) * jax.nn.silu(zd.reshape(bsz, L, D_HEADS, D_VAL_DIM).astype(F32))
            d_out = o_d.reshape(bsz, L, D_WIDTH).astype(x.dtype)
            mix = jnp.concatenate([c_out, d_out], axis=-1) @ odd_w_out[j]
            nk.append(kc)
            nv.append(vc)
            ns.append(S)
            nconv.append(conv_new)
        h = h + mix
        gate = jax.nn.sigmoid(rmsnorm(h, ple_norm_g[i]) @ ple_gate_w[i])
        h = h + gate * (p[i] @ ple_proj[i])
    y = rmsnorm(h, final_norm_g)
    return (y, jnp.stack(nb_re), jnp.stack(nb_im), jnp.stack(na_v), jnp.stack(nk), jnp.stack(nv),
            jnp.stack(ns), jnp.stack(nconv))


def setup_inputs(seed: int = 0) -> dict:
    key = jax.random.key(seed)
    ks = iter(jax.random.split(key, 48))

    def nrm(shape, scale):
        return jax.random.normal(next(ks), shape, F32) * scale

    def unif(shape, lo, hi):
        return jax.random.uniform(next(ks), shape, F32, lo, hi)

    dt_d = jnp.exp(unif((N_ODD, D_HEADS), math.log(1e-3), math.log(1e-1)))
    return {
        'x_prompt': nrm((BATCH, SEQ, D_MODEL), 1.0),
        'x_sample': nrm((DEC_BATCH, DEC_SEQ, D_MODEL), 1.0),
        'state_b_re': nrm((N_EVEN, DEC_BATCH, B_GROUPS, S5_STATE), 0.5),
        'state_b_im': nrm((N_EVEN, DEC_BATCH, B_GROUPS, S5_STATE), 0.5),
        'cache_k_c': nrm((N_ODD, DEC_BATCH, PAST_LEN, C_HEADS, C_HEAD_DIM), 1.0),
        'cache_v_c': nrm((N_ODD, DEC_BATCH, PAST_LEN, C_HEADS, C_HEAD_DIM), 1.0),
        'state_d': nrm((N_ODD, DEC_BATCH, D_HEADS, D_KEY_DIM, D_VAL_DIM), 0.3),
        'state_conv_d': nrm((N_ODD, DEC_BATCH, CONV_W - 1, D_CONV_CH), 1.0),
        'p_prompt': nrm((DEPTH, BATCH, SEQ, PLE_DIM), 1.0),
        'p_sample': nrm((DEPTH, DEC_BATCH, DEC_SEQ, PLE_DIM), 1.0),
        'norm_g': 1.0 + nrm((DEPTH, D_MODEL), 0.02),
        'final_norm_g': 1.0 + nrm((D_MODEL,), 0.02),
        'ple_proj': nrm((DEPTH, PLE_DIM, D_MODEL), 0.5 * PLE_DIM ** -0.5),
        'ple_gate_w': nrm((DEPTH, D_MODEL, D_MODEL), D_MODEL ** -0.5),
        'ple_norm_g': 1.0 + nrm((DEPTH, D_MODEL), 0.02),
        'even_w_in': nrm((N_EVEN, D_MODEL, EVEN_IN), D_MODEL ** -0.5),
        'even_w_out': nrm((N_EVEN, EVEN_MIX, D_MODEL), 0.5 * EVEN_MIX ** -0.5),
        'a_ln_g': 1.0 + nrm((N_EVEN, A_WIDTH), 0.02),
        'a_ln_b': nrm((N_EVEN, A_WIDTH), 0.02),
        'a_w_s': nrm((N_EVEN, A_GROUPS, MLP_CHUNK, MLP_CHUNK), MLP_CHUNK ** -0.5),
        'a_b_s': 1.0 + nrm((N_EVEN, A_GROUPS, MLP_CHUNK), 0.1),
        'b_lam_re': -0.5 + nrm((N_EVEN, B_GROUPS, S5_STATE), 0.01),
        'b_lam_im': math.pi * jnp.arange(S5_STATE, dtype=F32) + nrm((N_EVEN, B_GROUPS, S5_STATE), 0.01),
        'b_log_dt': unif((N_EVEN, B_GROUPS), math.log(1e-3), math.log(1e-1)),
        'b_B_re': nrm((N_EVEN, B_GROUPS, S5_STATE, B_GROUP_DIM), (2 * B_GROUP_DIM) ** -0.5),
        'b_B_im': nrm((N_EVEN, B_GROUPS, S5_STATE, B_GROUP_DIM), (2 * B_GROUP_DIM) ** -0.5),
        'b_C_re': nrm((N_EVEN, B_GROUPS, B_GROUP_DIM, S5_STATE), S5_STATE ** -0.5),
        'b_C_im': nrm((N_EVEN, B_GROUPS, B_GROUP_DIM, S5_STATE), S5_STATE ** -0.5),
        'b_D': nrm((N_EVEN, B_GROUPS, B_GROUP_DIM), 0.5),
        'b_glu_w': nrm((N_EVEN, B_WIDTH, B_WIDTH), B_WIDTH ** -0.5),
        'odd_w_in': nrm((N_ODD, D_MODEL, ODD_IN), D_MODEL ** -0.5),
        'odd_w_out': nrm((N_ODD, ODD_MIX, D_MODEL), 0.5 * ODD_MIX ** -0.5),
        'd_conv_w': nrm((N_ODD, CONV_W, D_CONV_CH), CONV_W ** -0.5),
        'd_A_log': jnp.log(unif((N_ODD, D_HEADS), 1.0, 16.0)),
        'd_dt_bias': dt_d + jnp.log(-jnp.expm1(-dt_d)),
        'd_norm_g': 1.0 + nrm((N_ODD, D_VAL_DIM), 0.02),
    }


def reference(x_prompt, x_sample, state_b_re, state_b_im, cache_k_c, cache_v_c, state_d, state_conv_d,
              p_prompt, p_sample,
              norm_g, final_norm_g, ple_proj, ple_gate_w, ple_norm_g,
              even_w_in, even_w_out, a_ln_g, a_ln_b, a_w_s, a_b_s,
              b_lam_re, b_lam_im, b_log_dt, b_B_re, b_B_im, b_C_re, b_C_im, b_D, b_glu_w,
              odd_w_in, odd_w_out, d_conv_w, d_A_log, d_dt_bias, d_norm_g):
    y_prompt, b_re_p, b_im_p, _a_v_p, k_c_p, v_c_p, s_d_p, conv_d_p = run_trunk(
        x_prompt, p_prompt, None, None, None, None, None, None,
        norm_g, final_norm_g, ple_proj, ple_gate_w, ple_norm_g,
        even_w_in, even_w_out, a_ln_g, a_ln_b, a_w_s, a_b_s,
        b_lam_re, b_lam_im, b_log_dt, b_B_re, b_B_im, b_C_re, b_C_im, b_D, b_glu_w,
        odd_w_in, odd_w_out, d_conv_w, d_A_log, d_dt_bias, d_norm_g)
    y_sample, b_re_s, b_im_s, a_v_s, k_c_s, v_c_s, s_d_s, conv_d_s = run_trunk(
        x_sample, p_sample, state_b_re, state_b_im, cache_k_c, cache_v_c, state_d, state_conv_d,
        norm_g, final_norm_g, ple_proj, ple_gate_w, ple_norm_g,
        even_w_in, even_w_out, a_ln_g, a_ln_b, a_w_s, a_b_s,
        b_lam_re, b_lam_im, b_log_dt, b_B_re, b_B_im, b_C_re, b_C_im, b_D, b_glu_w,
        odd_w_in, odd_w_out, d_conv_w, d_A_log, d_dt_bias, d_norm_g)
    return (y_prompt, y_sample,
            b_re_p, b_im_p, k_c_p, v_c_p, s_d_p, conv_d_p,
            b_re_s, b_im_s, a_v_s, k_c_s, v_c_s, s_d_s, conv_d_s)
```

```python
import contextlib
import math
import numpy as np
import concourse.bass as bass
import concourse.mybir as mybir
from concourse.bass_utils import run_bass_kernel_spmd

F32 = mybir.dt.float32
BF16 = mybir.dt.bfloat16
I32 = mybir.dt.int32
AF = mybir.ActivationFunctionType
ALU = mybir.AluOpType

NCORES = 8
D = 1024
SEQ = 8192
NTP = SEQ // 128
EPS = 1e-6
EPOCH = 3000
DMA_EPOCH = 200
DMA_SLOTS = 8
TWO_PI = 2.0 * math.pi


class Op:
    __slots__ = ("eng", "fn", "waits", "need_inc", "is_dma", "slot", "slot_val", "inc_no")

    def __init__(self, eng, fn, is_dma):
        self.eng = eng
        self.fn = fn
        self.waits = []
        self.need_inc = False
        self.is_dma = is_dma
        self.slot = None
        self.slot_val = None
        self.inc_no = None


class Sched:
    ENGS = ("pe", "act", "dve", "pool", "sp")

    def __init__(self, nc):
        self.nc = nc
        self.ops = {e: [] for e in self.ENGS}
        self.last_w = {}
        self.readers = {}

    max_ops = None
    n_added = 0
    ALIAS = {"xraw": ("ebuf", "lm"), "xc": ("cum", "wb0", "wb1"), "kf": ("gate",), "vf": ("gate",), "Usb": ("zcs",), "osb": ("cout",),
             "gB": ("Pm",), "PTm": ("D2",)}

    def add(self, eng, fn, reads=(), writes=(), is_dma=False):
        op = Op(eng, fn, is_dma)
        Sched.n_added += 1
        if Sched.max_ops is not None and Sched.n_added > Sched.max_ops:
            return op
        reads = [a for k in reads for a in Sched.ALIAS.get(k, (k,))]
        writes = [a for k in writes for a in Sched.ALIAS.get(k, (k,))]
        writes = list(writes) + [k for k in reads if k.startswith(("ps", "pt", "qs", "qt", "pz", "p2", "acc"))]
        deps = []
        for k in reads:
            w = self.last_w.get(k)
            if w is not None:
                deps.append(w)
        for k in writes:
            w = self.last_w.get(k)
            if w is not None:
                deps.append(w)
            deps.extend(self.readers.get(k, ()))
        seen = set()
        for d in deps:
            if d is op or id(d) in seen:
                continue
            seen.add(id(d))
            if (not d.is_dma) and (not is_dma) and d.eng == eng == "pe":
                continue
            op.waits.append(d)
            d.need_inc = True
        for k in reads:
            self.readers.setdefault(k, []).append(op)
        for k in writes:
            self.last_w[k] = op
            self.readers[k] = []
        self.ops[eng].append(op)
        return op

    def emit(self):
        nc = self.nc
        n_epochs = {}
        n_dma = {}
        for e in self.ENGS:
            c = 0
            for op in self.ops[e]:
                if (not op.is_dma) and op.need_inc:
                    op.inc_no = c
                    c += 1
            n_epochs[e] = max(1, (c + EPOCH - 1) // EPOCH)
            j = 0
            for op in self.ops[e]:
                if op.is_dma:
                    u = j // DMA_SLOTS
                    op.slot = (u // DMA_EPOCH) * DMA_SLOTS + j % DMA_SLOTS
                    op.slot_val = 16 * (u % DMA_EPOCH + 1)
                    j += 1
            n_dma[e] = j
        with contextlib.ExitStack() as st:
            sems = {e: [st.enter_context(nc.semaphore(f"s_{e}_{i}")) for i in range(n_epochs[e])]
                    for e in self.ENGS}
            dsems = {e: [st.enter_context(nc.semaphore(f"d_{e}_{i}"))
                         for i in range(DMA_SLOTS * ((n_dma[e] // DMA_SLOTS) // DMA_EPOCH + 1))]
                     for e in self.ENGS if n_dma[e] > 0}
            block = st.enter_context(nc.Block())

            def target(d):
                if d.is_dma:
                    return dsems[d.eng][d.slot], d.slot_val
                return sems[d.eng][d.inc_no // EPOCH], d.inc_no % EPOCH + 1

            def run(e, eng):
                waited = {}
                last_on_slot = {}
                for op in self.ops[e]:
                    ws = list(op.waits)
                    if op.is_dma and (op.slot % DMA_SLOTS) in last_on_slot:
                        ws.append(last_on_slot[op.slot % DMA_SLOTS])
                    for d in ws:
                        sem, val = target(d)
                        if waited.get(sem.num, 0) >= val:
                            continue
                        waited[sem.num] = val
                        eng.wait_ge(sem, val)
                    ins = op.fn(eng)
                    if op.is_dma:
                        ins.then_inc(dsems[e][op.slot], 16)
                        last_on_slot[op.slot % DMA_SLOTS] = op
                    elif op.need_inc:
                        ins.then_inc(sems[e][op.inc_no // EPOCH], 1)
                for d in last_on_slot.values():
                    sem, val = target(d)
                    eng.wait_ge(sem, val)

            block.tensor(lambda eng: run("pe", eng))
            block.scalar(lambda eng: run("act", eng))
            block.vector(lambda eng: run("dve", eng))
            block.gpsimd(lambda eng: run("pool", eng))
            block.sync(lambda eng: run("sp", eng))


class Rot:
    def __init__(self, bufs, name):
        self.bufs = bufs
        self.name = name
        self.i = 0

    def next(self):
        j = self.i % len(self.bufs)
        self.i += 1
        return self.bufs[j], f"{self.name}{j}"


def build_program(ntp=NTP, debug_h=False):
    nt = ntp + 1
    ntok = nt * 128
    nc = bass.Bass("TRN2", target_bir_lowering=False)

    def din(name, shape):
        return nc.dram_tensor(name, list(shape), F32, kind="ExternalInput").ap()

    def dout(name, shape):
        return nc.dram_tensor(name, list(shape), F32, kind="ExternalOutput").ap()

    xin = din("xin", [ntok, D])
    pin = din("pin", [2, ntok, 256])
    sbre = din("sbre", [2, 32, 64])
    sbim = din("sbim", [2, 32, 64])
    norm_g = din("norm_g", [2, D])
    final_norm_g = din("final_norm_g", [D])
    ple_proj = din("ple_proj", [2, 256, D])
    ple_gate_w = din("ple_gate_w", [2, D, D])
    ple_norm_g = din("ple_norm_g", [2, D])
    even_w_in = din("even_w_in", [1, D, 2560])
    even_w_out = din("even_w_out", [1, 1024, D])
    a_ln_g = din("a_ln_g", [1, 512])
    a_ln_b = din("a_ln_b", [1, 512])
    a_w_s = din("a_w_s", [1, 4, 128, 128])
    a_b_s = din("a_b_s", [1, 4, 128])
    b_lam_re = din("b_lam_re", [1, 32, 64])
    b_lam_im = din("b_lam_im", [1, 32, 64])
    b_log_dt = din("b_log_dt", [1, 32])
    b_B_re = din("b_B_re", [1, 32, 64, 16])
    b_B_im = din("b_B_im", [1, 32, 64, 16])
    b_C_re = din("b_C_re", [1, 32, 16, 64])
    b_C_im = din("b_C_im", [1, 32, 16, 64])
    b_D = din("b_D", [1, 32, 16])
    b_glu_w = din("b_glu_w", [1, 512, 512])
    odd_w_in = din("odd_w_in", [1, D, 4104])
    odd_w_out = din("odd_w_out", [1, 1024, D])
    cache_k = din("cache_k", [2, 4096, 512])
    cache_v = din("cache_v", [2, 4096, 512])
    state_d = din("state_d", [2, 4, 128, 128])
    state_conv = din("state_conv", [2, 3, 1536])
    d_conv_w = din("d_conv_w", [1, 4, 1536])
    d_A_log = din("d_A_log", [1, 4])
    d_dt_bias = din("d_dt_bias", [1, 4])
    d_norm_g = din("d_norm_g", [1, 128])

    o_y = dout("o_y", [ntok, D])
    o_bre = dout("o_bre", [3, 32, 64])
    o_bim = dout("o_bim", [3, 32, 64])
    o_av = dout("o_av", [128, 512])
    o_kc = dout("o_kc", [ntok, 512])
    o_vc = dout("o_vc", [ntok, 512])
    o_conv = dout("o_conv", [3, 3, 1536])
    o_sd = dout("o_sd", [3, 4, 128, 128])
    h1_scr = nc.dram_tensor("h1_scr", [ntok, D], F32, kind="ExternalOutput" if debug_h else "Internal").ap()

    dbg = nc.dram_tensor("dbg", [128, 512], F32, kind="ExternalOutput").ap() if debug_h else None
    S = Sched(nc)
    st = contextlib.ExitStack()
    with st:
        def sb(name, shape, dt=F32):
            return st.enter_context(nc.sbuf_tensor(name, list(shape), dt))

        def ps(name, shape, dt=F32):
            return st.enter_context(nc.psum_tensor(name, list(shape), dt))

        st.enter_context(nc.allow_non_contiguous_dma("small one-time parameter layout loads"))

        PS = Rot([ps(f"ps{i}", [128, 512]) for i in range(5)], "ps")
        py_bank = ps("ps_y", [128, 512])
        PT = Rot([ps(f"pt{i}", [128, 8, 128], BF16) for i in range(2)], "pt")

        H = Rot([sb(f"h{i}", [128, D]) for i in range(2)], "h")
        P0 = Rot([sb(f"p0_{i}", [128, 256]) for i in range(2)], "p0_")
        xn = sb("xn", [128, D], BF16)
        hnT = sb("hnT", [128, 8, 128], BF16)
        ssq = sb("ssq", [128, 1])
        rstd = sb("rstd", [128, 1])
        ua = sb("ua", [128, 512])
        va = sb("va", [128, 512])
        vln = sb("vln", [128, 512])
        vbf = sb("vbf", [128, 512], BF16)
        za = sb("za", [128, 512])
        zb = sb("zb", [128, 512])
        ubf = sb("ubf", [128, 512], BF16)
        ubd = sb("ubd", [128, 512])
        ubT = sb("ubT", [128, 4, 128], BF16)
        bst = sb("bst", [128, 6])
        bag = sb("bag", [128, 2])
        mixin = sb("mixin", [128, 1024], BF16)
        mixT = sb("mixT", [128, 8, 128], BF16)
        yb = sb("yb", [128, 512])
        yg = sb("yg", [128, 512])
        ygb = sb("ygb", [128, 512], BF16)
        ygT = sb("ygT", [128, 4, 128], BF16)
        gate = sb("gate", [128, D])
        pbf = sb("pbf", [128, 256], BF16)
        pT = sb("pT", [128, 2, 128], BF16)
        W5 = {n: sb("w5" + n, [128, 512]) for n in ("T1", "T2", "T3", "T4", "vr", "vi", "zr", "zi")}
        xrb = sb("xrb", [128, 4, 128], BF16)
        xib = sb("xib", [128, 4, 128], BF16)

        ident = sb("ident", [128, 128], BF16)
        S.add("pool", lambda e: e.memset(ident[:], 0.0), writes=["ident"])
        S.add("pool", lambda e: e.affine_select(out=ident[:], in_=ident[:], compare_op=ALU.not_equal, fill=1.0,
                                               base=0, pattern=[[-1, 128]], channel_multiplier=1),
              reads=["ident"], writes=["ident"])
        mhalf = sb("mhalf", [128, 16])
        S.add("pool", lambda e: e.memset(mhalf[:], -0.5), writes=["mhalf"])

        def bcast_load(name, src, n):
            t = sb(name, [128, n])
            S.add("sp", lambda e: e.dma_start(out=t[:], in_=src.partition_broadcast(128)), writes=[name], is_dma=True)
            return t

        g_l0 = bcast_load("g_l0", norm_g[0], D)
        g_p0 = bcast_load("g_p0", ple_norm_g[0], D)
        lng = bcast_load("lng", a_ln_g[0], 512)
        lnb = bcast_load("lnb", a_ln_b[0], 512)
        Db = bcast_load("Db", b_D[0].rearrange("g p -> (g p)"), 512)

        def wload(name, src, kt, n, csz=512):
            t = sb(name, [128, kt, n], BF16)
            v = src.rearrange("(k p) n -> p k n", p=128)
            for c0 in range(0, n, csz):
                c1 = min(n, c0 + csz)
                S.add("pool", lambda e, c0=c0, c1=c1: e.dma_start(out=t[:, :, c0:c1], in_=v[:, :, c0:c1]),
                      writes=[name], is_dma=True)
            return t

        w_in0 = wload("w_in0", even_w_in[0], 8, 2560)
        w_out0 = wload("w_out0", even_w_out[0], 8, 1024)
        w_glu = wload("w_glu", b_glu_w[0], 4, 512)
        w_gate0 = wload("w_gate0", ple_gate_w[0], 8, 1024)
        w_pp0 = wload("w_pp0", ple_proj[0], 2, 1024)

        wl = gate[:, 0:512].rearrange("p (a b) -> p a b", a=4)
        wls = gate[:, 512:1024].rearrange("p (a b) -> p a b", a=4)
        S.add("sp", lambda e: e.dma_start(out=wl, in_=a_w_s[0].rearrange("g t s -> t g s")), writes=["gate"], is_dma=True)
        S.add("pool", lambda e: e.memset(wls, 0.0), writes=["gate"])
        S.add("sp", lambda e: e.dma_start(out=wls[0:64, :, 0:64], in_=a_w_s[0, :, 0:64, 0:64].rearrange("g t s -> t g s")),
              reads=["gate"], writes=["gate"], is_dma=True)
        S.add("sp", lambda e: e.dma_start(out=wls[64:128, :, 64:128], in_=a_w_s[0, :, 0:64, 0:64].rearrange("g t s -> t g s")),
              reads=["gate"], writes=["gate"], is_dma=True)
        wmixT = sb("wmixT", [128, 4, 128], BF16)
        wmixTs = sb("wmixTs", [128, 4, 128], BF16)
        for (src, dst, nm) in ((wl, wmixT, "wl"), (wls, wmixTs, "wls")):
            wbf = (xn[:, 0:512] if nm == "wl" else xn[:, 512:1024]).rearrange("p (a b) -> p a b", a=4)
            S.add("pool", lambda e, src=src: e.affine_select(out=src, in_=src, compare_op=ALU.is_ge, fill=0.0, base=0,
                                                            pattern=[[0, 4], [-1, 128]], channel_multiplier=1),
                  reads=["gate"], writes=["gate"])
            S.add("dve", lambda e, src=src, wbf=wbf: e.tensor_copy(out=wbf, in_=src), reads=["gate"], writes=["xn"])
            pt, ptk = PT.next()
            for g in range(4):
                S.add("pe", lambda e, g=g, wbf=wbf, pt=pt: e.transpose(out=pt[:, g, :], in_=wbf[:, g, :], identity=ident[:]),
                      reads=["xn", "ident"], writes=[ptk])
            S.add("dve", lambda e, dst=dst, pt=pt: e.tensor_copy(out=dst[:], in_=pt[:, 0:4, :]), reads=[ptk], writes=[nm + "T"])
        bsb = sb("bsb", [128, 4])
        bsbs = sb("bsbs", [128, 4])
        S.add("sp", lambda e: e.dma_start(out=bsb[:], in_=a_b_s[0].rearrange("g t -> t g")), writes=["bsb"], is_dma=True)
        S.add("sp", lambda e: e.dma_start(out=bsbs[0:64, :], in_=a_b_s[0, :, 0:64].rearrange("g t -> t g")), writes=["bsbs"], is_dma=True)
        S.add("sp", lambda e: e.dma_start(out=bsbs[64:128, :], in_=a_b_s[0, :, 0:64].rearrange("g t -> t g")), writes=["bsbs"], is_dma=True)

        lamr = sb("lamr", [128, 16])
        lami = sb("lami", [128, 16])
        ldt = sb("ldt", [128, 16])
        S.add("sp", lambda e: e.dma_start(out=lamr[:], in_=b_lam_re[0].rearrange("(k g) n -> (g n) k", g=2)), writes=["lamr"], is_dma=True)
        S.add("sp", lambda e: e.dma_start(out=lami[:], in_=b_lam_im[0].rearrange("(k g) n -> (g n) k", g=2)), writes=["lami"], is_dma=True)
        ldv = b_log_dt[0].rearrange("(k g) -> g k", g=2)
        S.add("sp", lambda e: e.dma_start(out=ldt[0:64, :], in_=ldv[0].partition_broadcast(64)), writes=["ldt"], is_dma=True)
        S.add("sp", lambda e: e.dma_start(out=ldt[64:128, :], in_=ldv[1].partition_broadcast(64)), writes=["ldt"], is_dma=True)
        Bl_r = sb("Bl_r", [128, 16, 16])
        Bl_i = sb("Bl_i", [128, 16, 16])
        Cl_r = sb("Cl_r", [128, 16, 16])
        Cl_i = sb("Cl_i", [128, 16, 16])
        S.add("sp", lambda e: e.dma_start(out=Bl_r[:], in_=b_B_re[0].rearrange("(k g) n p -> (g n) k p", g=2)), writes=["Bl_r"], is_dma=True)
        S.add("sp", lambda e: e.dma_start(out=Bl_i[:], in_=b_B_im[0].rearrange("(k g) n p -> (g n) k p", g=2)), writes=["Bl_i"], is_dma=True)
        for (srcd, dstt, nm) in ((b_C_re, Cl_r, "Cl_r"), (b_C_im, Cl_i, "Cl_i")):
            for k in range(16):
                for g2 in range(2):
                    S.add("sp", lambda e, srcd=srcd, dstt=dstt, k=k, g2=g2: e.dma_start(
                        out=dstt[g2 * 64:(g2 + 1) * 64, k, :],
                        in_=srcd[0, 2 * k + g2].rearrange("p n -> n p")),
                        writes=[nm], is_dma=True)

        dts = sb("dts", [128, 16])
        ldr = sb("ldr", [128, 16])
        ldi = sb("ldi", [128, 16])
        rmag = sb("rmag", [128, 16])
        S.add("act", lambda e: e.activation(out=dts[:], in_=ldt[:], func=AF.Exp), reads=["ldt"], writes=["dts"])
        S.add("dve", lambda e: e.tensor_tensor(out=ldr[:], in0=lamr[:], in1=dts[:], op=ALU.mult), reads=["lamr", "dts"], writes=["ldr"])
        S.add("dve", lambda e: e.tensor_tensor(out=ldi[:], in0=lami[:], in1=dts[:], op=ALU.mult), reads=["lami", "dts"], writes=["ldi"])
        S.add("act", lambda e: e.activation(out=rmag[:], in_=ldr[:], func=AF.Exp), reads=["ldr"], writes=["rmag"])
        idx = sb("idx", [128, 128])
        S.add("pool", lambda e: e.iota(idx[:], pattern=[[1, 128]], base=1, channel_multiplier=0, allow_small_or_imprecise_dtypes=True), writes=["idx"])
        Rc_p = sb("Rc", [128, 16, 128])
        Rs_p = sb("Rs", [128, 16, 128])
        Rm = sb("Rm", [128, 16, 128])
        kRc, kRs, kRm = "Rc", "Rs", "Rm"
        sc_a = W5["T1"][:].rearrange("p (a b) -> p a b", a=4)
        sc_t = W5["T2"][:].rearrange("p (a b) -> p a b", a=4)
        sc_f = W5["T3"][:].rearrange("p (a b) -> p a b", a=4)
        sc_i = W5["T4"][:].bitcast(I32).rearrange("p (a b) -> p a b", a=4)
        for c4 in range(4):
            ksl = slice(4 * c4, 4 * c4 + 4)
            S.add("dve", lambda e, ksl=ksl: e.tensor_tensor(out=sc_a, in0=ldi[:, ksl].unsqueeze(2).broadcast_to([128, 4, 128]),
                                                           in1=idx[:].unsqueeze(1).broadcast_to([128, 4, 128]), op=ALU.mult),
                  reads=["ldi", "idx"], writes=["T1"])
            for R_, kR, off in ((Rc_p, "Rc", 0.25), (Rs_p, "Rs", 0.0)):
                S.add("dve", lambda e, off=off: e.tensor_scalar(out=sc_t, in0=sc_a, scalar1=1.0 / TWO_PI, scalar2=off,
                                                               op0=ALU.mult, op1=ALU.add), reads=["T1"], writes=["T2"])
                S.add("dve", lambda e: e.tensor_copy(out=sc_i, in_=sc_t), reads=["T2"], writes=["T4"])
                S.add("dve", lambda e: e.tensor_copy(out=sc_f, in_=sc_i), reads=["T4"], writes=["T3"])
                S.add("dve", lambda e: e.tensor_tensor(out=sc_t, in0=sc_t, in1=sc_f, op=ALU.subtract),
                      reads=["T2", "T3"], writes=["T2"])
                S.add("act", lambda e, R_=R_, ksl=ksl: e.activation(out=R_[:, ksl, :], in_=sc_t, func=AF.Sin, scale=6.283179),
                      reads=["T2"], writes=[kR])
        S.add("dve", lambda e: e.tensor_copy(out=Rm[:], in_=rmag[:].unsqueeze(2).broadcast_to([128, 16, 128])),
              reads=["rmag"], writes=["Rm"])

        ar1 = sb("ar1", [128, 16])
        ai = sb("ai_", [128, 16])
        den = sb("den", [128, 16])
        t1 = sb("t1_", [128, 16])
        t2 = sb("t2_", [128, 16])
        cre = sb("cre", [128, 16])
        cim = sb("cim", [128, 16])
        S.add("dve", lambda e: e.tensor_tensor(out=ar1[:], in0=rmag[:], in1=Rc_p[:, :, 0], op=ALU.mult), reads=["rmag", kRc], writes=["ar1"])
        S.add("dve", lambda e: e.tensor_scalar(out=ar1[:], in0=ar1[:], scalar1=-1.0, scalar2=None, op0=ALU.add), reads=["ar1"], writes=["ar1"])
        S.add("dve", lambda e: e.tensor_tensor(out=ai[:], in0=rmag[:], in1=Rs_p[:, :, 0], op=ALU.mult), reads=["rmag", kRs], writes=["ai"])
        S.add("dve", lambda e: e.tensor_tensor(out=den[:], in0=lamr[:], in1=lamr[:], op=ALU.mult), reads=["lamr"], writes=["den"])
        S.add("dve", lambda e: e.tensor_tensor(out=t1[:], in0=lami[:], in1=lami[:], op=ALU.mult), reads=["lami"], writes=["t1"])
        S.add("dve", lambda e: e.tensor_tensor(out=den[:], in0=den[:], in1=t1[:], op=ALU.add), reads=["den", "t1"], writes=["den"])
        S.add("dve", lambda e: e.reciprocal(out=den[:], in_=den[:]), reads=["den"], writes=["den"])
        S.add("dve", lambda e: e.tensor_tensor(out=t1[:], in0=ar1[:], in1=lamr[:], op=ALU.mult), reads=["ar1", "lamr", "den"], writes=["t1"])
        S.add("dve", lambda e: e.tensor_tensor(out=t2[:], in0=ai[:], in1=lami[:], op=ALU.mult), reads=["ai", "lami"], writes=["t2"])
        S.add("dve", lambda e: e.tensor_tensor(out=t1[:], in0=t1[:], in1=t2[:], op=ALU.add), reads=["t1", "t2"], writes=["t1"])
        S.add("dve", lambda e: e.tensor_tensor(out=cre[:], in0=t1[:], in1=den[:], op=ALU.mult), reads=["t1", "den"], writes=["cre"])
        S.add("dve", lambda e: e.tensor_tensor(out=t1[:], in0=ai[:], in1=lamr[:], op=ALU.mult), reads=["ai", "lamr", "cre"], writes=["t1"])
        S.add("dve", lambda e: e.tensor_tensor(out=t2[:], in0=ar1[:], in1=lami[:], op=ALU.mult), reads=["ar1", "lami"], writes=["t2"])
        S.add("dve", lambda e: e.tensor_tensor(out=t1[:], in0=t1[:], in1=t2[:], op=ALU.subtract), reads=["t1", "t2"], writes=["t1"])
        S.add("dve", lambda e: e.tensor_tensor(out=cim[:], in0=t1[:], in1=den[:], op=ALU.mult), reads=["t1", "den"], writes=["cim"])
        bb = {}
        u1 = sb("u1_", [128, 16, 16])
        u2 = sb("u2_", [128, 16, 16])
        creb = cre[:].unsqueeze(2).broadcast_to([128, 16, 16])
        cimb = cim[:].unsqueeze(2).broadcast_to([128, 16, 16])
        for nm, (a0, b0, a1, b1, op) in (("bbr", (creb, Bl_r, cimb, Bl_i, ALU.subtract)),
                                         ("bbi", (creb, Bl_i, cimb, Bl_r, ALU.add))):
            t = sb(nm, [128, 16, 16])
            S.add("dve", lambda e, a0=a0, b0=b0: e.tensor_tensor(out=u1[:], in0=b0[:], in1=a0, op=ALU.mult),
                  reads=["cre", "cim", "Bl_r", "Bl_i"], writes=["u1"])
            S.add("dve", lambda e, a1=a1, b1=b1: e.tensor_tensor(out=u2[:], in0=b1[:], in1=a1, op=ALU.mult),
                  reads=["cre", "cim", "Bl_r", "Bl_i"], writes=["u2"])
            S.add("dve", lambda e, t=t, op=op: e.tensor_tensor(out=t[:], in0=u1[:], in1=u2[:], op=op),
                  reads=["u1", "u2"], writes=[nm])
            bb[nm] = t
        BT = {}
        bpad = sb("bpad", [128, 16, 128], BF16)
        for nm in ("bbr", "bbi"):
            pad = bpad
            S.add("pool", lambda e, pad=pad: e.memset(pad[:], 0.0), writes=["bpad"])
            for k in range(16):
                for g2 in range(2):
                    off = ((2 * k + g2) % 8) * 16
                    S.add("dve", lambda e, pad=pad, k=k, g2=g2, off=off, nm=nm: e.tensor_copy(
                        out=pad[g2 * 64:(g2 + 1) * 64, k, off:off + 16], in_=bb[nm][g2 * 64:(g2 + 1) * 64, k, :]),
                        reads=[nm, "bpad"], writes=["bpad"])
            T = sb(nm + "T", [128, 16, 128], BF16)
            for h in range(2):
                pt, ptk = PT.next()
                for j in range(8):
                    S.add("pe", lambda e, pad=pad, pt=pt, j=j, h=h: e.transpose(out=pt[:, j, :], in_=pad[:, h * 8 + j, :], identity=ident[:]),
                          reads=["bpad", "ident"], writes=[ptk])
                S.add("dve", lambda e, T=T, pt=pt, h=h: e.tensor_copy(out=T[:, h * 8:(h + 1) * 8, :], in_=pt[:]),
                      reads=[ptk], writes=[nm + "T"])
            BT[nm] = T
        Cb = {}
        for nm, src, sgn in (("Cbr", Cl_r, 1.0), ("Cbi", Cl_i, -1.0)):
            t = sb(nm, [128, 16, 32], BF16)
            S.add("pool", lambda e, t=t: e.memset(t[:], 0.0), writes=[nm])
            for g2 in range(2):
                S.add("dve", lambda e, t=t, src=src, g2=g2, sgn=sgn: e.tensor_scalar(
                    out=t[g2 * 64:(g2 + 1) * 64, :, g2 * 16:(g2 + 1) * 16], in0=src[g2 * 64:(g2 + 1) * 64, :, :],
                    scalar1=sgn, scalar2=None, op0=ALU.mult), reads=["Cl_r", "Cl_i", nm], writes=[nm])
            Cb[nm] = t

        cxr = sb("cxr", [128, 16])
        cxi = sb("cxi", [128, 16])
        S.add("pool", lambda e: e.memset(cxr[:], 0.0), writes=["cxr"])
        S.add("pool", lambda e: e.memset(cxi[:], 0.0), writes=["cxi"])
        s0r = sb("s0r", [128, 2, 16])
        s0i = sb("s0i", [128, 2, 16])
        S.add("sp", lambda e: e.dma_start(out=s0r[:], in_=sbre.rearrange("s (k g) n -> (g n) s k", g=2)), writes=["s0r"], is_dma=True)
        S.add("sp", lambda e: e.dma_start(out=s0i[:], in_=sbim.rearrange("s (k g) n -> (g n) s k", g=2)), writes=["s0i"], is_dma=True)
        cs_r = sb("cs_r", [128, 2, 16])
        cs_i = sb("cs_i", [128, 2, 16])

        def rmsnorm_T(src, srck, gtile, gk):
            S.add("act", lambda e: e.activation(out=gate[:], in_=src[:], func=AF.Square, accum_out=ssq[:]),
                  reads=[srck], writes=["gate", "ssq"])
            S.add("dve", lambda e: e.tensor_scalar(out=rstd[:], in0=ssq[:], scalar1=1.0 / D, scalar2=EPS, op0=ALU.mult, op1=ALU.add),
                  reads=["ssq"], writes=["rstd"])
            S.add("pool", lambda e: e.tensor_tensor(out=rstd[:], in0=rstd[:], in1=mhalf[:, 0:1], op=ALU.pow),
                  reads=["rstd", "mhalf"], writes=["rstd"])
            S.add("dve", lambda e: e.scalar_tensor_tensor(out=xn[:], in0=src[:], scalar=rstd[:], in1=gtile[:], op0=ALU.mult, op1=ALU.mult),
                  reads=[srck, "rstd", gk], writes=["xn"])
            pt, ptk = PT.next()
            for k in range(8):
                S.add("pe", lambda e, k=k, pt=pt: e.transpose(out=pt[:, k, :], in_=xn[:, k * 128:(k + 1) * 128], identity=ident[:]),
                      reads=["xn", "ident"], writes=[ptk])
            S.add("act", lambda e, pt=pt: e.copy(out=hnT[:], in_=pt[:]), reads=[ptk], writes=["hnT"])

        def linear(lhsT, lk, nk, w, wk, c0, c1):
            p, pk = PS.next()
            for k in range(nk):
                S.add("pe", lambda e, k=k, p=p: e.matmul(p[:, 0:c1 - c0], lhsT=lhsT[:, k, :], rhs=w[:, k, c0:c1],
                                                        start=(k == 0), stop=(k == nk - 1)),
                      reads=[lk, wk], writes=[pk])
            return p, pk

        def load_tile(i):
            h, hk = H.next()
            p0, p0k = P0.next()
            S.add("sp", lambda e: e.dma_start(out=h[:], in_=xin[i * 128:(i + 1) * 128, :]), writes=[hk], is_dma=True)
            S.add("sp", lambda e: e.dma_start(out=p0[:], in_=pin[0, i * 128:(i + 1) * 128, :]), writes=[p0k], is_dma=True)
            return h, hk, p0, p0k

        nxt = load_tile(0)
        for i in range(nt):
            h, hk, p0, p0k = nxt
            if i + 1 < nt:
                nxt = load_tile(i + 1)
            samp = (i == ntp)
            Rc, Rs = Rc_p, Rs_p

            rmsnorm_T(h, hk, g_l0, "g_l0")
            p, pk = linear(hnT, "hnT", 8, w_in0, "w_in0", 0, 512)
            S.add("act", lambda e, p=p: e.activation(out=ua[:], in_=p[:], func=AF.Gelu_apprx_tanh), reads=[pk], writes=["ua"])
            p, pk = linear(hnT, "hnT", 8, w_in0, "w_in0", 512, 1024)
            S.add("act", lambda e, p=p: e.activation(out=va[:], in_=p[:], func=AF.Gelu_apprx_tanh), reads=[pk], writes=["va"])
            p, pk = linear(hnT, "hnT", 8, w_in0, "w_in0", 1024, 1536)
            S.add("act", lambda e, p=p: e.activation(out=za[:], in_=p[:], func=AF.Silu), reads=[pk], writes=["za"])
            p, pk = linear(hnT, "hnT", 8, w_in0, "w_in0", 1536, 2048)
            S.add("act", lambda e, p=p: e.copy(out=ubf[:], in_=p[:]), reads=[pk], writes=["ubf"])
            S.add("dve", lambda e, p=p: e.tensor_tensor(out=ubd[:], in0=p[:], in1=Db[:], op=ALU.mult), reads=[pk, "Db"], writes=["ubd"])
            p, pk = linear(hnT, "hnT", 8, w_in0, "w_in0", 2048, 2560)
            S.add("act", lambda e, p=p: e.activation(out=zb[:], in_=p[:], func=AF.Silu), reads=[pk], writes=["zb"])

            S.add("dve", lambda e: e.bn_stats(out=bst[:], in_=va[:]), reads=["va"], writes=["bst"])
            S.add("dve", lambda e: e.bn_aggr(out=bag[:], in_=bst[:]), reads=["bst"], writes=["bag"])
            S.add("dve", lambda e: e.tensor_scalar(out=bag[:, 1:2], in0=bag[:, 1:2], scalar1=EPS, scalar2=None, op0=ALU.add),
                  reads=["bag"], writes=["bag"])
            S.add("pool", lambda e: e.tensor_tensor(out=bag[:, 1:2], in0=bag[:, 1:2], in1=mhalf[:, 0:1], op=ALU.pow),
                  reads=["bag", "mhalf"], writes=["bag"])
            S.add("dve", lambda e: e.tensor_scalar(out=vln[:], in0=va[:], scalar1=bag[:, 0:1], scalar2=bag[:, 1:2],
                                                  op0=ALU.subtract, op1=ALU.mult), reads=["va", "bag"], writes=["vln"])
            S.add("dve", lambda e: e.tensor_tensor(out=vln[:], in0=vln[:], in1=lng[:], op=ALU.mult), reads=["vln", "lng"], writes=["vln"])
            S.add("dve", lambda e: e.tensor_tensor(out=vln[:], in0=vln[:], in1=lnb[:], op=ALU.add), reads=["vln", "lnb"], writes=["vln"])
            S.add("act", lambda e: e.copy(out=vbf[:], in_=vln[:]), reads=["vln"], writes=["vbf"])
            if samp:
                S.add("sp", lambda e: e.dma_start(out=o_av, in_=vln[:]), reads=["vln"], writes=["o_av"], is_dma=True)
            p, pk = PS.next()
            wm, wmk = (wmixTs, "wlsT") if samp else (wmixT, "wlT")
            for g in range(4):
                S.add("pe", lambda e, g=g, p=p, wm=wm: e.matmul(p[:, g * 128:(g + 1) * 128], lhsT=wm[:, g, :], rhs=vbf[:, g * 128:(g + 1) * 128],
                                                               start=True, stop=True), reads=[wmk, "vbf"], writes=[pk])
            bs_t, bsk = (bsbs, "bsbs") if samp else (bsb, "bsb")
            for g in range(4):
                S.add("dve", lambda e, g=g, p=p, bs_t=bs_t: e.scalar_tensor_tensor(
                    out=ua[:, g * 128:(g + 1) * 128], in0=p[:, g * 128:(g + 1) * 128], scalar=bs_t[:, g:g + 1],
                    in1=ua[:, g * 128:(g + 1) * 128], op0=ALU.add, op1=ALU.mult), reads=[pk, bsk, "ua"], writes=["ua"])
            S.add("dve", lambda e: e.tensor_tensor(out=mixin[:, 0:512], in0=ua[:], in1=za[:], op=ALU.mult),
                  reads=["ua", "za"], writes=["mixin"])

            pt, ptk = PT.next()
            for q in range(4):
                S.add("pe", lambda e, q=q, pt=pt: e.transpose(out=pt[:, q, :], in_=ubf[:, q * 128:(q + 1) * 128], identity=ident[:]),
                      reads=["ubf", "ident"], writes=[ptk])
            S.add("act", lambda e, pt=pt: e.copy(out=ubT[:], in_=pt[:, 0:4, :]), reads=[ptk], writes=["ubT"])
            py, pyk = py_bank, "ps_y"
            for q in range(4):
                pbr, pbrk = PS.next()
                pbi, pbik = PS.next()
                for j in range(4):
                    k = 4 * q + j
                    S.add("pe", lambda e, j=j, k=k, q=q, pbr=pbr: e.matmul(pbr[:, j * 128:(j + 1) * 128], lhsT=BT["bbr"][:, k, :], rhs=ubT[:, q, :],
                                                                          start=True, stop=True), reads=["bbrT", "ubT"], writes=[pbrk])
                    S.add("pe", lambda e, j=j, k=k, q=q, pbi=pbi: e.matmul(pbi[:, j * 128:(j + 1) * 128], lhsT=BT["bbi"][:, k, :], rhs=ubT[:, q, :],
                                                                          start=True, stop=True), reads=["bbiT", "ubT"], writes=[pbik])
                T1, T2, T3, T4 = W5["T1"], W5["T2"], W5["T3"], W5["T4"]
                vr, vi, zr, zi = W5["vr"], W5["vi"], W5["zr"], W5["zi"]
                if samp:
                    def V(t):
                        return t[:].rearrange("p (a s b) -> p a s b", a=4, s=2)
                    def Vp(t):
                        return t[:].rearrange("p (a s b) -> p a s b", a=4, s=2)
                    rc = Rc[:, 4 * q:4 * q + 4, 0:64].unsqueeze(2).broadcast_to([128, 4, 2, 64])
                    rs = Rs[:, 4 * q:4 * q + 4, 0:64].unsqueeze(2).broadcast_to([128, 4, 2, 64])
                else:
                    def V(t):
                        return t[:]
                    def Vp(t):
                        return t[:]
                    rc = Rc[:, 4 * q:4 * q + 4, :].rearrange("p a b -> p (a b)")
                    rs = Rs[:, 4 * q:4 * q + 4, :].rearrange("p a b -> p (a b)")
                S.add("dve", lambda e, pbr=pbr, rc=rc, V=V, Vp=Vp: e.tensor_tensor(out=V(T1), in0=Vp(pbr), in1=rc, op=ALU.mult), reads=[pbrk, kRc], writes=["T1"])
                S.add("dve", lambda e, pbi=pbi, rs=rs, V=V, Vp=Vp: e.tensor_tensor(out=V(T2), in0=Vp(pbi), in1=rs, op=ALU.mult), reads=[pbik, kRs], writes=["T2"])
                S.add("dve", lambda e, pbi=pbi, rc=rc, V=V, Vp=Vp: e.tensor_tensor(out=V(T3), in0=Vp(pbi), in1=rc, op=ALU.mult), reads=[pbik, kRc], writes=["T3"])
                S.add("dve", lambda e, pbr=pbr, rs=rs, V=V, Vp=Vp: e.tensor_tensor(out=V(T4), in0=Vp(pbr), in1=rs, op=ALU.mult), reads=[pbrk, kRs], writes=["T4"])
                S.add("pool", lambda e: e.tensor_tensor(out=vr[:], in0=T1[:], in1=T2[:], op=ALU.add), reads=["T1", "T2"], writes=["vr"])
                S.add("pool", lambda e: e.tensor_tensor(out=vi[:], in0=T3[:], in1=T4[:], op=ALU.subtract), reads=["T3", "T4"], writes=["vi"])
                for j in range(4):
                    k = 4 * q + j
                    if samp:
                        segs = [(j * 128 + 64 * s_, 64, s0r[:, s_, k:k + 1], s0i[:, s_, k:k + 1], "s0r", "s0i") for s_ in range(2)]
                    else:
                        segs = [(j * 128, 128, cxr[:, k:k + 1], cxi[:, k:k + 1], "cxr", "cxi")]
                    for (c0, ln, ir, ii, irk, iik) in segs:
                        S.add("dve", lambda e, c0=c0, ln=ln, ir=ir, k=k: e.tensor_tensor_scan(
                            out=zr[:, c0:c0 + ln], data0=Rm[:, k, 0:ln], data1=vr[:, c0:c0 + ln], initial=ir, op0=ALU.mult, op1=ALU.add),
                            reads=["vr", kRm, irk], writes=["zr"])
                        S.add("dve", lambda e, c0=c0, ln=ln, ii=ii, k=k: e.tensor_tensor_scan(
                            out=zi[:, c0:c0 + ln], data0=Rm[:, k, 0:ln], data1=vi[:, c0:c0 + ln], initial=ii, op0=ALU.mult, op1=ALU.add),
                            reads=["vi", kRm, iik], writes=["zi"])
                S.add("pool", lambda e, rc=rc, V=V: e.tensor_tensor(out=V(T1), in0=V(zr), in1=rc, op=ALU.mult), reads=["zr", kRc], writes=["T1"])
                S.add("pool", lambda e, rs=rs, V=V: e.tensor_tensor(out=V(T2), in0=V(zi), in1=rs, op=ALU.mult), reads=["zi", kRs], writes=["T2"])
                S.add("pool", lambda e, rs=rs, V=V: e.tensor_tensor(out=V(T3), in0=V(zr), in1=rs, op=ALU.mult), reads=["zr", kRs], writes=["T3"])
                S.add("pool", lambda e, rc=rc, V=V: e.tensor_tensor(out=V(T4), in0=V(zi), in1=rc, op=ALU.mult), reads=["zi", kRc], writes=["T4"])
                S.add("dve", lambda e: e.tensor_tensor(out=xrb[:].rearrange("p a b -> p (a b)"), in0=T1[:], in1=T2[:], op=ALU.subtract),
                      reads=["T1", "T2"], writes=["xrb"])
                S.add("dve", lambda e: e.tensor_tensor(out=xib[:].rearrange("p a b -> p (a b)"), in0=T3[:], in1=T4[:], op=ALU.add),
                      reads=["T3", "T4"], writes=["xib"])
                T13 = T1[:].rearrange("p (a b) -> p a b", a=4)
                T23 = T2[:].rearrange("p (a b) -> p a b", a=4)
                T33 = T3[:].rearrange("p (a b) -> p a b", a=4)
                T43 = T4[:].rearrange("p (a b) -> p a b", a=4)
                if samp:
                    for s_ in range(2):
                        c_ = 64 * s_ + 63
                        S.add("dve", lambda e, s_=s_, c_=c_, q=q, T13=T13, T23=T23: e.tensor_tensor(out=cs_r[:, s_, 4 * q:4 * q + 4], in0=T13[:, :, c_], in1=T23[:, :, c_], op=ALU.subtract),
                              reads=["T1", "T2"], writes=["cs_r"])
                        S.add("dve", lambda e, s_=s_, c_=c_, q=q, T33=T33, T43=T43: e.tensor_tensor(out=cs_i[:, s_, 4 * q:4 * q + 4], in0=T33[:, :, c_], in1=T43[:, :, c_], op=ALU.add),
                              reads=["T3", "T4"], writes=["cs_i"])
                else:
                    S.add("dve", lambda e, q=q, T13=T13, T23=T23: e.tensor_tensor(out=cxr[:, 4 * q:4 * q + 4], in0=T13[:, :, 127], in1=T23[:, :, 127], op=ALU.subtract),
                          reads=["T1", "T2"], writes=["cxr"])
                    S.add("dve", lambda e, q=q, T33=T33, T43=T43: e.tensor_tensor(out=cxi[:, 4 * q:4 * q + 4], in0=T33[:, :, 127], in1=T43[:, :, 127], op=ALU.add),
                          reads=["T3", "T4"], writes=["cxi"])
                for j in range(4):
                    k = 4 * q + j
                    S.add("pe", lambda e, j=j, k=k, py=py: e.matmul(py[:, 32 * k:32 * k + 32], lhsT=xrb[:, j, :], rhs=Cb["Cbr"][:, k, :], start=True, stop=False),
                          reads=["xrb", "Cbr"], writes=[pyk])
                    S.add("pe", lambda e, j=j, k=k, py=py: e.matmul(py[:, 32 * k:32 * k + 32], lhsT=xib[:, j, :], rhs=Cb["Cbi"][:, k, :], start=False, stop=True),
                          reads=["xib", "Cbi"], writes=[pyk])
            if not samp:
                if i == ntp - 1:
                    S.add("sp", lambda e: e.dma_start(out=o_bre[0].rearrange("(k g) n -> (g n) k", g=2), in_=cxr[:]), reads=["cxr"], writes=["o_bre"], is_dma=True)
                    S.add("sp", lambda e: e.dma_start(out=o_bim[0].rearrange("(k g) n -> (g n) k", g=2), in_=cxi[:]), reads=["cxi"], writes=["o_bim"], is_dma=True)
            else:
                S.add("sp", lambda e: e.dma_start(out=o_bre[1:3].rearrange("s (k g) n -> (g n) s k", g=2), in_=cs_r[:]), reads=["cs_r"], writes=["o_bre"], is_dma=True)
                S.add("sp", lambda e: e.dma_start(out=o_bim[1:3].rearrange("s (k g) n -> (g n) s k", g=2), in_=cs_i[:]), reads=["cs_i"], writes=["o_bim"], is_dma=True)
            S.add("dve", lambda e, py=py: e.tensor_tensor(out=yb[:], in0=py[:], in1=ubd[:], op=ALU.add), reads=[pyk, "ubd"], writes=["yb"])
            if samp and debug_h:
                S.add("sp", lambda e: e.dma_start(out=dbg, in_=yb[:]), reads=["yb"], writes=["dbg"], is_dma=True)
            S.add("act", lambda e: e.activation(out=yg[:], in_=yb[:], func=AF.Gelu_apprx_tanh), reads=["yb"], writes=["yg"])
            S.add("act", lambda e: e.copy(out=ygb[:], in_=yg[:]), reads=["yg"], writes=["ygb"])
            pt, ptk = PT.next()
            for q in range(4):
                S.add("pe", lambda e, q=q, pt=pt: e.transpose(out=pt[:, q, :], in_=ygb[:, q * 128:(q + 1) * 128], identity=ident[:]),
                      reads=["ygb", "ident"], writes=[ptk])
            S.add("act", lambda e, pt=pt: e.copy(out=ygT[:], in_=pt[:, 0:4, :]), reads=[ptk], writes=["ygT"])
            p, pk = linear(ygT, "ygT", 4, w_glu, "w_glu", 0, 512)
            S.add("act", lambda e, p=p: e.activation(out=yb[:], in_=p[:], func=AF.Sigmoid), reads=[pk], writes=["yb"])
            S.add("dve", lambda e: e.tensor_tensor(out=yg[:], in0=yg[:], in1=yb[:], op=ALU.mult), reads=["yg", "yb"], writes=["yg"])
            S.add("dve", lambda e: e.tensor_tensor(out=mixin[:, 512:1024], in0=yg[:], in1=zb[:], op=ALU.mult), reads=["yg", "zb"], writes=["mixin"])

            pt, ptk = PT.next()
            for k in range(8):
                S.add("pe", lambda e, k=k, pt=pt: e.transpose(out=pt[:, k, :], in_=mixin[:, k * 128:(k + 1) * 128], identity=ident[:]),
                      reads=["mixin", "ident"], writes=[ptk])
            S.add("act", lambda e, pt=pt: e.copy(out=mixT[:], in_=pt[:]), reads=[ptk], writes=["mixT"])
            for hh in range(2):
                p, pk = linear(mixT, "mixT", 8, w_out0, "w_out0", hh * 512, (hh + 1) * 512)
                S.add("dve", lambda e, p=p, hh=hh, h=h: e.tensor_tensor(out=h[:, hh * 512:(hh + 1) * 512], in0=h[:, hh * 512:(hh + 1) * 512], in1=p[:], op=ALU.add),
                      reads=[pk, hk], writes=[hk])

            rmsnorm_T(h, hk, g_p0, "g_p0")
            S.add("act", lambda e, p0=p0: e.copy(out=pbf[:], in_=p0[:]), reads=[p0k], writes=["pbf"])
            pt, ptk = PT.next()
            for k in range(2):
                S.add("pe", lambda e, k=k, pt=pt: e.transpose(out=pt[:, k, :], in_=pbf[:, k * 128:(k + 1) * 128], identity=ident[:]),
                      reads=["pbf", "ident"], writes=[ptk])
            S.add("act", lambda e, pt=pt: e.copy(out=pT[:], in_=pt[:, 0:2, :]), reads=[ptk], writes=["pT"])
            for hh in range(2):
                p, pk = linear(hnT, "hnT", 8, w_gate0, "w_gate0", hh * 512, (hh + 1) * 512)
                S.add("act", lambda e, p=p, hh=hh: e.activation(out=gate[:, hh * 512:(hh + 1) * 512], in_=p[:], func=AF.Sigmoid), reads=[pk], writes=["gate"])
                p, pk = linear(pT, "pT", 2, w_pp0, "w_pp0", hh * 512, (hh + 1) * 512)
                S.add("dve", lambda e, p=p, hh=hh: e.tensor_tensor(out=gate[:, hh * 512:(hh + 1) * 512], in0=gate[:, hh * 512:(hh + 1) * 512], in1=p[:], op=ALU.mult),
                      reads=[pk, "gate"], writes=["gate"])
            S.add("dve", lambda e, h=h: e.tensor_tensor(out=h[:], in0=h[:], in1=gate[:], op=ALU.add), reads=[hk, "gate"], writes=[hk])
            S.add("sp", lambda e, h=h, i=i: e.dma_start(out=h1_scr[i * 128:(i + 1) * 128, :], in_=h[:]), reads=[hk], writes=["h1_scr"], is_dma=True)

        S.emit()
    nc.all_engine_barrier()
    T = dict(locals())
    phase_b(nc, T, ntp, debug_h)
    return nc


WEIGHT_KEYS = ["d_conv_w", "d_A_log", "d_dt_bias", "d_norm_g", "norm_g", "final_norm_g", "ple_proj", "ple_gate_w", "ple_norm_g", "even_w_in", "even_w_out",
               "a_ln_g", "a_ln_b", "a_w_s", "a_b_s", "b_lam_re", "b_lam_im", "b_log_dt", "b_B_re", "b_B_im",
               "b_C_re", "b_C_im", "b_D", "b_glu_w", "odd_w_in", "odd_w_out"]


def make_in_maps(inp, ntp=NTP):
    maps = []
    for c in range(NCORES):
        m = {k: np.ascontiguousarray(inp[k], dtype=np.float32) for k in WEIGHT_KEYS}
        xs = np.asarray(inp["x_sample"])[2 * c:2 * c + 2].reshape(128, D)
        m["xin"] = np.ascontiguousarray(np.concatenate([np.asarray(inp["x_prompt"])[c, :ntp * 128], xs], axis=0))
        ps_ = np.asarray(inp["p_sample"])[:, 2 * c:2 * c + 2].reshape(2, 128, 256)
        m["pin"] = np.ascontiguousarray(np.concatenate([np.asarray(inp["p_prompt"])[:, c, :ntp * 128], ps_], axis=1))
        m["sbre"] = np.ascontiguousarray(np.asarray(inp["state_b_re"])[0, 2 * c:2 * c + 2])
        m["sbim"] = np.ascontiguousarray(np.asarray(inp["state_b_im"])[0, 2 * c:2 * c + 2])
        m["cache_k"] = np.ascontiguousarray(np.asarray(inp["cache_k_c"])[0, 2 * c:2 * c + 2].reshape(2, 4096, 512))
        m["cache_v"] = np.ascontiguousarray(np.asarray(inp["cache_v_c"])[0, 2 * c:2 * c + 2].reshape(2, 4096, 512))
        m["state_d"] = np.ascontiguousarray(np.asarray(inp["state_d"])[0, 2 * c:2 * c + 2])
        m["state_conv"] = np.ascontiguousarray(np.asarray(inp["state_conv_d"])[0, 2 * c:2 * c + 2])
        maps.append(m)
    return maps


_NC_CACHE = {}


def kernel(**inputs):
    if "nc" not in _NC_CACHE:
        _NC_CACHE["nc"] = build_program()
    nc = _NC_CACHE["nc"]
    res = run_bass_kernel_spmd(nc, make_in_maps(inputs), core_ids=list(range(NCORES)))
    R = res.results
    B, DB = 8, 16
    y_prompt = np.stack([R[c]["o_y"][:SEQ] for c in range(B)]).astype(np.float32)
    y_sample = np.concatenate([R[c]["o_y"][SEQ:].reshape(2, 64, D) for c in range(B)]).astype(np.float32)
    b_re_p = np.stack([R[c]["o_bre"][0] for c in range(B)])[None]
    b_im_p = np.stack([R[c]["o_bim"][0] for c in range(B)])[None]
    b_re_s = np.concatenate([R[c]["o_bre"][1:3] for c in range(B)])[None]
    b_im_s = np.concatenate([R[c]["o_bim"][1:3] for c in range(B)])[None]
    a_v_s = np.concatenate([R[c]["o_av"].reshape(2, 64, 512) for c in range(B)])[None]
    k_c_p = np.stack([R[c]["o_kc"][:SEQ].reshape(SEQ, 8, 64) for c in range(B)])[None]
    v_c_p = np.stack([R[c]["o_vc"][:SEQ].reshape(SEQ, 8, 64) for c in range(B)])[None]
    k_c_s = np.concatenate([R[c]["o_kc"][SEQ:].reshape(2, 64, 8, 64) for c in range(B)])[None]
    v_c_s = np.concatenate([R[c]["o_vc"][SEQ:].reshape(2, 64, 8, 64) for c in range(B)])[None]
    conv_d_p = np.stack([R[c]["o_conv"][0] for c in range(B)])[None]
    conv_d_s = np.concatenate([R[c]["o_conv"][1:3] for c in range(B)])[None]
    s_d_p = np.stack([R[c]["o_sd"][0] for c in range(B)])[None]
    s_d_s = np.concatenate([R[c]["o_sd"][1:3] for c in range(B)])[None]
    f = lambda a: np.ascontiguousarray(a, dtype=np.float32)
    return tuple(f(a) for a in (y_prompt, y_sample, b_re_p, b_im_p, k_c_p, v_c_p, s_d_p, conv_d_p,
                                b_re_s, b_im_s, a_v_s, k_c_s, v_c_s, s_d_s, conv_d_s))


def phase_b(nc, T, ntp, debug_h):
    nt = ntp + 1
    g = lambda n: T[n]
    h1_scr, pin, o_y, o_kc, o_vc, o_conv = g("h1_scr"), g("pin"), g("o_y"), g("o_kc"), g("o_vc"), g("o_conv")
    norm_g, final_norm_g, ple_proj, ple_gate_w, ple_norm_g = g("norm_g"), g("final_norm_g"), g("ple_proj"), g("ple_gate_w"), g("ple_norm_g")
    odd_w_in, odd_w_out, cache_k, cache_v = g("odd_w_in"), g("odd_w_out"), g("cache_k"), g("cache_v")
    ntok = nt * 128
    kts = nc.dram_tensor("kts", [128, 4, ntok], BF16, kind="Internal").ap()
    vs = nc.dram_tensor("vs", [ntok, 512], BF16, kind="Internal").ap()
    dbg2 = nc.dram_tensor("dbg2", [ntok, 512], F32, kind="ExternalOutput").ap() if debug_h else None

    S = Sched(nc)
    st = contextlib.ExitStack()
    with st:
        def sb(name, shape, dt=F32):
            return st.enter_context(nc.sbuf_tensor(name, list(shape), dt))

        def ps(name, shape, dt=F32):
            return st.enter_context(nc.psum_tensor(name, list(shape), dt))

        st.enter_context(nc.allow_non_contiguous_dma("small parameter layout loads"))
        PS = Rot([ps(f"qs{i}", [128, 512]) for i in range(2)], "qs")
        PT = Rot([ps("qt0", [128, 8, 128], BF16)], "qt")
        PZ = ps("pz", [128, 1024])
        P2 = ps("p2", [128, 1024])
        ACC = ps("acc", [128, 512])

        ident = sb("identb", [128, 128], BF16)
        S.add("pool", lambda e: e.memset(ident[:], 0.0), writes=["ident"])
        S.add("pool", lambda e: e.affine_select(out=ident[:], in_=ident[:], compare_op=ALU.not_equal, fill=1.0,
                                               base=0, pattern=[[-1, 128]], channel_multiplier=1), reads=["ident"], writes=["ident"])
        mhalf = sb("mhalfb", [128, 1])
        onec = sb("onec", [128, 1])
        S.add("pool", lambda e: e.memset(onec[:], 1.0), writes=["onec"])
        S.add("pool", lambda e: e.memset(mhalf[:], -0.5), writes=["mhalf"])
        negU = sb("negU", [128, 128], BF16)
        zer = sb("zer", [128, 512], BF16)
        S.add("pool", lambda e: e.memset(zer[:], 0.0), writes=["zer"])
        negO = sb("negO", [128, 128], BF16)
        mask01 = sb("mask01", [128, 128], BF16)
        S.add("pool", lambda e: e.memset(negU[:], -1.0), writes=["negU"])
        S.add("pool", lambda e: e.affine_select(out=negU[:], in_=negU[:], compare_op=ALU.is_ge, fill=0.0, base=0,
                                               pattern=[[-1, 128]], channel_multiplier=1), reads=["negU"], writes=["negU"])
        S.add("pool", lambda e: e.memset(negO[:], -1.0), writes=["negO"])
        S.add("pool", lambda e: e.memset(mask01[:], 1.0), writes=["mask01"])
        S.add("pool", lambda e: e.affine_select(out=mask01[:], in_=mask01[:], compare_op=ALU.is_gt, fill=0.0, base=0,
                                               pattern=[[1, 128]], channel_multiplier=-1), reads=["mask01"], writes=["mask01"])

        def bcast_load(name, src, n):
            t = sb(name, [128, n])
            S.add("sp", lambda e: e.dma_start(out=t[:], in_=src.partition_broadcast(128)), writes=[name], is_dma=True)
            return t

        g_l1 = bcast_load("g_l1", norm_g[1], D)
        g_p1 = bcast_load("g_p1", ple_norm_g[1], D)
        g_f = bcast_load("g_f", final_norm_g, D)

        def wload(name, src, kt, n, csz=512):
            t = sb(name, [128, kt, n], BF16)
            v = src.rearrange("(k p) n -> p k n", p=128)
            for c0 in range(0, n, csz):
                c1 = min(n, c0 + csz)
                S.add("pool", lambda e, c0=c0, c1=c1: e.dma_start(out=t[:, :, c0:c1], in_=v[:, :, c0:c1]),
                      writes=[name], is_dma=True)
            return t

        w_in1 = wload("w_in1", odd_w_in[0], 8, 4104)
        w_out1 = wload("w_out1", odd_w_out[0], 8, 1024)
        w_gate1 = wload("w_gate1", ple_gate_w[1], 8, 1024)
        w_pp1 = wload("w_pp1", ple_proj[1], 2, 1024)

        H = Rot([sb(f"hb{i}", [128, D]) for i in range(1)], "hb")
        P1 = Rot([sb(f"p1_{i}", [128, 256]) for i in range(1)], "p1_")
        xn = sb("xnb", [128, D], BF16)
        hnT = sb("hnTb", [128, 8, 128], BF16)
        ssq = sb("ssqb", [128, 1])
        rstd = sb("rstdb", [128, 1])
        gate = sb("gateb", [128, D])
        qbf = sb("qbf", [128, 512], BF16)
        kf = gate[:, 0:512]
        kbf = sb("kbf", [128, 512], BF16)
        vf = gate[:, 512:1024]
        vbf = sb("vbf1", [128, 512], BF16)
        vbs = sb("vbs", [64, 512], BF16)
        zcs = sb("zcs", [128, 512])
        QT = sb("QT", [128, 8, 128], BF16)
        S.add("pool", lambda e: e.memset(QT[:], 0.0), writes=["QT"])
        KTc = sb("KTc", [128, 4, 128], BF16)
        KB = Rot([sb(f"ktb{i}", [128, 4, 128], BF16) for i in range(2)], "ktb")
        VB = Rot([sb(f"vb{i}", [128, 512], BF16) for i in range(2)], "vb")
        CK = Rot([sb(f"ck{i}", [128, 512], BF16) for i in range(1)], "ck")
        arena = sb("arena", [128, 3072])
        ebuf = arena[:, 0:1024]
        lm = arena[:, 1024:1536].bitcast(BF16)
        cum = arena[:, 1536:2048].bitcast(BF16)
        WB = Rot([arena[:, 2048:2560].bitcast(BF16), arena[:, 2560:3072].bitcast(BF16)], "wb")
        mixin = sb("mixinb", [128, 1024], BF16)
        mixT = sb("mixTb", [128, 8, 128], BF16)
        pbf = sb("pbfb", [128, 256], BF16)
        pT = sb("pTb", [128, 2, 128], BF16)
        cout = sb("cout", [128, 512])
        yout = ebuf

        def rmsnorm(src, srck, gtile, gk, out, outk):
            S.add("act", lambda e: e.activation(out=gate[:], in_=src[:], func=AF.Square, accum_out=ssq[:]),
                  reads=[srck], writes=["gate", "ssq"])
            S.add("dve", lambda e: e.tensor_scalar(out=rstd[:], in0=ssq[:], scalar1=1.0 / D, scalar2=EPS, op0=ALU.mult, op1=ALU.add),
                  reads=["ssq"], writes=["rstd"])
            S.add("pool", lambda e: e.tensor_tensor(out=rstd[:], in0=rstd[:], in1=mhalf[:], op=ALU.pow),
                  reads=["rstd", "mhalf"], writes=["rstd"])
            S.add("dve", lambda e: e.scalar_tensor_tensor(out=out[:], in0=src[:], scalar=rstd[:], in1=gtile[:], op0=ALU.mult, op1=ALU.mult),
                  reads=[srck, "rstd", gk], writes=[outk])

        def transpose_to(src, srck, nk, dst, dstk):
            pt, ptk = PT.next()
            for k in range(nk):
                S.add("pe", lambda e, k=k, pt=pt: e.transpose(out=pt[:, k, :], in_=src[:, k * 128:(k + 1) * 128], identity=ident[:]),
                      reads=[srck, "ident"], writes=[ptk])
            S.add("act", lambda e, pt=pt: e.copy(out=dst[:, 0:nk, :], in_=pt[:, 0:nk, :]), reads=[ptk], writes=[dstk])

        def linear(lhsT, lk, nk, w, wk, c0, c1, msl=slice(0, 128)):
            p, pk = PS.next()
            m = msl.stop - msl.start
            for k in range(nk):
                S.add("pe", lambda e, k=k, p=p: e.matmul(p[0:m, 0:c1 - c0], lhsT=lhsT[:, k, msl], rhs=w[:, k, c0:c1],
                                                        start=(k == 0), stop=(k == nk - 1)), reads=[lk, wk], writes=[pk])
            return p, pk

        def attn_step(kt_of, ktk, v_ap, vk, q0, nq, ns, diag, first, last):
            W_ = 8 * nq
            for h in range(8):
                S.add("pe", lambda e, h=h: e.matmul(PZ[0:ns, h * nq:(h + 1) * nq], lhsT=kt_of(h), rhs=QT[:, h, q0:q0 + nq], start=True, stop=True),
                      reads=[ktk, "QT"], writes=["pz"])
            for c0 in range(0, W_, 512):
                S.add("act", lambda e, c0=c0: e.activation(out=ebuf[0:ns, c0:c0 + 512], in_=PZ[0:ns, c0:c0 + 512], func=AF.Exp), reads=["pz"], writes=["ebuf"])
            S.add("act", lambda e: e.activation(out=lm[0:ns, 0:W_], in_=ebuf[0:ns, 0:W_], func=AF.Ln, bias=1.0), reads=["ebuf"], writes=["lm"])
            if diag:
                S.add("dve", lambda e: e.tensor_tensor(out=lm[0:ns, 0:W_].rearrange("p (h t) -> p h t", h=8), in0=lm[0:ns, 0:W_].rearrange("p (h t) -> p h t", h=8),
                                                      in1=mask01[0:ns, 0:nq].unsqueeze(1).broadcast_to([ns, 8, nq]), op=ALU.mult),
                      reads=["lm", "mask01"], writes=["lm"])
            nchunk = (W_ + 511) // 512
            cw = W_ // nchunk
            hpc = 8 // nchunk
            for c in range(nchunk):
                S.add("pe", lambda e, c=c: e.matmul(P2[0:ns, c * cw:(c + 1) * cw], lhsT=negU[0:ns, 0:ns], rhs=lm[0:ns, c * cw:(c + 1) * cw], start=True, stop=False),
                      reads=["negU", "lm"], writes=["p2"])
                if not first:
                    S.add("pe", lambda e, c=c: e.matmul(P2[0:ns, c * cw:(c + 1) * cw], lhsT=negO[:, 0:ns], rhs=cum[:, c * cw:(c + 1) * cw], start=False, stop=False),
                          reads=["negO", "cum"], writes=["p2"])
                for h in range(c * hpc, (c + 1) * hpc):
                    S.add("pe", lambda e, h=h, c=c: e.matmul(P2[0:ns, h * nq:(h + 1) * nq], lhsT=kt_of(h), rhs=QT[:, h, q0:q0 + nq], start=False, stop=(h == (c + 1) * hpc - 1)),
                          reads=[ktk, "QT"], writes=["p2"])
            wb, wbk = WB.next()
            for c0 in range(0, W_, 512):
                S.add("act", lambda e, wb=wb, c0=c0: e.activation(out=wb[0:ns, c0:c0 + 512], in_=P2[0:ns, c0:c0 + 512], func=AF.Exp), reads=["p2"], writes=[wbk])
            if diag:
                S.add("dve", lambda e, wb=wb: e.tensor_tensor(out=wb[0:ns, 0:W_].rearrange("p (h t) -> p h t", h=8), in0=wb[0:ns, 0:W_].rearrange("p (h t) -> p h t", h=8),
                                                             in1=mask01[0:ns, 0:nq].unsqueeze(1).broadcast_to([ns, 8, nq]), op=ALU.mult),
                      reads=[wbk, "mask01"], writes=[wbk])
            if first:
                S.add("pe", lambda e: e.matmul(ACC[0:nq, :], lhsT=zer[:, 0:nq], rhs=zer[:, :], start=True, stop=False), reads=["zer"], writes=["acc"])
            for h in range(8):
                S.add("pe", lambda e, h=h, wb=wb: e.matmul(ACC[0:nq, h * 64:(h + 1) * 64], lhsT=wb[0:ns, h * nq:(h + 1) * nq], rhs=v_ap[0:ns, h * 64:(h + 1) * 64],
                                                          start=False, stop=(last and h == 7)), reads=[wbk, vk], writes=["acc"])
            if not last:
                S.add("pool", lambda e: e.tensor_tensor(out=cum[0:ns, 0:W_], in0=cum[0:ns, 0:W_], in1=lm[0:ns, 0:W_], op=ALU.add),
                      reads=["cum", "lm"], writes=["cum"])

        S.add("pool", lambda e: e.memset(mixin[:, 512:1024], 0.0), writes=["mixin"])


        o_sd, state_d, state_conv = T["o_sd"], T["state_d"], T["state_conv"]
        d_conv_w, d_A_log, d_dt_bias, d_norm_g = T["d_conv_w"], T["d_A_log"], T["d_dt_bias"], T["d_norm_g"]
        Uincl = sb("Uincl", [64, 64])
        identf = sb("identf", [64, 64])
        nm_incl = sb("nm_incl", [64, 64])
        nm_low = sb("nm_low", [64, 64])
        ones128 = sb("ones128", [64, 128])
        S.add("pool", lambda e: e.memset(Uincl[:], 1.0), writes=["Uincl"])
        S.add("pool", lambda e: e.affine_select(out=Uincl[:], in_=Uincl[:], compare_op=ALU.is_ge, fill=0.0, base=0, pattern=[[1, 64]], channel_multiplier=-1),
              reads=["Uincl"], writes=["Uincl"])
        S.add("pool", lambda e: e.memset(identf[:], 0.0), writes=["identf"])
        S.add("pool", lambda e: e.affine_select(out=identf[:], in_=identf[:], compare_op=ALU.not_equal, fill=1.0, base=0, pattern=[[-1, 64]], channel_multiplier=1),
              reads=["identf"], writes=["identf"])
        S.add("pool", lambda e: e.memset(nm_incl[:], 0.0), writes=["nm_incl"])
        S.add("pool", lambda e: e.affine_select(out=nm_incl[:], in_=nm_incl[:], compare_op=ALU.is_ge, fill=-30000.0, base=0, pattern=[[1, 64]], channel_multiplier=-1),
              reads=["nm_incl"], writes=["nm_incl"])
        S.add("pool", lambda e: e.memset(nm_low[:], 0.0), writes=["nm_low"])
        S.add("pool", lambda e: e.affine_select(out=nm_low[:], in_=nm_low[:], compare_op=ALU.is_gt, fill=-30000.0, base=0, pattern=[[-1, 64]], channel_multiplier=1),
              reads=["nm_low"], writes=["nm_low"])
        S.add("pool", lambda e: e.memset(ones128[:], 1.0), writes=["ones128"])
        Sh = sb("Sh", [64, 3, 64], BF16)
        ShP = sb("ShP", [64, 3, 64], BF16)
        S.add("pool", lambda e: e.memset(Sh[:], 0.0), writes=["Sh"])
        S.add("pool", lambda e: e.memset(ShP[:], 0.0), writes=["ShP"])
        for i_ in range(1, 4):
            S.add("dve", lambda e, i_=i_: e.tensor_copy(out=Sh[:, i_ - 1, i_:64], in_=ident[0:64, 0:64 - i_]), reads=["ident", "Sh"], writes=["Sh"])
            S.add("dve", lambda e, i_=i_: e.tensor_copy(out=ShP[:, i_ - 1, 0:i_], in_=ident[0:64, 64 - i_:64]), reads=["ident", "ShP"], writes=["ShP"])
        cwb = sb("cwb", [64, 4, 1536], BF16)
        S.add("pool", lambda e: e.dma_start(out=cwb[:].rearrange("p a b -> p (a b)"), in_=d_conv_w[0].rearrange("a b -> (a b)").partition_broadcast(64)), writes=["cwb"], is_dma=True)
        dtb = sb("dtb", [64, 4])
        negA = sb("negA", [64, 4])
        gdn_g = sb("gdn_g", [64, 128])
        S.add("sp", lambda e: e.dma_start(out=dtb[:], in_=d_dt_bias[0].partition_broadcast(64)), writes=["dtb"], is_dma=True)
        S.add("sp", lambda e: e.dma_start(out=negA[:], in_=d_A_log[0].partition_broadcast(64)), writes=["negA"], is_dma=True)
        S.add("sp", lambda e: e.dma_start(out=gdn_g[:], in_=d_norm_g[0].partition_broadcast(64)), writes=["gdn_g"], is_dma=True)
        S.add("act", lambda e: e.activation(out=negA[:], in_=negA[:], func=AF.Exp), reads=["negA"], writes=["negA"])
        S.add("dve", lambda e: e.tensor_scalar(out=negA[:], in0=negA[:], scalar1=-1.0, scalar2=None, op0=ALU.mult), reads=["negA"], writes=["negA"])

        xraw = arena[0:64, 0:1536]
        XB = Rot([sb(f"xbf{i}", [64, 1536], BF16) for i in range(2)], "xbf")
        xc = arena[0:64, 1536:3072]
        gtmp = sb("gtmp", [64, 512])
        zdc = sb("zdc", [64, 512])
        ab = sb("ab", [64, 8])
        ssn = sb("ssn", [64, 8])
        gg = sb("gg", [64, 4])
        beta = sb("beta", [64, 4])
        Gcol = sb("Gcol", [64, 4])
        eG = sb("eG", [64, 4])
        dlast = sb("dlast", [64, 4])
        egl = sb("egl", [128, 4])
        D1 = sb("D1", [64, 4, 64])
        D2 = sb("D2", [64, 4, 64])
        Mm = sb("Mm", [64, 4, 64])
        Lm = sb("Lm", [64, 4, 64])
        Xm = sb("Xm", [64, 4, 64])
        Pm = sb("Pm", [64, 4, 64])
        PTm = D2
        gB = Pm
        TTb = sb("TTb", [64, 4, 64], BF16)
        ATb = sb("ATb", [64, 4, 64], BF16)
        knb = sb("knb", [64, 512], BF16)
        kbb = sb("kbb", [64, 512], BF16)
        kbgb = sb("kbgb", [64, 512], BF16)
        kdecb = sb("kdecb", [64, 512], BF16)
        qnb = sb("qnb", [64, 512], BF16)
        qgb = sb("qgb", [64, 512], BF16)
        bvb = sb("bvb", [64, 512], BF16)
        trT = sb("trT", [128, 16, 64], BF16)
        Usb = zcs[0:64, :].rearrange("p (h d) -> p h d", h=4)
        WmT = sb("WmT", [128, 4, 64], BF16)
        dlt = sb("dlt", [64, 4, 128], BF16)
        osb = cout[0:64, :]
        dob = sb("dob", [64, 512], BF16)
        Sst = sb("Sst", [128, 4, 128])
        Sbf = sb("Sbf", [128, 4, 128], BF16)
        GA, GB_, GC = PZ, P2, ACC
        gstate = {"prev": None}

        def lin64(cs, c0, c1):
            return linear(hnT, "hnT", 8, w_in1, "w_in1", c0, c1, msl=slice(cs, cs + 64))

        def gdn_chunk(cs, seq_start, seq_end, sidx, conv_out_idx, init_state_idx, row0):
            xb, xbk = XB.next()
            for c3 in range(3):
                p, pk = lin64(cs, 2048 + c3 * 512, 2560 + c3 * 512)
                S.add("dve", lambda e, p=p, c3=c3: e.tensor_copy(out=xraw[:, c3 * 512:(c3 + 1) * 512], in_=p[0:64, :]), reads=[pk], writes=["xraw"])
                S.add("act", lambda e, p=p, c3=c3, xb=xb: e.copy(out=xb[:, c3 * 512:(c3 + 1) * 512], in_=p[0:64, :]), reads=[pk], writes=[xbk])
            p, pk = lin64(cs, 3584, 4096)
            S.add("act", lambda e, p=p: e.activation(out=zdc[:], in_=p[0:64, :], func=AF.Silu), reads=[pk], writes=["zdc"])
            p, pk = lin64(cs, 4096, 4104)
            S.add("dve", lambda e, p=p: e.tensor_copy(out=ab[:], in_=p[0:64, 0:8]), reads=[pk], writes=["ab"])
            if conv_out_idx is not None:
                S.add("pool", lambda e: e.dma_start(out=o_conv[conv_out_idx], in_=xraw[61:64, :]), reads=["xraw"], writes=["o_conv"], is_dma=True)
            if seq_start:
                if init_state_idx is None:
                    xp, xpk = None, None
                else:
                    xp, xpk = XB.next()
                    S.add("pool", lambda e, xp=xp: e.dma_start(out=xp[61:64, :], in_=state_conv[init_state_idx]), writes=[xpk], is_dma=True)
            else:
                xp, xpk = gstate["prev"]
            gstate["prev"] = (xb, xbk)
            S.add("dve", lambda e: e.tensor_tensor(out=xc, in0=xraw, in1=cwb[:, 3, :], op=ALU.mult), reads=["xraw", "cwb"], writes=["xc"])
            for i_ in range(1, 4):
                for c3 in range(3):
                    cs3 = slice(c3 * 512, (c3 + 1) * 512)
                    p, pk = PS.next()
                    S.add("pe", lambda e, p=p, i_=i_, cs3=cs3, xb=xb: e.matmul(p[0:64, :], lhsT=Sh[:, i_ - 1, :], rhs=xb[:, cs3], start=True, stop=(xp is None)),
                          reads=["Sh", xbk], writes=[pk])
                    if xp is not None:
                        S.add("pe", lambda e, p=p, i_=i_, cs3=cs3, xp=xp: e.matmul(p[0:64, :], lhsT=ShP[:, i_ - 1, :], rhs=xp[:, cs3], start=False, stop=True),
                              reads=["ShP", xpk], writes=[pk])
                    S.add("dve", lambda e, p=p, i_=i_, cs3=cs3: e.tensor_tensor(out=gtmp[:], in0=p[0:64, :], in1=cwb[:, 3 - i_, cs3], op=ALU.mult),
                          reads=[pk, "cwb"], writes=["gtmp"])
                    S.add("pool", lambda e, cs3=cs3: e.tensor_tensor(out=xc[:, cs3], in0=xc[:, cs3], in1=gtmp[:], op=ALU.add), reads=["xc", "gtmp"], writes=["xc"])
            S.add("act", lambda e: e.activation(out=xc, in_=xc, func=AF.Silu), reads=["xc"], writes=["xc"])
            for j in range(8):
                S.add("act", lambda e, j=j: e.activation(out=gtmp[:, 0:128], in_=xc[:, j * 128:(j + 1) * 128], func=AF.Square, accum_out=ssn[:, j:j + 1]),
                      reads=["xc"], writes=["gtmp", "ssn"])
            S.add("dve", lambda e: e.tensor_scalar(out=ssn[:], in0=ssn[:], scalar1=EPS, scalar2=None, op0=ALU.add), reads=["ssn"], writes=["ssn"])
            S.add("pool", lambda e: e.tensor_tensor(out=ssn[:], in0=ssn[:], in1=mhalf[0:64, :].broadcast_to([64, 8]), op=ALU.pow), reads=["ssn", "mhalf"], writes=["ssn"])
            S.add("dve", lambda e: e.tensor_scalar(out=ssn[:, 0:4], in0=ssn[:, 0:4], scalar1=128.0 ** -0.5, scalar2=None, op0=ALU.mult), reads=["ssn"], writes=["ssn"])
            S.add("dve", lambda e: e.tensor_tensor(out=gg[:], in0=ab[:, 0:4], in1=dtb[:], op=ALU.add), reads=["ab", "dtb"], writes=["gg"])
            S.add("act", lambda e: e.activation(out=gg[:], in_=gg[:], func=AF.Exp), reads=["gg"], writes=["gg"])
            S.add("act", lambda e: e.activation(out=gg[:], in_=gg[:], func=AF.Ln, bias=1.0), reads=["gg"], writes=["gg"])
            S.add("dve", lambda e: e.tensor_tensor(out=gg[:], in0=gg[:], in1=negA[:], op=ALU.mult), reads=["gg", "negA"], writes=["gg"])
            S.add("act", lambda e: e.activation(out=beta[:], in_=ab[:, 4:8], func=AF.Sigmoid), reads=["ab"], writes=["beta"])
            S.add("pe", lambda e: e.matmul(GC[0:64, 0:4], lhsT=Uincl[:], rhs=gg[:], start=True, stop=True), reads=["Uincl", "gg"], writes=["acc"])
            S.add("pe", lambda e: e.matmul(GC[:, 8:12], lhsT=ones128[:], rhs=gg[:], start=True, stop=True), reads=["ones128", "gg"], writes=["acc"])
            S.add("dve", lambda e: e.tensor_copy(out=Gcol[:], in_=GC[0:64, 0:4]), reads=["acc"], writes=["Gcol"])
            S.add("act", lambda e: e.activation(out=egl[:], in_=GC[:, 8:12], func=AF.Exp), reads=["acc"], writes=["egl"])
            S.add("dve", lambda e: e.tensor_tensor(out=dlast[:], in0=GC[0:64, 8:12], in1=Gcol[:], op=ALU.subtract), reads=["acc", "Gcol"], writes=["dlast"])
            S.add("act", lambda e: e.activation(out=dlast[:], in_=dlast[:], func=AF.Exp), reads=["dlast"], writes=["dlast"])
            S.add("act", lambda e: e.activation(out=eG[:], in_=Gcol[:], func=AF.Exp), reads=["Gcol"], writes=["eG"])
            S.add("dve", lambda e: e.tensor_copy(out=gB[:], in_=gg[:].unsqueeze(2).broadcast_to([64, 4, 64])), reads=["gg"], writes=["gB"])
            for hh in range(4):
                S.add("pe", lambda e, hh=hh: e.matmul(GA[0:64, hh * 64:(hh + 1) * 64], lhsT=gB[:, hh, :], rhs=Uincl[:], start=True, stop=True),
                      reads=["gB", "Uincl"], writes=["pz"])
            for hh in range(4):
                S.add("dve", lambda e, hh=hh: e.scalar_tensor_tensor(out=D1[:, hh, :], in0=GA[0:64, hh * 64:(hh + 1) * 64], scalar=Gcol[:, hh:hh + 1], in1=nm_incl[:],
                                                                  op0=ALU.subtract, op1=ALU.add), reads=["pz", "Gcol", "nm_incl"], writes=["D1"])
                S.add("dve", lambda e, hh=hh: e.tensor_scalar(out=D2[:, hh, :], in0=GA[0:64, hh * 64:(hh + 1) * 64], scalar1=Gcol[:, hh:hh + 1], scalar2=-1.0,
                                                           op0=ALU.subtract, op1=ALU.mult), reads=["pz", "Gcol"], writes=["D2"])
            S.add("dve", lambda e: e.tensor_tensor(out=D2[:], in0=D2[:], in1=nm_low[:].unsqueeze(1).broadcast_to([64, 4, 64]), op=ALU.add), reads=["D2", "nm_low"], writes=["D2"])
            S.add("act", lambda e: e.activation(out=D1[:], in_=D1[:], func=AF.Exp), reads=["D1"], writes=["D1"])
            S.add("act", lambda e: e.activation(out=D2[:], in_=D2[:], func=AF.Exp), reads=["D2"], writes=["D2"])
            xq = xc[:, 0:512].rearrange("p (h d) -> p h d", h=4)
            xk = xc[:, 512:1024].rearrange("p (h d) -> p h d", h=4)
            xv = xc[:, 1024:1536].rearrange("p (h d) -> p h d", h=4)
            def bc(t_, lo):
                return t_[:, lo:lo + 4].unsqueeze(2).broadcast_to([64, 4, 128])
            def v3(t_):
                return t_[:].rearrange("p (h d) -> p h d", h=4)
            S.add("dve", lambda e: e.tensor_tensor(out=v3(qnb), in0=xq, in1=bc(ssn, 0), op=ALU.mult), reads=["xc", "ssn"], writes=["qnb"])
            S.add("dve", lambda e: e.tensor_tensor(out=v3(knb), in0=xk, in1=bc(ssn, 4), op=ALU.mult), reads=["xc", "ssn"], writes=["knb"])
            S.add("dve", lambda e: e.tensor_tensor(out=v3(bvb), in0=xv, in1=bc(beta, 0), op=ALU.mult), reads=["xc", "beta"], writes=["bvb"])
            S.add("pool", lambda e: e.tensor_tensor(out=v3(kbb), in0=v3(knb), in1=bc(beta, 0), op=ALU.mult), reads=["knb", "beta"], writes=["kbb"])
            S.add("pool", lambda e: e.tensor_tensor(out=v3(kbgb), in0=v3(kbb), in1=bc(eG, 0), op=ALU.mult), reads=["kbb", "eG"], writes=["kbgb"])
            S.add("pool", lambda e: e.tensor_tensor(out=v3(kdecb), in0=v3(knb), in1=bc(dlast, 0), op=ALU.mult), reads=["knb", "dlast"], writes=["kdecb"])
            S.add("pool", lambda e: e.tensor_tensor(out=v3(qgb), in0=v3(qnb), in1=bc(eG, 0), op=ALU.mult), reads=["qnb", "eG"], writes=["qgb"])
            for gi, (src, srck) in enumerate(((knb, "knb"), (kbb, "kbb"), (qnb, "qnb"), (qgb, "qgb"))):
                pt, ptk = PT.next()
                for hh in range(4):
                    S.add("pe", lambda e, hh=hh, src=src, pt=pt: e.transpose(out=pt[:, hh, 0:64], in_=src[:, hh * 128:(hh + 1) * 128], identity=ident[0:64, 0:64]),
                          reads=[srck, "ident"], writes=[ptk])
                S.add("act", lambda e, pt=pt, gi=gi: e.copy(out=trT[:, gi * 4:(gi + 1) * 4, :], in_=pt[:, 0:4, 0:64]), reads=[ptk], writes=["trT"])
            knT = lambda hh: trT[:, hh, :]
            kbT = lambda hh: trT[:, 4 + hh, :]
            qnT = lambda hh: trT[:, 8 + hh, :]
            qgT = lambda hh: trT[:, 12 + hh, :]
            for hh in range(4):
                S.add("pe", lambda e, hh=hh: e.matmul(GA[0:64, hh * 64:(hh + 1) * 64], lhsT=knT(hh), rhs=kbT(hh), start=True, stop=True), reads=["trT"], writes=["pz"])
                S.add("pe", lambda e, hh=hh: e.matmul(GA[0:64, 256 + hh * 64:256 + (hh + 1) * 64], lhsT=kbT(hh), rhs=knT(hh), start=True, stop=True), reads=["trT"], writes=["pz"])
                S.add("pe", lambda e, hh=hh: e.matmul(GA[0:64, 512 + hh * 64:512 + (hh + 1) * 64], lhsT=knT(hh), rhs=qnT(hh), start=True, stop=True), reads=["trT"], writes=["pz"])
            GA3 = lambda o_: GA[0:64, o_:o_ + 256].rearrange("p (h t) -> p h t", h=4)
            S.add("dve", lambda e: e.tensor_tensor(out=ATb[:], in0=GA3(512), in1=D1[:], op=ALU.mult), reads=["pz", "D1"], writes=["ATb"])
            S.add("dve", lambda e: e.tensor_tensor(out=Mm[:], in0=GA3(0), in1=D1[:], op=ALU.mult), reads=["pz", "D1"], writes=["Mm"])
            S.add("dve", lambda e: e.tensor_tensor(out=Mm[:], in0=Mm[:], in1=mask01[0:64, 0:64].unsqueeze(1).broadcast_to([64, 4, 64]), op=ALU.mult), reads=["Mm", "mask01"], writes=["Mm"])
            S.add("dve", lambda e: e.tensor_tensor(out=Lm[:], in0=GA3(256), in1=D2[:], op=ALU.mult), reads=["pz", "D2"], writes=["Lm"])
            S.add("dve", lambda e: e.tensor_tensor(out=Xm[:], in0=identf[:].unsqueeze(1).broadcast_to([64, 4, 64]), in1=Mm[:], op=ALU.subtract), reads=["identf", "Mm"], writes=["Xm"])
            for hh in range(4):
                S.add("pe", lambda e, hh=hh: e.matmul(GB_[0:64, hh * 64:(hh + 1) * 64], lhsT=Lm[:, hh, :], rhs=Mm[:, hh, :], start=True, stop=True), reads=["Lm", "Mm"], writes=["p2"])
                S.add("pe", lambda e, hh=hh: e.matmul(GB_[0:64, 256 + hh * 64:256 + (hh + 1) * 64], lhsT=Mm[:, hh, :], rhs=Lm[:, hh, :], start=True, stop=True), reads=["Lm", "Mm"], writes=["p2"])
            GB3 = lambda o_: GB_[0:64, o_:o_ + 256].rearrange("p (h t) -> p h t", h=4)
            S.add("dve", lambda e: e.tensor_copy(out=Pm[:], in_=GB3(0)), reads=["p2"], writes=["Pm"])
            S.add("act", lambda e: e.copy(out=PTm[:], in_=GB3(256)), reads=["p2"], writes=["PTm"])
            for lvl in range(5):
                for hh in range(4):
                    S.add("pe", lambda e, hh=hh: e.matmul(GB_[0:64, 512 + hh * 64:512 + (hh + 1) * 64], lhsT=PTm[:, hh, :], rhs=Xm[:, hh, :], start=True, stop=True), reads=["PTm", "Xm"], writes=["p2"])
                    if lvl < 4:
                        S.add("pe", lambda e, hh=hh: e.matmul(GB_[0:64, hh * 64:(hh + 1) * 64], lhsT=PTm[:, hh, :], rhs=Pm[:, hh, :], start=True, stop=True), reads=["PTm", "Pm"], writes=["p2"])
                        S.add("pe", lambda e, hh=hh: e.matmul(GB_[0:64, 256 + hh * 64:256 + (hh + 1) * 64], lhsT=Pm[:, hh, :], rhs=PTm[:, hh, :], start=True, stop=True), reads=["PTm", "Pm"], writes=["p2"])
                S.add("dve", lambda e: e.tensor_tensor(out=Xm[:], in0=Xm[:], in1=GB3(512), op=ALU.add), reads=["Xm", "p2"], writes=["Xm"])
                if lvl < 4:
                    S.add("dve", lambda e: e.tensor_copy(out=Pm[:], in_=GB3(0)), reads=["p2"], writes=["Pm"])
                    S.add("act", lambda e: e.copy(out=PTm[:], in_=GB3(256)), reads=["p2"], writes=["PTm"])
            S.add("act", lambda e: e.copy(out=TTb[:], in_=Xm[:]), reads=["Xm"], writes=["TTb"])
            for hh in range(4):
                S.add("pe", lambda e, hh=hh: e.matmul(GA[0:64, hh * 128:(hh + 1) * 128], lhsT=TTb[:, hh, :], rhs=bvb[:, hh * 128:(hh + 1) * 128], start=True, stop=True),
                      reads=["TTb", "bvb"], writes=["pz"])
                S.add("pe", lambda e, hh=hh: e.matmul(GB_[:, hh * 64:(hh + 1) * 64], lhsT=kbgb[:, hh * 128:(hh + 1) * 128], rhs=TTb[:, hh, :], start=True, stop=True),
                      reads=["TTb", "kbgb"], writes=["p2"])
            S.add("dve", lambda e: e.tensor_copy(out=Usb, in_=GA[0:64, 0:512].rearrange("p (h d) -> p h d", h=4)), reads=["pz"], writes=["Usb"])
            S.add("act", lambda e: e.copy(out=WmT[:], in_=GB_[:, 0:256].rearrange("p (h t) -> p h t", h=4)), reads=["p2"], writes=["WmT"])
            if seq_start:
                if init_state_idx is None:
                    S.add("pool", lambda e: e.memset(Sst[:], 0.0), writes=["Sst"])
                else:
                    S.add("sp", lambda e: e.dma_start(out=Sst[:], in_=state_d[init_state_idx].rearrange("h d e -> d h e")), writes=["Sst"], is_dma=True)
                S.add("act", lambda e: e.copy(out=Sbf[:], in_=Sst[:]), reads=["Sst"], writes=["Sbf"])
            for hh in range(4):
                S.add("pe", lambda e, hh=hh: e.matmul(GA[0:64, hh * 128:(hh + 1) * 128], lhsT=WmT[:, hh, :], rhs=Sbf[:, hh, :], start=True, stop=True), reads=["WmT", "Sbf"], writes=["pz"])
            S.add("dve", lambda e: e.tensor_tensor(out=dlt[:], in0=Usb, in1=GA[0:64, 0:512].rearrange("p (h d) -> p h d", h=4), op=ALU.subtract), reads=["Usb", "pz"], writes=["dlt"])
            for hh in range(4):
                S.add("pe", lambda e, hh=hh: e.matmul(GB_[0:64, hh * 128:(hh + 1) * 128], lhsT=qgT(hh), rhs=Sbf[:, hh, :], start=True, stop=False), reads=["trT", "Sbf"], writes=["p2"])
                S.add("pe", lambda e, hh=hh: e.matmul(GB_[0:64, hh * 128:(hh + 1) * 128], lhsT=ATb[:, hh, :], rhs=dlt[:, hh, :], start=False, stop=True), reads=["ATb", "dlt"], writes=["p2"])
                S.add("pe", lambda e, hh=hh: e.matmul(GC[:, hh * 128:(hh + 1) * 128], lhsT=kdecb[:, hh * 128:(hh + 1) * 128], rhs=dlt[:, hh, :], start=True, stop=True), reads=["kdecb", "dlt"], writes=["acc"])
            S.add("dve", lambda e: e.tensor_copy(out=osb, in_=GB_[0:64, 0:512]), reads=["p2"], writes=["osb"])
            for hh in range(4):
                S.add("dve", lambda e, hh=hh: e.scalar_tensor_tensor(out=Sst[:, hh, :], in0=Sst[:, hh, :], scalar=egl[:, hh:hh + 1], in1=GC[:, hh * 128:(hh + 1) * 128],
                                                                  op0=ALU.mult, op1=ALU.add), reads=["Sst", "egl", "acc"], writes=["Sst"])
            S.add("act", lambda e: e.copy(out=Sbf[:], in_=Sst[:]), reads=["Sst"], writes=["Sbf"])
            if seq_end:
                S.add("pool", lambda e: e.dma_start(out=o_sd[sidx].rearrange("h d e -> d h e"), in_=Sst[:]), reads=["Sst"], writes=["o_sd"], is_dma=True)
            for hh in range(4):
                S.add("act", lambda e, hh=hh: e.activation(out=gtmp[:, 0:128], in_=osb[:, hh * 128:(hh + 1) * 128], func=AF.Square, accum_out=ssn[:, hh:hh + 1]),
                      reads=["osb"], writes=["gtmp", "ssn"])
            S.add("dve", lambda e: e.tensor_scalar(out=ssn[:, 0:4], in0=ssn[:, 0:4], scalar1=1.0 / 128, scalar2=EPS, op0=ALU.mult, op1=ALU.add), reads=["ssn"], writes=["ssn"])
            S.add("pool", lambda e: e.tensor_tensor(out=ssn[:, 0:4], in0=ssn[:, 0:4], in1=mhalf[0:64, :].broadcast_to([64, 4]), op=ALU.pow), reads=["ssn", "mhalf"], writes=["ssn"])
            S.add("dve", lambda e: e.tensor_tensor(out=v3(osb), in0=v3(osb), in1=bc(ssn, 0), op=ALU.mult), reads=["osb", "ssn"], writes=["osb"])
            S.add("dve", lambda e: e.tensor_tensor(out=v3(osb), in0=v3(osb), in1=gdn_g[:].unsqueeze(1).broadcast_to([64, 4, 128]), op=ALU.mult), reads=["osb", "gdn_g"], writes=["osb"])
            S.add("dve", lambda e: e.tensor_tensor(out=dob[:], in0=osb, in1=zdc[:], op=ALU.mult), reads=["osb", "zdc"], writes=["dob"])
            S.add("pool", lambda e: e.dma_start(out=mixin[row0:row0 + 64, 512:1024], in_=dob[:]), reads=["dob"], writes=["mixin"], is_dma=True)

        def gdn_tile(i, samp, h, hk):
            if samp:
                for s_ in range(2):
                    gdn_chunk(s_ * 64, True, True, 1 + s_, 1 + s_, s_, s_ * 64)
            else:
                for c_ in range(2):
                    first = (i == 0 and c_ == 0)
                    lastc = (i == ntp - 1 and c_ == 1)
                    gdn_chunk(c_ * 64, first, lastc, 0, 0 if lastc else None, None, c_ * 64)

        def load_tile(i):
            h, hk = H.next()
            p1, p1k = P1.next()
            S.add("sp", lambda e: e.dma_start(out=h[:], in_=h1_scr[i * 128:(i + 1) * 128, :]), reads=["h1_scr"], writes=[hk], is_dma=True)
            S.add("sp", lambda e: e.dma_start(out=p1[:], in_=pin[1, i * 128:(i + 1) * 128, :]), writes=[p1k], is_dma=True)
            return h, hk, p1, p1k

        for i in range(nt):
            h, hk, p1, p1k = load_tile(i)
            samp = (i == ntp)
            r0 = i * 128
            rmsnorm(h, hk, g_l1, "g_l1", xn, "xn")
            transpose_to(xn, "xn", 8, hnT, "hnT")
            p, pk = linear(hnT, "hnT", 8, w_in1, "w_in1", 0, 512)
            S.add("dve", lambda e, p=p: e.tensor_scalar(out=qbf[:], in0=p[:], scalar1=0.125, scalar2=None, op0=ALU.mult), reads=[pk], writes=["qbf"])
            p, pk = linear(hnT, "hnT", 8, w_in1, "w_in1", 512, 1024)
            S.add("act", lambda e, p=p: e.copy(out=kf, in_=p[:]), reads=[pk], writes=["kf"])
            S.add("dve", lambda e, p=p: e.tensor_copy(out=kbf[:], in_=p[:]), reads=[pk], writes=["kbf"])
            S.add("pool", lambda e, r0=r0: e.dma_start(out=o_kc[r0:r0 + 128, :], in_=kf), reads=["kf"], writes=["o_kc"], is_dma=True)
            p, pk = linear(hnT, "hnT", 8, w_in1, "w_in1", 1024, 1536)
            S.add("act", lambda e, p=p: e.copy(out=vf, in_=p[:]), reads=[pk], writes=["vf"])
            S.add("dve", lambda e, p=p: e.tensor_copy(out=vbf[:], in_=p[:]), reads=[pk], writes=["vbf"])
            S.add("pool", lambda e, r0=r0: e.dma_start(out=o_vc[r0:r0 + 128, :], in_=vf), reads=["vf"], writes=["o_vc"], is_dma=True)
            p, pk = linear(hnT, "hnT", 8, w_in1, "w_in1", 1536, 2048)
            S.add("act", lambda e, p=p: e.activation(out=zcs[:], in_=p[:], func=AF.Silu), reads=[pk], writes=["zcs"])
            if samp:
                p, pk = linear(hnT, "hnT", 8, w_in1, "w_in1", 1024, 1536, msl=slice(64, 128))
                S.add("dve", lambda e, p=p: e.tensor_copy(out=vbs[:], in_=p[0:64, :]), reads=[pk], writes=["vbs"])
            pt, ptk = PT.next()
            for k in range(4):
                S.add("pe", lambda e, k=k, pt=pt: e.transpose(out=pt[:, k, :], in_=qbf[:, k * 128:(k + 1) * 128], identity=ident[:]),
                      reads=["qbf", "ident"], writes=[ptk])
            QTv = QT[:].rearrange("p (a two) t -> p a two t", two=2)
            S.add("act", lambda e, pt=pt, QTv=QTv: e.copy(out=QTv[0:64, :, 0, :], in_=pt[0:64, 0:4, :]), reads=[ptk], writes=["QT"])
            S.add("act", lambda e, pt=pt, QTv=QTv: e.copy(out=QTv[64:128, :, 1, :], in_=pt[64:128, 0:4, :]), reads=[ptk], writes=["QT"])
            transpose_to(kbf, "kbf", 4, KTc, "KTc")
            if not samp:
                S.add("pool", lambda e, r0=r0: e.dma_start(out=kts[:, :, r0:r0 + 128], in_=KTc[:]), reads=["KTc"], writes=[f"kts{i}"], is_dma=True)
                S.add("pool", lambda e, r0=r0: e.dma_start(out=vs[r0:r0 + 128, :], in_=vbf[:]), reads=["vbf"], writes=[f"vs{i}"], is_dma=True)
            S.add("pool", lambda e: e.memset(cum, 0.0), writes=["cum"])
            if not samp:
                nblk = i + 1
                attn_step(lambda h: KTc[:, h // 2, :], "KTc", vbf, "vbf", 0, 128, 128, True, True, nblk == 1)
                for n_, kb in enumerate(range(i - 1, -1, -1)):
                    ktb, ktbk = KB.next()
                    vb, vbk = VB.next()
                    S.add("sp", lambda e, ktb=ktb, kb=kb: e.dma_start(out=ktb[:], in_=kts[:, :, kb * 128:(kb + 1) * 128]), reads=[f"kts{kb}"], writes=[ktbk], is_dma=True)
                    S.add("sp", lambda e, vb=vb, kb=kb: e.dma_start(out=vb[:], in_=vs[kb * 128:(kb + 1) * 128, :]), reads=[f"vs{kb}"], writes=[vbk], is_dma=True)
                    attn_step(lambda h, ktb=ktb: ktb[:, h // 2, :], ktbk, vb, vbk, 0, 128, 128, False, False, kb == 0)
                S.add("dve", lambda e: e.tensor_tensor(out=cout[:], in0=ACC[:], in1=zcs[:], op=ALU.mult), reads=["acc", "zcs"], writes=["cout"])
            else:
                for s_ in range(2):
                    if s_ == 1:
                        S.add("pool", lambda e: e.memset(cum, 0.0), writes=["cum"])
                    vnew, vnk = (vbf, "vbf") if s_ == 0 else (vbs, "vbs")
                    attn_step(lambda h, s_=s_: KTc[:, h // 2, s_ * 64:(s_ + 1) * 64], "KTc", vnew, vnk, s_ * 64, 64, 64, True, True, False)
                    for kb in range(31, -1, -1):
                        ck, ckk = CK.next()
                        vb, vbk = VB.next()
                        ktb, ktbk = KB.next()
                        S.add("pool", lambda e, ck=ck, kb=kb, s_=s_: e.dma_start(out=ck[:], in_=cache_k[s_, kb * 128:(kb + 1) * 128, :]), writes=[ckk], is_dma=True)
                        S.add("pool", lambda e, vb=vb, kb=kb, s_=s_: e.dma_start(out=vb[:], in_=cache_v[s_, kb * 128:(kb + 1) * 128, :]), writes=[vbk], is_dma=True)
                        transpose_to(ck, ckk, 4, ktb, ktbk)
                        attn_step(lambda h, ktb=ktb: ktb[:, h // 2, :], ktbk, vb, vbk, s_ * 64, 64, 128, False, False, kb == 0)
                    if s_ == 0:
                        S.add("dve", lambda e: e.tensor_tensor(out=cout[0:64, :], in0=ACC[0:64, :], in1=zcs[0:64, :], op=ALU.mult), reads=["acc", "zcs"], writes=["cout"])
                    else:
                        S.add("dve", lambda e: e.tensor_copy(out=ebuf[0:64, 0:512], in_=ACC[0:64, :]), reads=["acc"], writes=["ebuf"])
                        S.add("pool", lambda e: e.dma_start(out=cout[64:128, :], in_=ebuf[0:64, 0:512]), reads=["ebuf"], writes=["cout"], is_dma=True)
                        S.add("dve", lambda e: e.tensor_tensor(out=cout[64:128, :], in0=cout[64:128, :], in1=zcs[64:128, :], op=ALU.mult), reads=["cout", "zcs"], writes=["cout"])
            if debug_h:
                S.add("pool", lambda e, r0=r0: e.dma_start(out=dbg2[r0:r0 + 128, :], in_=cout[:]), reads=["cout"], writes=["dbg2"], is_dma=True)
            S.add("act", lambda e: e.copy(out=mixin[:, 0:512], in_=cout[:]), reads=["cout"], writes=["mixin"])
            gdn_tile(i, samp, h, hk)
            transpose_to(mixin, "mixin", 8, mixT, "mixT")
            for hh in range(2):
                p, pk = linear(mixT, "mixT", 8, w_out1, "w_out1", hh * 512, (hh + 1) * 512)
                S.add("dve", lambda e, p=p, hh=hh, h=h: e.tensor_tensor(out=h[:, hh * 512:(hh + 1) * 512], in0=h[:, hh * 512:(hh + 1) * 512], in1=p[:], op=ALU.add),
                      reads=[pk, hk], writes=[hk])
            rmsnorm(h, hk, g_p1, "g_p1", xn, "xn")
            transpose_to(xn, "xn", 8, hnT, "hnT")
            S.add("act", lambda e, p1=p1: e.copy(out=pbf[:], in_=p1[:]), reads=[p1k], writes=["pbf"])
            transpose_to(pbf, "pbf", 2, pT, "pT")
            for hh in range(2):
                p, pk = linear(hnT, "hnT", 8, w_gate1, "w_gate1", hh * 512, (hh + 1) * 512)
                S.add("act", lambda e, p=p, hh=hh: e.activation(out=gate[:, hh * 512:(hh + 1) * 512], in_=p[:], func=AF.Sigmoid), reads=[pk], writes=["gate"])
                p, pk = linear(pT, "pT", 2, w_pp1, "w_pp1", hh * 512, (hh + 1) * 512)
                S.add("dve", lambda e, p=p, hh=hh: e.tensor_tensor(out=gate[:, hh * 512:(hh + 1) * 512], in0=gate[:, hh * 512:(hh + 1) * 512], in1=p[:], op=ALU.mult),
                      reads=[pk, "gate"], writes=["gate"])
            S.add("dve", lambda e, h=h: e.tensor_tensor(out=h[:], in0=h[:], in1=gate[:], op=ALU.add), reads=[hk, "gate"], writes=[hk])
            rmsnorm(h, hk, g_f, "g_f", yout, "ebuf")
            S.add("pool", lambda e, r0=r0: e.dma_start(out=o_y[r0:r0 + 128, :], in_=yout), reads=["ebuf"], writes=["o_y"], is_dma=True)
        S.emit()
```

```python
import contextlib
import math
import numpy as np
import concourse.bass as bass
import concourse.mybir as mybir
from concourse.bass_utils import run_bass_kernel_spmd

F32 = mybir.dt.float32
BF16 = mybir.dt.bfloat16
I32 = mybir.dt.int32
AF = mybir.ActivationFunctionType
ALU = mybir.AluOpType

NCORES = 8
D = 1024
SEQ = 8192
NTP = SEQ // 128
EPS = 1e-6
EPOCH = 3000
DMA_EPOCH = 200
DMA_SLOTS = 4
TWO_PI = 2.0 * math.pi


class Op:
    __slots__ = ("eng", "fn", "waits", "need_inc", "is_dma", "slot", "slot_val", "inc_no")

    def __init__(self, eng, fn, is_dma):
        self.eng = eng
        self.fn = fn
        self.waits = []
        self.need_inc = False
        self.is_dma = is_dma
        self.slot = None
        self.slot_val = None
        self.inc_no = None


class Sched:
    ENGS = ("pe", "act", "dve", "pool", "sp")

    def __init__(self, nc):
        self.nc = nc
        self.ops = {e: [] for e in self.ENGS}
        self.last_w = {}
        self.readers = {}

    max_ops = None
    n_added = 0
    ALIAS = {"xraw": ("ebuf0", "ebuf1", "lm0", "lm1"), "xc": ("cum0", "cum1", "wb0_0", "wb0_1", "wb1_0", "wb1_1"),
             "pz": ("pz0", "pz1"), "p2": ("p20", "p21"), "ebuf": ("ebuf0", "ebuf1"), "lm": ("lm0", "lm1"), "cum": ("cum0", "cum1"), "kf": ("gate",), "vf": ("gate",), "Usb": ("zcs",), "osb": ("cout",),
             "gB": ("Pm",), "PTm": ("D2",)}

    def add(self, eng, fn, reads=(), writes=(), is_dma=False):
        op = Op(eng, fn, is_dma)
        Sched.n_added += 1
        if Sched.max_ops is not None and Sched.n_added > Sched.max_ops:
            return op
        reads = [a for k in reads for a in Sched.ALIAS.get(k, (k,))]
        writes = [a for k in writes for a in Sched.ALIAS.get(k, (k,))]
        writes = list(writes) + [k for k in reads if k.startswith(("ps", "pt", "qs", "qt", "pz", "p2", "acc"))]
        deps = []
        for k in reads:
            w = self.last_w.get(k)
            if w is not None:
                deps.append(w)
        for k in writes:
            w = self.last_w.get(k)
            if w is not None:
                deps.append(w)
            deps.extend(self.readers.get(k, ()))
        seen = set()
        for d in deps:
            if d is op or id(d) in seen:
                continue
            seen.add(id(d))
            if (not d.is_dma) and (not is_dma) and d.eng == eng == "pe":
                continue
            op.waits.append(d)
            d.need_inc = True
        for k in reads:
            self.readers.setdefault(k, []).append(op)
        for k in writes:
            self.last_w[k] = op
            self.readers[k] = []
        self.ops[eng].append(op)
        return op

    def emit(self):
        nc = self.nc
        n_epochs = {}
        n_dma = {}
        for e in self.ENGS:
            c = 0
            for op in self.ops[e]:
                if (not op.is_dma) and op.need_inc:
                    op.inc_no = c
                    c += 1
            n_epochs[e] = max(1, (c + EPOCH - 1) // EPOCH)
            j = 0
            for op in self.ops[e]:
                if op.is_dma:
                    u = j // DMA_SLOTS
                    op.slot = (u // DMA_EPOCH) * DMA_SLOTS + j % DMA_SLOTS
                    op.slot_val = 16 * (u % DMA_EPOCH + 1)
                    j += 1
            n_dma[e] = j
        with contextlib.ExitStack() as st:
            sems = {e: [st.enter_context(nc.semaphore(f"s_{e}_{i}")) for i in range(n_epochs[e])]
                    for e in self.ENGS}
            dsems = {e: [st.enter_context(nc.semaphore(f"d_{e}_{i}"))
                         for i in range(DMA_SLOTS * ((n_dma[e] // DMA_SLOTS) // DMA_EPOCH + 1))]
                     for e in self.ENGS if n_dma[e] > 0}
            block = st.enter_context(nc.Block())

            def target(d):
                if d.is_dma:
                    return dsems[d.eng][d.slot], d.slot_val
                return sems[d.eng][d.inc_no // EPOCH], d.inc_no % EPOCH + 1

            def run(e, eng):
                waited = {}
                last_on_slot = {}
                for op in self.ops[e]:
                    ws = list(op.waits)
                    if op.is_dma and (op.slot % DMA_SLOTS) in last_on_slot:
                        ws.append(last_on_slot[op.slot % DMA_SLOTS])
                    for d in ws:
                        sem, val = target(d)
                        if waited.get(sem.num, 0) >= val:
                            continue
                        waited[sem.num] = val
                        eng.wait_ge(sem, val)
                    ins = op.fn(eng)
                    if op.is_dma:
                        ins.then_inc(dsems[e][op.slot], 16)
                        last_on_slot[op.slot % DMA_SLOTS] = op
                    elif op.need_inc:
                        ins.then_inc(sems[e][op.inc_no // EPOCH], 1)
                for d in last_on_slot.values():
                    sem, val = target(d)
                    eng.wait_ge(sem, val)

            block.tensor(lambda eng: run("pe", eng))
            block.scalar(lambda eng: run("act", eng))
            block.vector(lambda eng: run("dve", eng))
            block.gpsimd(lambda eng: run("pool", eng))
            block.sync(lambda eng: run("sp", eng))


class Rot:
    def __init__(self, bufs, name):
        self.bufs = bufs
        self.name = name
        self.i = 0

    def next(self):
        j = self.i % len(self.bufs)
        self.i += 1
        return self.bufs[j], f"{self.name}{j}"


def build_program(ntp=NTP, debug_h=False):
    nt = ntp + 1
    ntok = nt * 128
    nc = bass.Bass("TRN2", target_bir_lowering=False)

    def din(name, shape):
        return nc.dram_tensor(name, list(shape), F32, kind="ExternalInput").ap()

    def dout(name, shape):
        return nc.dram_tensor(name, list(shape), F32, kind="ExternalOutput").ap()

    xin = din("xin", [ntok, D])
    pin = din("pin", [2, ntok, 256])
    sbre = din("sbre", [2, 32, 64])
    sbim = din("sbim", [2, 32, 64])
    norm_g = din("norm_g", [2, D])
    final_norm_g = din("final_norm_g", [D])
    ple_proj = din("ple_proj", [2, 256, D])
    ple_gate_w = din("ple_gate_w", [2, D, D])
    ple_norm_g = din("ple_norm_g", [2, D])
    even_w_in = din("even_w_in", [1, D, 2560])
    even_w_out = din("even_w_out", [1, 1024, D])
    a_ln_g = din("a_ln_g", [1, 512])
    a_ln_b = din("a_ln_b", [1, 512])
    a_w_s = din("a_w_s", [1, 4, 128, 128])
    a_b_s = din("a_b_s", [1, 4, 128])
    b_lam_re = din("b_lam_re", [1, 32, 64])
    b_lam_im = din("b_lam_im", [1, 32, 64])
    b_log_dt = din("b_log_dt", [1, 32])
    b_B_re = din("b_B_re", [1, 32, 64, 16])
    b_B_im = din("b_B_im", [1, 32, 64, 16])
    b_C_re = din("b_C_re", [1, 32, 16, 64])
    b_C_im = din("b_C_im", [1, 32, 16, 64])
    b_D = din("b_D", [1, 32, 16])
    b_glu_w = din("b_glu_w", [1, 512, 512])
    odd_w_in = din("odd_w_in", [1, D, 4104])
    odd_w_out = din("odd_w_out", [1, 1024, D])
    cache_k = din("cache_k", [2, 4096, 512])
    cache_v = din("cache_v", [2, 4096, 512])
    state_d = din("state_d", [2, 4, 128, 128])
    state_conv = din("state_conv", [2, 3, 1536])
    d_conv_w = din("d_conv_w", [1, 4, 1536])
    d_A_log = din("d_A_log", [1, 4])
    d_dt_bias = din("d_dt_bias", [1, 4])
    d_norm_g = din("d_norm_g", [1, 128])

    o_y = dout("o_y", [ntok, D])
    o_bre = dout("o_bre", [3, 32, 64])
    o_bim = dout("o_bim", [3, 32, 64])
    o_av = dout("o_av", [128, 512])
    o_kc = dout("o_kc", [ntok, 512])
    o_vc = dout("o_vc", [ntok, 512])
    o_conv = dout("o_conv", [3, 3, 1536])
    o_sd = dout("o_sd", [3, 4, 128, 128])
    h1_scr = nc.dram_tensor("h1_scr", [ntok, D], F32, kind="ExternalOutput" if debug_h else "Internal").ap()

    dbg = nc.dram_tensor("dbg", [128, 512], F32, kind="ExternalOutput").ap() if debug_h else None
    S = Sched(nc)
    st = contextlib.ExitStack()
    with st:
        def sb(name, shape, dt=F32):
            return st.enter_context(nc.sbuf_tensor(name, list(shape), dt))

        def ps(name, shape, dt=F32):
            return st.enter_context(nc.psum_tensor(name, list(shape), dt))

        st.enter_context(nc.allow_non_contiguous_dma("small one-time parameter layout loads"))

        PS = Rot([ps(f"ps{i}", [128, 512]) for i in range(5)], "ps")
        py_bank = ps("ps_y", [128, 512])
        PT = Rot([ps(f"pt{i}", [128, 8, 128], BF16) for i in range(2)], "pt")

        H = Rot([sb(f"h{i}", [128, D]) for i in range(2)], "h")
        P0 = Rot([sb(f"p0_{i}", [128, 256]) for i in range(2)], "p0_")
        xn = sb("xn", [128, D], BF16)
        hnT = sb("hnT", [128, 8, 128], BF16)
        ssq = sb("ssq", [128, 1])
        rstd = sb("rstd", [128, 1])
        ua = sb("ua", [128, 512])
        va = sb("va", [128, 512])
        vln = sb("vln", [128, 512])
        vbf = sb("vbf", [128, 512], BF16)
        za = sb("za", [128, 512])
        zb = sb("zb", [128, 512])
        ubf = sb("ubf", [128, 512], BF16)
        ubd = sb("ubd", [128, 512])
        ubT = sb("ubT", [128, 4, 128], BF16)
        bst = sb("bst", [128, 6])
        bag = sb("bag", [128, 2])
        mixin = sb("mixin", [128, 1024], BF16)
        mixT = sb("mixT", [128, 8, 128], BF16)
        yb = sb("yb", [128, 512])
        yg = sb("yg", [128, 512])
        ygb = sb("ygb", [128, 512], BF16)
        ygT = sb("ygT", [128, 4, 128], BF16)
        gate = sb("gate", [128, D])
        pbf = sb("pbf", [128, 256], BF16)
        pT = sb("pT", [128, 2, 128], BF16)
        W5 = {n: sb("w5" + n, [128, 512]) for n in ("T1", "T2", "T3", "T4", "vr", "vi", "zr", "zi")}
        xrb = sb("xrb", [128, 4, 128], BF16)
        xib = sb("xib", [128, 4, 128], BF16)

        ident = sb("ident", [128, 128], BF16)
        S.add("pool", lambda e: e.memset(ident[:], 0.0), writes=["ident"])
        S.add("pool", lambda e: e.affine_select(out=ident[:], in_=ident[:], compare_op=ALU.not_equal, fill=1.0,
                                               base=0, pattern=[[-1, 128]], channel_multiplier=1),
              reads=["ident"], writes=["ident"])
        mhalf = sb("mhalf", [128, 16])
        S.add("pool", lambda e: e.memset(mhalf[:], -0.5), writes=["mhalf"])

        def bcast_load(name, src, n):
            t = sb(name, [128, n])
            S.add("sp", lambda e: e.dma_start(out=t[:], in_=src.partition_broadcast(128)), writes=[name], is_dma=True)
            return t

        g_l0 = bcast_load("g_l0", norm_g[0], D)
        g_p0 = bcast_load("g_p0", ple_norm_g[0], D)
        lng = bcast_load("lng", a_ln_g[0], 512)
        lnb = bcast_load("lnb", a_ln_b[0], 512)
        Db = bcast_load("Db", b_D[0].rearrange("g p -> (g p)"), 512)

        def wload(name, src, kt, n, csz=512):
            t = sb(name, [128, kt, n], BF16)
            v = src.rearrange("(k p) n -> p k n", p=128)
            for c0 in range(0, n, csz):
                c1 = min(n, c0 + csz)
                S.add("pool", lambda e, c0=c0, c1=c1: e.dma_start(out=t[:, :, c0:c1], in_=v[:, :, c0:c1]),
                      writes=[name], is_dma=True)
            return t

        w_in0 = wload("w_in0", even_w_in[0], 8, 2560)
        w_out0 = wload("w_out0", even_w_out[0], 8, 1024)
        w_glu = wload("w_glu", b_glu_w[0], 4, 512)
        w_gate0 = wload("w_gate0", ple_gate_w[0], 8, 1024)
        w_pp0 = wload("w_pp0", ple_proj[0], 2, 1024)

        wl = gate[:, 0:512].rearrange("p (a b) -> p a b", a=4)
        wls = gate[:, 512:1024].rearrange("p (a b) -> p a b", a=4)
        S.add("sp", lambda e: e.dma_start(out=wl, in_=a_w_s[0].rearrange("g t s -> t g s")), writes=["gate"], is_dma=True)
        S.add("pool", lambda e: e.memset(wls, 0.0), writes=["gate"])
        S.add("sp", lambda e: e.dma_start(out=wls[0:64, :, 0:64], in_=a_w_s[0, :, 0:64, 0:64].rearrange("g t s -> t g s")),
              reads=["gate"], writes=["gate"], is_dma=True)
        S.add("sp", lambda e: e.dma_start(out=wls[64:128, :, 64:128], in_=a_w_s[0, :, 0:64, 0:64].rearrange("g t s -> t g s")),
              reads=["gate"], writes=["gate"], is_dma=True)
        wmixT = sb("wmixT", [128, 4, 128], BF16)
        wmixTs = sb("wmixTs", [128, 4, 128], BF16)
        for (src, dst, nm) in ((wl, wmixT, "wl"), (wls, wmixTs, "wls")):
            wbf = (xn[:, 0:512] if nm == "wl" else xn[:, 512:1024]).rearrange("p (a b) -> p a b", a=4)
            S.add("pool", lambda e, src=src: e.affine_select(out=src, in_=src, compare_op=ALU.is_ge, fill=0.0, base=0,
                                                            pattern=[[0, 4], [-1, 128]], channel_multiplier=1),
                  reads=["gate"], writes=["gate"])
            S.add("dve", lambda e, src=src, wbf=wbf: e.tensor_copy(out=wbf, in_=src), reads=["gate"], writes=["xn"])
            pt, ptk = PT.next()
            for g in range(4):
                S.add("pe", lambda e, g=g, wbf=wbf, pt=pt: e.transpose(out=pt[:, g, :], in_=wbf[:, g, :], identity=ident[:]),
                      reads=["xn", "ident"], writes=[ptk])
            S.add("dve", lambda e, dst=dst, pt=pt: e.tensor_copy(out=dst[:], in_=pt[:, 0:4, :]), reads=[ptk], writes=[nm + "T"])
        bsb = sb("bsb", [128, 4])
        bsbs = sb("bsbs", [128, 4])
        S.add("sp", lambda e: e.dma_start(out=bsb[:], in_=a_b_s[0].rearrange("g t -> t g")), writes=["bsb"], is_dma=True)
        S.add("sp", lambda e: e.dma_start(out=bsbs[0:64, :], in_=a_b_s[0, :, 0:64].rearrange("g t -> t g")), writes=["bsbs"], is_dma=True)
        S.add("sp", lambda e: e.dma_start(out=bsbs[64:128, :], in_=a_b_s[0, :, 0:64].rearrange("g t -> t g")), writes=["bsbs"], is_dma=True)

        lamr = sb("lamr", [128, 16])
        lami = sb("lami", [128, 16])
        ldt = sb("ldt", [128, 16])
        S.add("sp", lambda e: e.dma_start(out=lamr[:], in_=b_lam_re[0].rearrange("(k g) n -> (g n) k", g=2)), writes=["lamr"], is_dma=True)
        S.add("sp", lambda e: e.dma_start(out=lami[:], in_=b_lam_im[0].rearrange("(k g) n -> (g n) k", g=2)), writes=["lami"], is_dma=True)
        ldv = b_log_dt[0].rearrange("(k g) -> g k", g=2)
        S.add("sp", lambda e: e.dma_start(out=ldt[0:64, :], in_=ldv[0].partition_broadcast(64)), writes=["ldt"], is_dma=True)
        S.add("sp", lambda e: e.dma_start(out=ldt[64:128, :], in_=ldv[1].partition_broadcast(64)), writes=["ldt"], is_dma=True)
        Bl_r = sb("Bl_r", [128, 16, 16])
        Bl_i = sb("Bl_i", [128, 16, 16])
        Cl_r = sb("Cl_r", [128, 16, 16])
        Cl_i = sb("Cl_i", [128, 16, 16])
        S.add("sp", lambda e: e.dma_start(out=Bl_r[:], in_=b_B_re[0].rearrange("(k g) n p -> (g n) k p", g=2)), writes=["Bl_r"], is_dma=True)
        S.add("sp", lambda e: e.dma_start(out=Bl_i[:], in_=b_B_im[0].rearrange("(k g) n p -> (g n) k p", g=2)), writes=["Bl_i"], is_dma=True)
        for (srcd, dstt, nm) in ((b_C_re, Cl_r, "Cl_r"), (b_C_im, Cl_i, "Cl_i")):
            for k in range(16):
                for g2 in range(2):
                    S.add("sp", lambda e, srcd=srcd, dstt=dstt, k=k, g2=g2: e.dma_start(
                        out=dstt[g2 * 64:(g2 + 1) * 64, k, :],
                        in_=srcd[0, 2 * k + g2].rearrange("p n -> n p")),
                        writes=[nm], is_dma=True)

        dts = sb("dts", [128, 16])
        ldr = sb("ldr", [128, 16])
        ldi = sb("ldi", [128, 16])
        rmag = sb("rmag", [128, 16])
        S.add("act", lambda e: e.activation(out=dts[:], in_=ldt[:], func=AF.Exp), reads=["ldt"], writes=["dts"])
        S.add("dve", lambda e: e.tensor_tensor(out=ldr[:], in0=lamr[:], in1=dts[:], op=ALU.mult), reads=["lamr", "dts"], writes=["ldr"])
        S.add("dve", lambda e: e.tensor_tensor(out=ldi[:], in0=lami[:], in1=dts[:], op=ALU.mult), reads=["lami", "dts"], writes=["ldi"])
        S.add("act", lambda e: e.activation(out=rmag[:], in_=ldr[:], func=AF.Exp), reads=["ldr"], writes=["rmag"])
        idx = sb("idx", [128, 128])
        S.add("pool", lambda e: e.iota(idx[:], pattern=[[1, 128]], base=1, channel_multiplier=0, allow_small_or_imprecise_dtypes=True), writes=["idx"])
        Rc_p = sb("Rc", [128, 16, 128])
        Rs_p = sb("Rs", [128, 16, 128])
        Rm = sb("Rm", [128, 16, 128])
        kRc, kRs, kRm = "Rc", "Rs", "Rm"
        sc_a = W5["T1"][:].rearrange("p (a b) -> p a b", a=4)
        sc_t = W5["T2"][:].rearrange("p (a b) -> p a b", a=4)
        sc_f = W5["T3"][:].rearrange("p (a b) -> p a b", a=4)
        sc_i = W5["T4"][:].bitcast(I32).rearrange("p (a b) -> p a b", a=4)
        for c4 in range(4):
            ksl = slice(4 * c4, 4 * c4 + 4)
            S.add("dve", lambda e, ksl=ksl: e.tensor_tensor(out=sc_a, in0=ldi[:, ksl].unsqueeze(2).broadcast_to([128, 4, 128]),
                                                           in1=idx[:].unsqueeze(1).broadcast_to([128, 4, 128]), op=ALU.mult),
                  reads=["ldi", "idx"], writes=["T1"])
            for R_, kR, off in ((Rc_p, "Rc", 0.25), (Rs_p, "Rs", 0.0)):
                S.add("dve", lambda e, off=off: e.tensor_scalar(out=sc_t, in0=sc_a, scalar1=1.0 / TWO_PI, scalar2=off,
                                                               op0=ALU.mult, op1=ALU.add), reads=["T1"], writes=["T2"])
                S.add("dve", lambda e: e.tensor_copy(out=sc_i, in_=sc_t), reads=["T2"], writes=["T4"])
                S.add("dve", lambda e: e.tensor_copy(out=sc_f, in_=sc_i), reads=["T4"], writes=["T3"])
                S.add("dve", lambda e: e.tensor_tensor(out=sc_t, in0=sc_t, in1=sc_f, op=ALU.subtract),
                      reads=["T2", "T3"], writes=["T2"])
                S.add("act", lambda e, R_=R_, ksl=ksl: e.activation(out=R_[:, ksl, :], in_=sc_t, func=AF.Sin, scale=6.283179),
                      reads=["T2"], writes=[kR])
        S.add("dve", lambda e: e.tensor_copy(out=Rm[:], in_=rmag[:].unsqueeze(2).broadcast_to([128, 16, 128])),
              reads=["rmag"], writes=["Rm"])

        ar1 = sb("ar1", [128, 16])
        ai = sb("ai_", [128, 16])
        den = sb("den", [128, 16])
        t1 = sb("t1_", [128, 16])
        t2 = sb("t2_", [128, 16])
        cre = sb("cre", [128, 16])
        cim = sb("cim", [128, 16])
        S.add("dve", lambda e: e.tensor_tensor(out=ar1[:], in0=rmag[:], in1=Rc_p[:, :, 0], op=ALU.mult), reads=["rmag", kRc], writes=["ar1"])
        S.add("dve", lambda e: e.tensor_scalar(out=ar1[:], in0=ar1[:], scalar1=-1.0, scalar2=None, op0=ALU.add), reads=["ar1"], writes=["ar1"])
        S.add("dve", lambda e: e.tensor_tensor(out=ai[:], in0=rmag[:], in1=Rs_p[:, :, 0], op=ALU.mult), reads=["rmag", kRs], writes=["ai"])
        S.add("dve", lambda e: e.tensor_tensor(out=den[:], in0=lamr[:], in1=lamr[:], op=ALU.mult), reads=["lamr"], writes=["den"])
        S.add("dve", lambda e: e.tensor_tensor(out=t1[:], in0=lami[:], in1=lami[:], op=ALU.mult), reads=["lami"], writes=["t1"])
        S.add("dve", lambda e: e.tensor_tensor(out=den[:], in0=den[:], in1=t1[:], op=ALU.add), reads=["den", "t1"], writes=["den"])
        S.add("dve", lambda e: e.reciprocal(out=den[:], in_=den[:]), reads=["den"], writes=["den"])
        S.add("dve", lambda e: e.tensor_tensor(out=t1[:], in0=ar1[:], in1=lamr[:], op=ALU.mult), reads=["ar1", "lamr", "den"], writes=["t1"])
        S.add("dve", lambda e: e.tensor_tensor(out=t2[:], in0=ai[:], in1=lami[:], op=ALU.mult), reads=["ai", "lami"], writes=["t2"])
        S.add("dve", lambda e: e.tensor_tensor(out=t1[:], in0=t1[:], in1=t2[:], op=ALU.add), reads=["t1", "t2"], writes=["t1"])
        S.add("dve", lambda e: e.tensor_tensor(out=cre[:], in0=t1[:], in1=den[:], op=ALU.mult), reads=["t1", "den"], writes=["cre"])
        S.add("dve", lambda e: e.tensor_tensor(out=t1[:], in0=ai[:], in1=lamr[:], op=ALU.mult), reads=["ai", "lamr", "cre"], writes=["t1"])
        S.add("dve", lambda e: e.tensor_tensor(out=t2[:], in0=ar1[:], in1=lami[:], op=ALU.mult), reads=["ar1", "lami"], writes=["t2"])
        S.add("dve", lambda e: e.tensor_tensor(out=t1[:], in0=t1[:], in1=t2[:], op=ALU.subtract), reads=["t1", "t2"], writes=["t1"])
        S.add("dve", lambda e: e.tensor_tensor(out=cim[:], in0=t1[:], in1=den[:], op=ALU.mult), reads=["t1", "den"], writes=["cim"])
        bb = {}
        u1 = sb("u1_", [128, 16, 16])
        u2 = sb("u2_", [128, 16, 16])
        creb = cre[:].unsqueeze(2).broadcast_to([128, 16, 16])
        cimb = cim[:].unsqueeze(2).broadcast_to([128, 16, 16])
        for nm, (a0, b0, a1, b1, op) in (("bbr", (creb, Bl_r, cimb, Bl_i, ALU.subtract)),
                                         ("bbi", (creb, Bl_i, cimb, Bl_r, ALU.add))):
            t = sb(nm, [128, 16, 16])
            S.add("dve", lambda e, a0=a0, b0=b0: e.tensor_tensor(out=u1[:], in0=b0[:], in1=a0, op=ALU.mult),
                  reads=["cre", "cim", "Bl_r", "Bl_i"], writes=["u1"])
            S.add("dve", lambda e, a1=a1, b1=b1: e.tensor_tensor(out=u2[:], in0=b1[:], in1=a1, op=ALU.mult),
                  reads=["cre", "cim", "Bl_r", "Bl_i"], writes=["u2"])
            S.add("dve", lambda e, t=t, op=op: e.tensor_tensor(out=t[:], in0=u1[:], in1=u2[:], op=op),
                  reads=["u1", "u2"], writes=[nm])
            bb[nm] = t
        BT = {}
        bpad = sb("bpad", [128, 16, 128], BF16)
        for nm in ("bbr", "bbi"):
            pad = bpad
            S.add("pool", lambda e, pad=pad: e.memset(pad[:], 0.0), writes=["bpad"])
            for k in range(16):
                for g2 in range(2):
                    off = ((2 * k + g2) % 8) * 16
                    S.add("dve", lambda e, pad=pad, k=k, g2=g2, off=off, nm=nm: e.tensor_copy(
                        out=pad[g2 * 64:(g2 + 1) * 64, k, off:off + 16], in_=bb[nm][g2 * 64:(g2 + 1) * 64, k, :]),
                        reads=[nm, "bpad"], writes=["bpad"])
            T = sb(nm + "T", [128, 16, 128], BF16)
            for h in range(2):
                pt, ptk = PT.next()
                for j in range(8):
                    S.add("pe", lambda e, pad=pad, pt=pt, j=j, h=h: e.transpose(out=pt[:, j, :], in_=pad[:, h * 8 + j, :], identity=ident[:]),
                          reads=["bpad", "ident"], writes=[ptk])
                S.add("dve", lambda e, T=T, pt=pt, h=h: e.tensor_copy(out=T[:, h * 8:(h + 1) * 8, :], in_=pt[:]),
                      reads=[ptk], writes=[nm + "T"])
            BT[nm] = T
        Cb = {}
        for nm, src, sgn in (("Cbr", Cl_r, 1.0), ("Cbi", Cl_i, -1.0)):
            t = sb(nm, [128, 16, 32], BF16)
            S.add("pool", lambda e, t=t: e.memset(t[:], 0.0), writes=[nm])
            for g2 in range(2):
                S.add("dve", lambda e, t=t, src=src, g2=g2, sgn=sgn: e.tensor_scalar(
                    out=t[g2 * 64:(g2 + 1) * 64, :, g2 * 16:(g2 + 1) * 16], in0=src[g2 * 64:(g2 + 1) * 64, :, :],
                    scalar1=sgn, scalar2=None, op0=ALU.mult), reads=["Cl_r", "Cl_i", nm], writes=[nm])
            Cb[nm] = t

        cxr = sb("cxr", [128, 16])
        cxi = sb("cxi", [128, 16])
        S.add("pool", lambda e: e.memset(cxr[:], 0.0), writes=["cxr"])
        S.add("pool", lambda e: e.memset(cxi[:], 0.0), writes=["cxi"])
        s0r = sb("s0r", [128, 2, 16])
        s0i = sb("s0i", [128, 2, 16])
        S.add("sp", lambda e: e.dma_start(out=s0r[:], in_=sbre.rearrange("s (k g) n -> (g n) s k", g=2)), writes=["s0r"], is_dma=True)
        S.add("sp", lambda e: e.dma_start(out=s0i[:], in_=sbim.rearrange("s (k g) n -> (g n) s k", g=2)), writes=["s0i"], is_dma=True)
        cs_r = sb("cs_r", [128, 2, 16])
        cs_i = sb("cs_i", [128, 2, 16])

        def rmsnorm_T(src, srck, gtile, gk):
            S.add("act", lambda e: e.activation(out=gate[:], in_=src[:], func=AF.Square, accum_out=ssq[:]),
                  reads=[srck], writes=["gate", "ssq"])
            S.add("dve", lambda e: e.tensor_scalar(out=rstd[:], in0=ssq[:], scalar1=1.0 / D, scalar2=EPS, op0=ALU.mult, op1=ALU.add),
                  reads=["ssq"], writes=["rstd"])
            S.add("pool", lambda e: e.tensor_tensor(out=rstd[:], in0=rstd[:], in1=mhalf[:, 0:1], op=ALU.pow),
                  reads=["rstd", "mhalf"], writes=["rstd"])
            S.add("dve", lambda e: e.scalar_tensor_tensor(out=xn[:], in0=src[:], scalar=rstd[:], in1=gtile[:], op0=ALU.mult, op1=ALU.mult),
                  reads=[srck, "rstd", gk], writes=["xn"])
            pt, ptk = PT.next()
            for k in range(8):
                S.add("pe", lambda e, k=k, pt=pt: e.transpose(out=pt[:, k, :], in_=xn[:, k * 128:(k + 1) * 128], identity=ident[:]),
                      reads=["xn", "ident"], writes=[ptk])
            S.add("act", lambda e, pt=pt: e.copy(out=hnT[:], in_=pt[:]), reads=[ptk], writes=["hnT"])

        def linear(lhsT, lk, nk, w, wk, c0, c1):
            p, pk = PS.next()
            for k in range(nk):
                S.add("pe", lambda e, k=k, p=p: e.matmul(p[:, 0:c1 - c0], lhsT=lhsT[:, k, :], rhs=w[:, k, c0:c1],
                                                        start=(k == 0), stop=(k == nk - 1)),
                      reads=[lk, wk], writes=[pk])
            return p, pk

        def load_tile(i):
            h, hk = H.next()
            p0, p0k = P0.next()
            S.add("sp", lambda e: e.dma_start(out=h[:], in_=xin[i * 128:(i + 1) * 128, :]), writes=[hk], is_dma=True)
            S.add("sp", lambda e: e.dma_start(out=p0[:], in_=pin[0, i * 128:(i + 1) * 128, :]), writes=[p0k], is_dma=True)
            return h, hk, p0, p0k

        nxt = load_tile(0)
        for i in range(nt):
            h, hk, p0, p0k = nxt
            if i + 1 < nt:
                nxt = load_tile(i + 1)
            samp = (i == ntp)
            Rc, Rs = Rc_p, Rs_p

            rmsnorm_T(h, hk, g_l0, "g_l0")
            p, pk = linear(hnT, "hnT", 8, w_in0, "w_in0", 0, 512)
            S.add("act", lambda e, p=p: e.activation(out=ua[:], in_=p[:], func=AF.Gelu_apprx_tanh), reads=[pk], writes=["ua"])
            p, pk = linear(hnT, "hnT", 8, w_in0, "w_in0", 512, 1024)
            S.add("act", lambda e, p=p: e.activation(out=va[:], in_=p[:], func=AF.Gelu_apprx_tanh), reads=[pk], writes=["va"])
            p, pk = linear(hnT, "hnT", 8, w_in0, "w_in0", 1024, 1536)
            S.add("act", lambda e, p=p: e.activation(out=za[:], in_=p[:], func=AF.Silu), reads=[pk], writes=["za"])
            p, pk = linear(hnT, "hnT", 8, w_in0, "w_in0", 1536, 2048)
            S.add("act", lambda e, p=p: e.copy(out=ubf[:], in_=p[:]), reads=[pk], writes=["ubf"])
            S.add("dve", lambda e, p=p: e.tensor_tensor(out=ubd[:], in0=p[:], in1=Db[:], op=ALU.mult), reads=[pk, "Db"], writes=["ubd"])
            p, pk = linear(hnT, "hnT", 8, w_in0, "w_in0", 2048, 2560)
            S.add("act", lambda e, p=p: e.activation(out=zb[:], in_=p[:], func=AF.Silu), reads=[pk], writes=["zb"])

            S.add("dve", lambda e: e.bn_stats(out=bst[:], in_=va[:]), reads=["va"], writes=["bst"])
            S.add("dve", lambda e: e.bn_aggr(out=bag[:], in_=bst[:]), reads=["bst"], writes=["bag"])
            S.add("dve", lambda e: e.tensor_scalar(out=bag[:, 1:2], in0=bag[:, 1:2], scalar1=EPS, scalar2=None, op0=ALU.add),
                  reads=["bag"], writes=["bag"])
            S.add("pool", lambda e: e.tensor_tensor(out=bag[:, 1:2], in0=bag[:, 1:2], in1=mhalf[:, 0:1], op=ALU.pow),
                  reads=["bag", "mhalf"], writes=["bag"])
            S.add("dve", lambda e: e.tensor_scalar(out=vln[:], in0=va[:], scalar1=bag[:, 0:1], scalar2=bag[:, 1:2],
                                                  op0=ALU.subtract, op1=ALU.mult), reads=["va", "bag"], writes=["vln"])
            S.add("dve", lambda e: e.tensor_tensor(out=vln[:], in0=vln[:], in1=lng[:], op=ALU.mult), reads=["vln", "lng"], writes=["vln"])
            S.add("dve", lambda e: e.tensor_tensor(out=vln[:], in0=vln[:], in1=lnb[:], op=ALU.add), reads=["vln", "lnb"], writes=["vln"])
            S.add("act", lambda e: e.copy(out=vbf[:], in_=vln[:]), reads=["vln"], writes=["vbf"])
            if samp:
                S.add("sp", lambda e: e.dma_start(out=o_av, in_=vln[:]), reads=["vln"], writes=["o_av"], is_dma=True)
            p, pk = PS.next()
            wm, wmk = (wmixTs, "wlsT") if samp else (wmixT, "wlT")
            for g in range(4):
                S.add("pe", lambda e, g=g, p=p, wm=wm: e.matmul(p[:, g * 128:(g + 1) * 128], lhsT=wm[:, g, :], rhs=vbf[:, g * 128:(g + 1) * 128],
                                                               start=True, stop=True), reads=[wmk, "vbf"], writes=[pk])
            bs_t, bsk = (bsbs, "bsbs") if samp else (bsb, "bsb")
            for g in range(4):
                S.add("dve", lambda e, g=g, p=p, bs_t=bs_t: e.scalar_tensor_tensor(
                    out=ua[:, g * 128:(g + 1) * 128], in0=p[:, g * 128:(g + 1) * 128], scalar=bs_t[:, g:g + 1],
                    in1=ua[:, g * 128:(g + 1) * 128], op0=ALU.add, op1=ALU.mult), reads=[pk, bsk, "ua"], writes=["ua"])
            S.add("dve", lambda e: e.tensor_tensor(out=mixin[:, 0:512], in0=ua[:], in1=za[:], op=ALU.mult),
                  reads=["ua", "za"], writes=["mixin"])

            pt, ptk = PT.next()
            for q in range(4):
                S.add("pe", lambda e, q=q, pt=pt: e.transpose(out=pt[:, q, :], in_=ubf[:, q * 128:(q + 1) * 128], identity=ident[:]),
                      reads=["ubf", "ident"], writes=[ptk])
            S.add("act", lambda e, pt=pt: e.copy(out=ubT[:], in_=pt[:, 0:4, :]), reads=[ptk], writes=["ubT"])
            py, pyk = py_bank, "ps_y"
            for q in range(4):
                pbr, pbrk = PS.next()
                pbi, pbik = PS.next()
                for j in range(4):
                    k = 4 * q + j
                    S.add("pe", lambda e, j=j, k=k, q=q, pbr=pbr: e.matmul(pbr[:, j * 128:(j + 1) * 128], lhsT=BT["bbr"][:, k, :], rhs=ubT[:, q, :],
                                                                          start=True, stop=True), reads=["bbrT", "ubT"], writes=[pbrk])
                    S.add("pe", lambda e, j=j, k=k, q=q, pbi=pbi: e.matmul(pbi[:, j * 128:(j + 1) * 128], lhsT=BT["bbi"][:, k, :], rhs=ubT[:, q, :],
                                                                          start=True, stop=True), reads=["bbiT", "ubT"], writes=[pbik])
                T1, T2, T3, T4 = W5["T1"], W5["T2"], W5["T3"], W5["T4"]
                vr, vi, zr, zi = W5["vr"], W5["vi"], W5["zr"], W5["zi"]
                if samp:
                    def V(t):
                        return t[:].rearrange("p (a s b) -> p a s b", a=4, s=2)
                    def Vp(t):
                        return t[:].rearrange("p (a s b) -> p a s b", a=4, s=2)
                    rc = Rc[:, 4 * q:4 * q + 4, 0:64].unsqueeze(2).broadcast_to([128, 4, 2, 64])
                    rs = Rs[:, 4 * q:4 * q + 4, 0:64].unsqueeze(2).broadcast_to([128, 4, 2, 64])
                else:
                    def V(t):
                        return t[:]
                    def Vp(t):
                        return t[:]
                    rc = Rc[:, 4 * q:4 * q + 4, :].rearrange("p a b -> p (a b)")
                    rs = Rs[:, 4 * q:4 * q + 4, :].rearrange("p a b -> p (a b)")
                S.add("dve", lambda e, pbr=pbr, rc=rc, V=V, Vp=Vp: e.tensor_tensor(out=V(T1), in0=Vp(pbr), in1=rc, op=ALU.mult), reads=[pbrk, kRc], writes=["T1"])
                S.add("dve", lambda e, pbi=pbi, rs=rs, V=V, Vp=Vp: e.tensor_tensor(out=V(T2), in0=Vp(pbi), in1=rs, op=ALU.mult), reads=[pbik, kRs], writes=["T2"])
                S.add("dve", lambda e, pbi=pbi, rc=rc, V=V, Vp=Vp: e.tensor_tensor(out=V(T3), in0=Vp(pbi), in1=rc, op=ALU.mult), reads=[pbik, kRc], writes=["T3"])
                S.add("dve", lambda e, pbr=pbr, rs=rs, V=V, Vp=Vp: e.tensor_tensor(out=V(T4), in0=Vp(pbr), in1=rs, op=ALU.mult), reads=[pbrk, kRs], writes=["T4"])
                S.add("pool", lambda e: e.tensor_tensor(out=vr[:], in0=T1[:], in1=T2[:], op=ALU.add), reads=["T1", "T2"], writes=["vr"])
                S.add("pool", lambda e: e.tensor_tensor(out=vi[:], in0=T3[:], in1=T4[:], op=ALU.subtract), reads=["T3", "T4"], writes=["vi"])
                for j in range(4):
                    k = 4 * q + j
                    if samp:
                        segs = [(j * 128 + 64 * s_, 64, s0r[:, s_, k:k + 1], s0i[:, s_, k:k + 1], "s0r", "s0i") for s_ in range(2)]
                    else:
                        segs = [(j * 128, 128, cxr[:, k:k + 1], cxi[:, k:k + 1], "cxr", "cxi")]
                    for (c0, ln, ir, ii, irk, iik) in segs:
                        S.add("dve", lambda e, c0=c0, ln=ln, ir=ir, k=k: e.tensor_tensor_scan(
                            out=zr[:, c0:c0 + ln], data0=Rm[:, k, 0:ln], data1=vr[:, c0:c0 + ln], initial=ir, op0=ALU.mult, op1=ALU.add),
                            reads=["vr", kRm, irk], writes=["zr"])
                        S.add("dve", lambda e, c0=c0, ln=ln, ii=ii, k=k: e.tensor_tensor_scan(
                            out=zi[:, c0:c0 + ln], data0=Rm[:, k, 0:ln], data1=vi[:, c0:c0 + ln], initial=ii, op0=ALU.mult, op1=ALU.add),
                            reads=["vi", kRm, iik], writes=["zi"])
                S.add("pool", lambda e, rc=rc, V=V: e.tensor_tensor(out=V(T1), in0=V(zr), in1=rc, op=ALU.mult), reads=["zr", kRc], writes=["T1"])
                S.add("pool", lambda e, rs=rs, V=V: e.tensor_tensor(out=V(T2), in0=V(zi), in1=rs, op=ALU.mult), reads=["zi", kRs], writes=["T2"])
                S.add("pool", lambda e, rs=rs, V=V: e.tensor_tensor(out=V(T3), in0=V(zr), in1=rs, op=ALU.mult), reads=["zr", kRs], writes=["T3"])
                S.add("pool", lambda e, rc=rc, V=V: e.tensor_tensor(out=V(T4), in0=V(zi), in1=rc, op=ALU.mult), reads=["zi", kRc], writes=["T4"])
                S.add("dve", lambda e: e.tensor_tensor(out=xrb[:].rearrange("p a b -> p (a b)"), in0=T1[:], in1=T2[:], op=ALU.subtract),
                      reads=["T1", "T2"], writes=["xrb"])
                S.add("dve", lambda e: e.tensor_tensor(out=xib[:].rearrange("p a b -> p (a b)"), in0=T3[:], in1=T4[:], op=ALU.add),
                      reads=["T3", "T4"], writes=["xib"])
                T13 = T1[:].rearrange("p (a b) -> p a b", a=4)
                T23 = T2[:].rearrange("p (a b) -> p a b", a=4)
                T33 = T3[:].rearrange("p (a b) -> p a b", a=4)
                T43 = T4[:].rearrange("p (a b) -> p a b", a=4)
                if samp:
                    for s_ in range(2):
                        c_ = 64 * s_ + 63
                        S.add("dve", lambda e, s_=s_, c_=c_, q=q, T13=T13, T23=T23: e.tensor_tensor(out=cs_r[:, s_, 4 * q:4 * q + 4], in0=T13[:, :, c_], in1=T23[:, :, c_], op=ALU.subtract),
                              reads=["T1", "T2"], writes=["cs_r"])
                        S.add("dve", lambda e, s_=s_, c_=c_, q=q, T33=T33, T43=T43: e.tensor_tensor(out=cs_i[:, s_, 4 * q:4 * q + 4], in0=T33[:, :, c_], in1=T43[:, :, c_], op=ALU.add),
                              reads=["T3", "T4"], writes=["cs_i"])
                else:
                    S.add("dve", lambda e, q=q, T13=T13, T23=T23: e.tensor_tensor(out=cxr[:, 4 * q:4 * q + 4], in0=T13[:, :, 127], in1=T23[:, :, 127], op=ALU.subtract),
                          reads=["T1", "T2"], writes=["cxr"])
                    S.add("dve", lambda e, q=q, T33=T33, T43=T43: e.tensor_tensor(out=cxi[:, 4 * q:4 * q + 4], in0=T33[:, :, 127], in1=T43[:, :, 127], op=ALU.add),
                          reads=["T3", "T4"], writes=["cxi"])
                for j in range(4):
                    k = 4 * q + j
                    S.add("pe", lambda e, j=j, k=k, py=py: e.matmul(py[:, 32 * k:32 * k + 32], lhsT=xrb[:, j, :], rhs=Cb["Cbr"][:, k, :], start=True, stop=False),
                          reads=["xrb", "Cbr"], writes=[pyk])
                    S.add("pe", lambda e, j=j, k=k, py=py: e.matmul(py[:, 32 * k:32 * k + 32], lhsT=xib[:, j, :], rhs=Cb["Cbi"][:, k, :], start=False, stop=True),
                          reads=["xib", "Cbi"], writes=[pyk])
            if not samp:
                if i == ntp - 1:
                    S.add("sp", lambda e: e.dma_start(out=o_bre[0].rearrange("(k g) n -> (g n) k", g=2), in_=cxr[:]), reads=["cxr"], writes=["o_bre"], is_dma=True)
                    S.add("sp", lambda e: e.dma_start(out=o_bim[0].rearrange("(k g) n -> (g n) k", g=2), in_=cxi[:]), reads=["cxi"], writes=["o_bim"], is_dma=True)
            else:
                S.add("sp", lambda e: e.dma_start(out=o_bre[1:3].rearrange("s (k g) n -> (g n) s k", g=2), in_=cs_r[:]), reads=["cs_r"], writes=["o_bre"], is_dma=True)
                S.add("sp", lambda e: e.dma_start(out=o_bim[1:3].rearrange("s (k g) n -> (g n) s k", g=2), in_=cs_i[:]), reads=["cs_i"], writes=["o_bim"], is_dma=True)
            S.add("dve", lambda e, py=py: e.tensor_tensor(out=yb[:], in0=py[:], in1=ubd[:], op=ALU.add), reads=[pyk, "ubd"], writes=["yb"])
            if samp and debug_h:
                S.add("sp", lambda e: e.dma_start(out=dbg, in_=yb[:]), reads=["yb"], writes=["dbg"], is_dma=True)
            S.add("act", lambda e: e.activation(out=yg[:], in_=yb[:], func=AF.Gelu_apprx_tanh), reads=["yb"], writes=["yg"])
            S.add("act", lambda e: e.copy(out=ygb[:], in_=yg[:]), reads=["yg"], writes=["ygb"])
            pt, ptk = PT.next()
            for q in range(4):
                S.add("pe", lambda e, q=q, pt=pt: e.transpose(out=pt[:, q, :], in_=ygb[:, q * 128:(q + 1) * 128], identity=ident[:]),
                      reads=["ygb", "ident"], writes=[ptk])
            S.add("act", lambda e, pt=pt: e.copy(out=ygT[:], in_=pt[:, 0:4, :]), reads=[ptk], writes=["ygT"])
            p, pk = linear(ygT, "ygT", 4, w_glu, "w_glu", 0, 512)
            S.add("act", lambda e, p=p: e.activation(out=yb[:], in_=p[:], func=AF.Sigmoid), reads=[pk], writes=["yb"])
            S.add("dve", lambda e: e.tensor_tensor(out=yg[:], in0=yg[:], in1=yb[:], op=ALU.mult), reads=["yg", "yb"], writes=["yg"])
            S.add("dve", lambda e: e.tensor_tensor(out=mixin[:, 512:1024], in0=yg[:], in1=zb[:], op=ALU.mult), reads=["yg", "zb"], writes=["mixin"])

            pt, ptk = PT.next()
            for k in range(8):
                S.add("pe", lambda e, k=k, pt=pt: e.transpose(out=pt[:, k, :], in_=mixin[:, k * 128:(k + 1) * 128], identity=ident[:]),
                      reads=["mixin", "ident"], writes=[ptk])
            S.add("act", lambda e, pt=pt: e.copy(out=mixT[:], in_=pt[:]), reads=[ptk], writes=["mixT"])
            for hh in range(2):
                p, pk = linear(mixT, "mixT", 8, w_out0, "w_out0", hh * 512, (hh + 1) * 512)
                S.add("dve", lambda e, p=p, hh=hh, h=h: e.tensor_tensor(out=h[:, hh * 512:(hh + 1) * 512], in0=h[:, hh * 512:(hh + 1) * 512], in1=p[:], op=ALU.add),
                      reads=[pk, hk], writes=[hk])

            rmsnorm_T(h, hk, g_p0, "g_p0")
            S.add("act", lambda e, p0=p0: e.copy(out=pbf[:], in_=p0[:]), reads=[p0k], writes=["pbf"])
            pt, ptk = PT.next()
            for k in range(2):
                S.add("pe", lambda e, k=k, pt=pt: e.transpose(out=pt[:, k, :], in_=pbf[:, k * 128:(k + 1) * 128], identity=ident[:]),
                      reads=["pbf", "ident"], writes=[ptk])
            S.add("act", lambda e, pt=pt: e.copy(out=pT[:], in_=pt[:, 0:2, :]), reads=[ptk], writes=["pT"])
            for hh in range(2):
                p, pk = linear(hnT, "hnT", 8, w_gate0, "w_gate0", hh * 512, (hh + 1) * 512)
                S.add("act", lambda e, p=p, hh=hh: e.activation(out=gate[:, hh * 512:(hh + 1) * 512], in_=p[:], func=AF.Sigmoid), reads=[pk], writes=["gate"])
                p, pk = linear(pT, "pT", 2, w_pp0, "w_pp0", hh * 512, (hh + 1) * 512)
                S.add("dve", lambda e, p=p, hh=hh: e.tensor_tensor(out=gate[:, hh * 512:(hh + 1) * 512], in0=gate[:, hh * 512:(hh + 1) * 512], in1=p[:], op=ALU.mult),
                      reads=[pk, "gate"], writes=["gate"])
            S.add("dve", lambda e, h=h: e.tensor_tensor(out=h[:], in0=h[:], in1=gate[:], op=ALU.add), reads=[hk, "gate"], writes=[hk])
            S.add("sp", lambda e, h=h, i=i: e.dma_start(out=h1_scr[i * 128:(i + 1) * 128, :], in_=h[:]), reads=[hk], writes=["h1_scr"], is_dma=True)

        S.emit()
    nc.all_engine_barrier()
    T = dict(locals())
    phase_b(nc, T, ntp, debug_h)
    return nc


WEIGHT_KEYS = ["d_conv_w", "d_A_log", "d_dt_bias", "d_norm_g", "norm_g", "final_norm_g", "ple_proj", "ple_gate_w", "ple_norm_g", "even_w_in", "even_w_out",
               "a_ln_g", "a_ln_b", "a_w_s", "a_b_s", "b_lam_re", "b_lam_im", "b_log_dt", "b_B_re", "b_B_im",
               "b_C_re", "b_C_im", "b_D", "b_glu_w", "odd_w_in", "odd_w_out"]


def make_in_maps(inp, ntp=NTP):
    maps = []
    for c in range(NCORES):
        m = {k: np.ascontiguousarray(inp[k], dtype=np.float32) for k in WEIGHT_KEYS}
        xs = np.asarray(inp["x_sample"])[2 * c:2 * c + 2].reshape(128, D)
        m["xin"] = np.ascontiguousarray(np.concatenate([np.asarray(inp["x_prompt"])[c, :ntp * 128], xs], axis=0))
        ps_ = np.asarray(inp["p_sample"])[:, 2 * c:2 * c + 2].reshape(2, 128, 256)
        m["pin"] = np.ascontiguousarray(np.concatenate([np.asarray(inp["p_prompt"])[:, c, :ntp * 128], ps_], axis=1))
        m["sbre"] = np.ascontiguousarray(np.asarray(inp["state_b_re"])[0, 2 * c:2 * c + 2])
        m["sbim"] = np.ascontiguousarray(np.asarray(inp["state_b_im"])[0, 2 * c:2 * c + 2])
        m["cache_k"] = np.ascontiguousarray(np.asarray(inp["cache_k_c"])[0, 2 * c:2 * c + 2].reshape(2, 4096, 512))
        m["cache_v"] = np.ascontiguousarray(np.asarray(inp["cache_v_c"])[0, 2 * c:2 * c + 2].reshape(2, 4096, 512))
        m["state_d"] = np.ascontiguousarray(np.asarray(inp["state_d"])[0, 2 * c:2 * c + 2])
        m["state_conv"] = np.ascontiguousarray(np.asarray(inp["state_conv_d"])[0, 2 * c:2 * c + 2])
        maps.append(m)
    return maps


_NC_CACHE = {}


def kernel(**inputs):
    if "nc" not in _NC_CACHE:
        _NC_CACHE["nc"] = build_program()
    nc = _NC_CACHE["nc"]
    res = run_bass_kernel_spmd(nc, make_in_maps(inputs), core_ids=list(range(NCORES)))
    R = res.results
    B, DB = 8, 16
    y_prompt = np.stack([R[c]["o_y"][:SEQ] for c in range(B)]).astype(np.float32)
    y_sample = np.concatenate([R[c]["o_y"][SEQ:].reshape(2, 64, D) for c in range(B)]).astype(np.float32)
    b_re_p = np.stack([R[c]["o_bre"][0] for c in range(B)])[None]
    b_im_p = np.stack([R[c]["o_bim"][0] for c in range(B)])[None]
    b_re_s = np.concatenate([R[c]["o_bre"][1:3] for c in range(B)])[None]
    b_im_s = np.concatenate([R[c]["o_bim"][1:3] for c in range(B)])[None]
    a_v_s = np.concatenate([R[c]["o_av"].reshape(2, 64, 512) for c in range(B)])[None]
    k_c_p = np.stack([R[c]["o_kc"][:SEQ].reshape(SEQ, 8, 64) for c in range(B)])[None]
    v_c_p = np.stack([R[c]["o_vc"][:SEQ].reshape(SEQ, 8, 64) for c in range(B)])[None]
    k_c_s = np.concatenate([R[c]["o_kc"][SEQ:].reshape(2, 64, 8, 64) for c in range(B)])[None]
    v_c_s = np.concatenate([R[c]["o_vc"][SEQ:].reshape(2, 64, 8, 64) for c in range(B)])[None]
    conv_d_p = np.stack([R[c]["o_conv"][0] for c in range(B)])[None]
    conv_d_s = np.concatenate([R[c]["o_conv"][1:3] for c in range(B)])[None]
    s_d_p = np.stack([R[c]["o_sd"][0] for c in range(B)])[None]
    s_d_s = np.concatenate([R[c]["o_sd"][1:3] for c in range(B)])[None]
    f = lambda a: np.ascontiguousarray(a, dtype=np.float32)
    return tuple(f(a) for a in (y_prompt, y_sample, b_re_p, b_im_p, k_c_p, v_c_p, s_d_p, conv_d_p,
                                b_re_s, b_im_s, a_v_s, k_c_s, v_c_s, s_d_s, conv_d_s))


def phase_b(nc, T, ntp, debug_h):
    nt = ntp + 1
    g = lambda n: T[n]
    h1_scr, pin, o_y, o_kc, o_vc, o_conv = g("h1_scr"), g("pin"), g("o_y"), g("o_kc"), g("o_vc"), g("o_conv")
    norm_g, final_norm_g, ple_proj, ple_gate_w, ple_norm_g = g("norm_g"), g("final_norm_g"), g("ple_proj"), g("ple_gate_w"), g("ple_norm_g")
    odd_w_in, odd_w_out, cache_k, cache_v = g("odd_w_in"), g("odd_w_out"), g("cache_k"), g("cache_v")
    ntok = nt * 128
    kts = nc.dram_tensor("kts", [nt, 128, 512], BF16, kind="Internal").ap()
    vs = nc.dram_tensor("vs", [ntok, 512], BF16, kind="Internal").ap()
    dbg2 = nc.dram_tensor("dbg2", [ntok, 512], F32, kind="ExternalOutput").ap() if debug_h else None

    S = Sched(nc)
    st = contextlib.ExitStack()
    with st:
        def sb(name, shape, dt=F32):
            return st.enter_context(nc.sbuf_tensor(name, list(shape), dt))

        def ps(name, shape, dt=F32):
            return st.enter_context(nc.psum_tensor(name, list(shape), dt))

        st.enter_context(nc.allow_non_contiguous_dma("small parameter layout loads"))
        PS = Rot([ps(f"qs{i}", [128, 512]) for i in range(2)], "qs")
        PT = Rot([ps("qt0", [128, 8, 128], BF16)], "qt")
        PZ = ps("pz", [128, 1024])
        P2 = ps("p2", [128, 1024])
        ACC = ps("acc", [128, 512])

        ident = sb("identb", [128, 128], BF16)
        S.add("pool", lambda e: e.memset(ident[:], 0.0), writes=["ident"])
        S.add("pool", lambda e: e.affine_select(out=ident[:], in_=ident[:], compare_op=ALU.not_equal, fill=1.0,
                                               base=0, pattern=[[-1, 128]], channel_multiplier=1), reads=["ident"], writes=["ident"])
        mhalf = sb("mhalfb", [128, 1])
        onec = sb("onec", [128, 1])
        S.add("pool", lambda e: e.memset(onec[:], 1.0), writes=["onec"])
        S.add("pool", lambda e: e.memset(mhalf[:], -0.5), writes=["mhalf"])
        negU = sb("negU", [128, 128], BF16)
        zer = sb("zer", [128, 512], BF16)
        S.add("pool", lambda e: e.memset(zer[:], 0.0), writes=["zer"])
        negO = sb("negO", [128, 128], BF16)
        mask01 = sb("mask01", [128, 128], BF16)
        S.add("pool", lambda e: e.memset(negU[:], -1.0), writes=["negU"])
        S.add("pool", lambda e: e.affine_select(out=negU[:], in_=negU[:], compare_op=ALU.is_ge, fill=0.0, base=0,
                                               pattern=[[-1, 128]], channel_multiplier=1), reads=["negU"], writes=["negU"])
        S.add("pool", lambda e: e.memset(negO[:], -1.0), writes=["negO"])
        S.add("pool", lambda e: e.memset(mask01[:], 1.0), writes=["mask01"])
        S.add("pool", lambda e: e.affine_select(out=mask01[:], in_=mask01[:], compare_op=ALU.is_gt, fill=0.0, base=0,
                                               pattern=[[1, 128]], channel_multiplier=-1), reads=["mask01"], writes=["mask01"])

        def bcast_load(name, src, n):
            t = sb(name, [128, n])
            S.add("sp", lambda e: e.dma_start(out=t[:], in_=src.partition_broadcast(128)), writes=[name], is_dma=True)
            return t

        g_l1 = bcast_load("g_l1", norm_g[1], D)
        g_p1 = bcast_load("g_p1", ple_norm_g[1], D)
        g_f = bcast_load("g_f", final_norm_g, D)

        def wload(name, src, kt, n, csz=512):
            t = sb(name, [128, kt, n], BF16)
            v = src.rearrange("(k p) n -> p k n", p=128)
            for c0 in range(0, n, csz):
                c1 = min(n, c0 + csz)
                S.add("pool", lambda e, c0=c0, c1=c1: e.dma_start(out=t[:, :, c0:c1], in_=v[:, :, c0:c1]),
                      writes=[name], is_dma=True)
            return t

        w_in1 = wload("w_in1", odd_w_in[0], 8, 4104)
        w_out1 = wload("w_out1", odd_w_out[0], 8, 1024)
        w_gate1 = wload("w_gate1", ple_gate_w[1], 8, 1024)
        w_pp1 = wload("w_pp1", ple_proj[1], 2, 1024)

        H = Rot([sb(f"hb{i}", [128, D]) for i in range(1)], "hb")
        P1 = Rot([sb(f"p1_{i}", [128, 256]) for i in range(1)], "p1_")
        xn = sb("xnb", [128, D], BF16)
        hnT = sb("hnTb", [128, 8, 128], BF16)
        ssq = sb("ssqb", [128, 1])
        rstd = sb("rstdb", [128, 1])
        gate = sb("gateb", [128, D])
        qbf = sb("qbf", [128, 512], BF16)
        kf = gate[:, 0:512]
        kbf = sb("kbf", [128, 512], BF16)
        vf = gate[:, 512:1024]
        vbf = sb("vbf1", [128, 512], BF16)
        vbs = sb("vbs", [64, 512], BF16)
        zcs = sb("zcs", [128, 512])
        QT = sb("QT", [128, 8, 128], BF16)
        S.add("pool", lambda e: e.memset(QT[:], 0.0), writes=["QT"])
        KTc = sb("KTc", [128, 4, 128], BF16)
        KB = Rot([sb(f"ktb{i}", [128, 4, 128], BF16) for i in range(2)], "ktb")
        VB = Rot([sb(f"vb{i}", [128, 512], BF16) for i in range(2)], "vb")
        CK = Rot([sb(f"ck{i}", [128, 512], BF16) for i in range(1)], "ck")
        arena = sb("arena", [128, 3072])
        ebuf = arena[:, 0:1024]
        lm = arena[:, 1024:1536].bitcast(BF16)
        cum = arena[:, 1536:2048].bitcast(BF16)
        WB = Rot([arena[:, 2048:2560].bitcast(BF16), arena[:, 2560:3072].bitcast(BF16)], "wb")
        mixin = sb("mixinb", [128, 1024], BF16)
        mixT = sb("mixTb", [128, 8, 128], BF16)
        pbf = sb("pbfb", [128, 256], BF16)
        pT = sb("pTb", [128, 2, 128], BF16)
        cout = sb("cout", [128, 512])
        yout = ebuf

        def rmsnorm(src, srck, gtile, gk, out, outk):
            S.add("act", lambda e: e.activation(out=gate[:], in_=src[:], func=AF.Square, accum_out=ssq[:]),
                  reads=[srck], writes=["gate", "ssq"])
            S.add("dve", lambda e: e.tensor_scalar(out=rstd[:], in0=ssq[:], scalar1=1.0 / D, scalar2=EPS, op0=ALU.mult, op1=ALU.add),
                  reads=["ssq"], writes=["rstd"])
            S.add("pool", lambda e: e.tensor_tensor(out=rstd[:], in0=rstd[:], in1=mhalf[:], op=ALU.pow),
                  reads=["rstd", "mhalf"], writes=["rstd"])
            S.add("dve", lambda e: e.scalar_tensor_tensor(out=out[:], in0=src[:], scalar=rstd[:], in1=gtile[:], op0=ALU.mult, op1=ALU.mult),
                  reads=[srck, "rstd", gk], writes=[outk])

        def transpose_to(src, srck, nk, dst, dstk):
            pt, ptk = PT.next()
            for k in range(nk):
                S.add("pe", lambda e, k=k, pt=pt: e.transpose(out=pt[:, k, :], in_=src[:, k * 128:(k + 1) * 128], identity=ident[:]),
                      reads=[srck, "ident"], writes=[ptk])
            S.add("act", lambda e, pt=pt: e.copy(out=dst[:, 0:nk, :], in_=pt[:, 0:nk, :]), reads=[ptk], writes=[dstk])

        def linear(lhsT, lk, nk, w, wk, c0, c1, msl=slice(0, 128)):
            p, pk = PS.next()
            m = msl.stop - msl.start
            for k in range(nk):
                S.add("pe", lambda e, k=k, p=p: e.matmul(p[0:m, 0:c1 - c0], lhsT=lhsT[:, k, msl], rhs=w[:, k, c0:c1],
                                                        start=(k == 0), stop=(k == nk - 1)), reads=[lk, wk], writes=[pk])
            return p, pk

        def attn_groups(nq):
            return [(0, 4), (4, 8)] if nq == 128 else [(0, 8)]

        def attn_z(st_):
            kt_of, ktk, v_ap, vk, q0, nq, ns, diag, first, last = st_
            for gi, (h0, h1) in enumerate(attn_groups(nq)):
                for h in range(h0, h1):
                    S.add("pe", lambda e, h=h: e.matmul(PZ[0:ns, h * nq:(h + 1) * nq], lhsT=kt_of(h), rhs=QT[:, h, q0:q0 + nq], start=True, stop=True),
                          reads=[ktk, "QT"], writes=[f"pz{gi}"])

        def attn_rest(st_, mid_hook=None):
            kt_of, ktk, v_ap, vk, q0, nq, ns, diag, first, last = st_
            grps = attn_groups(nq)
            wb, wbk = WB.next()
            def m3(ap_, nh):
                return ap_.rearrange("p (h t) -> p h t", h=nh)
            for gi, (h0, h1) in enumerate(grps):
                c0, c1 = h0 * nq, h1 * nq
                S.add("act", lambda e, c0=c0, c1=c1: e.activation(out=ebuf[0:ns, c0:c1], in_=PZ[0:ns, c0:c1], func=AF.Exp), reads=[f"pz{gi}"], writes=[f"ebuf{gi}"])
                S.add("act", lambda e, c0=c0, c1=c1: e.activation(out=lm[0:ns, c0:c1], in_=ebuf[0:ns, c0:c1], func=AF.Ln, bias=1.0), reads=[f"ebuf{gi}"], writes=[f"lm{gi}"])
                if diag:
                    S.add("dve", lambda e, c0=c0, c1=c1, nh=h1 - h0: e.tensor_tensor(out=m3(lm[0:ns, c0:c1], nh), in0=m3(lm[0:ns, c0:c1], nh),
                                                                                  in1=mask01[0:ns, 0:nq].unsqueeze(1).broadcast_to([ns, nh, nq]), op=ALU.mult),
                          reads=[f"lm{gi}", "mask01"], writes=[f"lm{gi}"])
            if mid_hook is not None:
                mid_hook()
            for gi, (h0, h1) in enumerate(grps):
                c0, c1 = h0 * nq, h1 * nq
                S.add("pe", lambda e, c0=c0, c1=c1: e.matmul(P2[0:ns, c0:c1], lhsT=negU[0:ns, 0:ns], rhs=lm[0:ns, c0:c1], start=True, stop=False),
                      reads=["negU", f"lm{gi}"], writes=[f"p2{gi}"])
                if not first:
                    S.add("pe", lambda e, c0=c0, c1=c1: e.matmul(P2[0:ns, c0:c1], lhsT=negO[:, 0:ns], rhs=cum[:, c0:c1], start=False, stop=False),
                          reads=["negO", f"cum{gi}"], writes=[f"p2{gi}"])
                for h in range(h0, h1):
                    S.add("pe", lambda e, h=h, h1=h1: e.matmul(P2[0:ns, h * nq:(h + 1) * nq], lhsT=kt_of(h), rhs=QT[:, h, q0:q0 + nq], start=False, stop=(h == h1 - 1)),
                          reads=[ktk, "QT"], writes=[f"p2{gi}"])
            for gi, (h0, h1) in enumerate(grps):
                c0, c1 = h0 * nq, h1 * nq
                S.add("act", lambda e, wb=wb, c0=c0, c1=c1: e.activation(out=wb[0:ns, c0:c1], in_=P2[0:ns, c0:c1], func=AF.Exp), reads=[f"p2{gi}"], writes=[f"{wbk}_{gi}"])
                if diag:
                    S.add("dve", lambda e, wb=wb, c0=c0, c1=c1, nh=h1 - h0: e.tensor_tensor(out=m3(wb[0:ns, c0:c1], nh), in0=m3(wb[0:ns, c0:c1], nh),
                                                                                         in1=mask01[0:ns, 0:nq].unsqueeze(1).broadcast_to([ns, nh, nq]), op=ALU.mult),
                          reads=[f"{wbk}_{gi}", "mask01"], writes=[f"{wbk}_{gi}"])
            if first:
                S.add("pe", lambda e: e.matmul(ACC[0:nq, :], lhsT=zer[:, 0:nq], rhs=zer[:, :], start=True, stop=False), reads=["zer"], writes=["acc"])
            for gi, (h0, h1) in enumerate(grps):
                for h in range(h0, h1):
                    S.add("pe", lambda e, h=h, wb=wb: e.matmul(ACC[0:nq, h * 64:(h + 1) * 64], lhsT=wb[0:ns, h * nq:(h + 1) * nq], rhs=v_ap[0:ns, h * 64:(h + 1) * 64],
                                                              start=False, stop=(last and h == 7)), reads=[f"{wbk}_{gi}", vk], writes=["acc"])
            if not last:
                for gi, (h0, h1) in enumerate(grps):
                    c0, c1 = h0 * nq, h1 * nq
                    S.add("pool", lambda e, c0=c0, c1=c1: e.tensor_tensor(out=cum[0:ns, c0:c1], in0=cum[0:ns, c0:c1], in1=lm[0:ns, c0:c1], op=ALU.add),
                          reads=[f"cum{gi}", f"lm{gi}"], writes=[f"cum{gi}"])

        def attn_run(step_iter):
            cur = next(step_iter, None)
            if cur is not None:
                attn_z(cur)
            while cur is not None:
                box = {}
                def hook():
                    box["n"] = next(step_iter, None)
                    if box["n"] is not None:
                        attn_z(box["n"])
                attn_rest(cur, hook)
                cur = box["n"]

        S.add("pool", lambda e: e.memset(mixin[:, 512:1024], 0.0), writes=["mixin"])


        o_sd, state_d, state_conv = T["o_sd"], T["state_d"], T["state_conv"]
        d_conv_w, d_A_log, d_dt_bias, d_norm_g = T["d_conv_w"], T["d_A_log"], T["d_dt_bias"], T["d_norm_g"]
        Uincl = sb("Uincl", [64, 64])
        identf = sb("identf", [64, 64])
        nm_incl = sb("nm_incl", [64, 64])
        nm_low = sb("nm_low", [64, 64])
        ones128 = sb("ones128", [64, 128])
        S.add("pool", lambda e: e.memset(Uincl[:], 1.0), writes=["Uincl"])
        S.add("pool", lambda e: e.affine_select(out=Uincl[:], in_=Uincl[:], compare_op=ALU.is_ge, fill=0.0, base=0, pattern=[[1, 64]], channel_multiplier=-1),
              reads=["Uincl"], writes=["Uincl"])
        S.add("pool", lambda e: e.memset(identf[:], 0.0), writes=["identf"])
        S.add("pool", lambda e: e.affine_select(out=identf[:], in_=identf[:], compare_op=ALU.not_equal, fill=1.0, base=0, pattern=[[-1, 64]], channel_multiplier=1),
              reads=["identf"], writes=["identf"])
        S.add("pool", lambda e: e.memset(nm_incl[:], 0.0), writes=["nm_incl"])
        S.add("pool", lambda e: e.affine_select(out=nm_incl[:], in_=nm_incl[:], compare_op=ALU.is_ge, fill=-30000.0, base=0, pattern=[[1, 64]], channel_multiplier=-1),
              reads=["nm_incl"], writes=["nm_incl"])
        S.add("pool", lambda e: e.memset(nm_low[:], 0.0), writes=["nm_low"])
        S.add("pool", lambda e: e.affine_select(out=nm_low[:], in_=nm_low[:], compare_op=ALU.is_gt, fill=-30000.0, base=0, pattern=[[-1, 64]], channel_multiplier=1),
              reads=["nm_low"], writes=["nm_low"])
        S.add("pool", lambda e: e.memset(ones128[:], 1.0), writes=["ones128"])
        Sh = sb("Sh", [64, 3, 64], BF16)
        ShP = sb("ShP", [64, 3, 64], BF16)
        S.add("pool", lambda e: e.memset(Sh[:], 0.0), writes=["Sh"])
        S.add("pool", lambda e: e.memset(ShP[:], 0.0), writes=["ShP"])
        for i_ in range(1, 4):
            S.add("dve", lambda e, i_=i_: e.tensor_copy(out=Sh[:, i_ - 1, i_:64], in_=ident[0:64, 0:64 - i_]), reads=["ident", "Sh"], writes=["Sh"])
            S.add("dve", lambda e, i_=i_: e.tensor_copy(out=ShP[:, i_ - 1, 0:i_], in_=ident[0:64, 64 - i_:64]), reads=["ident", "ShP"], writes=["ShP"])
        cwb = sb("cwb", [64, 4, 1536], BF16)
        S.add("pool", lambda e: e.dma_start(out=cwb[:].rearrange("p a b -> p (a b)"), in_=d_conv_w[0].rearrange("a b -> (a b)").partition_broadcast(64)), writes=["cwb"], is_dma=True)
        dtb = sb("dtb", [64, 4])
        negA = sb("negA", [64, 4])
        gdn_g = sb("gdn_g", [64, 128])
        S.add("sp", lambda e: e.dma_start(out=dtb[:], in_=d_dt_bias[0].partition_broadcast(64)), writes=["dtb"], is_dma=True)
        S.add("sp", lambda e: e.dma_start(out=negA[:], in_=d_A_log[0].partition_broadcast(64)), writes=["negA"], is_dma=True)
        S.add("sp", lambda e: e.dma_start(out=gdn_g[:], in_=d_norm_g[0].partition_broadcast(64)), writes=["gdn_g"], is_dma=True)
        S.add("act", lambda e: e.activation(out=negA[:], in_=negA[:], func=AF.Exp), reads=["negA"], writes=["negA"])
        S.add("dve", lambda e: e.tensor_scalar(out=negA[:], in0=negA[:], scalar1=-1.0, scalar2=None, op0=ALU.mult), reads=["negA"], writes=["negA"])

        xraw = arena[0:64, 0:1536]
        XB = Rot([sb(f"xbf{i}", [64, 1536], BF16) for i in range(2)], "xbf")
        xc = arena[0:64, 1536:3072]
        gtmp = sb("gtmp", [64, 512])
        zdc = sb("zdc", [64, 512])
        ab = sb("ab", [64, 8])
        ssn = sb("ssn", [64, 8])
        gg = sb("gg", [64, 4])
        beta = sb("beta", [64, 4])
        Gcol = sb("Gcol", [64, 4])
        eG = sb("eG", [64, 4])
        dlast = sb("dlast", [64, 4])
        egl = sb("egl", [128, 4])
        D1 = sb("D1", [64, 4, 64])
        D2 = sb("D2", [64, 4, 64])
        Mm = sb("Mm", [64, 4, 64])
        Lm = sb("Lm", [64, 4, 64])
        Xm = sb("Xm", [64, 4, 64])
        Pm = sb("Pm", [64, 4, 64])
        PTm = D2
        gB = Pm
        TTb = sb("TTb", [64, 4, 64], BF16)
        ATb = sb("ATb", [64, 4, 64], BF16)
        knb = sb("knb", [64, 512], BF16)
        kbb = sb("kbb", [64, 512], BF16)
        kbgb = sb("kbgb", [64, 512], BF16)
        kdecb = sb("kdecb", [64, 512], BF16)
        qnb = sb("qnb", [64, 512], BF16)
        qgb = sb("qgb", [64, 512], BF16)
        bvb = sb("bvb", [64, 512], BF16)
        trT = sb("trT", [128, 16, 64], BF16)
        Usb = zcs[0:64, :].rearrange("p (h d) -> p h d", h=4)
        WmT = sb("WmT", [128, 4, 64], BF16)
        dlt = sb("dlt", [64, 4, 128], BF16)
        osb = cout[0:64, :]
        dob = sb("dob", [64, 512], BF16)
        Sst = sb("Sst", [128, 4, 128])
        Sbf = sb("Sbf", [128, 4, 128], BF16)
        GA, GB_, GC = PZ, P2, ACC
        gstate = {"prev": None}

        def lin64(cs, c0, c1):
            return linear(hnT, "hnT", 8, w_in1, "w_in1", c0, c1, msl=slice(cs, cs + 64))

        def gdn_chunk(cs, seq_start, seq_end, sidx, conv_out_idx, init_state_idx, row0):
            xb, xbk = XB.next()
            for c3 in range(3):
                p, pk = lin64(cs, 2048 + c3 * 512, 2560 + c3 * 512)
                S.add("dve", lambda e, p=p, c3=c3: e.tensor_copy(out=xraw[:, c3 * 512:(c3 + 1) * 512], in_=p[0:64, :]), reads=[pk], writes=["xraw"])
                S.add("act", lambda e, p=p, c3=c3, xb=xb: e.copy(out=xb[:, c3 * 512:(c3 + 1) * 512], in_=p[0:64, :]), reads=[pk], writes=[xbk])
            p, pk = lin64(cs, 3584, 4096)
            S.add("act", lambda e, p=p: e.activation(out=zdc[:], in_=p[0:64, :], func=AF.Silu), reads=[pk], writes=["zdc"])
            p, pk = lin64(cs, 4096, 4104)
            S.add("dve", lambda e, p=p: e.tensor_copy(out=ab[:], in_=p[0:64, 0:8]), reads=[pk], writes=["ab"])
            if conv_out_idx is not None:
                S.add("pool", lambda e: e.dma_start(out=o_conv[conv_out_idx], in_=xraw[61:64, :]), reads=["xraw"], writes=["o_conv"], is_dma=True)
            if seq_start:
                if init_state_idx is None:
                    xp, xpk = None, None
                else:
                    xp, xpk = XB.next()
                    S.add("pool", lambda e, xp=xp: e.dma_start(out=xp[61:64, :], in_=state_conv[init_state_idx]), writes=[xpk], is_dma=True)
            else:
                xp, xpk = gstate["prev"]
            gstate["prev"] = (xb, xbk)
            S.add("dve", lambda e: e.tensor_tensor(out=xc, in0=xraw, in1=cwb[:, 3, :], op=ALU.mult), reads=["xraw", "cwb"], writes=["xc"])
            for i_ in range(1, 4):
                for c3 in range(3):
                    cs3 = slice(c3 * 512, (c3 + 1) * 512)
                    p, pk = PS.next()
                    S.add("pe", lambda e, p=p, i_=i_, cs3=cs3, xb=xb: e.matmul(p[0:64, :], lhsT=Sh[:, i_ - 1, :], rhs=xb[:, cs3], start=True, stop=(xp is None)),
                          reads=["Sh", xbk], writes=[pk])
                    if xp is not None:
                        S.add("pe", lambda e, p=p, i_=i_, cs3=cs3, xp=xp: e.matmul(p[0:64, :], lhsT=ShP[:, i_ - 1, :], rhs=xp[:, cs3], start=False, stop=True),
                              reads=["ShP", xpk], writes=[pk])
                    S.add("dve", lambda e, p=p, i_=i_, cs3=cs3: e.tensor_tensor(out=gtmp[:], in0=p[0:64, :], in1=cwb[:, 3 - i_, cs3], op=ALU.mult),
                          reads=[pk, "cwb"], writes=["gtmp"])
                    S.add("pool", lambda e, cs3=cs3: e.tensor_tensor(out=xc[:, cs3], in0=xc[:, cs3], in1=gtmp[:], op=ALU.add), reads=["xc", "gtmp"], writes=["xc"])
            S.add("act", lambda e: e.activation(out=xc, in_=xc, func=AF.Silu), reads=["xc"], writes=["xc"])
            for j in range(8):
                S.add("act", lambda e, j=j: e.activation(out=gtmp[:, 0:128], in_=xc[:, j * 128:(j + 1) * 128], func=AF.Square, accum_out=ssn[:, j:j + 1]),
                      reads=["xc"], writes=["gtmp", "ssn"])
            S.add("dve", lambda e: e.tensor_scalar(out=ssn[:], in0=ssn[:], scalar1=EPS, scalar2=None, op0=ALU.add), reads=["ssn"], writes=["ssn"])
            S.add("pool", lambda e: e.tensor_tensor(out=ssn[:], in0=ssn[:], in1=mhalf[0:64, :].broadcast_to([64, 8]), op=ALU.pow), reads=["ssn", "mhalf"], writes=["ssn"])
            S.add("dve", lambda e: e.tensor_scalar(out=ssn[:, 0:4], in0=ssn[:, 0:4], scalar1=128.0 ** -0.5, scalar2=None, op0=ALU.mult), reads=["ssn"], writes=["ssn"])
            S.add("dve", lambda e: e.tensor_tensor(out=gg[:], in0=ab[:, 0:4], in1=dtb[:], op=ALU.add), reads=["ab", "dtb"], writes=["gg"])
            S.add("act", lambda e: e.activation(out=gg[:], in_=gg[:], func=AF.Exp), reads=["gg"], writes=["gg"])
            S.add("act", lambda e: e.activation(out=gg[:], in_=gg[:], func=AF.Ln, bias=1.0), reads=["gg"], writes=["gg"])
            S.add("dve", lambda e: e.tensor_tensor(out=gg[:], in0=gg[:], in1=negA[:], op=ALU.mult), reads=["gg", "negA"], writes=["gg"])
            S.add("act", lambda e: e.activation(out=beta[:], in_=ab[:, 4:8], func=AF.Sigmoid), reads=["ab"], writes=["beta"])
            S.add("pe", lambda e: e.matmul(GC[0:64, 0:4], lhsT=Uincl[:], rhs=gg[:], start=True, stop=True), reads=["Uincl", "gg"], writes=["acc"])
            S.add("pe", lambda e: e.matmul(GC[:, 8:12], lhsT=ones128[:], rhs=gg[:], start=True, stop=True), reads=["ones128", "gg"], writes=["acc"])
            S.add("dve", lambda e: e.tensor_copy(out=Gcol[:], in_=GC[0:64, 0:4]), reads=["acc"], writes=["Gcol"])
            S.add("act", lambda e: e.activation(out=egl[:], in_=GC[:, 8:12], func=AF.Exp), reads=["acc"], writes=["egl"])
            S.add("dve", lambda e: e.tensor_tensor(out=dlast[:], in0=GC[0:64, 8:12], in1=Gcol[:], op=ALU.subtract), reads=["acc", "Gcol"], writes=["dlast"])
            S.add("act", lambda e: e.activation(out=dlast[:], in_=dlast[:], func=AF.Exp), reads=["dlast"], writes=["dlast"])
            S.add("act", lambda e: e.activation(out=eG[:], in_=Gcol[:], func=AF.Exp), reads=["Gcol"], writes=["eG"])
            S.add("dve", lambda e: e.tensor_copy(out=gB[:], in_=gg[:].unsqueeze(2).broadcast_to([64, 4, 64])), reads=["gg"], writes=["gB"])
            for hh in range(4):
                S.add("pe", lambda e, hh=hh: e.matmul(GA[0:64, hh * 64:(hh + 1) * 64], lhsT=gB[:, hh, :], rhs=Uincl[:], start=True, stop=True),
                      reads=["gB", "Uincl"], writes=["pz"])
            for hh in range(4):
                S.add("dve", lambda e, hh=hh: e.scalar_tensor_tensor(out=D1[:, hh, :], in0=GA[0:64, hh * 64:(hh + 1) * 64], scalar=Gcol[:, hh:hh + 1], in1=nm_incl[:],
                                                                  op0=ALU.subtract, op1=ALU.add), reads=["pz", "Gcol", "nm_incl"], writes=["D1"])
                S.add("dve", lambda e, hh=hh: e.tensor_scalar(out=D2[:, hh, :], in0=GA[0:64, hh * 64:(hh + 1) * 64], scalar1=Gcol[:, hh:hh + 1], scalar2=-1.0,
                                                           op0=ALU.subtract, op1=ALU.mult), reads=["pz", "Gcol"], writes=["D2"])
            S.add("dve", lambda e: e.tensor_tensor(out=D2[:], in0=D2[:], in1=nm_low[:].unsqueeze(1).broadcast_to([64, 4, 64]), op=ALU.add), reads=["D2", "nm_low"], writes=["D2"])
            S.add("act", lambda e: e.activation(out=D1[:], in_=D1[:], func=AF.Exp), reads=["D1"], writes=["D1"])
            S.add("act", lambda e: e.activation(out=D2[:], in_=D2[:], func=AF.Exp), reads=["D2"], writes=["D2"])
            xq = xc[:, 0:512].rearrange("p (h d) -> p h d", h=4)
            xk = xc[:, 512:1024].rearrange("p (h d) -> p h d", h=4)
            xv = xc[:, 1024:1536].rearrange("p (h d) -> p h d", h=4)
            def bc(t_, lo):
                return t_[:, lo:lo + 4].unsqueeze(2).broadcast_to([64, 4, 128])
            def v3(t_):
                return t_[:].rearrange("p (h d) -> p h d", h=4)
            S.add("dve", lambda e: e.tensor_tensor(out=v3(qnb), in0=xq, in1=bc(ssn, 0), op=ALU.mult), reads=["xc", "ssn"], writes=["qnb"])
            S.add("dve", lambda e: e.tensor_tensor(out=v3(knb), in0=xk, in1=bc(ssn, 4), op=ALU.mult), reads=["xc", "ssn"], writes=["knb"])
            S.add("dve", lambda e: e.tensor_tensor(out=v3(bvb), in0=xv, in1=bc(beta, 0), op=ALU.mult), reads=["xc", "beta"], writes=["bvb"])
            S.add("pool", lambda e: e.tensor_tensor(out=v3(kbb), in0=v3(knb), in1=bc(beta, 0), op=ALU.mult), reads=["knb", "beta"], writes=["kbb"])
            S.add("pool", lambda e: e.tensor_tensor(out=v3(kbgb), in0=v3(kbb), in1=bc(eG, 0), op=ALU.mult), reads=["kbb", "eG"], writes=["kbgb"])
            S.add("pool", lambda e: e.tensor_tensor(out=v3(kdecb), in0=v3(knb), in1=bc(dlast, 0), op=ALU.mult), reads=["knb", "dlast"], writes=["kdecb"])
            S.add("pool", lambda e: e.tensor_tensor(out=v3(qgb), in0=v3(qnb), in1=bc(eG, 0), op=ALU.mult), reads=["qnb", "eG"], writes=["qgb"])
            for gi, (src, srck) in enumerate(((knb, "knb"), (kbb, "kbb"), (qnb, "qnb"), (qgb, "qgb"))):
                pt, ptk = PT.next()
                for hh in range(4):
                    S.add("pe", lambda e, hh=hh, src=src, pt=pt: e.transpose(out=pt[:, hh, 0:64], in_=src[:, hh * 128:(hh + 1) * 128], identity=ident[0:64, 0:64]),
                          reads=[srck, "ident"], writes=[ptk])
                S.add("act", lambda e, pt=pt, gi=gi: e.copy(out=trT[:, gi * 4:(gi + 1) * 4, :], in_=pt[:, 0:4, 0:64]), reads=[ptk], writes=["trT"])
            knT = lambda hh: trT[:, hh, :]
            kbT = lambda hh: trT[:, 4 + hh, :]
            qnT = lambda hh: trT[:, 8 + hh, :]
            qgT = lambda hh: trT[:, 12 + hh, :]
            for hh in range(4):
                S.add("pe", lambda e, hh=hh: e.matmul(GA[0:64, hh * 64:(hh + 1) * 64], lhsT=knT(hh), rhs=kbT(hh), start=True, stop=True), reads=["trT"], writes=["pz"])
                S.add("pe", lambda e, hh=hh: e.matmul(GA[0:64, 256 + hh * 64:256 + (hh + 1) * 64], lhsT=kbT(hh), rhs=knT(hh), start=True, stop=True), reads=["trT"], writes=["pz"])
                S.add("pe", lambda e, hh=hh: e.matmul(GA[0:64, 512 + hh * 64:512 + (hh + 1) * 64], lhsT=knT(hh), rhs=qnT(hh), start=True, stop=True), reads=["trT"], writes=["pz"])
            GA3 = lambda o_: GA[0:64, o_:o_ + 256].rearrange("p (h t) -> p h t", h=4)
            S.add("dve", lambda e: e.tensor_tensor(out=ATb[:], in0=GA3(512), in1=D1[:], op=ALU.mult), reads=["pz", "D1"], writes=["ATb"])
            S.add("dve", lambda e: e.tensor_tensor(out=Mm[:], in0=GA3(0), in1=D1[:], op=ALU.mult), reads=["pz", "D1"], writes=["Mm"])
            S.add("dve", lambda e: e.tensor_tensor(out=Mm[:], in0=Mm[:], in1=mask01[0:64, 0:64].unsqueeze(1).broadcast_to([64, 4, 64]), op=ALU.mult), reads=["Mm", "mask01"], writes=["Mm"])
            S.add("dve", lambda e: e.tensor_tensor(out=Lm[:], in0=GA3(256), in1=D2[:], op=ALU.mult), reads=["pz", "D2"], writes=["Lm"])
            S.add("dve", lambda e: e.tensor_tensor(out=Xm[:], in0=identf[:].unsqueeze(1).broadcast_to([64, 4, 64]), in1=Mm[:], op=ALU.subtract), reads=["identf", "Mm"], writes=["Xm"])
            for hh in range(4):
                S.add("pe", lambda e, hh=hh: e.matmul(GB_[0:64, hh * 64:(hh + 1) * 64], lhsT=Lm[:, hh, :], rhs=Mm[:, hh, :], start=True, stop=True), reads=["Lm", "Mm"], writes=["p2"])
                S.add("pe", lambda e, hh=hh: e.matmul(GB_[0:64, 256 + hh * 64:256 + (hh + 1) * 64], lhsT=Mm[:, hh, :], rhs=Lm[:, hh, :], start=True, stop=True), reads=["Lm", "Mm"], writes=["p2"])
            GB3 = lambda o_: GB_[0:64, o_:o_ + 256].rearrange("p (h t) -> p h t", h=4)
            S.add("dve", lambda e: e.tensor_copy(out=Pm[:], in_=GB3(0)), reads=["p2"], writes=["Pm"])
            S.add("act", lambda e: e.copy(out=PTm[:], in_=GB3(256)), reads=["p2"], writes=["PTm"])
            for lvl in range(5):
                for hh in range(4):
                    S.add("pe", lambda e, hh=hh: e.matmul(GB_[0:64, 512 + hh * 64:512 + (hh + 1) * 64], lhsT=PTm[:, hh, :], rhs=Xm[:, hh, :], start=True, stop=True), reads=["PTm", "Xm"], writes=["p2"])
                    if lvl < 4:
                        S.add("pe", lambda e, hh=hh: e.matmul(GB_[0:64, hh * 64:(hh + 1) * 64], lhsT=PTm[:, hh, :], rhs=Pm[:, hh, :], start=True, stop=True), reads=["PTm", "Pm"], writes=["p2"])
                        S.add("pe", lambda e, hh=hh: e.matmul(GB_[0:64, 256 + hh * 64:256 + (hh + 1) * 64], lhsT=Pm[:, hh, :], rhs=PTm[:, hh, :], start=True, stop=True), reads=["PTm", "Pm"], writes=["p2"])
                S.add("dve", lambda e: e.tensor_tensor(out=Xm[:], in0=Xm[:], in1=GB3(512), op=ALU.add), reads=["Xm", "p2"], writes=["Xm"])
                if lvl < 4:
                    S.add("dve", lambda e: e.tensor_copy(out=Pm[:], in_=GB3(0)), reads=["p2"], writes=["Pm"])
                    S.add("act", lambda e: e.copy(out=PTm[:], in_=GB3(256)), reads=["p2"], writes=["PTm"])
            S.add("act", lambda e: e.copy(out=TTb[:], in_=Xm[:]), reads=["Xm"], writes=["TTb"])
            for hh in range(4):
                S.add("pe", lambda e, hh=hh: e.matmul(GA[0:64, hh * 128:(hh + 1) * 128], lhsT=TTb[:, hh, :], rhs=bvb[:, hh * 128:(hh + 1) * 128], start=True, stop=True),
                      reads=["TTb", "bvb"], writes=["pz"])
                S.add("pe", lambda e, hh=hh: e.matmul(GB_[:, hh * 64:(hh + 1) * 64], lhsT=kbgb[:, hh * 128:(hh + 1) * 128], rhs=TTb[:, hh, :], start=True, stop=True),
                      reads=["TTb", "kbgb"], writes=["p2"])
            S.add("dve", lambda e: e.tensor_copy(out=Usb, in_=GA[0:64, 0:512].rearrange("p (h d) -> p h d", h=4)), reads=["pz"], writes=["Usb"])
            S.add("act", lambda e: e.copy(out=WmT[:], in_=GB_[:, 0:256].rearrange("p (h t) -> p h t", h=4)), reads=["p2"], writes=["WmT"])
            if seq_start:
                if init_state_idx is None:
                    S.add("pool", lambda e: e.memset(Sst[:], 0.0), writes=["Sst"])
                else:
                    S.add("sp", lambda e: e.dma_start(out=Sst[:], in_=state_d[init_state_idx].rearrange("h d e -> d h e")), writes=["Sst"], is_dma=True)
                S.add("act", lambda e: e.copy(out=Sbf[:], in_=Sst[:]), reads=["Sst"], writes=["Sbf"])
            for hh in range(4):
                S.add("pe", lambda e, hh=hh: e.matmul(GA[0:64, hh * 128:(hh + 1) * 128], lhsT=WmT[:, hh, :], rhs=Sbf[:, hh, :], start=True, stop=True), reads=["WmT", "Sbf"], writes=["pz"])
            S.add("dve", lambda e: e.tensor_tensor(out=dlt[:], in0=Usb, in1=GA[0:64, 0:512].rearrange("p (h d) -> p h d", h=4), op=ALU.subtract), reads=["Usb", "pz"], writes=["dlt"])
            for hh in range(4):
                S.add("pe", lambda e, hh=hh: e.matmul(GB_[0:64, hh * 128:(hh + 1) * 128], lhsT=qgT(hh), rhs=Sbf[:, hh, :], start=True, stop=False), reads=["trT", "Sbf"], writes=["p2"])
                S.add("pe", lambda e, hh=hh: e.matmul(GB_[0:64, hh * 128:(hh + 1) * 128], lhsT=ATb[:, hh, :], rhs=dlt[:, hh, :], start=False, stop=True), reads=["ATb", "dlt"], writes=["p2"])
                S.add("pe", lambda e, hh=hh: e.matmul(GC[:, hh * 128:(hh + 1) * 128], lhsT=kdecb[:, hh * 128:(hh + 1) * 128], rhs=dlt[:, hh, :], start=True, stop=True), reads=["kdecb", "dlt"], writes=["acc"])
            S.add("dve", lambda e: e.tensor_copy(out=osb, in_=GB_[0:64, 0:512]), reads=["p2"], writes=["osb"])
            for hh in range(4):
                S.add("dve", lambda e, hh=hh: e.scalar_tensor_tensor(out=Sst[:, hh, :], in0=Sst[:, hh, :], scalar=egl[:, hh:hh + 1], in1=GC[:, hh * 128:(hh + 1) * 128],
                                                                  op0=ALU.mult, op1=ALU.add), reads=["Sst", "egl", "acc"], writes=["Sst"])
            S.add("act", lambda e: e.copy(out=Sbf[:], in_=Sst[:]), reads=["Sst"], writes=["Sbf"])
            if seq_end:
                S.add("pool", lambda e: e.dma_start(out=o_sd[sidx].rearrange("h d e -> d h e"), in_=Sst[:]), reads=["Sst"], writes=["o_sd"], is_dma=True)
            for hh in range(4):
                S.add("act", lambda e, hh=hh: e.activation(out=gtmp[:, 0:128], in_=osb[:, hh * 128:(hh + 1) * 128], func=AF.Square, accum_out=ssn[:, hh:hh + 1]),
                      reads=["osb"], writes=["gtmp", "ssn"])
            S.add("dve", lambda e: e.tensor_scalar(out=ssn[:, 0:4], in0=ssn[:, 0:4], scalar1=1.0 / 128, scalar2=EPS, op0=ALU.mult, op1=ALU.add), reads=["ssn"], writes=["ssn"])
            S.add("pool", lambda e: e.tensor_tensor(out=ssn[:, 0:4], in0=ssn[:, 0:4], in1=mhalf[0:64, :].broadcast_to([64, 4]), op=ALU.pow), reads=["ssn", "mhalf"], writes=["ssn"])
            S.add("dve", lambda e: e.tensor_tensor(out=v3(osb), in0=v3(osb), in1=bc(ssn, 0), op=ALU.mult), reads=["osb", "ssn"], writes=["osb"])
            S.add("dve", lambda e: e.tensor_tensor(out=v3(osb), in0=v3(osb), in1=gdn_g[:].unsqueeze(1).broadcast_to([64, 4, 128]), op=ALU.mult), reads=["osb", "gdn_g"], writes=["osb"])
            S.add("dve", lambda e: e.tensor_tensor(out=dob[:], in0=osb, in1=zdc[:], op=ALU.mult), reads=["osb", "zdc"], writes=["dob"])
            S.add("pool", lambda e: e.dma_start(out=mixin[row0:row0 + 64, 512:1024], in_=dob[:]), reads=["dob"], writes=["mixin"], is_dma=True)

        def gdn_tile(i, samp, h, hk):
            if samp:
                for s_ in range(2):
                    gdn_chunk(s_ * 64, True, True, 1 + s_, 1 + s_, s_, s_ * 64)
            else:
                for c_ in range(2):
                    first = (i == 0 and c_ == 0)
                    lastc = (i == ntp - 1 and c_ == 1)
                    gdn_chunk(c_ * 64, first, lastc, 0, 0 if lastc else None, None, c_ * 64)

        def load_tile(i):
            h, hk = H.next()
            p1, p1k = P1.next()
            S.add("sp", lambda e: e.dma_start(out=h[:], in_=h1_scr[i * 128:(i + 1) * 128, :]), reads=["h1_scr"], writes=[hk], is_dma=True)
            S.add("sp", lambda e: e.dma_start(out=p1[:], in_=pin[1, i * 128:(i + 1) * 128, :]), writes=[p1k], is_dma=True)
            return h, hk, p1, p1k

        for i in range(nt):
            h, hk, p1, p1k = load_tile(i)
            samp = (i == ntp)
            r0 = i * 128
            rmsnorm(h, hk, g_l1, "g_l1", xn, "xn")
            transpose_to(xn, "xn", 8, hnT, "hnT")
            p, pk = linear(hnT, "hnT", 8, w_in1, "w_in1", 0, 512)
            S.add("dve", lambda e, p=p: e.tensor_scalar(out=qbf[:], in0=p[:], scalar1=0.125, scalar2=None, op0=ALU.mult), reads=[pk], writes=["qbf"])
            p, pk = linear(hnT, "hnT", 8, w_in1, "w_in1", 512, 1024)
            S.add("act", lambda e, p=p: e.copy(out=kf, in_=p[:]), reads=[pk], writes=["kf"])
            S.add("dve", lambda e, p=p: e.tensor_copy(out=kbf[:], in_=p[:]), reads=[pk], writes=["kbf"])
            S.add("pool", lambda e, r0=r0: e.dma_start(out=o_kc[r0:r0 + 128, :], in_=kf), reads=["kf"], writes=["o_kc"], is_dma=True)
            p, pk = linear(hnT, "hnT", 8, w_in1, "w_in1", 1024, 1536)
            S.add("act", lambda e, p=p: e.copy(out=vf, in_=p[:]), reads=[pk], writes=["vf"])
            S.add("dve", lambda e, p=p: e.tensor_copy(out=vbf[:], in_=p[:]), reads=[pk], writes=["vbf"])
            S.add("pool", lambda e, r0=r0: e.dma_start(out=o_vc[r0:r0 + 128, :], in_=vf), reads=["vf"], writes=["o_vc"], is_dma=True)
            p, pk = linear(hnT, "hnT", 8, w_in1, "w_in1", 1536, 2048)
            S.add("act", lambda e, p=p: e.activation(out=zcs[:], in_=p[:], func=AF.Silu), reads=[pk], writes=["zcs"])
            if samp:
                p, pk = linear(hnT, "hnT", 8, w_in1, "w_in1", 1024, 1536, msl=slice(64, 128))
                S.add("dve", lambda e, p=p: e.tensor_copy(out=vbs[:], in_=p[0:64, :]), reads=[pk], writes=["vbs"])
            pt, ptk = PT.next()
            for k in range(4):
                S.add("pe", lambda e, k=k, pt=pt: e.transpose(out=pt[:, k, :], in_=qbf[:, k * 128:(k + 1) * 128], identity=ident[:]),
                      reads=["qbf", "ident"], writes=[ptk])
            QTv = QT[:].rearrange("p (a two) t -> p a two t", two=2)
            S.add("act", lambda e, pt=pt, QTv=QTv: e.copy(out=QTv[0:64, :, 0, :], in_=pt[0:64, 0:4, :]), reads=[ptk], writes=["QT"])
            S.add("act", lambda e, pt=pt, QTv=QTv: e.copy(out=QTv[64:128, :, 1, :], in_=pt[64:128, 0:4, :]), reads=[ptk], writes=["QT"])
            transpose_to(kbf, "kbf", 4, KTc, "KTc")
            if not samp:
                S.add("pool", lambda e, r0=r0: e.dma_start(out=kts[r0 // 128], in_=KTc[:].rearrange("p a b -> p (a b)")), reads=["KTc"], writes=[f"kts{i}"], is_dma=True)
                S.add("pool", lambda e, r0=r0: e.dma_start(out=vs[r0:r0 + 128, :], in_=vbf[:]), reads=["vbf"], writes=[f"vs{i}"], is_dma=True)
            S.add("pool", lambda e: e.memset(cum, 0.0), writes=["cum"])
            if not samp:
                def prompt_steps(i=i):
                    yield (lambda h: KTc[:, h // 2, :], "KTc", vbf, "vbf", 0, 128, 128, True, True, i == 0)
                    for kb in range(i - 1, -1, -1):
                        ktb, ktbk = KB.next()
                        vb, vbk = VB.next()
                        S.add("sp", lambda e, ktb=ktb, kb=kb: e.dma_start(out=ktb[:].rearrange("p a b -> p (a b)"), in_=kts[kb]), reads=[f"kts{kb}"], writes=[ktbk], is_dma=True)
                        S.add("sp", lambda e, vb=vb, kb=kb: e.dma_start(out=vb[:], in_=vs[kb * 128:(kb + 1) * 128, :]), reads=[f"vs{kb}"], writes=[vbk], is_dma=True)
                        yield (lambda h, ktb=ktb: ktb[:, h // 2, :], ktbk, vb, vbk, 0, 128, 128, False, False, kb == 0)
                attn_run(prompt_steps())
                S.add("dve", lambda e: e.tensor_tensor(out=cout[:], in0=ACC[:], in1=zcs[:], op=ALU.mult), reads=["acc", "zcs"], writes=["cout"])
            else:
                for s_ in range(2):
                    if s_ == 1:
                        S.add("pool", lambda e: e.memset(cum, 0.0), writes=["cum"])
                    vnew, vnk = (vbf, "vbf") if s_ == 0 else (vbs, "vbs")
                    def samp_steps(s_=s_, vnew=vnew, vnk=vnk):
                        yield (lambda h: KTc[:, h // 2, s_ * 64:(s_ + 1) * 64], "KTc", vnew, vnk, s_ * 64, 64, 64, True, True, False)
                        for kb in range(31, -1, -1):
                            ck, ckk = CK.next()
                            vb, vbk = VB.next()
                            ktb, ktbk = KB.next()
                            S.add("pool", lambda e, ck=ck, kb=kb: e.dma_start(out=ck[:], in_=cache_k[s_, kb * 128:(kb + 1) * 128, :]), writes=[ckk], is_dma=True)
                            S.add("pool", lambda e, vb=vb, kb=kb: e.dma_start(out=vb[:], in_=cache_v[s_, kb * 128:(kb + 1) * 128, :]), writes=[vbk], is_dma=True)
                            transpose_to(ck, ckk, 4, ktb, ktbk)
                            yield (lambda h, ktb=ktb: ktb[:, h // 2, :], ktbk, vb, vbk, s_ * 64, 64, 128, False, False, kb == 0)
                    attn_run(samp_steps())
                    if s_ == 0:
                        S.add("dve", lambda e: e.tensor_tensor(out=cout[0:64, :], in0=ACC[0:64, :], in1=zcs[0:64, :], op=ALU.mult), reads=["acc", "zcs"], writes=["cout"])
                    else:
                        S.add("dve", lambda e: e.tensor_copy(out=ebuf[0:64, 0:512], in_=ACC[0:64, :]), reads=["acc"], writes=["ebuf"])
                        S.add("pool", lambda e: e.dma_start(out=cout[64:128, :], in_=ebuf[0:64, 0:512]), reads=["ebuf"], writes=["cout"], is_dma=True)
                        S.add("dve", lambda e: e.tensor_tensor(out=cout[64:128, :], in0=cout[64:128, :], in1=zcs[64:128, :], op=ALU.mult), reads=["cout", "zcs"], writes=["cout"])
            if debug_h:
                S.add("pool", lambda e, r0=r0: e.dma_start(out=dbg2[r0:r0 + 128, :], in_=cout[:]), reads=["cout"], writes=["dbg2"], is_dma=True)
            S.add("act", lambda e: e.copy(out=mixin[:, 0:512], in_=cout[:]), reads=["cout"], writes=["mixin"])
            gdn_tile(i, samp, h, hk)
            transpose_to(mixin, "mixin", 8, mixT, "mixT")
            for hh in range(2):
                p, pk = linear(mixT, "mixT", 8, w_out1, "w_out1", hh * 512, (hh + 1) * 512)
                S.add("dve", lambda e, p=p, hh=hh, h=h: e.tensor_tensor(out=h[:, hh * 512:(hh + 1) * 512], in0=h[:, hh * 512:(hh + 1) * 512], in1=p[:], op=ALU.add),
                      reads=[pk, hk], writes=[hk])
            rmsnorm(h, hk, g_p1, "g_p1", xn, "xn")
            transpose_to(xn, "xn", 8, hnT, "hnT")
            S.add("act", lambda e, p1=p1: e.copy(out=pbf[:], in_=p1[:]), reads=[p1k], writes=["pbf"])
            transpose_to(pbf, "pbf", 2, pT, "pT")
            for hh in range(2):
                p, pk = linear(hnT, "hnT", 8, w_gate1, "w_gate1", hh * 512, (hh + 1) * 512)
                S.add("act", lambda e, p=p, hh=hh: e.activation(out=gate[:, hh * 512:(hh + 1) * 512], in_=p[:], func=AF.Sigmoid), reads=[pk], writes=["gate"])
                p, pk = linear(pT, "pT", 2, w_pp1, "w_pp1", hh * 512, (hh + 1) * 512)
                S.add("dve", lambda e, p=p, hh=hh: e.tensor_tensor(out=gate[:, hh * 512:(hh + 1) * 512], in0=gate[:, hh * 512:(hh + 1) * 512], in1=p[:], op=ALU.mult),
                      reads=[pk, "gate"], writes=["gate"])
            S.add("dve", lambda e, h=h: e.tensor_tensor(out=h[:], in0=h[:], in1=gate[:], op=ALU.add), reads=[hk, "gate"], writes=[hk])
            rmsnorm(h, hk, g_f, "g_f", yout, "ebuf")
            S.add("pool", lambda e, r0=r0: e.dma_start(out=o_y[r0:r0 + 128, :], in_=yout), reads=["ebuf"], writes=["o_y"], is_dma=True)
        S.emit()
```

```python
import contextlib
import math
import numpy as np
import concourse.bass as bass
import concourse.mybir as mybir
from concourse.bass_utils import run_bass_kernel_spmd

F32 = mybir.dt.float32
BF16 = mybir.dt.bfloat16
I32 = mybir.dt.int32
AF = mybir.ActivationFunctionType
ALU = mybir.AluOpType

NCORES = 8
D = 1024
SEQ = 8192
NTP = SEQ // 128
EPS = 1e-6
EPOCH = 3000
DMA_EPOCH = 200
DMA_SLOTS = 4
TWO_PI = 2.0 * math.pi


class Op:
    __slots__ = ("eng", "fn", "waits", "need_inc", "is_dma", "slot", "slot_val", "inc_no")

    def __init__(self, eng, fn, is_dma):
        self.eng = eng
        self.fn = fn
        self.waits = []
        self.need_inc = False
        self.is_dma = is_dma
        self.slot = None
        self.slot_val = None
        self.inc_no = None


class Sched:
    ENGS = ("pe", "act", "dve", "pool", "sp")

    def __init__(self, nc):
        self.nc = nc
        self.ops = {e: [] for e in self.ENGS}
        self.last_w = {}
        self.readers = {}

    max_ops = None
    n_added = 0
    ALIAS = {'xraw': ('e0_0', 'e0_1', 'e1_0', 'e1_1', 'lm0_0', 'lm0_1'), 'xc': ('lm1_0', 'lm1_1', 'cum0', 'cum1', 'wb_0', 'wb_1'), 'pz': ('pz0', 'pz1'), 'p2': ('p20', 'p21'), 'ebuf': ('e0_0', 'e0_1', 'e1_0', 'e1_1'), 'cum': ('cum0', 'cum1'), 'kf': ('gate',), 'vf': ('gate',), 'Usb': ('zcs',), 'osb': ('cout',), 'gB': ('Pm',), 'PTm': ('D2',)}

    def add(self, eng, fn, reads=(), writes=(), is_dma=False):
        op = Op(eng, fn, is_dma)
        Sched.n_added += 1
        if Sched.max_ops is not None and Sched.n_added > Sched.max_ops:
            return op
        reads = [a for k in reads for a in Sched.ALIAS.get(k, (k,))]
        writes = [a for k in writes for a in Sched.ALIAS.get(k, (k,))]
        writes = list(writes) + [k for k in reads if k.startswith(("ps", "pt", "qs", "qt", "pz", "p2", "acc"))]
        deps = []
        for k in reads:
            w = self.last_w.get(k)
            if w is not None:
                deps.append(w)
        for k in writes:
            w = self.last_w.get(k)
            if w is not None:
                deps.append(w)
            deps.extend(self.readers.get(k, ()))
        seen = set()
        for d in deps:
            if d is op or id(d) in seen:
                continue
            seen.add(id(d))
            if (not d.is_dma) and (not is_dma) and d.eng == eng == "pe":
                continue
            op.waits.append(d)
            d.need_inc = True
        for k in reads:
            self.readers.setdefault(k, []).append(op)
        for k in writes:
            self.last_w[k] = op
            self.readers[k] = []
        self.ops[eng].append(op)
        return op

    def emit(self):
        nc = self.nc
        n_epochs = {}
        n_dma = {}
        for e in self.ENGS:
            c = 0
            for op in self.ops[e]:
                if (not op.is_dma) and op.need_inc:
                    op.inc_no = c
                    c += 1
            n_epochs[e] = max(1, (c + EPOCH - 1) // EPOCH)
            j = 0
            for op in self.ops[e]:
                if op.is_dma:
                    u = j // DMA_SLOTS
                    op.slot = (u // DMA_EPOCH) * DMA_SLOTS + j % DMA_SLOTS
                    op.slot_val = 16 * (u % DMA_EPOCH + 1)
                    j += 1
            n_dma[e] = j
        with contextlib.ExitStack() as st:
            sems = {e: [st.enter_context(nc.semaphore(f"s_{e}_{i}")) for i in range(n_epochs[e])]
                    for e in self.ENGS}
            dsems = {e: [st.enter_context(nc.semaphore(f"d_{e}_{i}"))
                         for i in range(DMA_SLOTS * ((n_dma[e] // DMA_SLOTS) // DMA_EPOCH + 1))]
                     for e in self.ENGS if n_dma[e] > 0}
            block = st.enter_context(nc.Block())

            def target(d):
                if d.is_dma:
                    return dsems[d.eng][d.slot], d.slot_val
                return sems[d.eng][d.inc_no // EPOCH], d.inc_no % EPOCH + 1

            def run(e, eng):
                waited = {}
                last_on_slot = {}
                for op in self.ops[e]:
                    ws = list(op.waits)
                    if op.is_dma and (op.slot % DMA_SLOTS) in last_on_slot:
                        ws.append(last_on_slot[op.slot % DMA_SLOTS])
                    for d in ws:
                        sem, val = target(d)
                        if waited.get(sem.num, 0) >= val:
                            continue
                        waited[sem.num] = val
                        eng.wait_ge(sem, val)
                    ins = op.fn(eng)
                    if op.is_dma:
                        ins.then_inc(dsems[e][op.slot], 16)
                        last_on_slot[op.slot % DMA_SLOTS] = op
                    elif op.need_inc:
                        ins.then_inc(sems[e][op.inc_no // EPOCH], 1)
                for d in last_on_slot.values():
                    sem, val = target(d)
                    eng.wait_ge(sem, val)

            block.tensor(lambda eng: run("pe", eng))
            block.scalar(lambda eng: run("act", eng))
            block.vector(lambda eng: run("dve", eng))
            block.gpsimd(lambda eng: run("pool", eng))
            block.sync(lambda eng: run("sp", eng))


class Rot:
    def __init__(self, bufs, name):
        self.bufs = bufs
        self.name = name
        self.i = 0

    def next(self):
        j = self.i % len(self.bufs)
        self.i += 1
        return self.bufs[j], f"{self.name}{j}"


def build_program(ntp=NTP, debug_h=False):
    nt = ntp + 1
    ntok = nt * 128
    nc = bass.Bass("TRN2", target_bir_lowering=False)

    def din(name, shape):
        return nc.dram_tensor(name, list(shape), F32, kind="ExternalInput").ap()

    def dout(name, shape):
        return nc.dram_tensor(name, list(shape), F32, kind="ExternalOutput").ap()

    xin = din("xin", [ntok, D])
    pin = din("pin", [2, ntok, 256])
    sbre = din("sbre", [2, 32, 64])
    sbim = din("sbim", [2, 32, 64])
    norm_g = din("norm_g", [2, D])
    final_norm_g = din("final_norm_g", [D])
    ple_proj = din("ple_proj", [2, 256, D])
    ple_gate_w = din("ple_gate_w", [2, D, D])
    ple_norm_g = din("ple_norm_g", [2, D])
    even_w_in = din("even_w_in", [1, D, 2560])
    even_w_out = din("even_w_out", [1, 1024, D])
    a_ln_g = din("a_ln_g", [1, 512])
    a_ln_b = din("a_ln_b", [1, 512])
    a_w_s = din("a_w_s", [1, 4, 128, 128])
    a_b_s = din("a_b_s", [1, 4, 128])
    b_lam_re = din("b_lam_re", [1, 32, 64])
    b_lam_im = din("b_lam_im", [1, 32, 64])
    b_log_dt = din("b_log_dt", [1, 32])
    b_B_re = din("b_B_re", [1, 32, 64, 16])
    b_B_im = din("b_B_im", [1, 32, 64, 16])
    b_C_re = din("b_C_re", [1, 32, 16, 64])
    b_C_im = din("b_C_im", [1, 32, 16, 64])
    b_D = din("b_D", [1, 32, 16])
    b_glu_w = din("b_glu_w", [1, 512, 512])
    odd_w_in = din("odd_w_in", [1, D, 4104])
    odd_w_out = din("odd_w_out", [1, 1024, D])
    cache_k = din("cache_k", [2, 4096, 512])
    cache_v = din("cache_v", [2, 4096, 512])
    state_d = din("state_d", [2, 4, 128, 128])
    state_conv = din("state_conv", [2, 3, 1536])
    d_conv_w = din("d_conv_w", [1, 4, 1536])
    d_A_log = din("d_A_log", [1, 4])
    d_dt_bias = din("d_dt_bias", [1, 4])
    d_norm_g = din("d_norm_g", [1, 128])

    o_y = dout("o_y", [ntok, D])
    o_bre = dout("o_bre", [3, 32, 64])
    o_bim = dout("o_bim", [3, 32, 64])
    o_av = dout("o_av", [128, 512])
    o_kc = dout("o_kc", [ntok, 512])
    o_vc = dout("o_vc", [ntok, 512])
    o_conv = dout("o_conv", [3, 3, 1536])
    o_sd = dout("o_sd", [3, 4, 128, 128])
    h1_scr = nc.dram_tensor("h1_scr", [ntok, D], F32, kind="ExternalOutput" if debug_h else "Internal").ap()

    dbg = nc.dram_tensor("dbg", [128, 512], F32, kind="ExternalOutput").ap() if debug_h else None
    S = Sched(nc)
    st = contextlib.ExitStack()
    with st:
        def sb(name, shape, dt=F32):
            return st.enter_context(nc.sbuf_tensor(name, list(shape), dt))

        def ps(name, shape, dt=F32):
            return st.enter_context(nc.psum_tensor(name, list(shape), dt))

        st.enter_context(nc.allow_non_contiguous_dma("small one-time parameter layout loads"))

        PS = Rot([ps(f"ps{i}", [128, 512]) for i in range(5)], "ps")
        py_bank = ps("ps_y", [128, 512])
        PT = Rot([ps(f"pt{i}", [128, 8, 128], BF16) for i in range(2)], "pt")

        H = Rot([sb(f"h{i}", [128, D]) for i in range(2)], "h")
        P0 = Rot([sb(f"p0_{i}", [128, 256]) for i in range(2)], "p0_")
        xn = sb("xn", [128, D], BF16)
        hnT = sb("hnT", [128, 8, 128], BF16)
        ssq = sb("ssq", [128, 1])
        rstd = sb("rstd", [128, 1])
        ua = sb("ua", [128, 512])
        va = sb("va", [128, 512])
        vln = sb("vln", [128, 512])
        vbf = sb("vbf", [128, 512], BF16)
        za = sb("za", [128, 512])
        zb = sb("zb", [128, 512])
        ubf = sb("ubf", [128, 512], BF16)
        ubd = sb("ubd", [128, 512])
        ubT = sb("ubT", [128, 4, 128], BF16)
        bst = sb("bst", [128, 6])
        bag = sb("bag", [128, 2])
        mixin = sb("mixin", [128, 1024], BF16)
        mixT = sb("mixT", [128, 8, 128], BF16)
        yb = sb("yb", [128, 512])
        yg = sb("yg", [128, 512])
        ygb = sb("ygb", [128, 512], BF16)
        ygT = sb("ygT", [128, 4, 128], BF16)
        gate = sb("gate", [128, D])
        pbf = sb("pbf", [128, 256], BF16)
        pT = sb("pT", [128, 2, 128], BF16)
        W5 = {n: sb("w5" + n, [128, 512]) for n in ("T1", "T2", "T3", "T4", "vr", "vi", "zr", "zi")}
        xrb = sb("xrb", [128, 4, 128], BF16)
        xib = sb("xib", [128, 4, 128], BF16)

        ident = sb("ident", [128, 128], BF16)
        S.add("pool", lambda e: e.memset(ident[:], 0.0), writes=["ident"])
        S.add("pool", lambda e: e.affine_select(out=ident[:], in_=ident[:], compare_op=ALU.not_equal, fill=1.0,
                                               base=0, pattern=[[-1, 128]], channel_multiplier=1),
              reads=["ident"], writes=["ident"])
        mhalf = sb("mhalf", [128, 16])
        S.add("pool", lambda e: e.memset(mhalf[:], -0.5), writes=["mhalf"])

        def bcast_load(name, src, n):
            t = sb(name, [128, n])
            S.add("sp", lambda e: e.dma_start(out=t[:], in_=src.partition_broadcast(128)), writes=[name], is_dma=True)
            return t

        g_l0 = bcast_load("g_l0", norm_g[0], D)
        g_p0 = bcast_load("g_p0", ple_norm_g[0], D)
        lng = bcast_load("lng", a_ln_g[0], 512)
        lnb = bcast_load("lnb", a_ln_b[0], 512)
        Db = bcast_load("Db", b_D[0].rearrange("g p -> (g p)"), 512)

        def wload(name, src, kt, n, csz=512):
            t = sb(name, [128, kt, n], BF16)
            v = src.rearrange("(k p) n -> p k n", p=128)
            for c0 in range(0, n, csz):
                c1 = min(n, c0 + csz)
                S.add("pool", lambda e, c0=c0, c1=c1: e.dma_start(out=t[:, :, c0:c1], in_=v[:, :, c0:c1]),
                      writes=[name], is_dma=True)
            return t

        w_in0 = wload("w_in0", even_w_in[0], 8, 2560)
        w_out0 = wload("w_out0", even_w_out[0], 8, 1024)
        w_glu = wload("w_glu", b_glu_w[0], 4, 512)
        w_gate0 = wload("w_gate0", ple_gate_w[0], 8, 1024)
        w_pp0 = wload("w_pp0", ple_proj[0], 2, 1024)

        wl = gate[:, 0:512].rearrange("p (a b) -> p a b", a=4)
        wls = gate[:, 512:1024].rearrange("p (a b) -> p a b", a=4)
        S.add("sp", lambda e: e.dma_start(out=wl, in_=a_w_s[0].rearrange("g t s -> t g s")), writes=["gate"], is_dma=True)
        S.add("pool", lambda e: e.memset(wls, 0.0), writes=["gate"])
        S.add("sp", lambda e: e.dma_start(out=wls[0:64, :, 0:64], in_=a_w_s[0, :, 0:64, 0:64].rearrange("g t s -> t g s")),
              reads=["gate"], writes=["gate"], is_dma=True)
        S.add("sp", lambda e: e.dma_start(out=wls[64:128, :, 64:128], in_=a_w_s[0, :, 0:64, 0:64].rearrange("g t s -> t g s")),
              reads=["gate"], writes=["gate"], is_dma=True)
        wmixT = sb("wmixT", [128, 4, 128], BF16)
        wmixTs = sb("wmixTs", [128, 4, 128], BF16)
        for (src, dst, nm) in ((wl, wmixT, "wl"), (wls, wmixTs, "wls")):
            wbf = (xn[:, 0:512] if nm == "wl" else xn[:, 512:1024]).rearrange("p (a b) -> p a b", a=4)
            S.add("pool", lambda e, src=src: e.affine_select(out=src, in_=src, compare_op=ALU.is_ge, fill=0.0, base=0,
                                                            pattern=[[0, 4], [-1, 128]], channel_multiplier=1),
                  reads=["gate"], writes=["gate"])
            S.add("dve", lambda e, src=src, wbf=wbf: e.tensor_copy(out=wbf, in_=src), reads=["gate"], writes=["xn"])
            pt, ptk = PT.next()
            for g in range(4):
                S.add("pe", lambda e, g=g, wbf=wbf, pt=pt: e.transpose(out=pt[:, g, :], in_=wbf[:, g, :], identity=ident[:]),
                      reads=["xn", "ident"], writes=[ptk])
            S.add("dve", lambda e, dst=dst, pt=pt: e.tensor_copy(out=dst[:], in_=pt[:, 0:4, :]), reads=[ptk], writes=[nm + "T"])
        bsb = sb("bsb", [128, 4])
        bsbs = sb("bsbs", [128, 4])
        S.add("sp", lambda e: e.dma_start(out=bsb[:], in_=a_b_s[0].rearrange("g t -> t g")), writes=["bsb"], is_dma=True)
        S.add("sp", lambda e: e.dma_start(out=bsbs[0:64, :], in_=a_b_s[0, :, 0:64].rearrange("g t -> t g")), writes=["bsbs"], is_dma=True)
        S.add("sp", lambda e: e.dma_start(out=bsbs[64:128, :], in_=a_b_s[0, :, 0:64].rearrange("g t -> t g")), writes=["bsbs"], is_dma=True)

        lamr = sb("lamr", [128, 16])
        lami = sb("lami", [128, 16])
        ldt = sb("ldt", [128, 16])
        S.add("sp", lambda e: e.dma_start(out=lamr[:], in_=b_lam_re[0].rearrange("(k g) n -> (g n) k", g=2)), writes=["lamr"], is_dma=True)
        S.add("sp", lambda e: e.dma_start(out=lami[:], in_=b_lam_im[0].rearrange("(k g) n -> (g n) k", g=2)), writes=["lami"], is_dma=True)
        ldv = b_log_dt[0].rearrange("(k g) -> g k", g=2)
        S.add("sp", lambda e: e.dma_start(out=ldt[0:64, :], in_=ldv[0].partition_broadcast(64)), writes=["ldt"], is_dma=True)
        S.add("sp", lambda e: e.dma_start(out=ldt[64:128, :], in_=ldv[1].partition_broadcast(64)), writes=["ldt"], is_dma=True)
        Bl_r = sb("Bl_r", [128, 16, 16])
        Bl_i = sb("Bl_i", [128, 16, 16])
        Cl_r = sb("Cl_r", [128, 16, 16])
        Cl_i = sb("Cl_i", [128, 16, 16])
        S.add("sp", lambda e: e.dma_start(out=Bl_r[:], in_=b_B_re[0].rearrange("(k g) n p -> (g n) k p", g=2)), writes=["Bl_r"], is_dma=True)
        S.add("sp", lambda e: e.dma_start(out=Bl_i[:], in_=b_B_im[0].rearrange("(k g) n p -> (g n) k p", g=2)), writes=["Bl_i"], is_dma=True)
        for (srcd, dstt, nm) in ((b_C_re, Cl_r, "Cl_r"), (b_C_im, Cl_i, "Cl_i")):
            for k in range(16):
                for g2 in range(2):
                    S.add("sp", lambda e, srcd=srcd, dstt=dstt, k=k, g2=g2: e.dma_start(
                        out=dstt[g2 * 64:(g2 + 1) * 64, k, :],
                        in_=srcd[0, 2 * k + g2].rearrange("p n -> n p")),
                        writes=[nm], is_dma=True)

        dts = sb("dts", [128, 16])
        ldr = sb("ldr", [128, 16])
        ldi = sb("ldi", [128, 16])
        rmag = sb("rmag", [128, 16])
        S.add("act", lambda e: e.activation(out=dts[:], in_=ldt[:], func=AF.Exp), reads=["ldt"], writes=["dts"])
        S.add("dve", lambda e: e.tensor_tensor(out=ldr[:], in0=lamr[:], in1=dts[:], op=ALU.mult), reads=["lamr", "dts"], writes=["ldr"])
        S.add("dve", lambda e: e.tensor_tensor(out=ldi[:], in0=lami[:], in1=dts[:], op=ALU.mult), reads=["lami", "dts"], writes=["ldi"])
        S.add("act", lambda e: e.activation(out=rmag[:], in_=ldr[:], func=AF.Exp), reads=["ldr"], writes=["rmag"])
        idx = sb("idx", [128, 128])
        S.add("pool", lambda e: e.iota(idx[:], pattern=[[1, 128]], base=1, channel_multiplier=0, allow_small_or_imprecise_dtypes=True), writes=["idx"])
        Rc_p = sb("Rc", [128, 16, 128])
        Rs_p = sb("Rs", [128, 16, 128])
        Rm = sb("Rm", [128, 16, 128])
        kRc, kRs, kRm = "Rc", "Rs", "Rm"
        sc_a = W5["T1"][:].rearrange("p (a b) -> p a b", a=4)
        sc_t = W5["T2"][:].rearrange("p (a b) -> p a b", a=4)
        sc_f = W5["T3"][:].rearrange("p (a b) -> p a b", a=4)
        sc_i = W5["T4"][:].bitcast(I32).rearrange("p (a b) -> p a b", a=4)
        for c4 in range(4):
            ksl = slice(4 * c4, 4 * c4 + 4)
            S.add("dve", lambda e, ksl=ksl: e.tensor_tensor(out=sc_a, in0=ldi[:, ksl].unsqueeze(2).broadcast_to([128, 4, 128]),
                                                           in1=idx[:].unsqueeze(1).broadcast_to([128, 4, 128]), op=ALU.mult),
                  reads=["ldi", "idx"], writes=["T1"])
            for R_, kR, off in ((Rc_p, "Rc", 0.25), (Rs_p, "Rs", 0.0)):
                S.add("dve", lambda e, off=off: e.tensor_scalar(out=sc_t, in0=sc_a, scalar1=1.0 / TWO_PI, scalar2=off,
                                                               op0=ALU.mult, op1=ALU.add), reads=["T1"], writes=["T2"])
                S.add("dve", lambda e: e.tensor_copy(out=sc_i, in_=sc_t), reads=["T2"], writes=["T4"])
                S.add("dve", lambda e: e.tensor_copy(out=sc_f, in_=sc_i), reads=["T4"], writes=["T3"])
                S.add("dve", lambda e: e.tensor_tensor(out=sc_t, in0=sc_t, in1=sc_f, op=ALU.subtract),
                      reads=["T2", "T3"], writes=["T2"])
                S.add("act", lambda e, R_=R_, ksl=ksl: e.activation(out=R_[:, ksl, :], in_=sc_t, func=AF.Sin, scale=6.283179),
                      reads=["T2"], writes=[kR])
        S.add("dve", lambda e: e.tensor_copy(out=Rm[:], in_=rmag[:].unsqueeze(2).broadcast_to([128, 16, 128])),
              reads=["rmag"], writes=["Rm"])

        ar1 = sb("ar1", [128, 16])
        ai = sb("ai_", [128, 16])
        den = sb("den", [128, 16])
        t1 = sb("t1_", [128, 16])
        t2 = sb("t2_", [128, 16])
        cre = sb("cre", [128, 16])
        cim = sb("cim", [128, 16])
        S.add("dve", lambda e: e.tensor_tensor(out=ar1[:], in0=rmag[:], in1=Rc_p[:, :, 0], op=ALU.mult), reads=["rmag", kRc], writes=["ar1"])
        S.add("dve", lambda e: e.tensor_scalar(out=ar1[:], in0=ar1[:], scalar1=-1.0, scalar2=None, op0=ALU.add), reads=["ar1"], writes=["ar1"])
        S.add("dve", lambda e: e.tensor_tensor(out=ai[:], in0=rmag[:], in1=Rs_p[:, :, 0], op=ALU.mult), reads=["rmag", kRs], writes=["ai"])
        S.add("dve", lambda e: e.tensor_tensor(out=den[:], in0=lamr[:], in1=lamr[:], op=ALU.mult), reads=["lamr"], writes=["den"])
        S.add("dve", lambda e: e.tensor_tensor(out=t1[:], in0=lami[:], in1=lami[:], op=ALU.mult), reads=["lami"], writes=["t1"])
        S.add("dve", lambda e: e.tensor_tensor(out=den[:], in0=den[:], in1=t1[:], op=ALU.add), reads=["den", "t1"], writes=["den"])
        S.add("dve", lambda e: e.reciprocal(out=den[:], in_=den[:]), reads=["den"], writes=["den"])
        S.add("dve", lambda e: e.tensor_tensor(out=t1[:], in0=ar1[:], in1=lamr[:], op=ALU.mult), reads=["ar1", "lamr", "den"], writes=["t1"])
        S.add("dve", lambda e: e.tensor_tensor(out=t2[:], in0=ai[:], in1=lami[:], op=ALU.mult), reads=["ai", "lami"], writes=["t2"])
        S.add("dve", lambda e: e.tensor_tensor(out=t1[:], in0=t1[:], in1=t2[:], op=ALU.add), reads=["t1", "t2"], writes=["t1"])
        S.add("dve", lambda e: e.tensor_tensor(out=cre[:], in0=t1[:], in1=den[:], op=ALU.mult), reads=["t1", "den"], writes=["cre"])
        S.add("dve", lambda e: e.tensor_tensor(out=t1[:], in0=ai[:], in1=lamr[:], op=ALU.mult), reads=["ai", "lamr", "cre"], writes=["t1"])
        S.add("dve", lambda e: e.tensor_tensor(out=t2[:], in0=ar1[:], in1=lami[:], op=ALU.mult), reads=["ar1", "lami"], writes=["t2"])
        S.add("dve", lambda e: e.tensor_tensor(out=t1[:], in0=t1[:], in1=t2[:], op=ALU.subtract), reads=["t1", "t2"], writes=["t1"])
        S.add("dve", lambda e: e.tensor_tensor(out=cim[:], in0=t1[:], in1=den[:], op=ALU.mult), reads=["t1", "den"], writes=["cim"])
        bb = {}
        u1 = sb("u1_", [128, 16, 16])
        u2 = sb("u2_", [128, 16, 16])
        creb = cre[:].unsqueeze(2).broadcast_to([128, 16, 16])
        cimb = cim[:].unsqueeze(2).broadcast_to([128, 16, 16])
        for nm, (a0, b0, a1, b1, op) in (("bbr", (creb, Bl_r, cimb, Bl_i, ALU.subtract)),
                                         ("bbi", (creb, Bl_i, cimb, Bl_r, ALU.add))):
            t = sb(nm, [128, 16, 16])
            S.add("dve", lambda e, a0=a0, b0=b0: e.tensor_tensor(out=u1[:], in0=b0[:], in1=a0, op=ALU.mult),
                  reads=["cre", "cim", "Bl_r", "Bl_i"], writes=["u1"])
            S.add("dve", lambda e, a1=a1, b1=b1: e.tensor_tensor(out=u2[:], in0=b1[:], in1=a1, op=ALU.mult),
                  reads=["cre", "cim", "Bl_r", "Bl_i"], writes=["u2"])
            S.add("dve", lambda e, t=t, op=op: e.tensor_tensor(out=t[:], in0=u1[:], in1=u2[:], op=op),
                  reads=["u1", "u2"], writes=[nm])
            bb[nm] = t
        BT = {}
        bpad = sb("bpad", [128, 16, 128], BF16)
        for nm in ("bbr", "bbi"):
            pad = bpad
            S.add("pool", lambda e, pad=pad: e.memset(pad[:], 0.0), writes=["bpad"])
            for k in range(16):
                for g2 in range(2):
                    off = ((2 * k + g2) % 8) * 16
                    S.add("dve", lambda e, pad=pad, k=k, g2=g2, off=off, nm=nm: e.tensor_copy(
                        out=pad[g2 * 64:(g2 + 1) * 64, k, off:off + 16], in_=bb[nm][g2 * 64:(g2 + 1) * 64, k, :]),
                        reads=[nm, "bpad"], writes=["bpad"])
            T = sb(nm + "T", [128, 16, 128], BF16)
            for h in range(2):
                pt, ptk = PT.next()
                for j in range(8):
                    S.add("pe", lambda e, pad=pad, pt=pt, j=j, h=h: e.transpose(out=pt[:, j, :], in_=pad[:, h * 8 + j, :], identity=ident[:]),
                          reads=["bpad", "ident"], writes=[ptk])
                S.add("dve", lambda e, T=T, pt=pt, h=h: e.tensor_copy(out=T[:, h * 8:(h + 1) * 8, :], in_=pt[:]),
                      reads=[ptk], writes=[nm + "T"])
            BT[nm] = T
        Cb = {}
        for nm, src, sgn in (("Cbr", Cl_r, 1.0), ("Cbi", Cl_i, -1.0)):
            t = sb(nm, [128, 16, 32], BF16)
            S.add("pool", lambda e, t=t: e.memset(t[:], 0.0), writes=[nm])
            for g2 in range(2):
                S.add("dve", lambda e, t=t, src=src, g2=g2, sgn=sgn: e.tensor_scalar(
                    out=t[g2 * 64:(g2 + 1) * 64, :, g2 * 16:(g2 + 1) * 16], in0=src[g2 * 64:(g2 + 1) * 64, :, :],
                    scalar1=sgn, scalar2=None, op0=ALU.mult), reads=["Cl_r", "Cl_i", nm], writes=[nm])
            Cb[nm] = t

        cxr = sb("cxr", [128, 16])
        cxi = sb("cxi", [128, 16])
        S.add("pool", lambda e: e.memset(cxr[:], 0.0), writes=["cxr"])
        S.add("pool", lambda e: e.memset(cxi[:], 0.0), writes=["cxi"])
        s0r = sb("s0r", [128, 2, 16])
        s0i = sb("s0i", [128, 2, 16])
        S.add("sp", lambda e: e.dma_start(out=s0r[:], in_=sbre.rearrange("s (k g) n -> (g n) s k", g=2)), writes=["s0r"], is_dma=True)
        S.add("sp", lambda e: e.dma_start(out=s0i[:], in_=sbim.rearrange("s (k g) n -> (g n) s k", g=2)), writes=["s0i"], is_dma=True)
        cs_r = sb("cs_r", [128, 2, 16])
        cs_i = sb("cs_i", [128, 2, 16])

        def rmsnorm_T(src, srck, gtile, gk):
            S.add("act", lambda e: e.activation(out=gate[:], in_=src[:], func=AF.Square, accum_out=ssq[:]),
                  reads=[srck], writes=["gate", "ssq"])
            S.add("dve", lambda e: e.tensor_scalar(out=rstd[:], in0=ssq[:], scalar1=1.0 / D, scalar2=EPS, op0=ALU.mult, op1=ALU.add),
                  reads=["ssq"], writes=["rstd"])
            S.add("pool", lambda e: e.tensor_tensor(out=rstd[:], in0=rstd[:], in1=mhalf[:, 0:1], op=ALU.pow),
                  reads=["rstd", "mhalf"], writes=["rstd"])
            S.add("dve", lambda e: e.scalar_tensor_tensor(out=xn[:], in0=src[:], scalar=rstd[:], in1=gtile[:], op0=ALU.mult, op1=ALU.mult),
                  reads=[srck, "rstd", gk], writes=["xn"])
            pt, ptk = PT.next()
            for k in range(8):
                S.add("pe", lambda e, k=k, pt=pt: e.transpose(out=pt[:, k, :], in_=xn[:, k * 128:(k + 1) * 128], identity=ident[:]),
                      reads=["xn", "ident"], writes=[ptk])
            S.add("act", lambda e, pt=pt: e.copy(out=hnT[:], in_=pt[:]), reads=[ptk], writes=["hnT"])

        def linear(lhsT, lk, nk, w, wk, c0, c1):
            p, pk = PS.next()
            for k in range(nk):
                S.add("pe", lambda e, k=k, p=p: e.matmul(p[:, 0:c1 - c0], lhsT=lhsT[:, k, :], rhs=w[:, k, c0:c1],
                                                        start=(k == 0), stop=(k == nk - 1)),
                      reads=[lk, wk], writes=[pk])
            return p, pk

        def load_tile(i):
            h, hk = H.next()
            p0, p0k = P0.next()
            S.add("sp", lambda e: e.dma_start(out=h[:], in_=xin[i * 128:(i + 1) * 128, :]), writes=[hk], is_dma=True)
            S.add("sp", lambda e: e.dma_start(out=p0[:], in_=pin[0, i * 128:(i + 1) * 128, :]), writes=[p0k], is_dma=True)
            return h, hk, p0, p0k

        nxt = load_tile(0)
        for i in range(nt):
            h, hk, p0, p0k = nxt
            if i + 1 < nt:
                nxt = load_tile(i + 1)
            samp = (i == ntp)
            Rc, Rs = Rc_p, Rs_p

            rmsnorm_T(h, hk, g_l0, "g_l0")
            p, pk = linear(hnT, "hnT", 8, w_in0, "w_in0", 0, 512)
            S.add("act", lambda e, p=p: e.activation(out=ua[:], in_=p[:], func=AF.Gelu_apprx_tanh), reads=[pk], writes=["ua"])
            p, pk = linear(hnT, "hnT", 8, w_in0, "w_in0", 512, 1024)
            S.add("act", lambda e, p=p: e.activation(out=va[:], in_=p[:], func=AF.Gelu_apprx_tanh), reads=[pk], writes=["va"])
            p, pk = linear(hnT, "hnT", 8, w_in0, "w_in0", 1024, 1536)
            S.add("act", lambda e, p=p: e.activation(out=za[:], in_=p[:], func=AF.Silu), reads=[pk], writes=["za"])
            p, pk = linear(hnT, "hnT", 8, w_in0, "w_in0", 1536, 2048)
            S.add("act", lambda e, p=p: e.copy(out=ubf[:], in_=p[:]), reads=[pk], writes=["ubf"])
            S.add("dve", lambda e, p=p: e.tensor_tensor(out=ubd[:], in0=p[:], in1=Db[:], op=ALU.mult), reads=[pk, "Db"], writes=["ubd"])
            p, pk = linear(hnT, "hnT", 8, w_in0, "w_in0", 2048, 2560)
            S.add("act", lambda e, p=p: e.activation(out=zb[:], in_=p[:], func=AF.Silu), reads=[pk], writes=["zb"])

            S.add("dve", lambda e: e.bn_stats(out=bst[:], in_=va[:]), reads=["va"], writes=["bst"])
            S.add("dve", lambda e: e.bn_aggr(out=bag[:], in_=bst[:]), reads=["bst"], writes=["bag"])
            S.add("dve", lambda e: e.tensor_scalar(out=bag[:, 1:2], in0=bag[:, 1:2], scalar1=EPS, scalar2=None, op0=ALU.add),
                  reads=["bag"], writes=["bag"])
            S.add("pool", lambda e: e.tensor_tensor(out=bag[:, 1:2], in0=bag[:, 1:2], in1=mhalf[:, 0:1], op=ALU.pow),
                  reads=["bag", "mhalf"], writes=["bag"])
            S.add("dve", lambda e: e.tensor_scalar(out=vln[:], in0=va[:], scalar1=bag[:, 0:1], scalar2=bag[:, 1:2],
                                                  op0=ALU.subtract, op1=ALU.mult), reads=["va", "bag"], writes=["vln"])
            S.add("dve", lambda e: e.tensor_tensor(out=vln[:], in0=vln[:], in1=lng[:], op=ALU.mult), reads=["vln", "lng"], writes=["vln"])
            S.add("dve", lambda e: e.tensor_tensor(out=vln[:], in0=vln[:], in1=lnb[:], op=ALU.add), reads=["vln", "lnb"], writes=["vln"])
            S.add("act", lambda e: e.copy(out=vbf[:], in_=vln[:]), reads=["vln"], writes=["vbf"])
            if samp:
                S.add("sp", lambda e: e.dma_start(out=o_av, in_=vln[:]), reads=["vln"], writes=["o_av"], is_dma=True)
            p, pk = PS.next()
            wm, wmk = (wmixTs, "wlsT") if samp else (wmixT, "wlT")
            for g in range(4):
                S.add("pe", lambda e, g=g, p=p, wm=wm: e.matmul(p[:, g * 128:(g + 1) * 128], lhsT=wm[:, g, :], rhs=vbf[:, g * 128:(g + 1) * 128],
                                                               start=True, stop=True), reads=[wmk, "vbf"], writes=[pk])
            bs_t, bsk = (bsbs, "bsbs") if samp else (bsb, "bsb")
            for g in range(4):
                S.add("dve", lambda e, g=g, p=p, bs_t=bs_t: e.scalar_tensor_tensor(
                    out=ua[:, g * 128:(g + 1) * 128], in0=p[:, g * 128:(g + 1) * 128], scalar=bs_t[:, g:g + 1],
                    in1=ua[:, g * 128:(g + 1) * 128], op0=ALU.add, op1=ALU.mult), reads=[pk, bsk, "ua"], writes=["ua"])
            S.add("dve", lambda e: e.tensor_tensor(out=mixin[:, 0:512], in0=ua[:], in1=za[:], op=ALU.mult),
                  reads=["ua", "za"], writes=["mixin"])

            pt, ptk = PT.next()
            for q in range(4):
                S.add("pe", lambda e, q=q, pt=pt: e.transpose(out=pt[:, q, :], in_=ubf[:, q * 128:(q + 1) * 128], identity=ident[:]),
                      reads=["ubf", "ident"], writes=[ptk])
            S.add("act", lambda e, pt=pt: e.copy(out=ubT[:], in_=pt[:, 0:4, :]), reads=[ptk], writes=["ubT"])
            py, pyk = py_bank, "ps_y"
            for q in range(4):
                pbr, pbrk = PS.next()
                pbi, pbik = PS.next()
                for j in range(4):
                    k = 4 * q + j
                    S.add("pe", lambda e, j=j, k=k, q=q, pbr=pbr: e.matmul(pbr[:, j * 128:(j + 1) * 128], lhsT=BT["bbr"][:, k, :], rhs=ubT[:, q, :],
                                                                          start=True, stop=True), reads=["bbrT", "ubT"], writes=[pbrk])
                    S.add("pe", lambda e, j=j, k=k, q=q, pbi=pbi: e.matmul(pbi[:, j * 128:(j + 1) * 128], lhsT=BT["bbi"][:, k, :], rhs=ubT[:, q, :],
                                                                          start=True, stop=True), reads=["bbiT", "ubT"], writes=[pbik])
                T1, T2, T3, T4 = W5["T1"], W5["T2"], W5["T3"], W5["T4"]
                vr, vi, zr, zi = W5["vr"], W5["vi"], W5["zr"], W5["zi"]
                if samp:
                    def V(t):
                        return t[:].rearrange("p (a s b) -> p a s b", a=4, s=2)
                    def Vp(t):
                        return t[:].rearrange("p (a s b) -> p a s b", a=4, s=2)
                    rc = Rc[:, 4 * q:4 * q + 4, 0:64].unsqueeze(2).broadcast_to([128, 4, 2, 64])
                    rs = Rs[:, 4 * q:4 * q + 4, 0:64].unsqueeze(2).broadcast_to([128, 4, 2, 64])
                else:
                    def V(t):
                        return t[:]
                    def Vp(t):
                        return t[:]
                    rc = Rc[:, 4 * q:4 * q + 4, :].rearrange("p a b -> p (a b)")
                    rs = Rs[:, 4 * q:4 * q + 4, :].rearrange("p a b -> p (a b)")
                S.add("dve", lambda e, pbr=pbr, rc=rc, V=V, Vp=Vp: e.tensor_tensor(out=V(T1), in0=Vp(pbr), in1=rc, op=ALU.mult), reads=[pbrk, kRc], writes=["T1"])
                S.add("dve", lambda e, pbi=pbi, rs=rs, V=V, Vp=Vp: e.tensor_tensor(out=V(T2), in0=Vp(pbi), in1=rs, op=ALU.mult), reads=[pbik, kRs], writes=["T2"])
                S.add("dve", lambda e, pbi=pbi, rc=rc, V=V, Vp=Vp: e.tensor_tensor(out=V(T3), in0=Vp(pbi), in1=rc, op=ALU.mult), reads=[pbik, kRc], writes=["T3"])
                S.add("dve", lambda e, pbr=pbr, rs=rs, V=V, Vp=Vp: e.tensor_tensor(out=V(T4), in0=Vp(pbr), in1=rs, op=ALU.mult), reads=[pbrk, kRs], writes=["T4"])
                S.add("pool", lambda e: e.tensor_tensor(out=vr[:], in0=T1[:], in1=T2[:], op=ALU.add), reads=["T1", "T2"], writes=["vr"])
                S.add("pool", lambda e: e.tensor_tensor(out=vi[:], in0=T3[:], in1=T4[:], op=ALU.subtract), reads=["T3", "T4"], writes=["vi"])
                for j in range(4):
                    k = 4 * q + j
                    if samp:
                        segs = [(j * 128 + 64 * s_, 64, s0r[:, s_, k:k + 1], s0i[:, s_, k:k + 1], "s0r", "s0i") for s_ in range(2)]
                    else:
                        segs = [(j * 128, 128, cxr[:, k:k + 1], cxi[:, k:k + 1], "cxr", "cxi")]
                    for (c0, ln, ir, ii, irk, iik) in segs:
                        S.add("dve", lambda e, c0=c0, ln=ln, ir=ir, k=k: e.tensor_tensor_scan(
                            out=zr[:, c0:c0 + ln], data0=Rm[:, k, 0:ln], data1=vr[:, c0:c0 + ln], initial=ir, op0=ALU.mult, op1=ALU.add),
                            reads=["vr", kRm, irk], writes=["zr"])
                        S.add("dve", lambda e, c0=c0, ln=ln, ii=ii, k=k: e.tensor_tensor_scan(
                            out=zi[:, c0:c0 + ln], data0=Rm[:, k, 0:ln], data1=vi[:, c0:c0 + ln], initial=ii, op0=ALU.mult, op1=ALU.add),
                            reads=["vi", kRm, iik], writes=["zi"])
                S.add("pool", lambda e, rc=rc, V=V: e.tensor_tensor(out=V(T1), in0=V(zr), in1=rc, op=ALU.mult), reads=["zr", kRc], writes=["T1"])
                S.add("pool", lambda e, rs=rs, V=V: e.tensor_tensor(out=V(T2), in0=V(zi), in1=rs, op=ALU.mult), reads=["zi", kRs], writes=["T2"])
                S.add("pool", lambda e, rs=rs, V=V: e.tensor_tensor(out=V(T3), in0=V(zr), in1=rs, op=ALU.mult), reads=["zr", kRs], writes=["T3"])
                S.add("pool", lambda e, rc=rc, V=V: e.tensor_tensor(out=V(T4), in0=V(zi), in1=rc, op=ALU.mult), reads=["zi", kRc], writes=["T4"])
                S.add("dve", lambda e: e.tensor_tensor(out=xrb[:].rearrange("p a b -> p (a b)"), in0=T1[:], in1=T2[:], op=ALU.subtract),
                      reads=["T1", "T2"], writes=["xrb"])
                S.add("dve", lambda e: e.tensor_tensor(out=xib[:].rearrange("p a b -> p (a b)"), in0=T3[:], in1=T4[:], op=ALU.add),
                      reads=["T3", "T4"], writes=["xib"])
                T13 = T1[:].rearrange("p (a b) -> p a b", a=4)
                T23 = T2[:].rearrange("p (a b) -> p a b", a=4)
                T33 = T3[:].rearrange("p (a b) -> p a b", a=4)
                T43 = T4[:].rearrange("p (a b) -> p a b", a=4)
                if samp:
                    for s_ in range(2):
                        c_ = 64 * s_ + 63
                        S.add("dve", lambda e, s_=s_, c_=c_, q=q, T13=T13, T23=T23: e.tensor_tensor(out=cs_r[:, s_, 4 * q:4 * q + 4], in0=T13[:, :, c_], in1=T23[:, :, c_], op=ALU.subtract),
                              reads=["T1", "T2"], writes=["cs_r"])
                        S.add("dve", lambda e, s_=s_, c_=c_, q=q, T33=T33, T43=T43: e.tensor_tensor(out=cs_i[:, s_, 4 * q:4 * q + 4], in0=T33[:, :, c_], in1=T43[:, :, c_], op=ALU.add),
                              reads=["T3", "T4"], writes=["cs_i"])
                else:
                    S.add("dve", lambda e, q=q, T13=T13, T23=T23: e.tensor_tensor(out=cxr[:, 4 * q:4 * q + 4], in0=T13[:, :, 127], in1=T23[:, :, 127], op=ALU.subtract),
                          reads=["T1", "T2"], writes=["cxr"])
                    S.add("dve", lambda e, q=q, T33=T33, T43=T43: e.tensor_tensor(out=cxi[:, 4 * q:4 * q + 4], in0=T33[:, :, 127], in1=T43[:, :, 127], op=ALU.add),
                          reads=["T3", "T4"], writes=["cxi"])
                for j in range(4):
                    k = 4 * q + j
                    S.add("pe", lambda e, j=j, k=k, py=py: e.matmul(py[:, 32 * k:32 * k + 32], lhsT=xrb[:, j, :], rhs=Cb["Cbr"][:, k, :], start=True, stop=False),
                          reads=["xrb", "Cbr"], writes=[pyk])
                    S.add("pe", lambda e, j=j, k=k, py=py: e.matmul(py[:, 32 * k:32 * k + 32], lhsT=xib[:, j, :], rhs=Cb["Cbi"][:, k, :], start=False, stop=True),
                          reads=["xib", "Cbi"], writes=[pyk])
            if not samp:
                if i == ntp - 1:
                    S.add("sp", lambda e: e.dma_start(out=o_bre[0].rearrange("(k g) n -> (g n) k", g=2), in_=cxr[:]), reads=["cxr"], writes=["o_bre"], is_dma=True)
                    S.add("sp", lambda e: e.dma_start(out=o_bim[0].rearrange("(k g) n -> (g n) k", g=2), in_=cxi[:]), reads=["cxi"], writes=["o_bim"], is_dma=True)
            else:
                S.add("sp", lambda e: e.dma_start(out=o_bre[1:3].rearrange("s (k g) n -> (g n) s k", g=2), in_=cs_r[:]), reads=["cs_r"], writes=["o_bre"], is_dma=True)
                S.add("sp", lambda e: e.dma_start(out=o_bim[1:3].rearrange("s (k g) n -> (g n) s k", g=2), in_=cs_i[:]), reads=["cs_i"], writes=["o_bim"], is_dma=True)
            S.add("dve", lambda e, py=py: e.tensor_tensor(out=yb[:], in0=py[:], in1=ubd[:], op=ALU.add), reads=[pyk, "ubd"], writes=["yb"])
            if samp and debug_h:
                S.add("sp", lambda e: e.dma_start(out=dbg, in_=yb[:]), reads=["yb"], writes=["dbg"], is_dma=True)
            S.add("act", lambda e: e.activation(out=yg[:], in_=yb[:], func=AF.Gelu_apprx_tanh), reads=["yb"], writes=["yg"])
            S.add("act", lambda e: e.copy(out=ygb[:], in_=yg[:]), reads=["yg"], writes=["ygb"])
            pt, ptk = PT.next()
            for q in range(4):
                S.add("pe", lambda e, q=q, pt=pt: e.transpose(out=pt[:, q, :], in_=ygb[:, q * 128:(q + 1) * 128], identity=ident[:]),
                      reads=["ygb", "ident"], writes=[ptk])
            S.add("act", lambda e, pt=pt: e.copy(out=ygT[:], in_=pt[:, 0:4, :]), reads=[ptk], writes=["ygT"])
            p, pk = linear(ygT, "ygT", 4, w_glu, "w_glu", 0, 512)
            S.add("act", lambda e, p=p: e.activation(out=yb[:], in_=p[:], func=AF.Sigmoid), reads=[pk], writes=["yb"])
            S.add("dve", lambda e: e.tensor_tensor(out=yg[:], in0=yg[:], in1=yb[:], op=ALU.mult), reads=["yg", "yb"], writes=["yg"])
            S.add("dve", lambda e: e.tensor_tensor(out=mixin[:, 512:1024], in0=yg[:], in1=zb[:], op=ALU.mult), reads=["yg", "zb"], writes=["mixin"])

            pt, ptk = PT.next()
            for k in range(8):
                S.add("pe", lambda e, k=k, pt=pt: e.transpose(out=pt[:, k, :], in_=mixin[:, k * 128:(k + 1) * 128], identity=ident[:]),
                      reads=["mixin", "ident"], writes=[ptk])
            S.add("act", lambda e, pt=pt: e.copy(out=mixT[:], in_=pt[:]), reads=[ptk], writes=["mixT"])
            for hh in range(2):
                p, pk = linear(mixT, "mixT", 8, w_out0, "w_out0", hh * 512, (hh + 1) * 512)
                S.add("dve", lambda e, p=p, hh=hh, h=h: e.tensor_tensor(out=h[:, hh * 512:(hh + 1) * 512], in0=h[:, hh * 512:(hh + 1) * 512], in1=p[:], op=ALU.add),
                      reads=[pk, hk], writes=[hk])

            rmsnorm_T(h, hk, g_p0, "g_p0")
            S.add("act", lambda e, p0=p0: e.copy(out=pbf[:], in_=p0[:]), reads=[p0k], writes=["pbf"])
            pt, ptk = PT.next()
            for k in range(2):
                S.add("pe", lambda e, k=k, pt=pt: e.transpose(out=pt[:, k, :], in_=pbf[:, k * 128:(k + 1) * 128], identity=ident[:]),
                      reads=["pbf", "ident"], writes=[ptk])
            S.add("act", lambda e, pt=pt: e.copy(out=pT[:], in_=pt[:, 0:2, :]), reads=[ptk], writes=["pT"])
            for hh in range(2):
                p, pk = linear(hnT, "hnT", 8, w_gate0, "w_gate0", hh * 512, (hh + 1) * 512)
                S.add("act", lambda e, p=p, hh=hh: e.activation(out=gate[:, hh * 512:(hh + 1) * 512], in_=p[:], func=AF.Sigmoid), reads=[pk], writes=["gate"])
                p, pk = linear(pT, "pT", 2, w_pp0, "w_pp0", hh * 512, (hh + 1) * 512)
                S.add("dve", lambda e, p=p, hh=hh: e.tensor_tensor(out=gate[:, hh * 512:(hh + 1) * 512], in0=gate[:, hh * 512:(hh + 1) * 512], in1=p[:], op=ALU.mult),
                      reads=[pk, "gate"], writes=["gate"])
            S.add("dve", lambda e, h=h: e.tensor_tensor(out=h[:], in0=h[:], in1=gate[:], op=ALU.add), reads=[hk, "gate"], writes=[hk])
            S.add("sp", lambda e, h=h, i=i: e.dma_start(out=h1_scr[i * 128:(i + 1) * 128, :], in_=h[:]), reads=[hk], writes=["h1_scr"], is_dma=True)

        S.emit()
    nc.all_engine_barrier()
    T = dict(locals())
    phase_b(nc, T, ntp, debug_h)
    return nc


WEIGHT_KEYS = ["d_conv_w", "d_A_log", "d_dt_bias", "d_norm_g", "norm_g", "final_norm_g", "ple_proj", "ple_gate_w", "ple_norm_g", "even_w_in", "even_w_out",
               "a_ln_g", "a_ln_b", "a_w_s", "a_b_s", "b_lam_re", "b_lam_im", "b_log_dt", "b_B_re", "b_B_im",
               "b_C_re", "b_C_im", "b_D", "b_glu_w", "odd_w_in", "odd_w_out"]


def make_in_maps(inp, ntp=NTP):
    maps = []
    for c in range(NCORES):
        m = {k: np.ascontiguousarray(inp[k], dtype=np.float32) for k in WEIGHT_KEYS}
        xs = np.asarray(inp["x_sample"])[2 * c:2 * c + 2].reshape(128, D)
        m["xin"] = np.ascontiguousarray(np.concatenate([np.asarray(inp["x_prompt"])[c, :ntp * 128], xs], axis=0))
        ps_ = np.asarray(inp["p_sample"])[:, 2 * c:2 * c + 2].reshape(2, 128, 256)
        m["pin"] = np.ascontiguousarray(np.concatenate([np.asarray(inp["p_prompt"])[:, c, :ntp * 128], ps_], axis=1))
        m["sbre"] = np.ascontiguousarray(np.asarray(inp["state_b_re"])[0, 2 * c:2 * c + 2])
        m["sbim"] = np.ascontiguousarray(np.asarray(inp["state_b_im"])[0, 2 * c:2 * c + 2])
        m["cache_k"] = np.ascontiguousarray(np.asarray(inp["cache_k_c"])[0, 2 * c:2 * c + 2].reshape(2, 4096, 512))
        m["cache_v"] = np.ascontiguousarray(np.asarray(inp["cache_v_c"])[0, 2 * c:2 * c + 2].reshape(2, 4096, 512))
        m["state_d"] = np.ascontiguousarray(np.asarray(inp["state_d"])[0, 2 * c:2 * c + 2])
        m["state_conv"] = np.ascontiguousarray(np.asarray(inp["state_conv_d"])[0, 2 * c:2 * c + 2])
        maps.append(m)
    return maps


_NC_CACHE = {}


def kernel(**inputs):
    if "nc" not in _NC_CACHE:
        _NC_CACHE["nc"] = build_program()
    nc = _NC_CACHE["nc"]
    res = run_bass_kernel_spmd(nc, make_in_maps(inputs), core_ids=list(range(NCORES)))
    R = res.results
    B, DB = 8, 16
    y_prompt = np.stack([R[c]["o_y"][:SEQ] for c in range(B)]).astype(np.float32)
    y_sample = np.concatenate([R[c]["o_y"][SEQ:].reshape(2, 64, D) for c in range(B)]).astype(np.float32)
    b_re_p = np.stack([R[c]["o_bre"][0] for c in range(B)])[None]
    b_im_p = np.stack([R[c]["o_bim"][0] for c in range(B)])[None]
    b_re_s = np.concatenate([R[c]["o_bre"][1:3] for c in range(B)])[None]
    b_im_s = np.concatenate([R[c]["o_bim"][1:3] for c in range(B)])[None]
    a_v_s = np.concatenate([R[c]["o_av"].reshape(2, 64, 512) for c in range(B)])[None]
    k_c_p = np.stack([R[c]["o_kc"][:SEQ].reshape(SEQ, 8, 64) for c in range(B)])[None]
    v_c_p = np.stack([R[c]["o_vc"][:SEQ].reshape(SEQ, 8, 64) for c in range(B)])[None]
    k_c_s = np.concatenate([R[c]["o_kc"][SEQ:].reshape(2, 64, 8, 64) for c in range(B)])[None]
    v_c_s = np.concatenate([R[c]["o_vc"][SEQ:].reshape(2, 64, 8, 64) for c in range(B)])[None]
    conv_d_p = np.stack([R[c]["o_conv"][0] for c in range(B)])[None]
    conv_d_s = np.concatenate([R[c]["o_conv"][1:3] for c in range(B)])[None]
    s_d_p = np.stack([R[c]["o_sd"][0] for c in range(B)])[None]
    s_d_s = np.concatenate([R[c]["o_sd"][1:3] for c in range(B)])[None]
    f = lambda a: np.ascontiguousarray(a, dtype=np.float32)
    return tuple(f(a) for a in (y_prompt, y_sample, b_re_p, b_im_p, k_c_p, v_c_p, s_d_p, conv_d_p,
                                b_re_s, b_im_s, a_v_s, k_c_s, v_c_s, s_d_s, conv_d_s))


def phase_b(nc, T, ntp, debug_h):
    nt = ntp + 1
    g = lambda n: T[n]
    h1_scr, pin, o_y, o_kc, o_vc, o_conv = g("h1_scr"), g("pin"), g("o_y"), g("o_kc"), g("o_vc"), g("o_conv")
    norm_g, final_norm_g, ple_proj, ple_gate_w, ple_norm_g = g("norm_g"), g("final_norm_g"), g("ple_proj"), g("ple_gate_w"), g("ple_norm_g")
    odd_w_in, odd_w_out, cache_k, cache_v = g("odd_w_in"), g("odd_w_out"), g("cache_k"), g("cache_v")
    ntok = nt * 128
    kts = nc.dram_tensor("kts", [nt, 128, 512], BF16, kind="Internal").ap()
    vs = nc.dram_tensor("vs", [ntok, 512], BF16, kind="Internal").ap()
    dbg2 = nc.dram_tensor("dbg2", [ntok, 512], F32, kind="ExternalOutput").ap() if debug_h else None

    S = Sched(nc)
    st = contextlib.ExitStack()
    with st:
        def sb(name, shape, dt=F32):
            return st.enter_context(nc.sbuf_tensor(name, list(shape), dt))

        def ps(name, shape, dt=F32):
            return st.enter_context(nc.psum_tensor(name, list(shape), dt))

        st.enter_context(nc.allow_non_contiguous_dma("small parameter layout loads"))
        PS = Rot([ps(f"qs{i}", [128, 512]) for i in range(2)], "qs")
        PT = Rot([ps("qt0", [128, 8, 128], BF16)], "qt")
        PZ = ps("pz", [128, 1024])
        P2 = ps("p2", [128, 1024])
        ACC = ps("acc", [128, 512])

        ident = sb("identb", [128, 128], BF16)
        S.add("pool", lambda e: e.memset(ident[:], 0.0), writes=["ident"])
        S.add("pool", lambda e: e.affine_select(out=ident[:], in_=ident[:], compare_op=ALU.not_equal, fill=1.0,
                                               base=0, pattern=[[-1, 128]], channel_multiplier=1), reads=["ident"], writes=["ident"])
        mhalf = sb("mhalfb", [128, 1])
        onec = sb("onec", [128, 1])
        S.add("pool", lambda e: e.memset(onec[:], 1.0), writes=["onec"])
        S.add("pool", lambda e: e.memset(mhalf[:], -0.5), writes=["mhalf"])
        negU = sb("negU", [128, 128], BF16)
        zer = sb("zer", [128, 512], BF16)
        S.add("pool", lambda e: e.memset(zer[:], 0.0), writes=["zer"])
        negO = sb("negO", [128, 128], BF16)
        mask01 = sb("mask01", [128, 128], BF16)
        S.add("pool", lambda e: e.memset(negU[:], -1.0), writes=["negU"])
        S.add("pool", lambda e: e.affine_select(out=negU[:], in_=negU[:], compare_op=ALU.is_ge, fill=0.0, base=0,
                                               pattern=[[-1, 128]], channel_multiplier=1), reads=["negU"], writes=["negU"])
        S.add("pool", lambda e: e.memset(negO[:], -1.0), writes=["negO"])
        S.add("pool", lambda e: e.memset(mask01[:], 1.0), writes=["mask01"])
        S.add("pool", lambda e: e.affine_select(out=mask01[:], in_=mask01[:], compare_op=ALU.is_gt, fill=0.0, base=0,
                                               pattern=[[1, 128]], channel_multiplier=-1), reads=["mask01"], writes=["mask01"])

        def bcast_load(name, src, n):
            t = sb(name, [128, n])
            S.add("sp", lambda e: e.dma_start(out=t[:], in_=src.partition_broadcast(128)), writes=[name], is_dma=True)
            return t

        g_l1 = bcast_load("g_l1", norm_g[1], D)
        g_p1 = bcast_load("g_p1", ple_norm_g[1], D)
        g_f = bcast_load("g_f", final_norm_g, D)

        def wload(name, src, kt, n, csz=512):
            t = sb(name, [128, kt, n], BF16)
            v = src.rearrange("(k p) n -> p k n", p=128)
            for c0 in range(0, n, csz):
                c1 = min(n, c0 + csz)
                S.add("pool", lambda e, c0=c0, c1=c1: e.dma_start(out=t[:, :, c0:c1], in_=v[:, :, c0:c1]),
                      writes=[name], is_dma=True)
            return t

        w_in1 = wload("w_in1", odd_w_in[0], 8, 4104)
        w_out1 = wload("w_out1", odd_w_out[0], 8, 1024)
        w_gate1 = wload("w_gate1", ple_gate_w[1], 8, 1024)
        w_pp1 = wload("w_pp1", ple_proj[1], 2, 1024)

        H = Rot([sb(f"hb{i}", [128, D]) for i in range(1)], "hb")
        P1 = Rot([sb(f"p1_{i}", [128, 256]) for i in range(1)], "p1_")
        xn = sb("xnb", [128, D], BF16)
        hnT = sb("hnTb", [128, 8, 128], BF16)
        ssq = sb("ssqb", [128, 1])
        rstd = sb("rstdb", [128, 1])
        gate = sb("gateb", [128, D])
        qbf = sb("qbf", [128, 512], BF16)
        kf = gate[:, 0:512]
        kbf = sb("kbf", [128, 512], BF16)
        vf = gate[:, 512:1024]
        vbf = sb("vbf1", [128, 512], BF16)
        vbs = sb("vbs", [64, 512], BF16)
        zcs = sb("zcs", [128, 512])
        QT = sb("QT", [128, 8, 128], BF16)
        S.add("pool", lambda e: e.memset(QT[:], 0.0), writes=["QT"])
        KTc = sb("KTc", [128, 4, 128], BF16)
        KB = Rot([sb(f"ktb{i}", [128, 4, 128], BF16) for i in range(3)], "ktb")
        VB = Rot([sb(f"vb{i}", [128, 512], BF16) for i in range(3)], "vb")
        CK = Rot([sb(f"ck{i}", [128, 512], BF16) for i in range(1)], "ck")
        arena = sb("arena", [128, 3072])
        ab16 = arena[:, :].bitcast(BF16)
        EB = [ab16[:, 0:1024], ab16[:, 1024:2048]]
        LM = [ab16[:, 2048:3072], ab16[:, 3072:4096]]
        cum = ab16[:, 4096:5120]
        wbuf = ab16[:, 5120:6144]
        ebuf = arena[:, 0:1024]
        mixin = sb("mixinb", [128, 1024], BF16)
        mixT = hnT
        pbf = sb("pbfb", [128, 256], BF16)
        pT = sb("pTb", [128, 2, 128], BF16)
        cout = sb("cout", [128, 512])
        yout = ebuf

        def rmsnorm(src, srck, gtile, gk, out, outk):
            S.add("act", lambda e: e.activation(out=gate[:], in_=src[:], func=AF.Square, accum_out=ssq[:]),
                  reads=[srck], writes=["gate", "ssq"])
            S.add("dve", lambda e: e.tensor_scalar(out=rstd[:], in0=ssq[:], scalar1=1.0 / D, scalar2=EPS, op0=ALU.mult, op1=ALU.add),
                  reads=["ssq"], writes=["rstd"])
            S.add("pool", lambda e: e.tensor_tensor(out=rstd[:], in0=rstd[:], in1=mhalf[:], op=ALU.pow),
                  reads=["rstd", "mhalf"], writes=["rstd"])
            S.add("dve", lambda e: e.scalar_tensor_tensor(out=out[:], in0=src[:], scalar=rstd[:], in1=gtile[:], op0=ALU.mult, op1=ALU.mult),
                  reads=[srck, "rstd", gk], writes=[outk])

        def transpose_to(src, srck, nk, dst, dstk):
            pt, ptk = PT.next()
            for k in range(nk):
                S.add("pe", lambda e, k=k, pt=pt: e.transpose(out=pt[:, k, :], in_=src[:, k * 128:(k + 1) * 128], identity=ident[:]),
                      reads=[srck, "ident"], writes=[ptk])
            S.add("act", lambda e, pt=pt: e.copy(out=dst[:, 0:nk, :], in_=pt[:, 0:nk, :]), reads=[ptk], writes=[dstk])

        def linear(lhsT, lk, nk, w, wk, c0, c1, msl=slice(0, 128)):
            p, pk = PS.next()
            m = msl.stop - msl.start
            for k in range(nk):
                S.add("pe", lambda e, k=k, p=p: e.matmul(p[0:m, 0:c1 - c0], lhsT=lhsT[:, k, msl], rhs=w[:, k, c0:c1],
                                                        start=(k == 0), stop=(k == nk - 1)), reads=[lk, wk], writes=[pk])
            return p, pk

        def attn_groups(nq):
            return [(0, 4), (4, 8)] if nq == 128 else [(0, 8)]

        def attn_z(st_):
            kt_of, ktk, v_ap, vk, q0, nq, ns, diag, first, last = st_
            for gi, (h0, h1) in enumerate(attn_groups(nq)):
                for h in range(h0, h1):
                    S.add("pe", lambda e, h=h: e.matmul(PZ[0:ns, h * nq:(h + 1) * nq], lhsT=kt_of(h), rhs=QT[:, h, q0:q0 + nq], start=True, stop=True),
                          reads=[ktk, "QT"], writes=[f"pz{gi}"])

        def m3(ap_, nh):
            return ap_.rearrange("p (h t) -> p h t", h=nh)

        def attn_el(st_, par):
            kt_of, ktk, v_ap, vk, q0, nq, ns, diag, first, last = st_
            eb, lm = EB[par], LM[par]
            for gi, (h0, h1) in enumerate(attn_groups(nq)):
                c0, c1 = h0 * nq, h1 * nq
                S.add("act", lambda e, c0=c0, c1=c1, eb=eb: e.activation(out=eb[0:ns, c0:c1], in_=PZ[0:ns, c0:c1], func=AF.Exp), reads=[f"pz{gi}"], writes=[f"e{par}_{gi}"])
                S.add("act", lambda e, c0=c0, c1=c1, eb=eb, lm=lm: e.activation(out=lm[0:ns, c0:c1], in_=eb[0:ns, c0:c1], func=AF.Ln, bias=1.0),
                      reads=[f"e{par}_{gi}"], writes=[f"lm{par}_{gi}"])
                if diag:
                    S.add("dve", lambda e, c0=c0, c1=c1, nh=h1 - h0, lm=lm: e.tensor_tensor(out=m3(lm[0:ns, c0:c1], nh), in0=m3(lm[0:ns, c0:c1], nh),
                                                                                         in1=mask01[0:ns, 0:nq].unsqueeze(1).broadcast_to([ns, nh, nq]), op=ALU.mult),
                          reads=[f"lm{par}_{gi}", "mask01"], writes=[f"lm{par}_{gi}"])

        def attn_p2(st_, par):
            kt_of, ktk, v_ap, vk, q0, nq, ns, diag, first, last = st_
            lm = LM[par]
            for gi, (h0, h1) in enumerate(attn_groups(nq)):
                c0, c1 = h0 * nq, h1 * nq
                S.add("pe", lambda e, c0=c0, c1=c1, lm=lm: e.matmul(P2[0:ns, c0:c1], lhsT=negU[0:ns, 0:ns], rhs=lm[0:ns, c0:c1], start=True, stop=False),
                      reads=["negU", f"lm{par}_{gi}"], writes=[f"p2{gi}"])
                if not first:
                    S.add("pe", lambda e, c0=c0, c1=c1: e.matmul(P2[0:ns, c0:c1], lhsT=negO[:, 0:ns], rhs=cum[:, c0:c1], start=False, stop=False),
                          reads=["negO", f"cum{gi}"], writes=[f"p2{gi}"])
                for h in range(h0, h1):
                    S.add("pe", lambda e, h=h, h1=h1: e.matmul(P2[0:ns, h * nq:(h + 1) * nq], lhsT=kt_of(h), rhs=QT[:, h, q0:q0 + nq], start=False, stop=(h == h1 - 1)),
                          reads=[ktk, "QT"], writes=[f"p2{gi}"])

        def attn_w(st_, par):
            kt_of, ktk, v_ap, vk, q0, nq, ns, diag, first, last = st_
            grps = attn_groups(nq)
            for gi, (h0, h1) in enumerate(grps):
                c0, c1 = h0 * nq, h1 * nq
                S.add("act", lambda e, c0=c0, c1=c1: e.activation(out=wbuf[0:ns, c0:c1], in_=P2[0:ns, c0:c1], func=AF.Exp), reads=[f"p2{gi}"], writes=[f"wb_{gi}"])
                if diag:
                    S.add("dve", lambda e, c0=c0, c1=c1, nh=h1 - h0: e.tensor_tensor(out=m3(wbuf[0:ns, c0:c1], nh), in0=m3(wbuf[0:ns, c0:c1], nh),
                                                                                  in1=mask01[0:ns, 0:nq].unsqueeze(1).broadcast_to([ns, nh, nq]), op=ALU.mult),
                          reads=[f"wb_{gi}", "mask01"], writes=[f"wb_{gi}"])

        def attn_wv(st_, par):
            kt_of, ktk, v_ap, vk, q0, nq, ns, diag, first, last = st_
            grps = attn_groups(nq)
            lm = LM[par]
            if first:
                S.add("pe", lambda e: e.matmul(ACC[0:nq, :], lhsT=zer[:, 0:nq], rhs=zer[:, :], start=True, stop=False), reads=["zer"], writes=["acc"])
            for gi, (h0, h1) in enumerate(grps):
                for h in range(h0, h1):
                    S.add("pe", lambda e, h=h: e.matmul(ACC[0:nq, h * 64:(h + 1) * 64], lhsT=wbuf[0:ns, h * nq:(h + 1) * nq], rhs=v_ap[0:ns, h * 64:(h + 1) * 64],
                                                       start=False, stop=(last and h == 7)), reads=[f"wb_{gi}", vk], writes=["acc"])
            if not last:
                for gi, (h0, h1) in enumerate(grps):
                    c0, c1 = h0 * nq, h1 * nq
                    S.add("pool", lambda e, c0=c0, c1=c1, lm=lm: e.tensor_tensor(out=cum[0:ns, c0:c1], in0=cum[0:ns, c0:c1], in1=lm[0:ns, c0:c1], op=ALU.add),
                          reads=[f"cum{gi}", f"lm{par}_{gi}"], writes=[f"cum{gi}"])

        def attn_run(step_iter):
            s0 = next(step_iter, None)
            if s0 is None:
                return
            attn_z(s0)
            attn_el(s0, 0)
            s1 = next(step_iter, None)
            if s1 is not None:
                attn_z(s1)
            cur, nxt, n = s0, s1, 0
            while cur is not None:
                par = n % 2
                attn_p2(cur, par)
                nn = None
                if nxt is not None:
                    attn_el(nxt, 1 - par)
                    nn = next(step_iter, None)
                    if nn is not None:
                        attn_z(nn)
                attn_w(cur, par)
                attn_wv(cur, par)
                cur, nxt, n = nxt, nn, n + 1

        S.add("pool", lambda e: e.memset(mixin[:, 512:1024], 0.0), writes=["mixin"])


        o_sd, state_d, state_conv = T["o_sd"], T["state_d"], T["state_conv"]
        d_conv_w, d_A_log, d_dt_bias, d_norm_g = T["d_conv_w"], T["d_A_log"], T["d_dt_bias"], T["d_norm_g"]
        Uincl = sb("Uincl", [64, 64])
        identf = sb("identf", [64, 64])
        nm_incl = sb("nm_incl", [64, 64])
        nm_low = sb("nm_low", [64, 64])
        ones128 = sb("ones128", [64, 128])
        S.add("pool", lambda e: e.memset(Uincl[:], 1.0), writes=["Uincl"])
        S.add("pool", lambda e: e.affine_select(out=Uincl[:], in_=Uincl[:], compare_op=ALU.is_ge, fill=0.0, base=0, pattern=[[1, 64]], channel_multiplier=-1),
              reads=["Uincl"], writes=["Uincl"])
        S.add("pool", lambda e: e.memset(identf[:], 0.0), writes=["identf"])
        S.add("pool", lambda e: e.affine_select(out=identf[:], in_=identf[:], compare_op=ALU.not_equal, fill=1.0, base=0, pattern=[[-1, 64]], channel_multiplier=1),
              reads=["identf"], writes=["identf"])
        S.add("pool", lambda e: e.memset(nm_incl[:], 0.0), writes=["nm_incl"])
        S.add("pool", lambda e: e.affine_select(out=nm_incl[:], in_=nm_incl[:], compare_op=ALU.is_ge, fill=-30000.0, base=0, pattern=[[1, 64]], channel_multiplier=-1),
              reads=["nm_incl"], writes=["nm_incl"])
        S.add("pool", lambda e: e.memset(nm_low[:], 0.0), writes=["nm_low"])
        S.add("pool", lambda e: e.affine_select(out=nm_low[:], in_=nm_low[:], compare_op=ALU.is_gt, fill=-30000.0, base=0, pattern=[[-1, 64]], channel_multiplier=1),
              reads=["nm_low"], writes=["nm_low"])
        S.add("pool", lambda e: e.memset(ones128[:], 1.0), writes=["ones128"])
        Sh = sb("Sh", [64, 3, 64], BF16)
        ShP = sb("ShP", [64, 3, 64], BF16)
        S.add("pool", lambda e: e.memset(Sh[:], 0.0), writes=["Sh"])
        S.add("pool", lambda e: e.memset(ShP[:], 0.0), writes=["ShP"])
        for i_ in range(1, 4):
            S.add("dve", lambda e, i_=i_: e.tensor_copy(out=Sh[:, i_ - 1, i_:64], in_=ident[0:64, 0:64 - i_]), reads=["ident", "Sh"], writes=["Sh"])
            S.add("dve", lambda e, i_=i_: e.tensor_copy(out=ShP[:, i_ - 1, 0:i_], in_=ident[0:64, 64 - i_:64]), reads=["ident", "ShP"], writes=["ShP"])
        cwb = sb("cwb", [64, 4, 1536], BF16)
        S.add("pool", lambda e: e.dma_start(out=cwb[:].rearrange("p a b -> p (a b)"), in_=d_conv_w[0].rearrange("a b -> (a b)").partition_broadcast(64)), writes=["cwb"], is_dma=True)
        dtb = sb("dtb", [64, 4])
        negA = sb("negA", [64, 4])
        gdn_g = sb("gdn_g", [64, 128])
        S.add("sp", lambda e: e.dma_start(out=dtb[:], in_=d_dt_bias[0].partition_broadcast(64)), writes=["dtb"], is_dma=True)
        S.add("sp", lambda e: e.dma_start(out=negA[:], in_=d_A_log[0].partition_broadcast(64)), writes=["negA"], is_dma=True)
        S.add("sp", lambda e: e.dma_start(out=gdn_g[:], in_=d_norm_g[0].partition_broadcast(64)), writes=["gdn_g"], is_dma=True)
        S.add("act", lambda e: e.activation(out=negA[:], in_=negA[:], func=AF.Exp), reads=["negA"], writes=["negA"])
        S.add("dve", lambda e: e.tensor_scalar(out=negA[:], in0=negA[:], scalar1=-1.0, scalar2=None, op0=ALU.mult), reads=["negA"], writes=["negA"])

        xraw = arena[0:64, 0:1536]
        XB = Rot([sb(f"xbf{i}", [64, 1536], BF16) for i in range(2)], "xbf")
        xc = arena[0:64, 1536:3072]
        gtmp = sb("gtmp", [64, 512])
        zdc = sb("zdc", [64, 512])
        ab = sb("ab", [64, 8])
        ssn = sb("ssn", [64, 8])
        gg = sb("gg", [64, 4])
        beta = sb("beta", [64, 4])
        Gcol = sb("Gcol", [64, 4])
        eG = sb("eG", [64, 4])
        dlast = sb("dlast", [64, 4])
        egl = sb("egl", [128, 4])
        D1 = sb("D1", [64, 4, 64])
        D2 = sb("D2", [64, 4, 64])
        Mm = sb("Mm", [64, 4, 64])
        Lm = sb("Lm", [64, 4, 64])
        Xm = sb("Xm", [64, 4, 64])
        Pm = sb("Pm", [64, 4, 64])
        PTm = D2
        gB = Pm
        TTb = sb("TTb", [64, 4, 64], BF16)
        ATb = sb("ATb", [64, 4, 64], BF16)
        knb = sb("knb", [64, 512], BF16)
        kbb = sb("kbb", [64, 512], BF16)
        kbgb = sb("kbgb", [64, 512], BF16)
        kdecb = sb("kdecb", [64, 512], BF16)
        qnb = sb("qnb", [64, 512], BF16)
        qgb = sb("qgb", [64, 512], BF16)
        bvb = sb("bvb", [64, 512], BF16)
        trT = sb("trT", [128, 16, 64], BF16)
        Usb = zcs[0:64, :].rearrange("p (h d) -> p h d", h=4)
        WmT = sb("WmT", [128, 4, 64], BF16)
        dlt = sb("dlt", [64, 4, 128], BF16)
        osb = cout[0:64, :]
        dob = sb("dob", [64, 512], BF16)
        Sst = sb("Sst", [128, 4, 128])
        Sbf = sb("Sbf", [128, 4, 128], BF16)
        GA, GB_, GC = PZ, P2, ACC
        gstate = {"prev": None}

        def lin64(cs, c0, c1):
            return linear(hnT, "hnT", 8, w_in1, "w_in1", c0, c1, msl=slice(cs, cs + 64))

        def gdn_chunk(cs, seq_start, seq_end, sidx, conv_out_idx, init_state_idx, row0):
            xb, xbk = XB.next()
            for c3 in range(3):
                p, pk = lin64(cs, 2048 + c3 * 512, 2560 + c3 * 512)
                S.add("dve", lambda e, p=p, c3=c3: e.tensor_copy(out=xraw[:, c3 * 512:(c3 + 1) * 512], in_=p[0:64, :]), reads=[pk], writes=["xraw"])
                S.add("act", lambda e, p=p, c3=c3, xb=xb: e.copy(out=xb[:, c3 * 512:(c3 + 1) * 512], in_=p[0:64, :]), reads=[pk], writes=[xbk])
            p, pk = lin64(cs, 3584, 4096)
            S.add("act", lambda e, p=p: e.activation(out=zdc[:], in_=p[0:64, :], func=AF.Silu), reads=[pk], writes=["zdc"])
            p, pk = lin64(cs, 4096, 4104)
            S.add("dve", lambda e, p=p: e.tensor_copy(out=ab[:], in_=p[0:64, 0:8]), reads=[pk], writes=["ab"])
            if conv_out_idx is not None:
                S.add("pool", lambda e: e.dma_start(out=o_conv[conv_out_idx], in_=xraw[61:64, :]), reads=["xraw"], writes=["o_conv"], is_dma=True)
            if seq_start:
                if init_state_idx is None:
                    xp, xpk = None, None
                else:
                    xp, xpk = XB.next()
                    S.add("pool", lambda e, xp=xp: e.dma_start(out=xp[61:64, :], in_=state_conv[init_state_idx]), writes=[xpk], is_dma=True)
            else:
                xp, xpk = gstate["prev"]
            gstate["prev"] = (xb, xbk)
            S.add("dve", lambda e: e.tensor_tensor(out=xc, in0=xraw, in1=cwb[:, 3, :], op=ALU.mult), reads=["xraw", "cwb"], writes=["xc"])
            for i_ in range(1, 4):
                for c3 in range(3):
                    cs3 = slice(c3 * 512, (c3 + 1) * 512)
                    p, pk = PS.next()
                    S.add("pe", lambda e, p=p, i_=i_, cs3=cs3, xb=xb: e.matmul(p[0:64, :], lhsT=Sh[:, i_ - 1, :], rhs=xb[:, cs3], start=True, stop=(xp is None)),
                          reads=["Sh", xbk], writes=[pk])
                    if xp is not None:
                        S.add("pe", lambda e, p=p, i_=i_, cs3=cs3, xp=xp: e.matmul(p[0:64, :], lhsT=ShP[:, i_ - 1, :], rhs=xp[:, cs3], start=False, stop=True),
                              reads=["ShP", xpk], writes=[pk])
                    S.add("dve", lambda e, p=p, i_=i_, cs3=cs3: e.tensor_tensor(out=gtmp[:], in0=p[0:64, :], in1=cwb[:, 3 - i_, cs3], op=ALU.mult),
                          reads=[pk, "cwb"], writes=["gtmp"])
                    S.add("pool", lambda e, cs3=cs3: e.tensor_tensor(out=xc[:, cs3], in0=xc[:, cs3], in1=gtmp[:], op=ALU.add), reads=["xc", "gtmp"], writes=["xc"])
            S.add("act", lambda e: e.activation(out=xc, in_=xc, func=AF.Silu), reads=["xc"], writes=["xc"])
            for j in range(8):
                S.add("act", lambda e, j=j: e.activation(out=gtmp[:, 0:128], in_=xc[:, j * 128:(j + 1) * 128], func=AF.Square, accum_out=ssn[:, j:j + 1]),
                      reads=["xc"], writes=["gtmp", "ssn"])
            S.add("dve", lambda e: e.tensor_scalar(out=ssn[:], in0=ssn[:], scalar1=EPS, scalar2=None, op0=ALU.add), reads=["ssn"], writes=["ssn"])
            S.add("pool", lambda e: e.tensor_tensor(out=ssn[:], in0=ssn[:], in1=mhalf[0:64, :].broadcast_to([64, 8]), op=ALU.pow), reads=["ssn", "mhalf"], writes=["ssn"])
            S.add("dve", lambda e: e.tensor_scalar(out=ssn[:, 0:4], in0=ssn[:, 0:4], scalar1=128.0 ** -0.5, scalar2=None, op0=ALU.mult), reads=["ssn"], writes=["ssn"])
            S.add("dve", lambda e: e.tensor_tensor(out=gg[:], in0=ab[:, 0:4], in1=dtb[:], op=ALU.add), reads=["ab", "dtb"], writes=["gg"])
            S.add("act", lambda e: e.activation(out=gg[:], in_=gg[:], func=AF.Exp), reads=["gg"], writes=["gg"])
            S.add("act", lambda e: e.activation(out=gg[:], in_=gg[:], func=AF.Ln, bias=1.0), reads=["gg"], writes=["gg"])
            S.add("dve", lambda e: e.tensor_tensor(out=gg[:], in0=gg[:], in1=negA[:], op=ALU.mult), reads=["gg", "negA"], writes=["gg"])
            S.add("act", lambda e: e.activation(out=beta[:], in_=ab[:, 4:8], func=AF.Sigmoid), reads=["ab"], writes=["beta"])
            S.add("pe", lambda e: e.matmul(GC[0:64, 0:4], lhsT=Uincl[:], rhs=gg[:], start=True, stop=True), reads=["Uincl", "gg"], writes=["acc"])
            S.add("pe", lambda e: e.matmul(GC[:, 8:12], lhsT=ones128[:], rhs=gg[:], start=True, stop=True), reads=["ones128", "gg"], writes=["acc"])
            S.add("dve", lambda e: e.tensor_copy(out=Gcol[:], in_=GC[0:64, 0:4]), reads=["acc"], writes=["Gcol"])
            S.add("act", lambda e: e.activation(out=egl[:], in_=GC[:, 8:12], func=AF.Exp), reads=["acc"], writes=["egl"])
            S.add("dve", lambda e: e.tensor_tensor(out=dlast[:], in0=GC[0:64, 8:12], in1=Gcol[:], op=ALU.subtract), reads=["acc", "Gcol"], writes=["dlast"])
            S.add("act", lambda e: e.activation(out=dlast[:], in_=dlast[:], func=AF.Exp), reads=["dlast"], writes=["dlast"])
            S.add("act", lambda e: e.activation(out=eG[:], in_=Gcol[:], func=AF.Exp), reads=["Gcol"], writes=["eG"])
            S.add("dve", lambda e: e.tensor_copy(out=gB[:], in_=gg[:].unsqueeze(2).broadcast_to([64, 4, 64])), reads=["gg"], writes=["gB"])
            for hh in range(4):
                S.add("pe", lambda e, hh=hh: e.matmul(GA[0:64, hh * 64:(hh + 1) * 64], lhsT=gB[:, hh, :], rhs=Uincl[:], start=True, stop=True),
                      reads=["gB", "Uincl"], writes=["pz"])
            for hh in range(4):
                S.add("dve", lambda e, hh=hh: e.scalar_tensor_tensor(out=D1[:, hh, :], in0=GA[0:64, hh * 64:(hh + 1) * 64], scalar=Gcol[:, hh:hh + 1], in1=nm_incl[:],
                                                                  op0=ALU.subtract, op1=ALU.add), reads=["pz", "Gcol", "nm_incl"], writes=["D1"])
                S.add("dve", lambda e, hh=hh: e.tensor_scalar(out=D2[:, hh, :], in0=GA[0:64, hh * 64:(hh + 1) * 64], scalar1=Gcol[:, hh:hh + 1], scalar2=-1.0,
                                                           op0=ALU.subtract, op1=ALU.mult), reads=["pz", "Gcol"], writes=["D2"])
            S.add("dve", lambda e: e.tensor_tensor(out=D2[:], in0=D2[:], in1=nm_low[:].unsqueeze(1).broadcast_to([64, 4, 64]), op=ALU.add), reads=["D2", "nm_low"], writes=["D2"])
            S.add("act", lambda e: e.activation(out=D1[:], in_=D1[:], func=AF.Exp), reads=["D1"], writes=["D1"])
            S.add("act", lambda e: e.activation(out=D2[:], in_=D2[:], func=AF.Exp), reads=["D2"], writes=["D2"])
            xq = xc[:, 0:512].rearrange("p (h d) -> p h d", h=4)
            xk = xc[:, 512:1024].rearrange("p (h d) -> p h d", h=4)
            xv = xc[:, 1024:1536].rearrange("p (h d) -> p h d", h=4)
            def bc(t_, lo):
                return t_[:, lo:lo + 4].unsqueeze(2).broadcast_to([64, 4, 128])
            def v3(t_):
                return t_[:].rearrange("p (h d) -> p h d", h=4)
            S.add("dve", lambda e: e.tensor_tensor(out=v3(qnb), in0=xq, in1=bc(ssn, 0), op=ALU.mult), reads=["xc", "ssn"], writes=["qnb"])
            S.add("dve", lambda e: e.tensor_tensor(out=v3(knb), in0=xk, in1=bc(ssn, 4), op=ALU.mult), reads=["xc", "ssn"], writes=["knb"])
            S.add("dve", lambda e: e.tensor_tensor(out=v3(bvb), in0=xv, in1=bc(beta, 0), op=ALU.mult), reads=["xc", "beta"], writes=["bvb"])
            S.add("pool", lambda e: e.tensor_tensor(out=v3(kbb), in0=v3(knb), in1=bc(beta, 0), op=ALU.mult), reads=["knb", "beta"], writes=["kbb"])
            S.add("pool", lambda e: e.tensor_tensor(out=v3(kbgb), in0=v3(kbb), in1=bc(eG, 0), op=ALU.mult), reads=["kbb", "eG"], writes=["kbgb"])
            S.add("pool", lambda e: e.tensor_tensor(out=v3(kdecb), in0=v3(knb), in1=bc(dlast, 0), op=ALU.mult), reads=["knb", "dlast"], writes=["kdecb"])
            S.add("pool", lambda e: e.tensor_tensor(out=v3(qgb), in0=v3(qnb), in1=bc(eG, 0), op=ALU.mult), reads=["qnb", "eG"], writes=["qgb"])
            for gi, (src, srck) in enumerate(((knb, "knb"), (kbb, "kbb"), (qnb, "qnb"), (qgb, "qgb"))):
                pt, ptk = PT.next()
                for hh in range(4):
                    S.add("pe", lambda e, hh=hh, src=src, pt=pt: e.transpose(out=pt[:, hh, 0:64], in_=src[:, hh * 128:(hh + 1) * 128], identity=ident[0:64, 0:64]),
                          reads=[srck, "ident"], writes=[ptk])
                S.add("act", lambda e, pt=pt, gi=gi: e.copy(out=trT[:, gi * 4:(gi + 1) * 4, :], in_=pt[:, 0:4, 0:64]), reads=[ptk], writes=["trT"])
            knT = lambda hh: trT[:, hh, :]
            kbT = lambda hh: trT[:, 4 + hh, :]
            qnT = lambda hh: trT[:, 8 + hh, :]
            qgT = lambda hh: trT[:, 12 + hh, :]
            for hh in range(4):
                S.add("pe", lambda e, hh=hh: e.matmul(GA[0:64, hh * 64:(hh + 1) * 64], lhsT=knT(hh), rhs=kbT(hh), start=True, stop=True), reads=["trT"], writes=["pz"])
                S.add("pe", lambda e, hh=hh: e.matmul(GA[0:64, 256 + hh * 64:256 + (hh + 1) * 64], lhsT=kbT(hh), rhs=knT(hh), start=True, stop=True), reads=["trT"], writes=["pz"])
                S.add("pe", lambda e, hh=hh: e.matmul(GA[0:64, 512 + hh * 64:512 + (hh + 1) * 64], lhsT=knT(hh), rhs=qnT(hh), start=True, stop=True), reads=["trT"], writes=["pz"])
            GA3 = lambda o_: GA[0:64, o_:o_ + 256].rearrange("p (h t) -> p h t", h=4)
            S.add("dve", lambda e: e.tensor_tensor(out=ATb[:], in0=GA3(512), in1=D1[:], op=ALU.mult), reads=["pz", "D1"], writes=["ATb"])
            S.add("dve", lambda e: e.tensor_tensor(out=Mm[:], in0=GA3(0), in1=D1[:], op=ALU.mult), reads=["pz", "D1"], writes=["Mm"])
            S.add("dve", lambda e: e.tensor_tensor(out=Mm[:], in0=Mm[:], in1=mask01[0:64, 0:64].unsqueeze(1).broadcast_to([64, 4, 64]), op=ALU.mult), reads=["Mm", "mask01"], writes=["Mm"])
            S.add("dve", lambda e: e.tensor_tensor(out=Lm[:], in0=GA3(256), in1=D2[:], op=ALU.mult), reads=["pz", "D2"], writes=["Lm"])
            S.add("dve", lambda e: e.tensor_tensor(out=Xm[:], in0=identf[:].unsqueeze(1).broadcast_to([64, 4, 64]), in1=Mm[:], op=ALU.subtract), reads=["identf", "Mm"], writes=["Xm"])
            for hh in range(4):
                S.add("pe", lambda e, hh=hh: e.matmul(GB_[0:64, hh * 64:(hh + 1) * 64], lhsT=Lm[:, hh, :], rhs=Mm[:, hh, :], start=True, stop=True), reads=["Lm", "Mm"], writes=["p2"])
                S.add("pe", lambda e, hh=hh: e.matmul(GB_[0:64, 256 + hh * 64:256 + (hh + 1) * 64], lhsT=Mm[:, hh, :], rhs=Lm[:, hh, :], start=True, stop=True), reads=["Lm", "Mm"], writes=["p2"])
            GB3 = lambda o_: GB_[0:64, o_:o_ + 256].rearrange("p (h t) -> p h t", h=4)
            S.add("dve", lambda e: e.tensor_copy(out=Pm[:], in_=GB3(0)), reads=["p2"], writes=["Pm"])
            S.add("act", lambda e: e.copy(out=PTm[:], in_=GB3(256)), reads=["p2"], writes=["PTm"])
            for lvl in range(5):
                for hh in range(4):
                    S.add("pe", lambda e, hh=hh: e.matmul(GB_[0:64, 512 + hh * 64:512 + (hh + 1) * 64], lhsT=PTm[:, hh, :], rhs=Xm[:, hh, :], start=True, stop=True), reads=["PTm", "Xm"], writes=["p2"])
                    if lvl < 4:
                        S.add("pe", lambda e, hh=hh: e.matmul(GB_[0:64, hh * 64:(hh + 1) * 64], lhsT=PTm[:, hh, :], rhs=Pm[:, hh, :], start=True, stop=True), reads=["PTm", "Pm"], writes=["p2"])
                        S.add("pe", lambda e, hh=hh: e.matmul(GB_[0:64, 256 + hh * 64:256 + (hh + 1) * 64], lhsT=Pm[:, hh, :], rhs=PTm[:, hh, :], start=True, stop=True), reads=["PTm", "Pm"], writes=["p2"])
                S.add("dve", lambda e: e.tensor_tensor(out=Xm[:], in0=Xm[:], in1=GB3(512), op=ALU.add), reads=["Xm", "p2"], writes=["Xm"])
                if lvl < 4:
                    S.add("dve", lambda e: e.tensor_copy(out=Pm[:], in_=GB3(0)), reads=["p2"], writes=["Pm"])
                    S.add("act", lambda e: e.copy(out=PTm[:], in_=GB3(256)), reads=["p2"], writes=["PTm"])
            S.add("act", lambda e: e.copy(out=TTb[:], in_=Xm[:]), reads=["Xm"], writes=["TTb"])
            for hh in range(4):
                S.add("pe", lambda e, hh=hh: e.matmul(GA[0:64, hh * 128:(hh + 1) * 128], lhsT=TTb[:, hh, :], rhs=bvb[:, hh * 128:(hh + 1) * 128], start=True, stop=True),
                      reads=["TTb", "bvb"], writes=["pz"])
                S.add("pe", lambda e, hh=hh: e.matmul(GB_[:, hh * 64:(hh + 1) * 64], lhsT=kbgb[:, hh * 128:(hh + 1) * 128], rhs=TTb[:, hh, :], start=True, stop=True),
                      reads=["TTb", "kbgb"], writes=["p2"])
            S.add("dve", lambda e: e.tensor_copy(out=Usb, in_=GA[0:64, 0:512].rearrange("p (h d) -> p h d", h=4)), reads=["pz"], writes=["Usb"])
            S.add("act", lambda e: e.copy(out=WmT[:], in_=GB_[:, 0:256].rearrange("p (h t) -> p h t", h=4)), reads=["p2"], writes=["WmT"])
            if seq_start:
                if init_state_idx is None:
                    S.add("pool", lambda e: e.memset(Sst[:], 0.0), writes=["Sst"])
                else:
                    S.add("sp", lambda e: e.dma_start(out=Sst[:], in_=state_d[init_state_idx].rearrange("h d e -> d h e")), writes=["Sst"], is_dma=True)
                S.add("act", lambda e: e.copy(out=Sbf[:], in_=Sst[:]), reads=["Sst"], writes=["Sbf"])
            for hh in range(4):
                S.add("pe", lambda e, hh=hh: e.matmul(GA[0:64, hh * 128:(hh + 1) * 128], lhsT=WmT[:, hh, :], rhs=Sbf[:, hh, :], start=True, stop=True), reads=["WmT", "Sbf"], writes=["pz"])
            S.add("dve", lambda e: e.tensor_tensor(out=dlt[:], in0=Usb, in1=GA[0:64, 0:512].rearrange("p (h d) -> p h d", h=4), op=ALU.subtract), reads=["Usb", "pz"], writes=["dlt"])
            for hh in range(4):
                S.add("pe", lambda e, hh=hh: e.matmul(GB_[0:64, hh * 128:(hh + 1) * 128], lhsT=qgT(hh), rhs=Sbf[:, hh, :], start=True, stop=False), reads=["trT", "Sbf"], writes=["p2"])
                S.add("pe", lambda e, hh=hh: e.matmul(GB_[0:64, hh * 128:(hh + 1) * 128], lhsT=ATb[:, hh, :], rhs=dlt[:, hh, :], start=False, stop=True), reads=["ATb", "dlt"], writes=["p2"])
                S.add("pe", lambda e, hh=hh: e.matmul(GC[:, hh * 128:(hh + 1) * 128], lhsT=kdecb[:, hh * 128:(hh + 1) * 128], rhs=dlt[:, hh, :], start=True, stop=True), reads=["kdecb", "dlt"], writes=["acc"])
            S.add("dve", lambda e: e.tensor_copy(out=osb, in_=GB_[0:64, 0:512]), reads=["p2"], writes=["osb"])
            for hh in range(4):
                S.add("dve", lambda e, hh=hh: e.scalar_tensor_tensor(out=Sst[:, hh, :], in0=Sst[:, hh, :], scalar=egl[:, hh:hh + 1], in1=GC[:, hh * 128:(hh + 1) * 128],
                                                                  op0=ALU.mult, op1=ALU.add), reads=["Sst", "egl", "acc"], writes=["Sst"])
            S.add("act", lambda e: e.copy(out=Sbf[:], in_=Sst[:]), reads=["Sst"], writes=["Sbf"])
            if seq_end:
                S.add("pool", lambda e: e.dma_start(out=o_sd[sidx].rearrange("h d e -> d h e"), in_=Sst[:]), reads=["Sst"], writes=["o_sd"], is_dma=True)
            for hh in range(4):
                S.add("act", lambda e, hh=hh: e.activation(out=gtmp[:, 0:128], in_=osb[:, hh * 128:(hh + 1) * 128], func=AF.Square, accum_out=ssn[:, hh:hh + 1]),
                      reads=["osb"], writes=["gtmp", "ssn"])
            S.add("dve", lambda e: e.tensor_scalar(out=ssn[:, 0:4], in0=ssn[:, 0:4], scalar1=1.0 / 128, scalar2=EPS, op0=ALU.mult, op1=ALU.add), reads=["ssn"], writes=["ssn"])
            S.add("pool", lambda e: e.tensor_tensor(out=ssn[:, 0:4], in0=ssn[:, 0:4], in1=mhalf[0:64, :].broadcast_to([64, 4]), op=ALU.pow), reads=["ssn", "mhalf"], writes=["ssn"])
            S.add("dve", lambda e: e.tensor_tensor(out=v3(osb), in0=v3(osb), in1=bc(ssn, 0), op=ALU.mult), reads=["osb", "ssn"], writes=["osb"])
            S.add("dve", lambda e: e.tensor_tensor(out=v3(osb), in0=v3(osb), in1=gdn_g[:].unsqueeze(1).broadcast_to([64, 4, 128]), op=ALU.mult), reads=["osb", "gdn_g"], writes=["osb"])
            S.add("dve", lambda e: e.tensor_tensor(out=dob[:], in0=osb, in1=zdc[:], op=ALU.mult), reads=["osb", "zdc"], writes=["dob"])
            S.add("pool", lambda e: e.dma_start(out=mixin[row0:row0 + 64, 512:1024], in_=dob[:]), reads=["dob"], writes=["mixin"], is_dma=True)

        def gdn_tile(i, samp, h, hk):
            if samp:
                for s_ in range(2):
                    gdn_chunk(s_ * 64, True, True, 1 + s_, 1 + s_, s_, s_ * 64)
            else:
                for c_ in range(2):
                    first = (i == 0 and c_ == 0)
                    lastc = (i == ntp - 1 and c_ == 1)
                    gdn_chunk(c_ * 64, first, lastc, 0, 0 if lastc else None, None, c_ * 64)

        def load_tile(i):
            h, hk = H.next()
            p1, p1k = P1.next()
            S.add("sp", lambda e: e.dma_start(out=h[:], in_=h1_scr[i * 128:(i + 1) * 128, :]), reads=["h1_scr"], writes=[hk], is_dma=True)
            S.add("sp", lambda e: e.dma_start(out=p1[:], in_=pin[1, i * 128:(i + 1) * 128, :]), writes=[p1k], is_dma=True)
            return h, hk, p1, p1k

        for i in range(nt):
            h, hk, p1, p1k = load_tile(i)
            samp = (i == ntp)
            r0 = i * 128
            rmsnorm(h, hk, g_l1, "g_l1", xn, "xn")
            transpose_to(xn, "xn", 8, hnT, "hnT")
            p, pk = linear(hnT, "hnT", 8, w_in1, "w_in1", 0, 512)
            S.add("dve", lambda e, p=p: e.tensor_scalar(out=qbf[:], in0=p[:], scalar1=0.125, scalar2=None, op0=ALU.mult), reads=[pk], writes=["qbf"])
            p, pk = linear(hnT, "hnT", 8, w_in1, "w_in1", 512, 1024)
            S.add("act", lambda e, p=p: e.copy(out=kf, in_=p[:]), reads=[pk], writes=["kf"])
            S.add("dve", lambda e, p=p: e.tensor_copy(out=kbf[:], in_=p[:]), reads=[pk], writes=["kbf"])
            S.add("pool", lambda e, r0=r0: e.dma_start(out=o_kc[r0:r0 + 128, :], in_=kf), reads=["kf"], writes=["o_kc"], is_dma=True)
            p, pk = linear(hnT, "hnT", 8, w_in1, "w_in1", 1024, 1536)
            S.add("act", lambda e, p=p: e.copy(out=vf, in_=p[:]), reads=[pk], writes=["vf"])
            S.add("dve", lambda e, p=p: e.tensor_copy(out=vbf[:], in_=p[:]), reads=[pk], writes=["vbf"])
            S.add("pool", lambda e, r0=r0: e.dma_start(out=o_vc[r0:r0 + 128, :], in_=vf), reads=["vf"], writes=["o_vc"], is_dma=True)
            p, pk = linear(hnT, "hnT", 8, w_in1, "w_in1", 1536, 2048)
            S.add("act", lambda e, p=p: e.activation(out=zcs[:], in_=p[:], func=AF.Silu), reads=[pk], writes=["zcs"])
            if samp:
                p, pk = linear(hnT, "hnT", 8, w_in1, "w_in1", 1024, 1536, msl=slice(64, 128))
                S.add("dve", lambda e, p=p: e.tensor_copy(out=vbs[:], in_=p[0:64, :]), reads=[pk], writes=["vbs"])
            pt, ptk = PT.next()
            for k in range(4):
                S.add("pe", lambda e, k=k, pt=pt: e.transpose(out=pt[:, k, :], in_=qbf[:, k * 128:(k + 1) * 128], identity=ident[:]),
                      reads=["qbf", "ident"], writes=[ptk])
            QTv = QT[:].rearrange("p (a two) t -> p a two t", two=2)
            S.add("act", lambda e, pt=pt, QTv=QTv: e.copy(out=QTv[0:64, :, 0, :], in_=pt[0:64, 0:4, :]), reads=[ptk], writes=["QT"])
            S.add("act", lambda e, pt=pt, QTv=QTv: e.copy(out=QTv[64:128, :, 1, :], in_=pt[64:128, 0:4, :]), reads=[ptk], writes=["QT"])
            transpose_to(kbf, "kbf", 4, KTc, "KTc")
            if not samp:
                S.add("pool", lambda e, r0=r0: e.dma_start(out=kts[r0 // 128], in_=KTc[:].rearrange("p a b -> p (a b)")), reads=["KTc"], writes=[f"kts{i}"], is_dma=True)
                S.add("pool", lambda e, r0=r0: e.dma_start(out=vs[r0:r0 + 128, :], in_=vbf[:]), reads=["vbf"], writes=[f"vs{i}"], is_dma=True)
            S.add("pool", lambda e: e.memset(cum, 0.0), writes=["cum"])
            if not samp:
                def prompt_steps(i=i):
                    yield (lambda h: KTc[:, h // 2, :], "KTc", vbf, "vbf", 0, 128, 128, True, True, i == 0)
                    for kb in range(i - 1, -1, -1):
                        ktb, ktbk = KB.next()
                        vb, vbk = VB.next()
                        S.add("sp", lambda e, ktb=ktb, kb=kb: e.dma_start(out=ktb[:].rearrange("p a b -> p (a b)"), in_=kts[kb]), reads=[f"kts{kb}"], writes=[ktbk], is_dma=True)
                        S.add("sp", lambda e, vb=vb, kb=kb: e.dma_start(out=vb[:], in_=vs[kb * 128:(kb + 1) * 128, :]), reads=[f"vs{kb}"], writes=[vbk], is_dma=True)
                        yield (lambda h, ktb=ktb: ktb[:, h // 2, :], ktbk, vb, vbk, 0, 128, 128, False, False, kb == 0)
                attn_run(prompt_steps())
                S.add("dve", lambda e: e.tensor_tensor(out=cout[:], in0=ACC[:], in1=zcs[:], op=ALU.mult), reads=["acc", "zcs"], writes=["cout"])
            else:
                for s_ in range(2):
                    if s_ == 1:
                        S.add("pool", lambda e: e.memset(cum, 0.0), writes=["cum"])
                    vnew, vnk = (vbf, "vbf") if s_ == 0 else (vbs, "vbs")
                    def samp_steps(s_=s_, vnew=vnew, vnk=vnk):
                        yield (lambda h: KTc[:, h // 2, s_ * 64:(s_ + 1) * 64], "KTc", vnew, vnk, s_ * 64, 64, 64, True, True, False)
                        for kb in range(31, -1, -1):
                            ck, ckk = CK.next()
                            vb, vbk = VB.next()
                            ktb, ktbk = KB.next()
                            S.add("pool", lambda e, ck=ck, kb=kb: e.dma_start(out=ck[:], in_=cache_k[s_, kb * 128:(kb + 1) * 128, :]), writes=[ckk], is_dma=True)
                            S.add("pool", lambda e, vb=vb, kb=kb: e.dma_start(out=vb[:], in_=cache_v[s_, kb * 128:(kb + 1) * 128, :]), writes=[vbk], is_dma=True)
                            transpose_to(ck, ckk, 4, ktb, ktbk)
                            yield (lambda h, ktb=ktb: ktb[:, h // 2, :], ktbk, vb, vbk, s_ * 64, 64, 128, False, False, kb == 0)
                    attn_run(samp_steps())
                    if s_ == 0:
                        S.add("dve", lambda e: e.tensor_tensor(out=cout[0:64, :], in0=ACC[0:64, :], in1=zcs[0:64, :], op=ALU.mult), reads=["acc", "zcs"], writes=["cout"])
                    else:
                        S.add("dve", lambda e: e.tensor_copy(out=ebuf[0:64, 0:512], in_=ACC[0:64, :]), reads=["acc"], writes=["ebuf"])
                        S.add("pool", lambda e: e.dma_start(out=cout[64:128, :], in_=ebuf[0:64, 0:512]), reads=["ebuf"], writes=["cout"], is_dma=True)
                        S.add("dve", lambda e: e.tensor_tensor(out=cout[64:128, :], in0=cout[64:128, :], in1=zcs[64:128, :], op=ALU.mult), reads=["cout", "zcs"], writes=["cout"])
            if debug_h:
                S.add("pool", lambda e, r0=r0: e.dma_start(out=dbg2[r0:r0 + 128, :], in_=cout[:]), reads=["cout"], writes=["dbg2"], is_dma=True)
            S.add("act", lambda e: e.copy(out=mixin[:, 0:512], in_=cout[:]), reads=["cout"], writes=["mixin"])
            gdn_tile(i, samp, h, hk)
            transpose_to(mixin, "mixin", 8, mixT, "hnT")
            for hh in range(2):
                p, pk = linear(mixT, "hnT", 8, w_out1, "w_out1", hh * 512, (hh + 1) * 512)
                S.add("dve", lambda e, p=p, hh=hh, h=h: e.tensor_tensor(out=h[:, hh * 512:(hh + 1) * 512], in0=h[:, hh * 512:(hh + 1) * 512], in1=p[:], op=ALU.add),
                      reads=[pk, hk], writes=[hk])
            rmsnorm(h, hk, g_p1, "g_p1", xn, "xn")
            transpose_to(xn, "xn", 8, hnT, "hnT")
            S.add("act", lambda e, p1=p1: e.copy(out=pbf[:], in_=p1[:]), reads=[p1k], writes=["pbf"])
            transpose_to(pbf, "pbf", 2, pT, "pT")
            for hh in range(2):
                p, pk = linear(hnT, "hnT", 8, w_gate1, "w_gate1", hh * 512, (hh + 1) * 512)
                S.add("act", lambda e, p=p, hh=hh: e.activation(out=gate[:, hh * 512:(hh + 1) * 512], in_=p[:], func=AF.Sigmoid), reads=[pk], writes=["gate"])
                p, pk = linear(pT, "pT", 2, w_pp1, "w_pp1", hh * 512, (hh + 1) * 512)
                S.add("dve", lambda e, p=p, hh=hh: e.tensor_tensor(out=gate[:, hh * 512:(hh + 1) * 512], in0=gate[:, hh * 512:(hh + 1) * 512], in1=p[:], op=ALU.mult),
                      reads=[pk, "gate"], writes=["gate"])
            S.add("dve", lambda e, h=h: e.tensor_tensor(out=h[:], in0=h[:], in1=gate[:], op=ALU.add), reads=[hk, "gate"], writes=[hk])
            rmsnorm(h, hk, g_f, "g_f", yout, "ebuf")
            S.add("pool", lambda e, r0=r0: e.dma_start(out=o_y[r0:r0 + 128, :], in_=yout), reads=["ebuf"], writes=["o_y"], is_dma=True)
        S.emit()
```

```python
import contextlib
import math
import numpy as np
import concourse.bass as bass
import concourse.mybir as mybir
from concourse.bass_utils import run_bass_kernel_spmd

F32 = mybir.dt.float32
BF16 = mybir.dt.bfloat16
I32 = mybir.dt.int32
AF = mybir.ActivationFunctionType
ALU = mybir.AluOpType

NCORES = 8
D = 1024
SEQ = 8192
NTP = SEQ // 128
EPS = 1e-6
EPOCH = 3000
DMA_EPOCH = 200
DMA_SLOTS = 4
TWO_PI = 2.0 * math.pi


class Op:
    __slots__ = ("eng", "fn", "waits", "need_inc", "is_dma", "slot", "slot_val", "inc_no")

    def __init__(self, eng, fn, is_dma):
        self.eng = eng
        self.fn = fn
        self.waits = []
        self.need_inc = False
        self.is_dma = is_dma
        self.slot = None
        self.slot_val = None
        self.inc_no = None


class Sched:
    ENGS = ("pe", "act", "dve", "pool", "sp")

    def __init__(self, nc):
        self.nc = nc
        self.ops = {e: [] for e in self.ENGS}
        self.last_w = {}
        self.readers = {}

    max_ops = None
    n_added = 0
    ALIAS = {'xraw': ('e0_0', 'e0_1', 'e1_0', 'e1_1', 'lm0_0', 'lm0_1'), 'xc': ('lm1_0', 'lm1_1', 'cum0', 'cum1', 'wb_0', 'wb_1'), 'pz': ('pz0', 'pz1'), 'p2': ('p20', 'p21'), 'ebuf': ('e0_0', 'e0_1', 'e1_0', 'e1_1'), 'cum': ('cum0', 'cum1'), 'kf': ('gate',), 'vf': ('gate',), 'Usb': ('zcs',), 'osb': ('cout',), 'gB': ('Pm',), 'PTm': ('D2',)}

    def add(self, eng, fn, reads=(), writes=(), is_dma=False):
        op = Op(eng, fn, is_dma)
        Sched.n_added += 1
        if Sched.max_ops is not None and Sched.n_added > Sched.max_ops:
            return op
        reads = [a for k in reads for a in Sched.ALIAS.get(k, (k,))]
        writes = [a for k in writes for a in Sched.ALIAS.get(k, (k,))]
        writes = list(writes) + [k for k in reads if k.startswith(("ps", "pt", "qs", "qt", "pz", "p2", "acc"))]
        deps = []
        for k in reads:
            w = self.last_w.get(k)
            if w is not None:
                deps.append(w)
        for k in writes:
            w = self.last_w.get(k)
            if w is not None:
                deps.append(w)
            deps.extend(self.readers.get(k, ()))
        seen = set()
        for d in deps:
            if d is op or id(d) in seen:
                continue
            seen.add(id(d))
            if (not d.is_dma) and (not is_dma) and d.eng == eng == "pe":
                continue
            op.waits.append(d)
            d.need_inc = True
        for k in reads:
            self.readers.setdefault(k, []).append(op)
        for k in writes:
            self.last_w[k] = op
            self.readers[k] = []
        self.ops[eng].append(op)
        return op

    def emit(self):
        nc = self.nc
        n_epochs = {}
        n_dma = {}
        for e in self.ENGS:
            c = 0
            for op in self.ops[e]:
                if (not op.is_dma) and op.need_inc:
                    op.inc_no = c
                    c += 1
            n_epochs[e] = max(1, (c + EPOCH - 1) // EPOCH)
            j = 0
            for op in self.ops[e]:
                if op.is_dma:
                    u = j // DMA_SLOTS
                    op.slot = (u // DMA_EPOCH) * DMA_SLOTS + j % DMA_SLOTS
                    op.slot_val = 16 * (u % DMA_EPOCH + 1)
                    j += 1
            n_dma[e] = j
        with contextlib.ExitStack() as st:
            sems = {e: [st.enter_context(nc.semaphore(f"s_{e}_{i}")) for i in range(n_epochs[e])]
                    for e in self.ENGS}
            dsems = {e: [st.enter_context(nc.semaphore(f"d_{e}_{i}"))
                         for i in range(DMA_SLOTS * ((n_dma[e] // DMA_SLOTS) // DMA_EPOCH + 1))]
                     for e in self.ENGS if n_dma[e] > 0}
            block = st.enter_context(nc.Block())

            def target(d):
                if d.is_dma:
                    return dsems[d.eng][d.slot], d.slot_val
                return sems[d.eng][d.inc_no // EPOCH], d.inc_no % EPOCH + 1

            def run(e, eng):
                waited = {}
                last_on_slot = {}
                for op in self.ops[e]:
                    ws = list(op.waits)
                    if op.is_dma and (op.slot % DMA_SLOTS) in last_on_slot:
                        ws.append(last_on_slot[op.slot % DMA_SLOTS])
                    for d in ws:
                        sem, val = target(d)
                        if waited.get(sem.num, 0) >= val:
                            continue
                        waited[sem.num] = val
                        eng.wait_ge(sem, val)
                    ins = op.fn(eng)
                    if op.is_dma:
                        ins.then_inc(dsems[e][op.slot], 16)
                        last_on_slot[op.slot % DMA_SLOTS] = op
                    elif op.need_inc:
                        ins.then_inc(sems[e][op.inc_no // EPOCH], 1)
                for d in last_on_slot.values():
                    sem, val = target(d)
                    eng.wait_ge(sem, val)

            block.tensor(lambda eng: run("pe", eng))
            block.scalar(lambda eng: run("act", eng))
            block.vector(lambda eng: run("dve", eng))
            block.gpsimd(lambda eng: run("pool", eng))
            block.sync(lambda eng: run("sp", eng))


class Rot:
    def __init__(self, bufs, name):
        self.bufs = bufs
        self.name = name
        self.i = 0

    def next(self):
        j = self.i % len(self.bufs)
        self.i += 1
        return self.bufs[j], f"{self.name}{j}"


def build_program(ntp=NTP, debug_h=False):
    nt = ntp + 1
    ntok = nt * 128
    nc = bass.Bass("TRN2", target_bir_lowering=False)

    def din(name, shape):
        return nc.dram_tensor(name, list(shape), F32, kind="ExternalInput").ap()

    def dout(name, shape):
        return nc.dram_tensor(name, list(shape), F32, kind="ExternalOutput").ap()

    xin = din("xin", [ntok, D])
    pin = din("pin", [2, ntok, 256])
    sbre = din("sbre", [2, 32, 64])
    sbim = din("sbim", [2, 32, 64])
    norm_g = din("norm_g", [2, D])
    final_norm_g = din("final_norm_g", [D])
    ple_proj = din("ple_proj", [2, 256, D])
    ple_gate_w = din("ple_gate_w", [2, D, D])
    ple_norm_g = din("ple_norm_g", [2, D])
    even_w_in = din("even_w_in", [1, D, 2560])
    even_w_out = din("even_w_out", [1, 1024, D])
    a_ln_g = din("a_ln_g", [1, 512])
    a_ln_b = din("a_ln_b", [1, 512])
    a_w_s = din("a_w_s", [1, 4, 128, 128])
    a_b_s = din("a_b_s", [1, 4, 128])
    b_lam_re = din("b_lam_re", [1, 32, 64])
    b_lam_im = din("b_lam_im", [1, 32, 64])
    b_log_dt = din("b_log_dt", [1, 32])
    b_B_re = din("b_B_re", [1, 32, 64, 16])
    b_B_im = din("b_B_im", [1, 32, 64, 16])
    b_C_re = din("b_C_re", [1, 32, 16, 64])
    b_C_im = din("b_C_im", [1, 32, 16, 64])
    b_D = din("b_D", [1, 32, 16])
    b_glu_w = din("b_glu_w", [1, 512, 512])
    odd_w_in = din("odd_w_in", [1, D, 4104])
    odd_w_out = din("odd_w_out", [1, 1024, D])
    cache_k = din("cache_k", [2, 4096, 512])
    cache_v = din("cache_v", [2, 4096, 512])
    state_d = din("state_d", [2, 4, 128, 128])
    state_conv = din("state_conv", [2, 3, 1536])
    d_conv_w = din("d_conv_w", [1, 4, 1536])
    d_A_log = din("d_A_log", [1, 4])
    d_dt_bias = din("d_dt_bias", [1, 4])
    d_norm_g = din("d_norm_g", [1, 128])

    o_y = dout("o_y", [ntok, D])
    o_bre = dout("o_bre", [3, 32, 64])
    o_bim = dout("o_bim", [3, 32, 64])
    o_av = dout("o_av", [128, 512])
    o_kc = dout("o_kc", [ntok, 512])
    o_vc = dout("o_vc", [ntok, 512])
    o_conv = dout("o_conv", [3, 3, 1536])
    o_sd = dout("o_sd", [3, 4, 128, 128])
    h1_scr = nc.dram_tensor("h1_scr", [ntok, D], F32, kind="ExternalOutput" if debug_h else "Internal").ap()

    dbg = nc.dram_tensor("dbg", [128, 512], F32, kind="ExternalOutput").ap() if debug_h else None
    S = Sched(nc)
    st = contextlib.ExitStack()
    with st:
        def sb(name, shape, dt=F32):
            return st.enter_context(nc.sbuf_tensor(name, list(shape), dt))

        def ps(name, shape, dt=F32):
            return st.enter_context(nc.psum_tensor(name, list(shape), dt))

        st.enter_context(nc.allow_non_contiguous_dma("small one-time parameter layout loads"))

        PS = Rot([ps(f"ps{i}", [128, 512]) for i in range(5)], "ps")
        py_bank = ps("ps_y", [128, 512])
        PT = Rot([ps(f"pt{i}", [128, 8, 128], BF16) for i in range(2)], "pt")

        H = Rot([sb(f"h{i}", [128, D]) for i in range(2)], "h")
        P0 = Rot([sb(f"p0_{i}", [128, 256]) for i in range(2)], "p0_")
        xn = sb("xn", [128, D], BF16)
        hnT = sb("hnT", [128, 8, 128], BF16)
        ssq = sb("ssq", [128, 1])
        rstd = sb("rstd", [128, 1])
        ua = sb("ua", [128, 512])
        va = sb("va", [128, 512])
        vln = sb("vln", [128, 512])
        vbf = sb("vbf", [128, 512], BF16)
        za = sb("za", [128, 512])
        zb = sb("zb", [128, 512])
        ubf = sb("ubf", [128, 512], BF16)
        ubd = sb("ubd", [128, 512])
        ubT = sb("ubT", [128, 4, 128], BF16)
        bst = sb("bst", [128, 6])
        bag = sb("bag", [128, 2])
        mixin = sb("mixin", [128, 1024], BF16)
        mixT = sb("mixT", [128, 8, 128], BF16)
        yb = sb("yb", [128, 512])
        yg = sb("yg", [128, 512])
        ygb = sb("ygb", [128, 512], BF16)
        ygT = sb("ygT", [128, 4, 128], BF16)
        gate = sb("gate", [128, D])
        pbf = sb("pbf", [128, 256], BF16)
        pT = sb("pT", [128, 2, 128], BF16)
        W5 = {n: sb("w5" + n, [128, 512]) for n in ("T1", "T2", "T3", "T4", "vr", "vi", "zr", "zi")}
        xrb = sb("xrb", [128, 4, 128], BF16)
        xib = sb("xib", [128, 4, 128], BF16)

        ident = sb("ident", [128, 128], BF16)
        S.add("pool", lambda e: e.memset(ident[:], 0.0), writes=["ident"])
        S.add("pool", lambda e: e.affine_select(out=ident[:], in_=ident[:], compare_op=ALU.not_equal, fill=1.0,
                                               base=0, pattern=[[-1, 128]], channel_multiplier=1),
              reads=["ident"], writes=["ident"])
        mhalf = sb("mhalf", [128, 16])
        S.add("pool", lambda e: e.memset(mhalf[:], -0.5), writes=["mhalf"])

        def bcast_load(name, src, n):
            t = sb(name, [128, n])
            S.add("sp", lambda e: e.dma_start(out=t[:], in_=src.partition_broadcast(128)), writes=[name], is_dma=True)
            return t

        g_l0 = bcast_load("g_l0", norm_g[0], D)
        g_p0 = bcast_load("g_p0", ple_norm_g[0], D)
        lng = bcast_load("lng", a_ln_g[0], 512)
        lnb = bcast_load("lnb", a_ln_b[0], 512)
        Db = bcast_load("Db", b_D[0].rearrange("g p -> (g p)"), 512)

        def wload(name, src, kt, n, csz=512):
            t = sb(name, [128, kt, n], BF16)
            v = src.rearrange("(k p) n -> p k n", p=128)
            for c0 in range(0, n, csz):
                c1 = min(n, c0 + csz)
                S.add("pool", lambda e, c0=c0, c1=c1: e.dma_start(out=t[:, :, c0:c1], in_=v[:, :, c0:c1]),
                      writes=[name], is_dma=True)
            return t

        w_in0 = wload("w_in0", even_w_in[0], 8, 2560)
        w_out0 = wload("w_out0", even_w_out[0], 8, 1024)
        w_glu = wload("w_glu", b_glu_w[0], 4, 512)
        w_gate0 = wload("w_gate0", ple_gate_w[0], 8, 1024)
        w_pp0 = wload("w_pp0", ple_proj[0], 2, 1024)

        wl = gate[:, 0:512].rearrange("p (a b) -> p a b", a=4)
        wls = gate[:, 512:1024].rearrange("p (a b) -> p a b", a=4)
        S.add("sp", lambda e: e.dma_start(out=wl, in_=a_w_s[0].rearrange("g t s -> t g s")), writes=["gate"], is_dma=True)
        S.add("pool", lambda e: e.memset(wls, 0.0), writes=["gate"])
        S.add("sp", lambda e: e.dma_start(out=wls[0:64, :, 0:64], in_=a_w_s[0, :, 0:64, 0:64].rearrange("g t s -> t g s")),
              reads=["gate"], writes=["gate"], is_dma=True)
        S.add("sp", lambda e: e.dma_start(out=wls[64:128, :, 64:128], in_=a_w_s[0, :, 0:64, 0:64].rearrange("g t s -> t g s")),
              reads=["gate"], writes=["gate"], is_dma=True)
        wmixT = sb("wmixT", [128, 4, 128], BF16)
        wmixTs = sb("wmixTs", [128, 4, 128], BF16)
        for (src, dst, nm) in ((wl, wmixT, "wl"), (wls, wmixTs, "wls")):
            wbf = (xn[:, 0:512] if nm == "wl" else xn[:, 512:1024]).rearrange("p (a b) -> p a b", a=4)
            S.add("pool", lambda e, src=src: e.affine_select(out=src, in_=src, compare_op=ALU.is_ge, fill=0.0, base=0,
                                                            pattern=[[0, 4], [-1, 128]], channel_multiplier=1),
                  reads=["gate"], writes=["gate"])
            S.add("dve", lambda e, src=src, wbf=wbf: e.tensor_copy(out=wbf, in_=src), reads=["gate"], writes=["xn"])
            pt, ptk = PT.next()
            for g in range(4):
                S.add("pe", lambda e, g=g, wbf=wbf, pt=pt: e.transpose(out=pt[:, g, :], in_=wbf[:, g, :], identity=ident[:]),
                      reads=["xn", "ident"], writes=[ptk])
            S.add("dve", lambda e, dst=dst, pt=pt: e.tensor_copy(out=dst[:], in_=pt[:, 0:4, :]), reads=[ptk], writes=[nm + "T"])
        bsb = sb("bsb", [128, 4])
        bsbs = sb("bsbs", [128, 4])
        S.add("sp", lambda e: e.dma_start(out=bsb[:], in_=a_b_s[0].rearrange("g t -> t g")), writes=["bsb"], is_dma=True)
        S.add("sp", lambda e: e.dma_start(out=bsbs[0:64, :], in_=a_b_s[0, :, 0:64].rearrange("g t -> t g")), writes=["bsbs"], is_dma=True)
        S.add("sp", lambda e: e.dma_start(out=bsbs[64:128, :], in_=a_b_s[0, :, 0:64].rearrange("g t -> t g")), writes=["bsbs"], is_dma=True)

        lamr = sb("lamr", [128, 16])
        lami = sb("lami", [128, 16])
        ldt = sb("ldt", [128, 16])
        S.add("sp", lambda e: e.dma_start(out=lamr[:], in_=b_lam_re[0].rearrange("(k g) n -> (g n) k", g=2)), writes=["lamr"], is_dma=True)
        S.add("sp", lambda e: e.dma_start(out=lami[:], in_=b_lam_im[0].rearrange("(k g) n -> (g n) k", g=2)), writes=["lami"], is_dma=True)
        ldv = b_log_dt[0].rearrange("(k g) -> g k", g=2)
        S.add("sp", lambda e: e.dma_start(out=ldt[0:64, :], in_=ldv[0].partition_broadcast(64)), writes=["ldt"], is_dma=True)
        S.add("sp", lambda e: e.dma_start(out=ldt[64:128, :], in_=ldv[1].partition_broadcast(64)), writes=["ldt"], is_dma=True)
        Bl_r = sb("Bl_r", [128, 16, 16])
        Bl_i = sb("Bl_i", [128, 16, 16])
        Cl_r = sb("Cl_r", [128, 16, 16])
        Cl_i = sb("Cl_i", [128, 16, 16])
        S.add("sp", lambda e: e.dma_start(out=Bl_r[:], in_=b_B_re[0].rearrange("(k g) n p -> (g n) k p", g=2)), writes=["Bl_r"], is_dma=True)
        S.add("sp", lambda e: e.dma_start(out=Bl_i[:], in_=b_B_im[0].rearrange("(k g) n p -> (g n) k p", g=2)), writes=["Bl_i"], is_dma=True)
        for (srcd, dstt, nm) in ((b_C_re, Cl_r, "Cl_r"), (b_C_im, Cl_i, "Cl_i")):
            for k in range(16):
                for g2 in range(2):
                    S.add("sp", lambda e, srcd=srcd, dstt=dstt, k=k, g2=g2: e.dma_start(
                        out=dstt[g2 * 64:(g2 + 1) * 64, k, :],
                        in_=srcd[0, 2 * k + g2].rearrange("p n -> n p")),
                        writes=[nm], is_dma=True)

        dts = sb("dts", [128, 16])
        ldr = sb("ldr", [128, 16])
        ldi = sb("ldi", [128, 16])
        rmag = sb("rmag", [128, 16])
        S.add("act", lambda e: e.activation(out=dts[:], in_=ldt[:], func=AF.Exp), reads=["ldt"], writes=["dts"])
        S.add("dve", lambda e: e.tensor_tensor(out=ldr[:], in0=lamr[:], in1=dts[:], op=ALU.mult), reads=["lamr", "dts"], writes=["ldr"])
        S.add("dve", lambda e: e.tensor_tensor(out=ldi[:], in0=lami[:], in1=dts[:], op=ALU.mult), reads=["lami", "dts"], writes=["ldi"])
        S.add("act", lambda e: e.activation(out=rmag[:], in_=ldr[:], func=AF.Exp), reads=["ldr"], writes=["rmag"])
        idx = sb("idx", [128, 128])
        S.add("pool", lambda e: e.iota(idx[:], pattern=[[1, 128]], base=1, channel_multiplier=0, allow_small_or_imprecise_dtypes=True), writes=["idx"])
        Rc_p = sb("Rc", [128, 16, 128])
        Rs_p = sb("Rs", [128, 16, 128])
        Rm = sb("Rm", [128, 16, 128])
        kRc, kRs, kRm = "Rc", "Rs", "Rm"
        sc_a = W5["T1"][:].rearrange("p (a b) -> p a b", a=4)
        sc_t = W5["T2"][:].rearrange("p (a b) -> p a b", a=4)
        sc_f = W5["T3"][:].rearrange("p (a b) -> p a b", a=4)
        sc_i = W5["T4"][:].bitcast(I32).rearrange("p (a b) -> p a b", a=4)
        for c4 in range(4):
            ksl = slice(4 * c4, 4 * c4 + 4)
            S.add("dve", lambda e, ksl=ksl: e.tensor_tensor(out=sc_a, in0=ldi[:, ksl].unsqueeze(2).broadcast_to([128, 4, 128]),
                                                           in1=idx[:].unsqueeze(1).broadcast_to([128, 4, 128]), op=ALU.mult),
                  reads=["ldi", "idx"], writes=["T1"])
            for R_, kR, off in ((Rc_p, "Rc", 0.25), (Rs_p, "Rs", 0.0)):
                S.add("dve", lambda e, off=off: e.tensor_scalar(out=sc_t, in0=sc_a, scalar1=1.0 / TWO_PI, scalar2=off,
                                                               op0=ALU.mult, op1=ALU.add), reads=["T1"], writes=["T2"])
                S.add("dve", lambda e: e.tensor_copy(out=sc_i, in_=sc_t), reads=["T2"], writes=["T4"])
                S.add("dve", lambda e: e.tensor_copy(out=sc_f, in_=sc_i), reads=["T4"], writes=["T3"])
                S.add("dve", lambda e: e.tensor_tensor(out=sc_t, in0=sc_t, in1=sc_f, op=ALU.subtract),
                      reads=["T2", "T3"], writes=["T2"])
                S.add("act", lambda e, R_=R_, ksl=ksl: e.activation(out=R_[:, ksl, :], in_=sc_t, func=AF.Sin, scale=6.283179),
                      reads=["T2"], writes=[kR])
        S.add("dve", lambda e: e.tensor_copy(out=Rm[:], in_=rmag[:].unsqueeze(2).broadcast_to([128, 16, 128])),
              reads=["rmag"], writes=["Rm"])

        ar1 = sb("ar1", [128, 16])
        ai = sb("ai_", [128, 16])
        den = sb("den", [128, 16])
        t1 = sb("t1_", [128, 16])
        t2 = sb("t2_", [128, 16])
        cre = sb("cre", [128, 16])
        cim = sb("cim", [128, 16])
        S.add("dve", lambda e: e.tensor_tensor(out=ar1[:], in0=rmag[:], in1=Rc_p[:, :, 0], op=ALU.mult), reads=["rmag", kRc], writes=["ar1"])
        S.add("dve", lambda e: e.tensor_scalar(out=ar1[:], in0=ar1[:], scalar1=-1.0, scalar2=None, op0=ALU.add), reads=["ar1"], writes=["ar1"])
        S.add("dve", lambda e: e.tensor_tensor(out=ai[:], in0=rmag[:], in1=Rs_p[:, :, 0], op=ALU.mult), reads=["rmag", kRs], writes=["ai"])
        S.add("dve", lambda e: e.tensor_tensor(out=den[:], in0=lamr[:], in1=lamr[:], op=ALU.mult), reads=["lamr"], writes=["den"])
        S.add("dve", lambda e: e.tensor_tensor(out=t1[:], in0=lami[:], in1=lami[:], op=ALU.mult), reads=["lami"], writes=["t1"])
        S.add("dve", lambda e: e.tensor_tensor(out=den[:], in0=den[:], in1=t1[:], op=ALU.add), reads=["den", "t1"], writes=["den"])
        S.add("dve", lambda e: e.reciprocal(out=den[:], in_=den[:]), reads=["den"], writes=["den"])
        S.add("dve", lambda e: e.tensor_tensor(out=t1[:], in0=ar1[:], in1=lamr[:], op=ALU.mult), reads=["ar1", "lamr", "den"], writes=["t1"])
        S.add("dve", lambda e: e.tensor_tensor(out=t2[:], in0=ai[:], in1=lami[:], op=ALU.mult), reads=["ai", "lami"], writes=["t2"])
        S.add("dve", lambda e: e.tensor_tensor(out=t1[:], in0=t1[:], in1=t2[:], op=ALU.add), reads=["t1", "t2"], writes=["t1"])
        S.add("dve", lambda e: e.tensor_tensor(out=cre[:], in0=t1[:], in1=den[:], op=ALU.mult), reads=["t1", "den"], writes=["cre"])
        S.add("dve", lambda e: e.tensor_tensor(out=t1[:], in0=ai[:], in1=lamr[:], op=ALU.mult), reads=["ai", "lamr", "cre"], writes=["t1"])
        S.add("dve", lambda e: e.tensor_tensor(out=t2[:], in0=ar1[:], in1=lami[:], op=ALU.mult), reads=["ar1", "lami"], writes=["t2"])
        S.add("dve", lambda e: e.tensor_tensor(out=t1[:], in0=t1[:], in1=t2[:], op=ALU.subtract), reads=["t1", "t2"], writes=["t1"])
        S.add("dve", lambda e: e.tensor_tensor(out=cim[:], in0=t1[:], in1=den[:], op=ALU.mult), reads=["t1", "den"], writes=["cim"])
        bb = {}
        u1 = sb("u1_", [128, 16, 16])
        u2 = sb("u2_", [128, 16, 16])
        creb = cre[:].unsqueeze(2).broadcast_to([128, 16, 16])
        cimb = cim[:].unsqueeze(2).broadcast_to([128, 16, 16])
        for nm, (a0, b0, a1, b1, op) in (("bbr", (creb, Bl_r, cimb, Bl_i, ALU.subtract)),
                                         ("bbi", (creb, Bl_i, cimb, Bl_r, ALU.add))):
            t = sb(nm, [128, 16, 16])
            S.add("dve", lambda e, a0=a0, b0=b0: e.tensor_tensor(out=u1[:], in0=b0[:], in1=a0, op=ALU.mult),
                  reads=["cre", "cim", "Bl_r", "Bl_i"], writes=["u1"])
            S.add("dve", lambda e, a1=a1, b1=b1: e.tensor_tensor(out=u2[:], in0=b1[:], in1=a1, op=ALU.mult),
                  reads=["cre", "cim", "Bl_r", "Bl_i"], writes=["u2"])
            S.add("dve", lambda e, t=t, op=op: e.tensor_tensor(out=t[:], in0=u1[:], in1=u2[:], op=op),
                  reads=["u1", "u2"], writes=[nm])
            bb[nm] = t
        BT = {}
        bpad = sb("bpad", [128, 16, 128], BF16)
        for nm in ("bbr", "bbi"):
            pad = bpad
            S.add("pool", lambda e, pad=pad: e.memset(pad[:], 0.0), writes=["bpad"])
            for k in range(16):
                for g2 in range(2):
                    off = ((2 * k + g2) % 8) * 16
                    S.add("dve", lambda e, pad=pad, k=k, g2=g2, off=off, nm=nm: e.tensor_copy(
                        out=pad[g2 * 64:(g2 + 1) * 64, k, off:off + 16], in_=bb[nm][g2 * 64:(g2 + 1) * 64, k, :]),
                        reads=[nm, "bpad"], writes=["bpad"])
            T = sb(nm + "T", [128, 16, 128], BF16)
            for h in range(2):
                pt, ptk = PT.next()
                for j in range(8):
                    S.add("pe", lambda e, pad=pad, pt=pt, j=j, h=h: e.transpose(out=pt[:, j, :], in_=pad[:, h * 8 + j, :], identity=ident[:]),
                          reads=["bpad", "ident"], writes=[ptk])
                S.add("dve", lambda e, T=T, pt=pt, h=h: e.tensor_copy(out=T[:, h * 8:(h + 1) * 8, :], in_=pt[:]),
                      reads=[ptk], writes=[nm + "T"])
            BT[nm] = T
        Cb = {}
        for nm, src, sgn in (("Cbr", Cl_r, 1.0), ("Cbi", Cl_i, -1.0)):
            t = sb(nm, [128, 16, 32], BF16)
            S.add("pool", lambda e, t=t: e.memset(t[:], 0.0), writes=[nm])
            for g2 in range(2):
                S.add("dve", lambda e, t=t, src=src, g2=g2, sgn=sgn: e.tensor_scalar(
                    out=t[g2 * 64:(g2 + 1) * 64, :, g2 * 16:(g2 + 1) * 16], in0=src[g2 * 64:(g2 + 1) * 64, :, :],
                    scalar1=sgn, scalar2=None, op0=ALU.mult), reads=["Cl_r", "Cl_i", nm], writes=[nm])
            Cb[nm] = t

        cxr = sb("cxr", [128, 16])
        cxi = sb("cxi", [128, 16])
        S.add("pool", lambda e: e.memset(cxr[:], 0.0), writes=["cxr"])
        S.add("pool", lambda e: e.memset(cxi[:], 0.0), writes=["cxi"])
        s0r = sb("s0r", [128, 2, 16])
        s0i = sb("s0i", [128, 2, 16])
        S.add("sp", lambda e: e.dma_start(out=s0r[:], in_=sbre.rearrange("s (k g) n -> (g n) s k", g=2)), writes=["s0r"], is_dma=True)
        S.add("sp", lambda e: e.dma_start(out=s0i[:], in_=sbim.rearrange("s (k g) n -> (g n) s k", g=2)), writes=["s0i"], is_dma=True)
        cs_r = sb("cs_r", [128, 2, 16])
        cs_i = sb("cs_i", [128, 2, 16])

        def rmsnorm_T(src, srck, gtile, gk):
            S.add("act", lambda e: e.activation(out=gate[:], in_=src[:], func=AF.Square, accum_out=ssq[:]),
                  reads=[srck], writes=["gate", "ssq"])
            S.add("dve", lambda e: e.tensor_scalar(out=rstd[:], in0=ssq[:], scalar1=1.0 / D, scalar2=EPS, op0=ALU.mult, op1=ALU.add),
                  reads=["ssq"], writes=["rstd"])
            S.add("pool", lambda e: e.tensor_tensor(out=rstd[:], in0=rstd[:], in1=mhalf[:, 0:1], op=ALU.pow),
                  reads=["rstd", "mhalf"], writes=["rstd"])
            S.add("dve", lambda e: e.scalar_tensor_tensor(out=xn[:], in0=src[:], scalar=rstd[:], in1=gtile[:], op0=ALU.mult, op1=ALU.mult),
                  reads=[srck, "rstd", gk], writes=["xn"])
            pt, ptk = PT.next()
            for k in range(8):
                S.add("pe", lambda e, k=k, pt=pt: e.transpose(out=pt[:, k, :], in_=xn[:, k * 128:(k + 1) * 128], identity=ident[:]),
                      reads=["xn", "ident"], writes=[ptk])
            S.add("act", lambda e, pt=pt: e.copy(out=hnT[:], in_=pt[:]), reads=[ptk], writes=["hnT"])

        def linear(lhsT, lk, nk, w, wk, c0, c1):
            p, pk = PS.next()
            for k in range(nk):
                S.add("pe", lambda e, k=k, p=p: e.matmul(p[:, 0:c1 - c0], lhsT=lhsT[:, k, :], rhs=w[:, k, c0:c1],
                                                        start=(k == 0), stop=(k == nk - 1)),
                      reads=[lk, wk], writes=[pk])
            return p, pk

        def load_tile(i):
            h, hk = H.next()
            p0, p0k = P0.next()
            S.add("sp", lambda e: e.dma_start(out=h[:], in_=xin[i * 128:(i + 1) * 128, :]), writes=[hk], is_dma=True)
            S.add("sp", lambda e: e.dma_start(out=p0[:], in_=pin[0, i * 128:(i + 1) * 128, :]), writes=[p0k], is_dma=True)
            return h, hk, p0, p0k

        nxt = load_tile(0)
        for i in range(nt):
            h, hk, p0, p0k = nxt
            if i + 1 < nt:
                nxt = load_tile(i + 1)
            samp = (i == ntp)
            Rc, Rs = Rc_p, Rs_p

            rmsnorm_T(h, hk, g_l0, "g_l0")
            p, pk = linear(hnT, "hnT", 8, w_in0, "w_in0", 0, 512)
            S.add("act", lambda e, p=p: e.activation(out=ua[:], in_=p[:], func=AF.Gelu_apprx_tanh), reads=[pk], writes=["ua"])
            p, pk = linear(hnT, "hnT", 8, w_in0, "w_in0", 512, 1024)
            S.add("act", lambda e, p=p: e.activation(out=va[:], in_=p[:], func=AF.Gelu_apprx_tanh), reads=[pk], writes=["va"])
            p, pk = linear(hnT, "hnT", 8, w_in0, "w_in0", 1024, 1536)
            S.add("act", lambda e, p=p: e.activation(out=za[:], in_=p[:], func=AF.Silu), reads=[pk], writes=["za"])
            p, pk = linear(hnT, "hnT", 8, w_in0, "w_in0", 1536, 2048)
            S.add("act", lambda e, p=p: e.copy(out=ubf[:], in_=p[:]), reads=[pk], writes=["ubf"])
            S.add("dve", lambda e, p=p: e.tensor_tensor(out=ubd[:], in0=p[:], in1=Db[:], op=ALU.mult), reads=[pk, "Db"], writes=["ubd"])
            p, pk = linear(hnT, "hnT", 8, w_in0, "w_in0", 2048, 2560)
            S.add("act", lambda e, p=p: e.activation(out=zb[:], in_=p[:], func=AF.Silu), reads=[pk], writes=["zb"])

            S.add("dve", lambda e: e.bn_stats(out=bst[:], in_=va[:]), reads=["va"], writes=["bst"])
            S.add("dve", lambda e: e.bn_aggr(out=bag[:], in_=bst[:]), reads=["bst"], writes=["bag"])
            S.add("dve", lambda e: e.tensor_scalar(out=bag[:, 1:2], in0=bag[:, 1:2], scalar1=EPS, scalar2=None, op0=ALU.add),
                  reads=["bag"], writes=["bag"])
            S.add("pool", lambda e: e.tensor_tensor(out=bag[:, 1:2], in0=bag[:, 1:2], in1=mhalf[:, 0:1], op=ALU.pow),
                  reads=["bag", "mhalf"], writes=["bag"])
            S.add("dve", lambda e: e.tensor_scalar(out=vln[:], in0=va[:], scalar1=bag[:, 0:1], scalar2=bag[:, 1:2],
                                                  op0=ALU.subtract, op1=ALU.mult), reads=["va", "bag"], writes=["vln"])
            S.add("dve", lambda e: e.tensor_tensor(out=vln[:], in0=vln[:], in1=lng[:], op=ALU.mult), reads=["vln", "lng"], writes=["vln"])
            S.add("dve", lambda e: e.tensor_tensor(out=vln[:], in0=vln[:], in1=lnb[:], op=ALU.add), reads=["vln", "lnb"], writes=["vln"])
            S.add("act", lambda e: e.copy(out=vbf[:], in_=vln[:]), reads=["vln"], writes=["vbf"])
            if samp:
                S.add("sp", lambda e: e.dma_start(out=o_av, in_=vln[:]), reads=["vln"], writes=["o_av"], is_dma=True)
            p, pk = PS.next()
            wm, wmk = (wmixTs, "wlsT") if samp else (wmixT, "wlT")
            for g in range(4):
                S.add("pe", lambda e, g=g, p=p, wm=wm: e.matmul(p[:, g * 128:(g + 1) * 128], lhsT=wm[:, g, :], rhs=vbf[:, g * 128:(g + 1) * 128],
                                                               start=True, stop=True), reads=[wmk, "vbf"], writes=[pk])
            bs_t, bsk = (bsbs, "bsbs") if samp else (bsb, "bsb")
            for g in range(4):
                S.add("dve", lambda e, g=g, p=p, bs_t=bs_t: e.scalar_tensor_tensor(
                    out=ua[:, g * 128:(g + 1) * 128], in0=p[:, g * 128:(g + 1) * 128], scalar=bs_t[:, g:g + 1],
                    in1=ua[:, g * 128:(g + 1) * 128], op0=ALU.add, op1=ALU.mult), reads=[pk, bsk, "ua"], writes=["ua"])
            S.add("dve", lambda e: e.tensor_tensor(out=mixin[:, 0:512], in0=ua[:], in1=za[:], op=ALU.mult),
                  reads=["ua", "za"], writes=["mixin"])

            pt, ptk = PT.next()
            for q in range(4):
                S.add("pe", lambda e, q=q, pt=pt: e.transpose(out=pt[:, q, :], in_=ubf[:, q * 128:(q + 1) * 128], identity=ident[:]),
                      reads=["ubf", "ident"], writes=[ptk])
            S.add("act", lambda e, pt=pt: e.copy(out=ubT[:], in_=pt[:, 0:4, :]), reads=[ptk], writes=["ubT"])
            py, pyk = py_bank, "ps_y"
            for q in range(4):
                pbr, pbrk = PS.next()
                pbi, pbik = PS.next()
                for j in range(4):
                    k = 4 * q + j
                    S.add("pe", lambda e, j=j, k=k, q=q, pbr=pbr: e.matmul(pbr[:, j * 128:(j + 1) * 128], lhsT=BT["bbr"][:, k, :], rhs=ubT[:, q, :],
                                                                          start=True, stop=True), reads=["bbrT", "ubT"], writes=[pbrk])
                    S.add("pe", lambda e, j=j, k=k, q=q, pbi=pbi: e.matmul(pbi[:, j * 128:(j + 1) * 128], lhsT=BT["bbi"][:, k, :], rhs=ubT[:, q, :],
                                                                          start=True, stop=True), reads=["bbiT", "ubT"], writes=[pbik])
                T1, T2, T3, T4 = W5["T1"], W5["T2"], W5["T3"], W5["T4"]
                vr, vi, zr, zi = W5["vr"], W5["vi"], W5["zr"], W5["zi"]
                if samp:
                    def V(t):
                        return t[:].rearrange("p (a s b) -> p a s b", a=4, s=2)
                    def Vp(t):
                        return t[:].rearrange("p (a s b) -> p a s b", a=4, s=2)
                    rc = Rc[:, 4 * q:4 * q + 4, 0:64].unsqueeze(2).broadcast_to([128, 4, 2, 64])
                    rs = Rs[:, 4 * q:4 * q + 4, 0:64].unsqueeze(2).broadcast_to([128, 4, 2, 64])
                else:
                    def V(t):
                        return t[:]
                    def Vp(t):
                        return t[:]
                    rc = Rc[:, 4 * q:4 * q + 4, :].rearrange("p a b -> p (a b)")
                    rs = Rs[:, 4 * q:4 * q + 4, :].rearrange("p a b -> p (a b)")
                S.add("dve", lambda e, pbr=pbr, rc=rc, V=V, Vp=Vp: e.tensor_tensor(out=V(T1), in0=Vp(pbr), in1=rc, op=ALU.mult), reads=[pbrk, kRc], writes=["T1"])
                S.add("dve", lambda e, pbi=pbi, rs=rs, V=V, Vp=Vp: e.tensor_tensor(out=V(T2), in0=Vp(pbi), in1=rs, op=ALU.mult), reads=[pbik, kRs], writes=["T2"])
                S.add("dve", lambda e, pbi=pbi, rc=rc, V=V, Vp=Vp: e.tensor_tensor(out=V(T3), in0=Vp(pbi), in1=rc, op=ALU.mult), reads=[pbik, kRc], writes=["T3"])
                S.add("dve", lambda e, pbr=pbr, rs=rs, V=V, Vp=Vp: e.tensor_tensor(out=V(T4), in0=Vp(pbr), in1=rs, op=ALU.mult), reads=[pbrk, kRs], writes=["T4"])
                S.add("pool", lambda e: e.tensor_tensor(out=vr[:], in0=T1[:], in1=T2[:], op=ALU.add), reads=["T1", "T2"], writes=["vr"])
                S.add("pool", lambda e: e.tensor_tensor(out=vi[:], in0=T3[:], in1=T4[:], op=ALU.subtract), reads=["T3", "T4"], writes=["vi"])
                for j in range(4):
                    k = 4 * q + j
                    if samp:
                        segs = [(j * 128 + 64 * s_, 64, s0r[:, s_, k:k + 1], s0i[:, s_, k:k + 1], "s0r", "s0i") for s_ in range(2)]
                    else:
                        segs = [(j * 128, 128, cxr[:, k:k + 1], cxi[:, k:k + 1], "cxr", "cxi")]
                    for (c0, ln, ir, ii, irk, iik) in segs:
                        S.add("dve", lambda e, c0=c0, ln=ln, ir=ir, k=k: e.tensor_tensor_scan(
                            out=zr[:, c0:c0 + ln], data0=Rm[:, k, 0:ln], data1=vr[:, c0:c0 + ln], initial=ir, op0=ALU.mult, op1=ALU.add),
                            reads=["vr", kRm, irk], writes=["zr"])
                        S.add("dve", lambda e, c0=c0, ln=ln, ii=ii, k=k: e.tensor_tensor_scan(
                            out=zi[:, c0:c0 + ln], data0=Rm[:, k, 0:ln], data1=vi[:, c0:c0 + ln], initial=ii, op0=ALU.mult, op1=ALU.add),
                            reads=["vi", kRm, iik], writes=["zi"])
                S.add("pool", lambda e, rc=rc, V=V: e.tensor_tensor(out=V(T1), in0=V(zr), in1=rc, op=ALU.mult), reads=["zr", kRc], writes=["T1"])
                S.add("pool", lambda e, rs=rs, V=V: e.tensor_tensor(out=V(T2), in0=V(zi), in1=rs, op=ALU.mult), reads=["zi", kRs], writes=["T2"])
                S.add("pool", lambda e, rs=rs, V=V: e.tensor_tensor(out=V(T3), in0=V(zr), in1=rs, op=ALU.mult), reads=["zr", kRs], writes=["T3"])
                S.add("pool", lambda e, rc=rc, V=V: e.tensor_tensor(out=V(T4), in0=V(zi), in1=rc, op=ALU.mult), reads=["zi", kRc], writes=["T4"])
                S.add("dve", lambda e: e.tensor_tensor(out=xrb[:].rearrange("p a b -> p (a b)"), in0=T1[:], in1=T2[:], op=ALU.subtract),
                      reads=["T1", "T2"], writes=["xrb"])
                S.add("dve", lambda e: e.tensor_tensor(out=xib[:].rearrange("p a b -> p (a b)"), in0=T3[:], in1=T4[:], op=ALU.add),
                      reads=["T3", "T4"], writes=["xib"])
                T13 = T1[:].rearrange("p (a b) -> p a b", a=4)
                T23 = T2[:].rearrange("p (a b) -> p a b", a=4)
                T33 = T3[:].rearrange("p (a b) -> p a b", a=4)
                T43 = T4[:].rearrange("p (a b) -> p a b", a=4)
                if samp:
                    for s_ in range(2):
                        c_ = 64 * s_ + 63
                        S.add("dve", lambda e, s_=s_, c_=c_, q=q, T13=T13, T23=T23: e.tensor_tensor(out=cs_r[:, s_, 4 * q:4 * q + 4], in0=T13[:, :, c_], in1=T23[:, :, c_], op=ALU.subtract),
                              reads=["T1", "T2"], writes=["cs_r"])
                        S.add("dve", lambda e, s_=s_, c_=c_, q=q, T33=T33, T43=T43: e.tensor_tensor(out=cs_i[:, s_, 4 * q:4 * q + 4], in0=T33[:, :, c_], in1=T43[:, :, c_], op=ALU.add),
                              reads=["T3", "T4"], writes=["cs_i"])
                else:
                    S.add("dve", lambda e, q=q, T13=T13, T23=T23: e.tensor_tensor(out=cxr[:, 4 * q:4 * q + 4], in0=T13[:, :, 127], in1=T23[:, :, 127], op=ALU.subtract),
                          reads=["T1", "T2"], writes=["cxr"])
                    S.add("dve", lambda e, q=q, T33=T33, T43=T43: e.tensor_tensor(out=cxi[:, 4 * q:4 * q + 4], in0=T33[:, :, 127], in1=T43[:, :, 127], op=ALU.add),
                          reads=["T3", "T4"], writes=["cxi"])
                for j in range(4):
                    k = 4 * q + j
                    S.add("pe", lambda e, j=j, k=k, py=py: e.matmul(py[:, 32 * k:32 * k + 32], lhsT=xrb[:, j, :], rhs=Cb["Cbr"][:, k, :], start=True, stop=False),
                          reads=["xrb", "Cbr"], writes=[pyk])
                    S.add("pe", lambda e, j=j, k=k, py=py: e.matmul(py[:, 32 * k:32 * k + 32], lhsT=xib[:, j, :], rhs=Cb["Cbi"][:, k, :], start=False, stop=True),
                          reads=["xib", "Cbi"], writes=[pyk])
            if not samp:
                if i == ntp - 1:
                    S.add("sp", lambda e: e.dma_start(out=o_bre[0].rearrange("(k g) n -> (g n) k", g=2), in_=cxr[:]), reads=["cxr"], writes=["o_bre"], is_dma=True)
                    S.add("sp", lambda e: e.dma_start(out=o_bim[0].rearrange("(k g) n -> (g n) k", g=2), in_=cxi[:]), reads=["cxi"], writes=["o_bim"], is_dma=True)
            else:
                S.add("sp", lambda e: e.dma_start(out=o_bre[1:3].rearrange("s (k g) n -> (g n) s k", g=2), in_=cs_r[:]), reads=["cs_r"], writes=["o_bre"], is_dma=True)
                S.add("sp", lambda e: e.dma_start(out=o_bim[1:3].rearrange("s (k g) n -> (g n) s k", g=2), in_=cs_i[:]), reads=["cs_i"], writes=["o_bim"], is_dma=True)
            S.add("dve", lambda e, py=py: e.tensor_tensor(out=yb[:], in0=py[:], in1=ubd[:], op=ALU.add), reads=[pyk, "ubd"], writes=["yb"])
            if samp and debug_h:
                S.add("sp", lambda e: e.dma_start(out=dbg, in_=yb[:]), reads=["yb"], writes=["dbg"], is_dma=True)
            S.add("act", lambda e: e.activation(out=yg[:], in_=yb[:], func=AF.Gelu_apprx_tanh), reads=["yb"], writes=["yg"])
            S.add("act", lambda e: e.copy(out=ygb[:], in_=yg[:]), reads=["yg"], writes=["ygb"])
            pt, ptk = PT.next()
            for q in range(4):
                S.add("pe", lambda e, q=q, pt=pt: e.transpose(out=pt[:, q, :], in_=ygb[:, q * 128:(q + 1) * 128], identity=ident[:]),
                      reads=["ygb", "ident"], writes=[ptk])
            S.add("act", lambda e, pt=pt: e.copy(out=ygT[:], in_=pt[:, 0:4, :]), reads=[ptk], writes=["ygT"])
            p, pk = linear(ygT, "ygT", 4, w_glu, "w_glu", 0, 512)
            S.add("act", lambda e, p=p: e.activation(out=yb[:], in_=p[:], func=AF.Sigmoid), reads=[pk], writes=["yb"])
            S.add("dve", lambda e: e.tensor_tensor(out=yg[:], in0=yg[:], in1=yb[:], op=ALU.mult), reads=["yg", "yb"], writes=["yg"])
            S.add("dve", lambda e: e.tensor_tensor(out=mixin[:, 512:1024], in0=yg[:], in1=zb[:], op=ALU.mult), reads=["yg", "zb"], writes=["mixin"])

            pt, ptk = PT.next()
            for k in range(8):
                S.add("pe", lambda e, k=k, pt=pt: e.transpose(out=pt[:, k, :], in_=mixin[:, k * 128:(k + 1) * 128], identity=ident[:]),
                      reads=["mixin", "ident"], writes=[ptk])
            S.add("act", lambda e, pt=pt: e.copy(out=mixT[:], in_=pt[:]), reads=[ptk], writes=["mixT"])
            for hh in range(2):
                p, pk = linear(mixT, "mixT", 8, w_out0, "w_out0", hh * 512, (hh + 1) * 512)
                S.add("dve", lambda e, p=p, hh=hh, h=h: e.tensor_tensor(out=h[:, hh * 512:(hh + 1) * 512], in0=h[:, hh * 512:(hh + 1) * 512], in1=p[:], op=ALU.add),
                      reads=[pk, hk], writes=[hk])

            rmsnorm_T(h, hk, g_p0, "g_p0")
            S.add("act", lambda e, p0=p0: e.copy(out=pbf[:], in_=p0[:]), reads=[p0k], writes=["pbf"])
            pt, ptk = PT.next()
            for k in range(2):
                S.add("pe", lambda e, k=k, pt=pt: e.transpose(out=pt[:, k, :], in_=pbf[:, k * 128:(k + 1) * 128], identity=ident[:]),
                      reads=["pbf", "ident"], writes=[ptk])
            S.add("act", lambda e, pt=pt: e.copy(out=pT[:], in_=pt[:, 0:2, :]), reads=[ptk], writes=["pT"])
            for hh in range(2):
                p, pk = linear(hnT, "hnT", 8, w_gate0, "w_gate0", hh * 512, (hh + 1) * 512)
                S.add("act", lambda e, p=p, hh=hh: e.activation(out=gate[:, hh * 512:(hh + 1) * 512], in_=p[:], func=AF.Sigmoid), reads=[pk], writes=["gate"])
                p, pk = linear(pT, "pT", 2, w_pp0, "w_pp0", hh * 512, (hh + 1) * 512)
                S.add("dve", lambda e, p=p, hh=hh: e.tensor_tensor(out=gate[:, hh * 512:(hh + 1) * 512], in0=gate[:, hh * 512:(hh + 1) * 512], in1=p[:], op=ALU.mult),
                      reads=[pk, "gate"], writes=["gate"])
            S.add("dve", lambda e, h=h: e.tensor_tensor(out=h[:], in0=h[:], in1=gate[:], op=ALU.add), reads=[hk, "gate"], writes=[hk])
            S.add("sp", lambda e, h=h, i=i: e.dma_start(out=h1_scr[i * 128:(i + 1) * 128, :], in_=h[:]), reads=[hk], writes=["h1_scr"], is_dma=True)

        S.emit()
    nc.all_engine_barrier()
    T = dict(locals())
    phase_b(nc, T, ntp, debug_h)
    return nc


WEIGHT_KEYS = ["d_conv_w", "d_A_log", "d_dt_bias", "d_norm_g", "norm_g", "final_norm_g", "ple_proj", "ple_gate_w", "ple_norm_g", "even_w_in", "even_w_out",
               "a_ln_g", "a_ln_b", "a_w_s", "a_b_s", "b_lam_re", "b_lam_im", "b_log_dt", "b_B_re", "b_B_im",
               "b_C_re", "b_C_im", "b_D", "b_glu_w", "odd_w_in", "odd_w_out"]


def make_in_maps(inp, ntp=NTP):
    maps = []
    for c in range(NCORES):
        m = {k: np.ascontiguousarray(inp[k], dtype=np.float32) for k in WEIGHT_KEYS}
        xs = np.asarray(inp["x_sample"])[2 * c:2 * c + 2].reshape(128, D)
        m["xin"] = np.ascontiguousarray(np.concatenate([np.asarray(inp["x_prompt"])[c, :ntp * 128], xs], axis=0))
        ps_ = np.asarray(inp["p_sample"])[:, 2 * c:2 * c + 2].reshape(2, 128, 256)
        m["pin"] = np.ascontiguousarray(np.concatenate([np.asarray(inp["p_prompt"])[:, c, :ntp * 128], ps_], axis=1))
        m["sbre"] = np.ascontiguousarray(np.asarray(inp["state_b_re"])[0, 2 * c:2 * c + 2])
        m["sbim"] = np.ascontiguousarray(np.asarray(inp["state_b_im"])[0, 2 * c:2 * c + 2])
        m["cache_k"] = np.ascontiguousarray(np.asarray(inp["cache_k_c"])[0, 2 * c:2 * c + 2].reshape(2, 4096, 512))
        m["cache_v"] = np.ascontiguousarray(np.asarray(inp["cache_v_c"])[0, 2 * c:2 * c + 2].reshape(2, 4096, 512))
        m["state_d"] = np.ascontiguousarray(np.asarray(inp["state_d"])[0, 2 * c:2 * c + 2])
        m["state_conv"] = np.ascontiguousarray(np.asarray(inp["state_conv_d"])[0, 2 * c:2 * c + 2])
        maps.append(m)
    return maps


_NC_CACHE = {}


def kernel(**inputs):
    if "nc" not in _NC_CACHE:
        _NC_CACHE["nc"] = build_program()
    nc = _NC_CACHE["nc"]
    res = run_bass_kernel_spmd(nc, make_in_maps(inputs), core_ids=list(range(NCORES)))
    R = res.results
    B, DB = 8, 16
    y_prompt = np.stack([R[c]["o_y"][:SEQ] for c in range(B)]).astype(np.float32)
    y_sample = np.concatenate([R[c]["o_y"][SEQ:].reshape(2, 64, D) for c in range(B)]).astype(np.float32)
    b_re_p = np.stack([R[c]["o_bre"][0] for c in range(B)])[None]
    b_im_p = np.stack([R[c]["o_bim"][0] for c in range(B)])[None]
    b_re_s = np.concatenate([R[c]["o_bre"][1:3] for c in range(B)])[None]
    b_im_s = np.concatenate([R[c]["o_bim"][1:3] for c in range(B)])[None]
    a_v_s = np.concatenate([R[c]["o_av"].reshape(2, 64, 512) for c in range(B)])[None]
    k_c_p = np.stack([R[c]["o_kc"][:SEQ].reshape(SEQ, 8, 64) for c in range(B)])[None]
    v_c_p = np.stack([R[c]["o_vc"][:SEQ].reshape(SEQ, 8, 64) for c in range(B)])[None]
    k_c_s = np.concatenate([R[c]["o_kc"][SEQ:].reshape(2, 64, 8, 64) for c in range(B)])[None]
    v_c_s = np.concatenate([R[c]["o_vc"][SEQ:].reshape(2, 64, 8, 64) for c in range(B)])[None]
    conv_d_p = np.stack([R[c]["o_conv"][0] for c in range(B)])[None]
    conv_d_s = np.concatenate([R[c]["o_conv"][1:3] for c in range(B)])[None]
    s_d_p = np.stack([R[c]["o_sd"][0] for c in range(B)])[None]
    s_d_s = np.concatenate([R[c]["o_sd"][1:3] for c in range(B)])[None]
    f = lambda a: np.ascontiguousarray(a, dtype=np.float32)
    return tuple(f(a) for a in (y_prompt, y_sample, b_re_p, b_im_p, k_c_p, v_c_p, s_d_p, conv_d_p,
                                b_re_s, b_im_s, a_v_s, k_c_s, v_c_s, s_d_s, conv_d_s))


def phase_b(nc, T, ntp, debug_h):
    nt = ntp + 1
    g = lambda n: T[n]
    h1_scr, pin, o_y, o_kc, o_vc, o_conv = g("h1_scr"), g("pin"), g("o_y"), g("o_kc"), g("o_vc"), g("o_conv")
    norm_g, final_norm_g, ple_proj, ple_gate_w, ple_norm_g = g("norm_g"), g("final_norm_g"), g("ple_proj"), g("ple_gate_w"), g("ple_norm_g")
    odd_w_in, odd_w_out, cache_k, cache_v = g("odd_w_in"), g("odd_w_out"), g("cache_k"), g("cache_v")
    ntok = nt * 128
    kts = nc.dram_tensor("kts", [nt, 128, 512], BF16, kind="Internal").ap()
    vs = nc.dram_tensor("vs", [ntok, 512], BF16, kind="Internal").ap()
    dbg2 = nc.dram_tensor("dbg2", [ntok, 512], F32, kind="ExternalOutput").ap() if debug_h else None

    S = Sched(nc)
    st = contextlib.ExitStack()
    with st:
        def sb(name, shape, dt=F32):
            return st.enter_context(nc.sbuf_tensor(name, list(shape), dt))

        def ps(name, shape, dt=F32):
            return st.enter_context(nc.psum_tensor(name, list(shape), dt))

        st.enter_context(nc.allow_non_contiguous_dma("small parameter layout loads"))
        PS = Rot([ps(f"qs{i}", [128, 512]) for i in range(2)], "qs")
        PT = Rot([ps("qt0", [128, 8, 128], BF16)], "qt")
        PZ = ps("pz", [128, 1024])
        P2 = ps("p2", [128, 1024])
        ACC = ps("acc", [128, 512])

        ident = sb("identb", [128, 128], BF16)
        S.add("pool", lambda e: e.memset(ident[:], 0.0), writes=["ident"])
        S.add("pool", lambda e: e.affine_select(out=ident[:], in_=ident[:], compare_op=ALU.not_equal, fill=1.0,
                                               base=0, pattern=[[-1, 128]], channel_multiplier=1), reads=["ident"], writes=["ident"])
        mhalf = sb("mhalfb", [128, 1])
        onec = sb("onec", [128, 1])
        S.add("pool", lambda e: e.memset(onec[:], 1.0), writes=["onec"])
        S.add("pool", lambda e: e.memset(mhalf[:], -0.5), writes=["mhalf"])
        negU = sb("negU", [128, 128], BF16)
        zer = sb("zer", [128, 512], BF16)
        S.add("pool", lambda e: e.memset(zer[:], 0.0), writes=["zer"])
        negO = sb("negO", [128, 128], BF16)
        mask01 = sb("mask01", [128, 128], BF16)
        S.add("pool", lambda e: e.memset(negU[:], -1.0), writes=["negU"])
        S.add("pool", lambda e: e.affine_select(out=negU[:], in_=negU[:], compare_op=ALU.is_ge, fill=0.0, base=0,
                                               pattern=[[-1, 128]], channel_multiplier=1), reads=["negU"], writes=["negU"])
        S.add("pool", lambda e: e.memset(negO[:], -1.0), writes=["negO"])
        S.add("pool", lambda e: e.memset(mask01[:], 1.0), writes=["mask01"])
        S.add("pool", lambda e: e.affine_select(out=mask01[:], in_=mask01[:], compare_op=ALU.is_gt, fill=0.0, base=0,
                                               pattern=[[1, 128]], channel_multiplier=-1), reads=["mask01"], writes=["mask01"])

        def bcast_load(name, src, n):
            t = sb(name, [128, n])
            S.add("sp", lambda e: e.dma_start(out=t[:], in_=src.partition_broadcast(128)), writes=[name], is_dma=True)
            return t

        g_l1 = bcast_load("g_l1", norm_g[1], D)
        g_p1 = bcast_load("g_p1", ple_norm_g[1], D)
        g_f = bcast_load("g_f", final_norm_g, D)

        def wload(name, src, kt, n, csz=512):
            t = sb(name, [128, kt, n], BF16)
            v = src.rearrange("(k p) n -> p k n", p=128)
            for c0 in range(0, n, csz):
                c1 = min(n, c0 + csz)
                S.add("pool", lambda e, c0=c0, c1=c1: e.dma_start(out=t[:, :, c0:c1], in_=v[:, :, c0:c1]),
                      writes=[name], is_dma=True)
            return t

        w_in1 = wload("w_in1", odd_w_in[0], 8, 4104)
        w_out1 = wload("w_out1", odd_w_out[0], 8, 1024)
        w_gate1 = wload("w_gate1", ple_gate_w[1], 8, 1024)
        w_pp1 = wload("w_pp1", ple_proj[1], 2, 1024)

        H = Rot([sb(f"hb{i}", [128, D]) for i in range(1)], "hb")
        P1 = Rot([sb(f"p1_{i}", [128, 256]) for i in range(1)], "p1_")
        xn = sb("xnb", [128, D], BF16)
        hnT = sb("hnTb", [128, 8, 128], BF16)
        ssq = sb("ssqb", [128, 1])
        rstd = sb("rstdb", [128, 1])
        gate = sb("gateb", [128, D])
        qbf = sb("qbf", [128, 512], BF16)
        kf = gate[:, 0:512]
        kbf = sb("kbf", [128, 512], BF16)
        vf = gate[:, 512:1024]
        vbf = sb("vbf1", [128, 512], BF16)
        vbs = sb("vbs", [64, 512], BF16)
        zcs = sb("zcs", [128, 512])
        QT = sb("QT", [128, 8, 128], BF16)
        S.add("pool", lambda e: e.memset(QT[:], 0.0), writes=["QT"])
        KTc = sb("KTc", [128, 4, 128], BF16)
        KB = Rot([sb(f"ktb{i}", [128, 4, 128], BF16) for i in range(3)], "ktb")
        VB = Rot([sb(f"vb{i}", [128, 512], BF16) for i in range(3)], "vb")
        CK = Rot([sb(f"ck{i}", [128, 512], BF16) for i in range(1)], "ck")
        arena = sb("arena", [128, 3072])
        ab16 = arena[:, :].bitcast(BF16)
        EB = [ab16[:, 0:1024], ab16[:, 1024:2048]]
        LM = [ab16[:, 2048:3072], ab16[:, 3072:4096]]
        cum = ab16[:, 4096:5120]
        wbuf = ab16[:, 5120:6144]
        ebuf = arena[:, 0:1024]
        mixin = sb("mixinb", [128, 1024], BF16)
        mixT = hnT
        pbf = sb("pbfb", [128, 256], BF16)
        pT = sb("pTb", [128, 2, 128], BF16)
        cout = sb("cout", [128, 512])
        yout = ebuf

        def rmsnorm(src, srck, gtile, gk, out, outk):
            S.add("act", lambda e: e.activation(out=gate[:], in_=src[:], func=AF.Square, accum_out=ssq[:]),
                  reads=[srck], writes=["gate", "ssq"])
            S.add("dve", lambda e: e.tensor_scalar(out=rstd[:], in0=ssq[:], scalar1=1.0 / D, scalar2=EPS, op0=ALU.mult, op1=ALU.add),
                  reads=["ssq"], writes=["rstd"])
            S.add("pool", lambda e: e.tensor_tensor(out=rstd[:], in0=rstd[:], in1=mhalf[:], op=ALU.pow),
                  reads=["rstd", "mhalf"], writes=["rstd"])
            S.add("dve", lambda e: e.scalar_tensor_tensor(out=out[:], in0=src[:], scalar=rstd[:], in1=gtile[:], op0=ALU.mult, op1=ALU.mult),
                  reads=[srck, "rstd", gk], writes=[outk])

        def transpose_to(src, srck, nk, dst, dstk):
            pt, ptk = PT.next()
            for k in range(nk):
                S.add("pe", lambda e, k=k, pt=pt: e.transpose(out=pt[:, k, :], in_=src[:, k * 128:(k + 1) * 128], identity=ident[:]),
                      reads=[srck, "ident"], writes=[ptk])
            S.add("act", lambda e, pt=pt: e.copy(out=dst[:, 0:nk, :], in_=pt[:, 0:nk, :]), reads=[ptk], writes=[dstk])

        def linear(lhsT, lk, nk, w, wk, c0, c1, msl=slice(0, 128)):
            p, pk = PS.next()
            m = msl.stop - msl.start
            for k in range(nk):
                S.add("pe", lambda e, k=k, p=p: e.matmul(p[0:m, 0:c1 - c0], lhsT=lhsT[:, k, msl], rhs=w[:, k, c0:c1],
                                                        start=(k == 0), stop=(k == nk - 1)), reads=[lk, wk], writes=[pk])
            return p, pk

        def attn_groups(nq):
            return [(0, 4), (4, 8)] if nq == 128 else [(0, 8)]

        def attn_z(st_):
            kt_of, ktk, v_ap, vk, q0, nq, ns, diag, first, last = st_
            for gi, (h0, h1) in enumerate(attn_groups(nq)):
                for h in range(h0, h1):
                    S.add("pe", lambda e, h=h: e.matmul(PZ[0:ns, h * nq:(h + 1) * nq], lhsT=kt_of(h), rhs=QT[:, h, q0:q0 + nq], start=True, stop=True),
                          reads=[ktk, "QT"], writes=[f"pz{gi}"])

        def m3(ap_, nh):
            return ap_.rearrange("p (h t) -> p h t", h=nh)

        def attn_el(st_, par):
            kt_of, ktk, v_ap, vk, q0, nq, ns, diag, first, last = st_
            eb, lm = EB[par], LM[par]
            for gi, (h0, h1) in enumerate(attn_groups(nq)):
                c0, c1 = h0 * nq, h1 * nq
                S.add("act", lambda e, c0=c0, c1=c1, eb=eb: e.activation(out=eb[0:ns, c0:c1], in_=PZ[0:ns, c0:c1], func=AF.Exp), reads=[f"pz{gi}"], writes=[f"e{par}_{gi}"])
                S.add("act", lambda e, c0=c0, c1=c1, eb=eb, lm=lm: e.activation(out=lm[0:ns, c0:c1], in_=eb[0:ns, c0:c1], func=AF.Ln, bias=1.0),
                      reads=[f"e{par}_{gi}"], writes=[f"lm{par}_{gi}"])
                if diag:
                    S.add("dve", lambda e, c0=c0, c1=c1, nh=h1 - h0, lm=lm: e.tensor_tensor(out=m3(lm[0:ns, c0:c1], nh), in0=m3(lm[0:ns, c0:c1], nh),
                                                                                         in1=mask01[0:ns, 0:nq].unsqueeze(1).broadcast_to([ns, nh, nq]), op=ALU.mult),
                          reads=[f"lm{par}_{gi}", "mask01"], writes=[f"lm{par}_{gi}"])

        def attn_p2(st_, par):
            kt_of, ktk, v_ap, vk, q0, nq, ns, diag, first, last = st_
            lm = LM[par]
            for gi, (h0, h1) in enumerate(attn_groups(nq)):
                c0, c1 = h0 * nq, h1 * nq
                S.add("pe", lambda e, c0=c0, c1=c1, lm=lm: e.matmul(P2[0:ns, c0:c1], lhsT=negU[0:ns, 0:ns], rhs=lm[0:ns, c0:c1], start=True, stop=False),
                      reads=["negU", f"lm{par}_{gi}"], writes=[f"p2{gi}"])
                if not first:
                    S.add("pe", lambda e, c0=c0, c1=c1: e.matmul(P2[0:ns, c0:c1], lhsT=negO[:, 0:ns], rhs=cum[:, c0:c1], start=False, stop=False),
                          reads=["negO", f"cum{gi}"], writes=[f"p2{gi}"])
                for h in range(h0, h1):
                    S.add("pe", lambda e, h=h, h1=h1: e.matmul(P2[0:ns, h * nq:(h + 1) * nq], lhsT=kt_of(h), rhs=QT[:, h, q0:q0 + nq], start=False, stop=(h == h1 - 1)),
                          reads=[ktk, "QT"], writes=[f"p2{gi}"])

        def attn_w(st_, par):
            kt_of, ktk, v_ap, vk, q0, nq, ns, diag, first, last = st_
            grps = attn_groups(nq)
            for gi, (h0, h1) in enumerate(grps):
                c0, c1 = h0 * nq, h1 * nq
                S.add("act", lambda e, c0=c0, c1=c1: e.activation(out=wbuf[0:ns, c0:c1], in_=P2[0:ns, c0:c1], func=AF.Exp), reads=[f"p2{gi}"], writes=[f"wb_{gi}"])
                if diag:
                    S.add("dve", lambda e, c0=c0, c1=c1, nh=h1 - h0: e.tensor_tensor(out=m3(wbuf[0:ns, c0:c1], nh), in0=m3(wbuf[0:ns, c0:c1], nh),
                                                                                  in1=mask01[0:ns, 0:nq].unsqueeze(1).broadcast_to([ns, nh, nq]), op=ALU.mult),
                          reads=[f"wb_{gi}", "mask01"], writes=[f"wb_{gi}"])

        def attn_wv(st_, par):
            kt_of, ktk, v_ap, vk, q0, nq, ns, diag, first, last = st_
            grps = attn_groups(nq)
            lm = LM[par]
            if first:
                S.add("pe", lambda e: e.matmul(ACC[0:nq, :], lhsT=zer[:, 0:nq], rhs=zer[:, :], start=True, stop=False), reads=["zer"], writes=["acc"])
            for gi, (h0, h1) in enumerate(grps):
                for h in range(h0, h1):
                    S.add("pe", lambda e, h=h: e.matmul(ACC[0:nq, h * 64:(h + 1) * 64], lhsT=wbuf[0:ns, h * nq:(h + 1) * nq], rhs=v_ap[0:ns, h * 64:(h + 1) * 64],
                                                       start=False, stop=(last and h == 7)), reads=[f"wb_{gi}", vk], writes=["acc"])
            if not last:
                for gi, (h0, h1) in enumerate(grps):
                    c0, c1 = h0 * nq, h1 * nq
                    S.add("pool", lambda e, c0=c0, c1=c1, lm=lm: e.tensor_tensor(out=cum[0:ns, c0:c1], in0=cum[0:ns, c0:c1], in1=lm[0:ns, c0:c1], op=ALU.add),
                          reads=[f"cum{gi}", f"lm{par}_{gi}"], writes=[f"cum{gi}"])

        def attn_run(step_iter):
            s0 = next(step_iter, None)
            if s0 is None:
                return
            attn_z(s0)
            attn_el(s0, 0)
            s1 = next(step_iter, None)
            if s1 is not None:
                attn_z(s1)
            cur, nxt, n = s0, s1, 0
            while cur is not None:
                par = n % 2
                attn_p2(cur, par)
                nn = None
                if nxt is not None:
                    attn_el(nxt, 1 - par)
                    nn = next(step_iter, None)
                    if nn is not None:
                        attn_z(nn)
                attn_w(cur, par)
                attn_wv(cur, par)
                cur, nxt, n = nxt, nn, n + 1

        S.add("pool", lambda e: e.memset(mixin[:, 512:1024], 0.0), writes=["mixin"])


        o_sd, state_d, state_conv = T["o_sd"], T["state_d"], T["state_conv"]
        d_conv_w, d_A_log, d_dt_bias, d_norm_g = T["d_conv_w"], T["d_A_log"], T["d_dt_bias"], T["d_norm_g"]
        Uincl = sb("Uincl", [64, 64])
        identf = sb("identf", [64, 64])
        nm_incl = sb("nm_incl", [64, 64])
        nm_low = sb("nm_low", [64, 64])
        ones128 = sb("ones128", [64, 128])
        S.add("pool", lambda e: e.memset(Uincl[:], 1.0), writes=["Uincl"])
        S.add("pool", lambda e: e.affine_select(out=Uincl[:], in_=Uincl[:], compare_op=ALU.is_ge, fill=0.0, base=0, pattern=[[1, 64]], channel_multiplier=-1),
              reads=["Uincl"], writes=["Uincl"])
        S.add("pool", lambda e: e.memset(identf[:], 0.0), writes=["identf"])
        S.add("pool", lambda e: e.affine_select(out=identf[:], in_=identf[:], compare_op=ALU.not_equal, fill=1.0, base=0, pattern=[[-1, 64]], channel_multiplier=1),
              reads=["identf"], writes=["identf"])
        S.add("pool", lambda e: e.memset(nm_incl[:], 0.0), writes=["nm_incl"])
        S.add("pool", lambda e: e.affine_select(out=nm_incl[:], in_=nm_incl[:], compare_op=ALU.is_ge, fill=-30000.0, base=0, pattern=[[1, 64]], channel_multiplier=-1),
              reads=["nm_incl"], writes=["nm_incl"])
        S.add("pool", lambda e: e.memset(nm_low[:], 0.0), writes=["nm_low"])
        S.add("pool", lambda e: e.affine_select(out=nm_low[:], in_=nm_low[:], compare_op=ALU.is_gt, fill=-30000.0, base=0, pattern=[[-1, 64]], channel_multiplier=1),
              reads=["nm_low"], writes=["nm_low"])
        S.add("pool", lambda e: e.memset(ones128[:], 1.0), writes=["ones128"])
        Sh = sb("Sh", [64, 3, 64], BF16)
        ShP = sb("ShP", [64, 3, 64], BF16)
        S.add("pool", lambda e: e.memset(Sh[:], 0.0), writes=["Sh"])
        S.add("pool", lambda e: e.memset(ShP[:], 0.0), writes=["ShP"])
        for i_ in range(1, 4):
            S.add("dve", lambda e, i_=i_: e.tensor_copy(out=Sh[:, i_ - 1, i_:64], in_=ident[0:64, 0:64 - i_]), reads=["ident", "Sh"], writes=["Sh"])
            S.add("dve", lambda e, i_=i_: e.tensor_copy(out=ShP[:, i_ - 1, 0:i_], in_=ident[0:64, 64 - i_:64]), reads=["ident", "ShP"], writes=["ShP"])
        cwb = sb("cwb", [64, 4, 1536], BF16)
        S.add("pool", lambda e: e.dma_start(out=cwb[:].rearrange("p a b -> p (a b)"), in_=d_conv_w[0].rearrange("a b -> (a b)").partition_broadcast(64)), writes=["cwb"], is_dma=True)
        dtb = sb("dtb", [64, 4])
        negA = sb("negA", [64, 4])
        gdn_g = sb("gdn_g", [64, 128])
        S.add("sp", lambda e: e.dma_start(out=dtb[:], in_=d_dt_bias[0].partition_broadcast(64)), writes=["dtb"], is_dma=True)
        S.add("sp", lambda e: e.dma_start(out=negA[:], in_=d_A_log[0].partition_broadcast(64)), writes=["negA"], is_dma=True)
        S.add("sp", lambda e: e.dma_start(out=gdn_g[:], in_=d_norm_g[0].partition_broadcast(64)), writes=["gdn_g"], is_dma=True)
        S.add("act", lambda e: e.activation(out=negA[:], in_=negA[:], func=AF.Exp), reads=["negA"], writes=["negA"])
        S.add("dve", lambda e: e.tensor_scalar(out=negA[:], in0=negA[:], scalar1=-1.0, scalar2=None, op0=ALU.mult), reads=["negA"], writes=["negA"])

        xraw = arena[0:64, 0:1536]
        XB = Rot([sb(f"xbf{i}", [64, 1536], BF16) for i in range(2)], "xbf")
        xc = arena[0:64, 1536:3072]
        gtmp = sb("gtmp", [64, 512])
        zdc = sb("zdc", [64, 512])
        ab = sb("ab", [64, 8])
        ssn = sb("ssn", [64, 8])
        gg = sb("gg", [64, 4])
        beta = sb("beta", [64, 4])
        Gcol = sb("Gcol", [64, 4])
        eG = sb("eG", [64, 4])
        dlast = sb("dlast", [64, 4])
        egl = sb("egl", [128, 4])
        D1 = sb("D1", [64, 4, 64])
        D2 = sb("D2", [64, 4, 64])
        Mm = sb("Mm", [64, 4, 64])
        Lm = sb("Lm", [64, 4, 64])
        Xm = sb("Xm", [64, 4, 64])
        Pm = sb("Pm", [64, 4, 64])
        PTm = D2
        gB = Pm
        TTb = sb("TTb", [64, 4, 64], BF16)
        ATb = sb("ATb", [64, 4, 64], BF16)
        knb = sb("knb", [64, 512], BF16)
        kbb = sb("kbb", [64, 512], BF16)
        kbgb = sb("kbgb", [64, 512], BF16)
        kdecb = sb("kdecb", [64, 512], BF16)
        qnb = sb("qnb", [64, 512], BF16)
        qgb = sb("qgb", [64, 512], BF16)
        bvb = sb("bvb", [64, 512], BF16)
        trT = sb("trT", [128, 16, 64], BF16)
        Usb = zcs[0:64, :].rearrange("p (h d) -> p h d", h=4)
        WmT = sb("WmT", [128, 4, 64], BF16)
        dlt = sb("dlt", [64, 4, 128], BF16)
        osb = cout[0:64, :]
        dob = sb("dob", [64, 512], BF16)
        Sst = sb("Sst", [128, 4, 128])
        Sbf = sb("Sbf", [128, 4, 128], BF16)
        GA, GB_, GC = PZ, P2, ACC
        gstate = {"prev": None}

        def lin64(cs, c0, c1):
            return linear(hnT, "hnT", 8, w_in1, "w_in1", c0, c1, msl=slice(cs, cs + 64))

        def gdn_chunk(cs, seq_start, seq_end, sidx, conv_out_idx, init_state_idx, row0):
            xb, xbk = XB.next()
            for c3 in range(3):
                p, pk = lin64(cs, 2048 + c3 * 512, 2560 + c3 * 512)
                S.add("dve", lambda e, p=p, c3=c3: e.tensor_copy(out=xraw[:, c3 * 512:(c3 + 1) * 512], in_=p[0:64, :]), reads=[pk], writes=["xraw"])
                S.add("act", lambda e, p=p, c3=c3, xb=xb: e.copy(out=xb[:, c3 * 512:(c3 + 1) * 512], in_=p[0:64, :]), reads=[pk], writes=[xbk])
            p, pk = lin64(cs, 3584, 4096)
            S.add("act", lambda e, p=p: e.activation(out=zdc[:], in_=p[0:64, :], func=AF.Silu), reads=[pk], writes=["zdc"])
            p, pk = lin64(cs, 4096, 4104)
            S.add("dve", lambda e, p=p: e.tensor_copy(out=ab[:], in_=p[0:64, 0:8]), reads=[pk], writes=["ab"])
            if conv_out_idx is not None:
                S.add("pool", lambda e: e.dma_start(out=o_conv[conv_out_idx], in_=xraw[61:64, :]), reads=["xraw"], writes=["o_conv"], is_dma=True)
            if seq_start:
                if init_state_idx is None:
                    xp, xpk = None, None
                else:
                    xp, xpk = XB.next()
                    S.add("pool", lambda e, xp=xp: e.dma_start(out=xp[61:64, :], in_=state_conv[init_state_idx]), writes=[xpk], is_dma=True)
            else:
                xp, xpk = gstate["prev"]
            gstate["prev"] = (xb, xbk)
            S.add("dve", lambda e: e.tensor_tensor(out=xc, in0=xraw, in1=cwb[:, 3, :], op=ALU.mult), reads=["xraw", "cwb"], writes=["xc"])
            for i_ in range(1, 4):
                for c3 in range(3):
                    cs3 = slice(c3 * 512, (c3 + 1) * 512)
                    p, pk = PS.next()
                    S.add("pe", lambda e, p=p, i_=i_, cs3=cs3, xb=xb: e.matmul(p[0:64, :], lhsT=Sh[:, i_ - 1, :], rhs=xb[:, cs3], start=True, stop=(xp is None)),
                          reads=["Sh", xbk], writes=[pk])
                    if xp is not None:
                        S.add("pe", lambda e, p=p, i_=i_, cs3=cs3, xp=xp: e.matmul(p[0:64, :], lhsT=ShP[:, i_ - 1, :], rhs=xp[:, cs3], start=False, stop=True),
                              reads=["ShP", xpk], writes=[pk])
                    S.add("dve", lambda e, p=p, i_=i_, cs3=cs3: e.tensor_tensor(out=gtmp[:], in0=p[0:64, :], in1=cwb[:, 3 - i_, cs3], op=ALU.mult),
                          reads=[pk, "cwb"], writes=["gtmp"])
                    S.add("pool", lambda e, cs3=cs3: e.tensor_tensor(out=xc[:, cs3], in0=xc[:, cs3], in1=gtmp[:], op=ALU.add), reads=["xc", "gtmp"], writes=["xc"])
            S.add("act", lambda e: e.activation(out=xc, in_=xc, func=AF.Silu), reads=["xc"], writes=["xc"])
            for j in range(8):
                S.add("act", lambda e, j=j: e.activation(out=gtmp[:, 0:128], in_=xc[:, j * 128:(j + 1) * 128], func=AF.Square, accum_out=ssn[:, j:j + 1]),
                      reads=["xc"], writes=["gtmp", "ssn"])
            S.add("dve", lambda e: e.tensor_scalar(out=ssn[:], in0=ssn[:], scalar1=EPS, scalar2=None, op0=ALU.add), reads=["ssn"], writes=["ssn"])
            S.add("pool", lambda e: e.tensor_tensor(out=ssn[:], in0=ssn[:], in1=mhalf[0:64, :].broadcast_to([64, 8]), op=ALU.pow), reads=["ssn", "mhalf"], writes=["ssn"])
            S.add("dve", lambda e: e.tensor_scalar(out=ssn[:, 0:4], in0=ssn[:, 0:4], scalar1=128.0 ** -0.5, scalar2=None, op0=ALU.mult), reads=["ssn"], writes=["ssn"])
            S.add("dve", lambda e: e.tensor_tensor(out=gg[:], in0=ab[:, 0:4], in1=dtb[:], op=ALU.add), reads=["ab", "dtb"], writes=["gg"])
            S.add("act", lambda e: e.activation(out=gg[:], in_=gg[:], func=AF.Exp), reads=["gg"], writes=["gg"])
            S.add("act", lambda e: e.activation(out=gg[:], in_=gg[:], func=AF.Ln, bias=1.0), reads=["gg"], writes=["gg"])
            S.add("dve", lambda e: e.tensor_tensor(out=gg[:], in0=gg[:], in1=negA[:], op=ALU.mult), reads=["gg", "negA"], writes=["gg"])
            S.add("act", lambda e: e.activation(out=beta[:], in_=ab[:, 4:8], func=AF.Sigmoid), reads=["ab"], writes=["beta"])
            S.add("pe", lambda e: e.matmul(GC[0:64, 0:4], lhsT=Uincl[:], rhs=gg[:], start=True, stop=True), reads=["Uincl", "gg"], writes=["acc"])
            S.add("pe", lambda e: e.matmul(GC[:, 8:12], lhsT=ones128[:], rhs=gg[:], start=True, stop=True), reads=["ones128", "gg"], writes=["acc"])
            S.add("dve", lambda e: e.tensor_copy(out=Gcol[:], in_=GC[0:64, 0:4]), reads=["acc"], writes=["Gcol"])
            S.add("act", lambda e: e.activation(out=egl[:], in_=GC[:, 8:12], func=AF.Exp), reads=["acc"], writes=["egl"])
            S.add("dve", lambda e: e.tensor_tensor(out=dlast[:], in0=GC[0:64, 8:12], in1=Gcol[:], op=ALU.subtract), reads=["acc", "Gcol"], writes=["dlast"])
            S.add("act", lambda e: e.activation(out=dlast[:], in_=dlast[:], func=AF.Exp), reads=["dlast"], writes=["dlast"])
            S.add("act", lambda e: e.activation(out=eG[:], in_=Gcol[:], func=AF.Exp), reads=["Gcol"], writes=["eG"])
            S.add("dve", lambda e: e.tensor_copy(out=gB[:], in_=gg[:].unsqueeze(2).broadcast_to([64, 4, 64])), reads=["gg"], writes=["gB"])
            for hh in range(4):
                S.add("pe", lambda e, hh=hh: e.matmul(GA[0:64, hh * 64:(hh + 1) * 64], lhsT=gB[:, hh, :], rhs=Uincl[:], start=True, stop=True),
                      reads=["gB", "Uincl"], writes=["pz"])
            for hh in range(4):
                S.add("dve", lambda e, hh=hh: e.scalar_tensor_tensor(out=D1[:, hh, :], in0=GA[0:64, hh * 64:(hh + 1) * 64], scalar=Gcol[:, hh:hh + 1], in1=nm_incl[:],
                                                                  op0=ALU.subtract, op1=ALU.add), reads=["pz", "Gcol", "nm_incl"], writes=["D1"])
                S.add("dve", lambda e, hh=hh: e.tensor_scalar(out=D2[:, hh, :], in0=GA[0:64, hh * 64:(hh + 1) * 64], scalar1=Gcol[:, hh:hh + 1], scalar2=-1.0,
                                                           op0=ALU.subtract, op1=ALU.mult), reads=["pz", "Gcol"], writes=["D2"])
            S.add("dve", lambda e: e.tensor_tensor(out=D2[:], in0=D2[:], in1=nm_low[:].unsqueeze(1).broadcast_to([64, 4, 64]), op=ALU.add), reads=["D2", "nm_low"], writes=["D2"])
            S.add("act", lambda e: e.activation(out=D1[:], in_=D1[:], func=AF.Exp), reads=["D1"], writes=["D1"])
            S.add("act", lambda e: e.activation(out=D2[:], in_=D2[:], func=AF.Exp), reads=["D2"], writes=["D2"])
            xq = xc[:, 0:512].rearrange("p (h d) -> p h d", h=4)
            xk = xc[:, 512:1024].rearrange("p (h d) -> p h d", h=4)
            xv = xc[:, 1024:1536].rearrange("p (h d) -> p h d", h=4)
            def bc(t_, lo):
                return t_[:, lo:lo + 4].unsqueeze(2).broadcast_to([64, 4, 128])
            def v3(t_):
                return t_[:].rearrange("p (h d) -> p h d", h=4)
            S.add("dve", lambda e: e.tensor_tensor(out=v3(qnb), in0=xq, in1=bc(ssn, 0), op=ALU.mult), reads=["xc", "ssn"], writes=["qnb"])
            S.add("dve", lambda e: e.tensor_tensor(out=v3(knb), in0=xk, in1=bc(ssn, 4), op=ALU.mult), reads=["xc", "ssn"], writes=["knb"])
            S.add("dve", lambda e: e.tensor_tensor(out=v3(bvb), in0=xv, in1=bc(beta, 0), op=ALU.mult), reads=["xc", "beta"], writes=["bvb"])
            S.add("pool", lambda e: e.tensor_tensor(out=v3(kbb), in0=v3(knb), in1=bc(beta, 0), op=ALU.mult), reads=["knb", "beta"], writes=["kbb"])
            S.add("pool", lambda e: e.tensor_tensor(out=v3(qgb), in0=v3(qnb), in1=bc(eG, 0), op=ALU.mult), reads=["qnb", "eG"], writes=["qgb"])
            S.add("pool", lambda e: e.tensor_tensor(out=v3(kbgb), in0=v3(kbb), in1=bc(eG, 0), op=ALU.mult), reads=["kbb", "eG"], writes=["kbgb"])
            S.add("pool", lambda e: e.tensor_tensor(out=v3(kdecb), in0=v3(knb), in1=bc(dlast, 0), op=ALU.mult), reads=["knb", "dlast"], writes=["kdecb"])
            pt, ptk = PT.next()
            ptv = pt[:].rearrange("p a b -> p (a b)").rearrange("p (s c) -> p s c", c=64)
            for gi, (src, srck) in enumerate(((knb, "knb"), (kbb, "kbb"), (qnb, "qnb"), (qgb, "qgb"))):
                for hh in range(4):
                    S.add("pe", lambda e, hh=hh, gi=gi, src=src: e.transpose(out=ptv[:, gi * 4 + hh, :], in_=src[:, hh * 128:(hh + 1) * 128], identity=ident[0:64, 0:64]),
                          reads=[srck, "ident"], writes=[ptk])
            S.add("act", lambda e: e.copy(out=trT[:], in_=ptv), reads=[ptk], writes=["trT"])
            knT = lambda hh: trT[:, hh, :]
            kbT = lambda hh: trT[:, 4 + hh, :]
            qnT = lambda hh: trT[:, 8 + hh, :]
            qgT = lambda hh: trT[:, 12 + hh, :]
            for hh in range(4):
                S.add("pe", lambda e, hh=hh: e.matmul(GA[0:64, hh * 64:(hh + 1) * 64], lhsT=knT(hh), rhs=kbT(hh), start=True, stop=True), reads=["trT"], writes=["pz"])
                S.add("pe", lambda e, hh=hh: e.matmul(GA[0:64, 256 + hh * 64:256 + (hh + 1) * 64], lhsT=kbT(hh), rhs=knT(hh), start=True, stop=True), reads=["trT"], writes=["pz"])
                S.add("pe", lambda e, hh=hh: e.matmul(GA[0:64, 512 + hh * 64:512 + (hh + 1) * 64], lhsT=knT(hh), rhs=qnT(hh), start=True, stop=True), reads=["trT"], writes=["pz"])
            GA3 = lambda o_: GA[0:64, o_:o_ + 256].rearrange("p (h t) -> p h t", h=4)
            S.add("dve", lambda e: e.tensor_tensor(out=ATb[:], in0=GA3(512), in1=D1[:], op=ALU.mult), reads=["pz", "D1"], writes=["ATb"])
            S.add("dve", lambda e: e.tensor_tensor(out=Mm[:], in0=GA3(0), in1=D1[:], op=ALU.mult), reads=["pz", "D1"], writes=["Mm"])
            S.add("dve", lambda e: e.tensor_tensor(out=Mm[:], in0=Mm[:], in1=mask01[0:64, 0:64].unsqueeze(1).broadcast_to([64, 4, 64]), op=ALU.mult), reads=["Mm", "mask01"], writes=["Mm"])
            S.add("dve", lambda e: e.tensor_tensor(out=Lm[:], in0=GA3(256), in1=D2[:], op=ALU.mult), reads=["pz", "D2"], writes=["Lm"])
            S.add("dve", lambda e: e.tensor_tensor(out=Xm[:], in0=identf[:].unsqueeze(1).broadcast_to([64, 4, 64]), in1=Mm[:], op=ALU.subtract), reads=["identf", "Mm"], writes=["Xm"])
            for hh in range(4):
                S.add("pe", lambda e, hh=hh: e.matmul(GB_[0:64, hh * 64:(hh + 1) * 64], lhsT=Lm[:, hh, :], rhs=Mm[:, hh, :], start=True, stop=True), reads=["Lm", "Mm"], writes=["p2"])
                S.add("pe", lambda e, hh=hh: e.matmul(GB_[0:64, 256 + hh * 64:256 + (hh + 1) * 64], lhsT=Mm[:, hh, :], rhs=Lm[:, hh, :], start=True, stop=True), reads=["Lm", "Mm"], writes=["p2"])
            GB3 = lambda o_: GB_[0:64, o_:o_ + 256].rearrange("p (h t) -> p h t", h=4)
            S.add("dve", lambda e: e.tensor_copy(out=Pm[:], in_=GB3(0)), reads=["p2"], writes=["Pm"])
            S.add("act", lambda e: e.copy(out=PTm[:], in_=GB3(256)), reads=["p2"], writes=["PTm"])
            for lvl in range(5):
                for hh in range(4):
                    S.add("pe", lambda e, hh=hh: e.matmul(GB_[0:64, 512 + hh * 64:512 + (hh + 1) * 64], lhsT=PTm[:, hh, :], rhs=Xm[:, hh, :], start=True, stop=True), reads=["PTm", "Xm"], writes=["p2"])
                    if lvl < 4:
                        S.add("pe", lambda e, hh=hh: e.matmul(GB_[0:64, hh * 64:(hh + 1) * 64], lhsT=PTm[:, hh, :], rhs=Pm[:, hh, :], start=True, stop=True), reads=["PTm", "Pm"], writes=["p2"])
                        S.add("pe", lambda e, hh=hh: e.matmul(GB_[0:64, 256 + hh * 64:256 + (hh + 1) * 64], lhsT=Pm[:, hh, :], rhs=PTm[:, hh, :], start=True, stop=True), reads=["PTm", "Pm"], writes=["p2"])
                S.add("dve", lambda e: e.tensor_tensor(out=Xm[:], in0=Xm[:], in1=GB3(512), op=ALU.add), reads=["Xm", "p2"], writes=["Xm"])
                if lvl < 4:
                    S.add("dve", lambda e: e.tensor_copy(out=Pm[:], in_=GB3(0)), reads=["p2"], writes=["Pm"])
                    S.add("act", lambda e: e.copy(out=PTm[:], in_=GB3(256)), reads=["p2"], writes=["PTm"])
            S.add("act", lambda e: e.copy(out=TTb[:], in_=Xm[:]), reads=["Xm"], writes=["TTb"])
            for hh in range(4):
                S.add("pe", lambda e, hh=hh: e.matmul(GA[0:64, hh * 128:(hh + 1) * 128], lhsT=TTb[:, hh, :], rhs=bvb[:, hh * 128:(hh + 1) * 128], start=True, stop=True),
                      reads=["TTb", "bvb"], writes=["pz"])
                S.add("pe", lambda e, hh=hh: e.matmul(GB_[:, hh * 64:(hh + 1) * 64], lhsT=kbgb[:, hh * 128:(hh + 1) * 128], rhs=TTb[:, hh, :], start=True, stop=True),
                      reads=["TTb", "kbgb"], writes=["p2"])
            S.add("dve", lambda e: e.tensor_copy(out=Usb, in_=GA[0:64, 0:512].rearrange("p (h d) -> p h d", h=4)), reads=["pz"], writes=["Usb"])
            S.add("act", lambda e: e.copy(out=WmT[:], in_=GB_[:, 0:256].rearrange("p (h t) -> p h t", h=4)), reads=["p2"], writes=["WmT"])
            if seq_start:
                if init_state_idx is None:
                    S.add("pool", lambda e: e.memset(Sst[:], 0.0), writes=["Sst"])
                else:
                    S.add("sp", lambda e: e.dma_start(out=Sst[:], in_=state_d[init_state_idx].rearrange("h d e -> d h e")), writes=["Sst"], is_dma=True)
                S.add("act", lambda e: e.copy(out=Sbf[:], in_=Sst[:]), reads=["Sst"], writes=["Sbf"])
            for hh in range(4):
                S.add("pe", lambda e, hh=hh: e.matmul(GA[0:64, hh * 128:(hh + 1) * 128], lhsT=WmT[:, hh, :], rhs=Sbf[:, hh, :], start=True, stop=True), reads=["WmT", "Sbf"], writes=["pz"])
            S.add("dve", lambda e: e.tensor_tensor(out=dlt[:], in0=Usb, in1=GA[0:64, 0:512].rearrange("p (h d) -> p h d", h=4), op=ALU.subtract), reads=["Usb", "pz"], writes=["dlt"])
            for hh in range(4):
                S.add("pe", lambda e, hh=hh: e.matmul(GB_[0:64, hh * 128:(hh + 1) * 128], lhsT=qgT(hh), rhs=Sbf[:, hh, :], start=True, stop=False), reads=["trT", "Sbf"], writes=["p2"])
                S.add("pe", lambda e, hh=hh: e.matmul(GB_[0:64, hh * 128:(hh + 1) * 128], lhsT=ATb[:, hh, :], rhs=dlt[:, hh, :], start=False, stop=True), reads=["ATb", "dlt"], writes=["p2"])
                S.add("pe", lambda e, hh=hh: e.matmul(GC[:, hh * 128:(hh + 1) * 128], lhsT=kdecb[:, hh * 128:(hh + 1) * 128], rhs=dlt[:, hh, :], start=True, stop=True), reads=["kdecb", "dlt"], writes=["acc"])
            S.add("dve", lambda e: e.tensor_copy(out=osb, in_=GB_[0:64, 0:512]), reads=["p2"], writes=["osb"])
            for hh in range(4):
                S.add("dve", lambda e, hh=hh: e.scalar_tensor_tensor(out=Sst[:, hh, :], in0=Sst[:, hh, :], scalar=egl[:, hh:hh + 1], in1=GC[:, hh * 128:(hh + 1) * 128],
                                                                  op0=ALU.mult, op1=ALU.add), reads=["Sst", "egl", "acc"], writes=["Sst"])
            S.add("act", lambda e: e.copy(out=Sbf[:], in_=Sst[:]), reads=["Sst"], writes=["Sbf"])
            if seq_end:
                S.add("pool", lambda e: e.dma_start(out=o_sd[sidx].rearrange("h d e -> d h e"), in_=Sst[:]), reads=["Sst"], writes=["o_sd"], is_dma=True)
            for hh in range(4):
                S.add("act", lambda e, hh=hh: e.activation(out=gtmp[:, 0:128], in_=osb[:, hh * 128:(hh + 1) * 128], func=AF.Square, accum_out=ssn[:, hh:hh + 1]),
                      reads=["osb"], writes=["gtmp", "ssn"])
            S.add("dve", lambda e: e.tensor_scalar(out=ssn[:, 0:4], in0=ssn[:, 0:4], scalar1=1.0 / 128, scalar2=EPS, op0=ALU.mult, op1=ALU.add), reads=["ssn"], writes=["ssn"])
            S.add("pool", lambda e: e.tensor_tensor(out=ssn[:, 0:4], in0=ssn[:, 0:4], in1=mhalf[0:64, :].broadcast_to([64, 4]), op=ALU.pow), reads=["ssn", "mhalf"], writes=["ssn"])
            S.add("dve", lambda e: e.tensor_tensor(out=v3(osb), in0=v3(osb), in1=bc(ssn, 0), op=ALU.mult), reads=["osb", "ssn"], writes=["osb"])
            S.add("dve", lambda e: e.tensor_tensor(out=v3(osb), in0=v3(osb), in1=gdn_g[:].unsqueeze(1).broadcast_to([64, 4, 128]), op=ALU.mult), reads=["osb", "gdn_g"], writes=["osb"])
            S.add("dve", lambda e: e.tensor_tensor(out=dob[:], in0=osb, in1=zdc[:], op=ALU.mult), reads=["osb", "zdc"], writes=["dob"])
            S.add("pool", lambda e: e.dma_start(out=mixin[row0:row0 + 64, 512:1024], in_=dob[:]), reads=["dob"], writes=["mixin"], is_dma=True)

        def gdn_tile(i, samp, h, hk):
            if samp:
                for s_ in range(2):
                    gdn_chunk(s_ * 64, True, True, 1 + s_, 1 + s_, s_, s_ * 64)
            else:
                for c_ in range(2):
                    first = (i == 0 and c_ == 0)
                    lastc = (i == ntp - 1 and c_ == 1)
                    gdn_chunk(c_ * 64, first, lastc, 0, 0 if lastc else None, None, c_ * 64)

        def load_tile(i):
            h, hk = H.next()
            p1, p1k = P1.next()
            S.add("sp", lambda e: e.dma_start(out=h[:], in_=h1_scr[i * 128:(i + 1) * 128, :]), reads=["h1_scr"], writes=[hk], is_dma=True)
            S.add("sp", lambda e: e.dma_start(out=p1[:], in_=pin[1, i * 128:(i + 1) * 128, :]), writes=[p1k], is_dma=True)
            return h, hk, p1, p1k

        for i in range(nt):
            h, hk, p1, p1k = load_tile(i)
            samp = (i == ntp)
            r0 = i * 128
            rmsnorm(h, hk, g_l1, "g_l1", xn, "xn")
            transpose_to(xn, "xn", 8, hnT, "hnT")
            p, pk = linear(hnT, "hnT", 8, w_in1, "w_in1", 0, 512)
            S.add("dve", lambda e, p=p: e.tensor_scalar(out=qbf[:], in0=p[:], scalar1=0.125, scalar2=None, op0=ALU.mult), reads=[pk], writes=["qbf"])
            p, pk = linear(hnT, "hnT", 8, w_in1, "w_in1", 512, 1024)
            S.add("act", lambda e, p=p: e.copy(out=kf, in_=p[:]), reads=[pk], writes=["kf"])
            S.add("dve", lambda e, p=p: e.tensor_copy(out=kbf[:], in_=p[:]), reads=[pk], writes=["kbf"])
            S.add("pool", lambda e, r0=r0: e.dma_start(out=o_kc[r0:r0 + 128, :], in_=kf), reads=["kf"], writes=["o_kc"], is_dma=True)
            p, pk = linear(hnT, "hnT", 8, w_in1, "w_in1", 1024, 1536)
            S.add("act", lambda e, p=p: e.copy(out=vf, in_=p[:]), reads=[pk], writes=["vf"])
            S.add("dve", lambda e, p=p: e.tensor_copy(out=vbf[:], in_=p[:]), reads=[pk], writes=["vbf"])
            S.add("pool", lambda e, r0=r0: e.dma_start(out=o_vc[r0:r0 + 128, :], in_=vf), reads=["vf"], writes=["o_vc"], is_dma=True)
            p, pk = linear(hnT, "hnT", 8, w_in1, "w_in1", 1536, 2048)
            S.add("act", lambda e, p=p: e.activation(out=zcs[:], in_=p[:], func=AF.Silu), reads=[pk], writes=["zcs"])
            if samp:
                p, pk = linear(hnT, "hnT", 8, w_in1, "w_in1", 1024, 1536, msl=slice(64, 128))
                S.add("dve", lambda e, p=p: e.tensor_copy(out=vbs[:], in_=p[0:64, :]), reads=[pk], writes=["vbs"])
            pt, ptk = PT.next()
            for k in range(4):
                S.add("pe", lambda e, k=k, pt=pt: e.transpose(out=pt[:, k, :], in_=qbf[:, k * 128:(k + 1) * 128], identity=ident[:]),
                      reads=["qbf", "ident"], writes=[ptk])
            QTv = QT[:].rearrange("p (a two) t -> p a two t", two=2)
            S.add("act", lambda e, pt=pt, QTv=QTv: e.copy(out=QTv[0:64, :, 0, :], in_=pt[0:64, 0:4, :]), reads=[ptk], writes=["QT"])
            S.add("act", lambda e, pt=pt, QTv=QTv: e.copy(out=QTv[64:128, :, 1, :], in_=pt[64:128, 0:4, :]), reads=[ptk], writes=["QT"])
            transpose_to(kbf, "kbf", 4, KTc, "KTc")
            if not samp:
                S.add("pool", lambda e, r0=r0: e.dma_start(out=kts[r0 // 128], in_=KTc[:].rearrange("p a b -> p (a b)")), reads=["KTc"], writes=[f"kts{i}"], is_dma=True)
                S.add("pool", lambda e, r0=r0: e.dma_start(out=vs[r0:r0 + 128, :], in_=vbf[:]), reads=["vbf"], writes=[f"vs{i}"], is_dma=True)
            S.add("pool", lambda e: e.memset(cum, 0.0), writes=["cum"])
            if not samp:
                def prompt_steps(i=i):
                    yield (lambda h: KTc[:, h // 2, :], "KTc", vbf, "vbf", 0, 128, 128, True, True, i == 0)
                    for kb in range(i - 1, -1, -1):
                        ktb, ktbk = KB.next()
                        vb, vbk = VB.next()
                        S.add("sp", lambda e, ktb=ktb, kb=kb: e.dma_start(out=ktb[:].rearrange("p a b -> p (a b)"), in_=kts[kb]), reads=[f"kts{kb}"], writes=[ktbk], is_dma=True)
                        S.add("sp", lambda e, vb=vb, kb=kb: e.dma_start(out=vb[:], in_=vs[kb * 128:(kb + 1) * 128, :]), reads=[f"vs{kb}"], writes=[vbk], is_dma=True)
                        yield (lambda h, ktb=ktb: ktb[:, h // 2, :], ktbk, vb, vbk, 0, 128, 128, False, False, kb == 0)
                attn_run(prompt_steps())
                S.add("dve", lambda e: e.tensor_tensor(out=cout[:], in0=ACC[:], in1=zcs[:], op=ALU.mult), reads=["acc", "zcs"], writes=["cout"])
            else:
                for s_ in range(2):
                    if s_ == 1:
                        S.add("pool", lambda e: e.memset(cum, 0.0), writes=["cum"])
                    vnew, vnk = (vbf, "vbf") if s_ == 0 else (vbs, "vbs")
                    def samp_steps(s_=s_, vnew=vnew, vnk=vnk):
                        yield (lambda h: KTc[:, h // 2, s_ * 64:(s_ + 1) * 64], "KTc", vnew, vnk, s_ * 64, 64, 64, True, True, False)
                        for kb in range(31, -1, -1):
                            ck, ckk = CK.next()
                            vb, vbk = VB.next()
                            ktb, ktbk = KB.next()
                            S.add("pool", lambda e, ck=ck, kb=kb: e.dma_start(out=ck[:], in_=cache_k[s_, kb * 128:(kb + 1) * 128, :]), writes=[ckk], is_dma=True)
                            S.add("pool", lambda e, vb=vb, kb=kb: e.dma_start(out=vb[:], in_=cache_v[s_, kb * 128:(kb + 1) * 128, :]), writes=[vbk], is_dma=True)
                            transpose_to(ck, ckk, 4, ktb, ktbk)
                            yield (lambda h, ktb=ktb: ktb[:, h // 2, :], ktbk, vb, vbk, s_ * 64, 64, 128, False, False, kb == 0)
                    attn_run(samp_steps())
                    if s_ == 0:
                        S.add("dve", lambda e: e.tensor_tensor(out=cout[0:64, :], in0=ACC[0:64, :], in1=zcs[0:64, :], op=ALU.mult), reads=["acc", "zcs"], writes=["cout"])
                    else:
                        S.add("dve", lambda e: e.tensor_copy(out=ebuf[0:64, 0:512], in_=ACC[0:64, :]), reads=["acc"], writes=["ebuf"])
                        S.add("pool", lambda e: e.dma_start(out=cout[64:128, :], in_=ebuf[0:64, 0:512]), reads=["ebuf"], writes=["cout"], is_dma=True)
                        S.add("dve", lambda e: e.tensor_tensor(out=cout[64:128, :], in0=cout[64:128, :], in1=zcs[64:128, :], op=ALU.mult), reads=["cout", "zcs"], writes=["cout"])
            if debug_h:
                S.add("pool", lambda e, r0=r0: e.dma_start(out=dbg2[r0:r0 + 128, :], in_=cout[:]), reads=["cout"], writes=["dbg2"], is_dma=True)
            S.add("act", lambda e: e.copy(out=mixin[:, 0:512], in_=cout[:]), reads=["cout"], writes=["mixin"])
            gdn_tile(i, samp, h, hk)
            transpose_to(mixin, "mixin", 8, mixT, "hnT")
            for hh in range(2):
                p, pk = linear(mixT, "hnT", 8, w_out1, "w_out1", hh * 512, (hh + 1) * 512)
                S.add("dve", lambda e, p=p, hh=hh, h=h: e.tensor_tensor(out=h[:, hh * 512:(hh + 1) * 512], in0=h[:, hh * 512:(hh + 1) * 512], in1=p[:], op=ALU.add),
                      reads=[pk, hk], writes=[hk])
            rmsnorm(h, hk, g_p1, "g_p1", xn, "xn")
            transpose_to(xn, "xn", 8, hnT, "hnT")
            S.add("act", lambda e, p1=p1: e.copy(out=pbf[:], in_=p1[:]), reads=[p1k], writes=["pbf"])
            transpose_to(pbf, "pbf", 2, pT, "pT")
            for hh in range(2):
                p, pk = linear(hnT, "hnT", 8, w_gate1, "w_gate1", hh * 512, (hh + 1) * 512)
                S.add("act", lambda e, p=p, hh=hh: e.activation(out=gate[:, hh * 512:(hh + 1) * 512], in_=p[:], func=AF.Sigmoid), reads=[pk], writes=["gate"])
                p, pk = linear(pT, "pT", 2, w_pp1, "w_pp1", hh * 512, (hh + 1) * 512)
                S.add("dve", lambda e, p=p, hh=hh: e.tensor_tensor(out=gate[:, hh * 512:(hh + 1) * 512], in0=gate[:, hh * 512:(hh + 1) * 512], in1=p[:], op=ALU.mult),
                      reads=[pk, "gate"], writes=["gate"])
            S.add("dve", lambda e, h=h: e.tensor_tensor(out=h[:], in0=h[:], in1=gate[:], op=ALU.add), reads=[hk, "gate"], writes=[hk])
            rmsnorm(h, hk, g_f, "g_f", yout, "ebuf")
            S.add("pool", lambda e, r0=r0: e.dma_start(out=o_y[r0:r0 + 128, :], in_=yout), reads=["ebuf"], writes=["o_y"], is_dma=True)
        S.emit()
```
